# Optimizing a Trainium2 kernel written in Bass

```python
import jax
import jax.numpy as jnp
from jax import lax
import numpy as np

D_MODEL = 1024
BATCH = 8
SEQ = 2048
DEPTH = 2
DEC_BATCH = 128
DEC_SEQ = 1
PAST_LEN = 16384
PAGE_SIZE = 128

N_EVEN = (DEPTH + 1) // 2
N_ODD = DEPTH // 2
EPS = 1e-6

A_WIDTH = D_MODEL
A_HEAD_DIM = 64
A_HEADS = A_WIDTH // A_HEAD_DIM
A_GROUPS = 2
A_STATE = 128
A_CONV = 4
A_CONV_DIM = A_WIDTH + 2 * A_GROUPS * A_STATE
SSD_CHUNK = 128

B_WIDTH = D_MODEL
B_CHUNK = 128
B_GROUPS = B_WIDTH // 128

C_WIDTH = D_MODEL
C_CONV = 31

D_WIDTH = D_MODEL
D_HEADS = 16
D_BLOCK = D_WIDTH // D_HEADS
D_CONV = 4
LRU_C = 8.0

EVEN_IN = 2 * A_WIDTH + 2 * A_GROUPS * A_STATE + A_HEADS + 3 * B_WIDTH
EVEN_MIX = A_WIDTH + B_WIDTH
EVEN_SPLITS = [A_WIDTH, A_WIDTH + A_CONV_DIM, A_WIDTH + A_CONV_DIM + A_HEADS,
               A_WIDTH + A_CONV_DIM + A_HEADS + B_WIDTH,
               A_WIDTH + A_CONV_DIM + A_HEADS + 2 * B_WIDTH]
ODD_IN = 3 * C_WIDTH + 2 * D_WIDTH
ODD_MIX = C_WIDTH + D_WIDTH
ODD_SPLITS = [C_WIDTH, 2 * C_WIDTH, 3 * C_WIDTH, 3 * C_WIDTH + D_WIDTH]

kernel_name = 'hybrid_ssd_gmlp_conformer_rglru_step'


def _rmsnorm(x, g):
    xf = x.astype(jnp.float32)
    y = xf * lax.rsqrt(jnp.mean(xf * xf, axis=-1, keepdims=True) + EPS)
    return (y * g.astype(jnp.float32)).astype(x.dtype)


def _layernorm(x, g, b):
    xf = x.astype(jnp.float32)
    mu = jnp.mean(xf, axis=-1, keepdims=True)
    xc = xf - mu
    var = jnp.mean(xc * xc, axis=-1, keepdims=True)
    return (xc * lax.rsqrt(var + EPS) * g.astype(jnp.float32) + b.astype(jnp.float32)).astype(x.dtype)


def _causal_dwconv(x, buf, w, b):
    xp = jnp.concatenate([buf.astype(x.dtype), x], axis=1)
    y = lax.conv_general_dilated(xp, w[:, None, :].astype(x.dtype), window_strides=(1,),
                                 padding='VALID', dimension_numbers=('NWC', 'WIO', 'NWC'),
                                 feature_group_count=x.shape[-1])
    return y + b.astype(x.dtype), xp[:, -(w.shape[0] - 1):]


def _segsum(a):
    cs = jnp.cumsum(a, axis=-1)
    diff = cs[..., :, None] - cs[..., None, :]
    n = a.shape[-1]
    return jnp.where(jnp.tril(jnp.ones((n, n), dtype=bool)), diff, -jnp.inf)


def _ssd_scan(xh, dt, a, bm, cm, h0):
    bsz, L, H, P = xh.shape
    G, N = bm.shape[2], bm.shape[3]
    E = H // G
    T = min(SSD_CHUNK, L)
    pad = (-L) % T
    if pad:
        xh = jnp.pad(xh, ((0, 0), (0, pad), (0, 0), (0, 0)))
        dt = jnp.pad(dt, ((0, 0), (0, pad), (0, 0)))
        bm = jnp.pad(bm, ((0, 0), (0, pad), (0, 0), (0, 0)))
        cm = jnp.pad(cm, ((0, 0), (0, pad), (0, 0), (0, 0)))
    nc = (L + pad) // T
    xs = (xh * dt[..., None]).reshape(bsz, nc, T, G, E, P)
    da = jnp.transpose((dt * a).reshape(bsz, nc, T, G, E), (0, 1, 3, 4, 2))
    bm = bm.reshape(bsz, nc, T, G, N)
    cm = cm.reshape(bsz, nc, T, G, N)
    cs = jnp.cumsum(da, axis=-1)
    lmat = jnp.exp(_segsum(da))
    cb = jnp.einsum('bclgn,bcsgn->bcgls', cm, bm)
    y_diag = jnp.einsum('bcgels,bcsgep->bclgep', cb[:, :, :, None] * lmat, xs)
    decay_states = jnp.exp(cs[..., -1:] - cs)
    states = jnp.einsum('bclgn,bcgel,bclgep->bcgepn', bm, decay_states, xs)
    states = jnp.concatenate([h0.reshape(bsz, 1, G, E, P, N), states], axis=1)
    chunk_tot = jnp.pad(jnp.transpose(cs[..., -1], (0, 2, 3, 1)), ((0, 0), (0, 0), (0, 0), (1, 0)))
    decay_chunk = jnp.exp(_segsum(chunk_tot))
    new_states = jnp.einsum('bgezc,bcgepn->bzgepn', decay_chunk, states)
    y_off = jnp.einsum('bclgn,bcgepn,bcgel->bclgep', cm, new_states[:, :-1], jnp.exp(cs))
    y = (y_diag + y_off).reshape(bsz, nc * T, H, P)[:, :L]
    return y, new_states[:, -1].reshape(bsz, H, P, N)


def _ssd_mixer(z, xbc, dt_raw, conv_buf, h0, conv_w, conv_b, dt_bias, a_log, d_skip, gnorm_g):
    bsz, L, _ = z.shape
    xbc_c, buf_new = _causal_dwconv(xbc, conv_buf, conv_w, conv_b)
    xbc_c = jax.nn.silu(xbc_c.astype(jnp.float32))
    xs, bm, cm = jnp.split(xbc_c, [A_WIDTH, A_WIDTH + A_GROUPS * A_STATE], axis=-1)
    xh = xs.reshape(bsz, L, A_HEADS, A_HEAD_DIM)
    bm = bm.reshape(bsz, L, A_GROUPS, A_STATE)
    cm = cm.reshape(bsz, L, A_GROUPS, A_STATE)
    dt = jax.nn.softplus(dt_raw.astype(jnp.float32) + dt_bias.astype(jnp.float32))
    a = -jnp.exp(a_log.astype(jnp.float32))
    y, h_last = _ssd_scan(xh, dt, a, bm, cm, h0.astype(jnp.float32))
    y = y + xh * d_skip.astype(jnp.float32)[:, None]
    gw = A_WIDTH // A_GROUPS
    y = y.reshape(bsz, L, A_GROUPS, gw) * jax.nn.silu(z.astype(jnp.float32)).reshape(bsz, L, A_GROUPS, gw)
    y = _rmsnorm(y, gnorm_g.reshape(A_GROUPS, gw)).reshape(bsz, L, A_WIDTH)
    return y.astype(z.dtype), h_last.astype(h0.dtype), buf_new


def _chunk_gmlp_mixer(u, v, ln_g, ln_b, w_s, b_s):
    bsz, L, _ = v.shape
    u = jax.nn.gelu(u)
    vn = _layernorm(jax.nn.gelu(v), ln_g, ln_b)
    T = min(B_CHUNK, L)
    pad = (-L) % T
    vp = jnp.pad(vn, ((0, 0), (0, pad), (0, 0)))
    nc = (L + pad) // T
    vg = vp.reshape(bsz, nc, T, B_GROUPS, B_WIDTH // B_GROUPS)
    w = jnp.where(jnp.tril(jnp.ones((T, T), dtype=bool)), w_s[:, :T, :T], 0.0).astype(v.dtype)
    mixed = jnp.einsum('gts,bcsgd->bctgd', w, vg) + b_s[:, :T].T.astype(v.dtype)[None, None, :, :, None]
    mixed = mixed.reshape(bsz, nc * T, B_WIDTH)[:, :L]
    return u * mixed, vn


def _conformer_conv_mixer(ga, gb, conv_buf, conv_w, conv_b, ln_g, ln_b):
    glu = ga * jax.nn.sigmoid(gb)
    c, buf_new = _causal_dwconv(glu, conv_buf, conv_w, conv_b)
    return jax.nn.silu(_layernorm(c, ln_g, ln_b)), buf_new


def _rglru_mixer(xd, conv_buf, h0, conv_w, conv_b, wa, ba, wx, bx, lam):
    xc, buf_new = _causal_dwconv(xd, conv_buf, conv_w, conv_b)
    bsz, L, W = xc.shape
    xf = xc.astype(jnp.float32)
    xb = xf.reshape(bsz, L, D_HEADS, D_BLOCK)
    r = jax.nn.sigmoid(jnp.einsum('blhi,hij->blhj', xb, wa.astype(jnp.float32)).reshape(bsz, L, W) + ba.astype(jnp.float32))
    i = jax.nn.sigmoid(jnp.einsum('blhi,hij->blhj', xb, wx.astype(jnp.float32)).reshape(bsz, L, W) + bx.astype(jnp.float32))
    log_a = -LRU_C * r * jax.nn.softplus(-lam.astype(jnp.float32))
    a = jnp.exp(log_a)
    bterm = jnp.sqrt(-jnp.expm1(2.0 * log_a)) * (i * xf)
    bterm = bterm.at[:, 0].add(a[:, 0] * h0.astype(jnp.float32))

    def combine(e1, e2):
        a1, b1 = e1
        a2, b2 = e2
        return a1 * a2, a2 * b1 + b2

    _, h = lax.associative_scan(combine, (a, bterm), axis=1)
    return h.astype(xd.dtype), h[:, -1].astype(h0.dtype), buf_new


def _trunk(x, ssm0, ssdbuf0, cbuf0, lbuf0, lru0,
           norm_even, w_in_even, ssd_conv_w, ssd_conv_b, ssd_dt_bias, ssd_a_log, ssd_d, ssd_norm,
           gmlp_ln_g, gmlp_ln_b, gmlp_w_s, gmlp_b_s, w_out_even,
           norm_odd, w_in_odd, ccv_w, ccv_b, ccv_ln_g, ccv_ln_b,
           lru_conv_w, lru_conv_b, lru_wa, lru_ba, lru_wx, lru_bx, lru_lambda, w_out_odd, final_norm):
    ssm_n, ssdbuf_n, v_n, cbuf_n, lbuf_n, lru_n = [], [], [], [], [], []
    for layer in range(DEPTH):
        k = layer // 2
        if layer % 2 == 0:
            hn = _rmsnorm(x, norm_even[k])
            proj = jnp.einsum('bld,de->ble', hn, w_in_even[k])
            z, xbc, dt_raw, u, v, g = jnp.split(proj, EVEN_SPLITS, axis=-1)
            ya, ssm_new, sbuf_new = _ssd_mixer(z, xbc, dt_raw, ssdbuf0[k], ssm0[k], ssd_conv_w[k], ssd_conv_b[k],
                                               ssd_dt_bias[k], ssd_a_log[k], ssd_d[k], ssd_norm[k])
            yb, v_rows = _chunk_gmlp_mixer(u, v, gmlp_ln_g[k], gmlp_ln_b[k], gmlp_w_s[k], gmlp_b_s[k])
            mix = jnp.concatenate([ya, yb * jax.nn.silu(g)], axis=-1)
            x = x + jnp.einsum('ble,ed->bld', mix, w_out_even[k])
            ssm_n.append(ssm_new)
            ssdbuf_n.append(sbuf_new)
            v_n.append(v_rows)
        else:
            hn = _rmsnorm(x, norm_odd[k])
            proj = jnp.einsum('bld,de->ble', hn, w_in_odd[k])
            ga, gb, gc, xd, gd = jnp.split(proj, ODD_SPLITS, axis=-1)
            yc, cbuf_new = _conformer_conv_mixer(ga, gb, cbuf0[k], ccv_w[k], ccv_b[k], ccv_ln_g[k], ccv_ln_b[k])
            yd, h_last, lbuf_new = _rglru_mixer(xd, lbuf0[k], lru0[k], lru_conv_w[k], lru_conv_b[k],
                                                lru_wa[k], lru_ba[k], lru_wx[k], lru_bx[k], lru_lambda[k])
            mix = jnp.concatenate([yc * jax.nn.silu(gc), yd * jax.nn.silu(gd)], axis=-1)
            x = x + jnp.einsum('ble,ed->bld', mix, w_out_odd[k])
            cbuf_n.append(cbuf_new)
            lbuf_n.append(lbuf_new)
            lru_n.append(h_last)
    return (_rmsnorm(x, final_norm), jnp.stack(ssm_n), jnp.stack(ssdbuf_n), jnp.stack(v_n),
            jnp.stack(cbuf_n), jnp.stack(lbuf_n), jnp.stack(lru_n))


def setup_inputs(seed: int = 0) -> dict:
    key = jax.random.key(seed)
    ks = iter(jax.random.split(key, 48))

    def nrm(shape, scale=1.0):
        return scale * jax.random.normal(next(ks), shape, jnp.float32)

    def unif(shape, lo, hi):
        return jax.random.uniform(next(ks), shape, jnp.float32, lo, hi)

    dt0 = jnp.exp(unif((N_EVEN, A_HEADS), float(np.log(1e-3)), float(np.log(1e-1))))
    a_c = unif((N_ODD, D_WIDTH), 0.9, 0.999)
    s = a_c ** (1.0 / LRU_C)
    return {
        'x_prompt': nrm((BATCH, SEQ, D_MODEL)),
        'x_sample': nrm((DEC_BATCH, DEC_SEQ, D_MODEL)),
        'state_ssm': nrm((N_EVEN, DEC_BATCH, A_HEADS, A_HEAD_DIM, A_STATE), 0.1),
        'state_ssd_conv': nrm((N_EVEN, DEC_BATCH, A_CONV - 1, A_CONV_DIM)),
        'state_ccv': nrm((N_ODD, DEC_BATCH, C_CONV - 1, C_WIDTH), 0.5),
        'state_lru_conv': nrm((N_ODD, DEC_BATCH, D_CONV - 1, D_WIDTH)),
        'state_lru': nrm((N_ODD, DEC_BATCH, D_WIDTH), 0.5),
        'norm_even': 1.0 + nrm((N_EVEN, D_MODEL), 0.1),
        'w_in_even': nrm((N_EVEN, D_MODEL, EVEN_IN), D_MODEL ** -0.5),
        'ssd_conv_w': nrm((N_EVEN, A_CONV, A_CONV_DIM), A_CONV ** -0.5),
        'ssd_conv_b': nrm((N_EVEN, A_CONV_DIM), 0.02),
        'ssd_dt_bias': dt0 + jnp.log(-jnp.expm1(-dt0)),
        'ssd_a_log': jnp.log(unif((N_EVEN, A_HEADS), 1.0, 16.0)),
        'ssd_d': 1.0 + nrm((N_EVEN, A_HEADS), 0.1),
        'ssd_norm': 1.0 + nrm((N_EVEN, A_WIDTH), 0.1),
        'gmlp_ln_g': 1.0 + nrm((N_EVEN, B_WIDTH), 0.1),
        'gmlp_ln_b': nrm((N_EVEN, B_WIDTH), 0.02),
        'gmlp_w_s': nrm((N_EVEN, B_GROUPS, B_CHUNK, B_CHUNK), B_CHUNK ** -0.5),
        'gmlp_b_s': 1.0 + nrm((N_EVEN, B_GROUPS, B_CHUNK), 0.1),
        'w_out_even': nrm((N_EVEN, EVEN_MIX, D_MODEL), EVEN_MIX ** -0.5),
        'norm_odd': 1.0 + nrm((N_ODD, D_MODEL), 0.1),
        'w_in_odd': nrm((N_ODD, D_MODEL, ODD_IN), D_MODEL ** -0.5),
        'ccv_w': nrm((N_ODD, C_CONV, C_WIDTH), C_CONV ** -0.5),
        'ccv_b': nrm((N_ODD, C_WIDTH), 0.02),
        'ccv_ln_g': 1.0 + nrm((N_ODD, C_WIDTH), 0.1),
        'ccv_ln_b': nrm((N_ODD, C_WIDTH), 0.02),
        'lru_conv_w': nrm((N_ODD, D_CONV, D_WIDTH), D_CONV ** -0.5),
        'lru_conv_b': nrm((N_ODD, D_WIDTH), 0.02),
        'lru_wa': nrm((N_ODD, D_HEADS, D_BLOCK, D_BLOCK), D_BLOCK ** -0.5),
        'lru_ba': nrm((N_ODD, D_WIDTH), 0.02),
        'lru_wx': nrm((N_ODD, D_HEADS, D_BLOCK, D_BLOCK), D_BLOCK ** -0.5),
        'lru_bx': nrm((N_ODD, D_WIDTH), 0.02),
        'lru_lambda': jnp.log(s) - jnp.log1p(-s),
        'w_out_odd': nrm((N_ODD, ODD_MIX, D_MODEL), ODD_MIX ** -0.5),
        'final_norm': 1.0 + nrm((D_MODEL,), 0.1),
    }


def reference(x_prompt, x_sample, state_ssm, state_ssd_conv, state_ccv, state_lru_conv, state_lru,
              norm_even, w_in_even, ssd_conv_w, ssd_conv_b, ssd_dt_bias, ssd_a_log, ssd_d, ssd_norm,
              gmlp_ln_g, gmlp_ln_b, gmlp_w_s, gmlp_b_s, w_out_even,
              norm_odd, w_in_odd, ccv_w, ccv_b, ccv_ln_g, ccv_ln_b,
              lru_conv_w, lru_conv_b, lru_wa, lru_ba, lru_wx, lru_bx, lru_lambda, w_out_odd, final_norm):
    params = (norm_even, w_in_even, ssd_conv_w, ssd_conv_b, ssd_dt_bias, ssd_a_log, ssd_d, ssd_norm,
              gmlp_ln_g, gmlp_ln_b, gmlp_w_s, gmlp_b_s, w_out_even,
              norm_odd, w_in_odd, ccv_w, ccv_b, ccv_ln_g, ccv_ln_b,
              lru_conv_w, lru_conv_b, lru_wa, lru_ba, lru_wx, lru_bx, lru_lambda, w_out_odd, final_norm)
    dt = x_prompt.dtype
    y_prompt, ssm_p, ssdbuf_p, _, cbuf_p, lbuf_p, lru_p = _trunk(
        x_prompt,
        jnp.zeros((N_EVEN, BATCH, A_HEADS, A_HEAD_DIM, A_STATE), dt),
        jnp.zeros((N_EVEN, BATCH, A_CONV - 1, A_CONV_DIM), dt),
        jnp.zeros((N_ODD, BATCH, C_CONV - 1, C_WIDTH), dt),
        jnp.zeros((N_ODD, BATCH, D_CONV - 1, D_WIDTH), dt),
        jnp.zeros((N_ODD, BATCH, D_WIDTH), dt),
        *params)
    y_sample, ssm_s, ssdbuf_s, v_s, cbuf_s, lbuf_s, lru_s = _trunk(
        x_sample, state_ssm, state_ssd_conv, state_ccv, state_lru_conv, state_lru, *params)
    return (y_prompt, y_sample, ssm_p, ssm_s, ssdbuf_p, ssdbuf_s, v_s, cbuf_p, cbuf_s, lbuf_p, lbuf_s, lru_p, lru_s)
```

```python
import numpy as np
import ml_dtypes
import concourse.bass as bass
import concourse.mybir as mybir
from concourse.bass_utils import run_bass_kernel_spmd
from contextlib import ExitStack

F32 = mybir.dt.float32
BF16 = mybir.dt.bfloat16
AF = mybir.ActivationFunctionType
ALU = mybir.AluOpType
AX = mybir.AxisListType
COMPUTE = ("pe", "act", "dve", "pool")
EPS = 1e-6
NCORES = 8
SEQ = 2048
TT = 512
NT = SEQ // TT
NB = 16


class Buf:
    _n = 0

    def __init__(self, t, name=None):
        self.t = t
        self.name = name or f"buf{Buf._n}"
        Buf._n += 1
        self.writers = []
        self.readers = []
        self.base = []
        self.dma_ops = []
        self.sem = None

    def __getitem__(self, idx):
        return self.t[idx]


class Op:
    __slots__ = ("eng", "fn", "reads", "writes", "pwrites", "is_dma", "group", "gidx",
                 "eidx", "deps", "waits", "signal", "vc", "gpos")


def _is_pw(w, b):
    return any(x is b for x in w.pwrites)


class Sched:
    def __init__(self, nc):
        self.nc = nc
        self.ops = []
        self.by_eng = {e: [] for e in ("pe", "act", "dve", "pool", "sp")}

    def add(self, eng, fn, reads=(), writes=(), pwrites=(), dma=False, group=None):
        op = Op()
        op.eng, op.fn = eng, fn
        op.reads, op.writes, op.pwrites = list(reads), list(writes), list(pwrites)
        op.is_dma, op.group = dma, group
        op.gpos = len(self.ops)
        op.deps, op.waits, op.signal, op.vc = [], [], False, None
        deps = []
        for b in op.reads:
            deps.extend(b.writers)
        for b in op.writes:
            deps.extend(b.writers)
            deps.extend(b.readers)
        for b in op.pwrites:
            if b.readers:
                b.base = list(b.readers) + list(b.writers)
                b.writers = []
                b.readers = []
            deps.extend(b.base)
            deps.extend(w for w in b.writers if not _is_pw(w, b))
        for b in op.reads:
            b.readers.append(op)
        for b in op.writes:
            b.writers = [op]
            b.readers = []
            b.base = []
        for b in op.pwrites:
            b.writers.append(op)
        seen = set()
        for d in deps:
            if d is op or id(d) in seen:
                continue
            seen.add(id(d))
            if d.eng == "pe" and eng == "pe" and not d.is_dma and not dma:
                continue
            op.deps.append(d)
        if dma:
            op.gidx = len(group.dma_ops)
            group.dma_ops.append(op)
        op.eidx = len(self.by_eng[eng])
        self.by_eng[eng].append(op)
        self.ops.append(op)
        return op

    def dma(self, eng, out_ap, in_ap, reads=(), writes=(), pwrites=(), group=None, **kw):
        return self.add(eng, lambda e: e.dma_start(out=out_ap, in_=in_ap, **kw),
                        reads=reads, writes=writes, pwrites=pwrites, dma=True, group=group)

    def finalize_and_emit(self, es):
        nc = self.nc
        know = {e: {} for e in self.by_eng}
        for op in self.ops:
            k = know[op.eng]
            for d in op.deps:
                if d.is_dma:
                    key = ("g", id(d.group))
                    val = sum(1 for x in d.group.dma_ops if x.gpos < op.gpos)
                    tok = (key, val, d.group)
                else:
                    key = d.eng
                    val = d.eidx + 1
                    tok = (key, val, None)
                if k.get(key, 0) >= val:
                    continue
                op.waits.append(tok)
                k[key] = val
                if d.vc is not None:
                    for kk, vv in d.vc.items():
                        if k.get(kk, 0) < vv:
                            k[kk] = vv
                if not d.is_dma:
                    d.signal = True
            best = {}
            for tok in op.waits:
                if tok[0] not in best or best[tok[0]][1] < tok[1]:
                    best[tok[0]] = tok
            op.waits = list(best.values())
            op.vc = dict(k)
            if not op.is_dma:
                op.vc[op.eng] = op.eidx + 1
        ordmap = {}
        for e in COMPUTE:
            c, m = 0, {}
            for op in self.by_eng[e]:
                if op.signal:
                    c += 1
                m[op.eidx + 1] = c
            ordmap[e] = m
        esem = {e: es.enter_context(nc.semaphore(f"s_{e}")) for e in COMPUTE}
        groups = {}
        for op in self.ops:
            if op.is_dma and id(op.group) not in groups:
                groups[id(op.group)] = op.group
        for g in groups.values():
            g.sem = es.enter_context(nc.semaphore(f"g_{g.name}"))
        self.n_sems = 4 + len(groups)

        def emit_stream(ename, engine):
            for op in self.by_eng[ename]:
                for (key, val, grp) in op.waits:
                    if grp is not None:
                        engine.wait_ge(grp.sem, 16 * val)
                    else:
                        engine.wait_ge(esem[key], ordmap[key][val])
                ins = op.fn(engine)
                if op.is_dma:
                    ins.then_inc(op.group.sem, 16)
                elif op.signal:
                    ins.then_inc(esem[ename], 1)

        block = es.enter_context(nc.Block())

        @block.tensor
        def _(eng):
            emit_stream("pe", eng)

        @block.scalar
        def _(eng):
            emit_stream("act", eng)

        @block.vector
        def _(eng):
            emit_stream("dve", eng)

        @block.gpsimd
        def _(eng):
            emit_stream("pool", eng)

        @block.sync
        def _(eng):
            emit_stream("sp", eng)


def _fm(v):
    v = np.asarray(v, np.float32).reshape(-1)
    return np.ascontiguousarray(v.reshape(-1, 128).T)


PCOLS = {}


def _build_pfm(inp):
    cols, off = [], 0

    def put(name, arr):
        nonlocal off
        PCOLS[name] = off
        cols.append(arr)
        off += arr.shape[1]

    put("ne", _fm(inp["norm_even"][0]))
    put("no", _fm(inp["norm_odd"][0]))
    put("nf", _fm(inp["final_norm"]))
    put("scb", _fm(inp["ssd_conv_b"][0]))
    put("scw", np.concatenate([_fm(inp["ssd_conv_w"][0][k]) for k in range(4)], 1))
    put("gn", _fm(inp["ssd_norm"][0]))
    put("Dfm", _fm(np.repeat(inp["ssd_d"][0], 64)))
    put("lng", _fm(inp["gmlp_ln_g"][0]))
    put("lnb", _fm(inp["gmlp_ln_b"][0]))
    put("ccb", _fm(inp["ccv_b"][0]))
    put("cclg", _fm(inp["ccv_ln_g"][0]))
    put("cclb", _fm(inp["ccv_ln_b"][0]))
    put("ccw", np.concatenate([_fm(inp["ccv_w"][0][k]) for k in range(31)], 1))
    put("lcw", np.concatenate([_fm(inp["lru_conv_w"][0][k]) for k in range(4)], 1))
    put("lcb", _fm(inp["lru_conv_b"][0]))
    put("lba", _fm(inp["lru_ba"][0]))
    put("lbx", _fm(inp["lru_bx"][0]))
    put("lam", _fm(inp["lru_lambda"][0]))
    put("w00", _fm(np.repeat(inp["gmlp_w_s"][0][:, 0, 0], 128)))
    put("b0", _fm(np.repeat(inp["gmlp_b_s"][0][:, 0], 128)))
    return np.ascontiguousarray(np.concatenate(cols, 1))


NPCOL = 468
E_EVEN = 5648
E_ODD = 5120


def build_program(stage=99):
    import os as _os
    nc = bass.Bass("TRN2", target_bir_lowering=False)

    def din(name, shape, dt=F32):
        return nc.dram_tensor(name, list(shape), dt, kind="ExternalInput").ap()

    def dout(name, shape, dt=F32):
        return nc.dram_tensor(name, list(shape), dt, kind="ExternalOutput").ap()

    xT = din("xT", [1024, SEQ])
    w_in_e = din("w_in_e", [1024, E_EVEN])
    w_out_e = din("w_out_e", [2048, 1024])
    w_in_o = din("w_in_o", [1024, E_ODD])
    w_out_o = din("w_out_o", [2048, 1024])
    d_pfm = din("pfm", [128, NPCOL])
    d_p16 = din("p16", [16, 2])
    d_drow = din("drow", [1, 16])
    d_wsT = din("wsT", [128, 1024])
    d_bsrow = din("bsrow", [1, 1024])
    d_idf = din("c_idf", [128, 128])
    d_negm = din("c_negm", [128, 128])
    d_triu = din("c_triu", [128, 128])
    d_sel16 = din("c_sel16", [16, 2048])
    d_sellast = din("c_sellast", [128, 128])

    d_xsT = din("xsT", [1024, NB])
    st_ssm = din("st_ssm", [NB, 1024, 128])
    st_sconv = din("st_sconv", [NB, 3, 1536])
    st_ccv = din("st_ccv", [NB, 30, 1024])
    st_lconv = din("st_lconv", [NB, 3, 1024])
    st_lru = din("st_lru", [NB, 1024])
    d_exp = din("c_exp", [16, 1024])
    o_y_s = dout("o_y_s", [128, 8 * NB])
    o_ssm_s = dout("o_ssm_s", [NB, 1024, 128])
    o_sconv_s_new = dout("o_sconv_s_new", [128, 12 * NB])
    o_sconv_s_hist = dout("o_sconv_s_hist", [NB, 2, 1536])
    o_gv_s = dout("o_gv_s", [128, 8 * NB])
    o_ccv_s_new = dout("o_ccv_s_new", [128, 8 * NB])
    o_ccv_s_hist = dout("o_ccv_s_hist", [NB, 29, 1024])
    o_lconv_s_new = dout("o_lconv_s_new", [128, 8 * NB])
    o_lconv_s_hist = dout("o_lconv_s_hist", [NB, 2, 1024])
    o_lru_s = dout("o_lru_s", [128, 8 * NB])
    d_bda = din("bda", [128, 1024])
    d_bdx = din("bdx", [128, 1024])
    yT = dout("yT", [1024, SEQ])
    o_ccv_p = dout("o_ccv_p", [128, 240])
    o_lconv_p = dout("o_lconv_p", [128, 32])
    o_lru_p = dout("o_lru_p", [128, 8])
    x1T = nc.dram_tensor("x1T", [1024, SEQ], F32, kind="Internal").ap()
    o_ssm_p = dout("o_ssm_p", [1024, 128])
    o_sconv_p = dout("o_sconv_p", [128, 48])

    es = ExitStack()
    with es:
        S = Sched(nc)
        x1buf = Buf(None, "x1buf")
        outbuf = Buf(None, "outs")

        def sb(name, shape, dt=F32):
            return Buf(es.enter_context(nc.sbuf_tensor(name, list(shape), dt)), name)

        def psum(name, shape, dt=F32):
            return Buf(es.enter_context(nc.psum_tensor(name, list(shape), dt)), name)

        def PC(c):
            return PCOLS[c]

        pfm = sb("pfm_sb", [128, NPCOL])
        cIDF = sb("cIDF", [128, 128])
        cIDB = sb("cIDB", [128, 128], BF16)
        cNEGM = sb("cNEGM", [128, 128])
        cSELLAST = sb("cSELLAST", [128, 128])
        cSEL16 = sb("cSEL16", [16, 2048])
        cONESB = sb("cONESB", [128, 128], BF16)
        cONES16 = sb("cONES16", [16, 128])
        p16 = sb("p16_sb", [16, 2])
        Dbc = sb("Dbc", [128, 16])
        WmT = sb("WmT", [128, 8, 128], BF16)
        Rg = sb("Rg", [128, 8, 128])
        ea16 = sb("ea16", [16, 1])

        S.dma("sp", pfm[:, :], d_pfm, writes=[pfm], group=pfm)
        S.dma("sp", cIDF[:, :], d_idf, writes=[cIDF], group=cIDF)
        S.dma("pool", cIDB[:, :], d_idf, writes=[cIDB], group=cIDB)
        S.dma("sp", cNEGM[:, :], d_negm, writes=[cNEGM], group=cNEGM)
        S.dma("sp", cSELLAST[:, :], d_sellast, writes=[cSELLAST], group=cSELLAST)
        S.dma("sp", cSEL16[:, :], d_sel16, writes=[cSEL16], group=cSEL16)
        S.dma("sp", p16[:, :], d_p16, writes=[p16], group=p16)
        S.dma("sp", Dbc[:, :], d_drow[0, :].partition_broadcast(128), writes=[Dbc], group=Dbc)
        S.add("dve", lambda e: e.memset(cONESB[:, :], 1.0), writes=[cONESB])
        S.add("dve", lambda e: e.memset(cONES16[:, :], 1.0), writes=[cONES16])
        S.add("act", lambda e: e.activation(out=ea16[:, :], in_=p16[:, 1:2], func=AF.Exp), reads=[p16], writes=[ea16])

        P0 = psum("P0", [128, 512])
        P1 = psum("P1", [128, 512])
        P2 = psum("P2", [128, 512])
        P3 = psum("P3", [128, 512])
        P4 = psum("P4", [128, 512])
        P5 = psum("P5", [128, 512])
        P6 = psum("P6", [128, 512])
        P7t = es.enter_context(nc.psum_tensor("P7", [128, 512], F32))
        P7 = Buf(P7t, "P7")
        P7a = P7t[:, 0:144]
        P7b = P7t[:, 144:400]
        P7c = P7t[:, 400:416]
        PA = [P0, P1]
        pa_i = [0]

        def set_pa(banks):
            PA[:] = banks

        def nextPA():
            b = PA[pa_i[0] % len(PA)]
            pa_i[0] += 1
            return b

        tmpw = sb("tmpw2", [128, 1024])
        ctriu = sb("ctriu", [128, 128])
        bsbc = sb("bsbc2", [128, 1024])
        S.dma("sp", tmpw[:, :], d_wsT, writes=[tmpw], group=tmpw)
        S.dma("sp", ctriu[:, :], d_triu, writes=[ctriu], group=ctriu)
        S.dma("sp", bsbc[:, :], d_bsrow[0, :].partition_broadcast(128), writes=[bsbc], group=bsbc)
        S.add("dve", lambda e: e.tensor_tensor(
            out=WmT[:, :, :], in0=tmpw[:, :].rearrange("p (g t) -> p g t", g=8),
            in1=ctriu[:, :].unsqueeze(1).broadcast_to([128, 8, 128]), op=ALU.mult),
            reads=[tmpw, ctriu], writes=[WmT])
        for hf in range(2):
            S.add("pe", lambda e, hf=hf: e.matmul(P3[:, :] if hf == 0 else P4[:, :], lhsT=cONESB[:, :],
                                                    rhs=WmT[:, 4 * hf:4 * hf + 4, :].rearrange("p g t -> p (g t)"),
                                                    start=True, stop=True),
                  reads=[cONESB, WmT], writes=[P3 if hf == 0 else P4])
        for g in range(8):
            pb = P3 if g < 4 else P4
            S.add("dve", lambda e, g=g, pb=pb: e.scalar_tensor_tensor(
                out=Rg[:, g, :], in0=pb[:, (g % 4) * 128:(g % 4 + 1) * 128], scalar=pfm[:, PC("lnb") + g:PC("lnb") + g + 1],
                in1=bsbc[:, g * 128:(g + 1) * 128], op0=ALU.mult, op1=ALU.add),
                reads=[pb, pfm, bsbc], pwrites=[Rg])

        NW = 3
        wbufs = [sb(f"wbuf{i}", [128, 4096], BF16) for i in range(NW)]
        w_i = [0]

        def wload(dram_ap, kk, ww):
            b = wbufs[w_i[0] % NW]
            w_i[0] += 1
            view = b.t[:, 0:kk * ww].rearrange("p (k e) -> p k e", k=kk)
            S.dma("pool", view, dram_ap.rearrange("(k p) e -> p k e", p=128), writes=[b], group=b)
            return b, view

        NDG = 8
        dgbufs = [sb(f"dg{i}", [128, 128], BF16) for i in range(NDG)]
        dg_i = [0]

        def diag(col, eng="act"):
            b = dgbufs[dg_i[0] % NDG]
            dg_i[0] += 1
            if eng == "act":
                S.add("act", lambda e: e.activation(out=b[:, :], in_=cIDB[:, :], func=AF.Copy, scale=pfm[:, col:col + 1]),
                      reads=[cIDB, pfm], writes=[b])
            else:
                S.add("dve", lambda e: e.tensor_scalar(out=b[:, :], in0=cIDB[:, :], scalar1=pfm[:, col:col + 1], scalar2=None, op0=ALU.mult),
                      reads=[cIDB, pfm], writes=[b])
            return b

        bigA = sb("bigA", [128, 8, 512])
        mix = sb("mix", [128, 16, 512], BF16)
        hn = sb("hn", [128, 8, 512], BF16)
        rstd = sb("rstd", [128, 512])
        raws = [sb(f"raw{i}", [128, 515], BF16) for i in range(2)]
        hist0 = sb("hist0", [128, 12, 3], BF16)
        lastraw = sb("lastraw", [128, 12, 4])
        xcf = [sb(f"xcf{i}", [128, 512]) for i in range(2)]
        xh_tm = sb("xh_tm", [128, 4, 1024])
        B_tm = sb("B_tm", [128, 4, 256], BF16)
        BT_bf = sb("BT_bf", [128, 2, 512], BF16)
        CT_bf = sb("CT_bf", [128, 2, 512], BF16)
        bigZ = sb("bigZ", [128, 4, 1024])
        vhat = sb("vhat", [128, 4, 1024], BF16)
        sg = [sb(f"sg{i}", [128, 512]) for i in range(2)]
        dtT = sb("dtT", [16, 512])
        daT = sb("daT", [16, 512])
        csT = sb("csT", [16, 512])
        tmpT = sb("tmpT", [16, 512])
        PK1 = sb("PK1", [128, 512])
        PK2 = sb("PK2", [16, 512])
        tmq = sb("tmq", [128, 144])
        xs_bf = sb("xs_bf", [128, 1024], BF16)
        xsd_bf = sb("xsd_bf", [128, 1024], BF16)
        Lbuf = sb("Lbuf", [128, 8, 128])
        MT_bf = sb("MT_bf", [128, 16, 128], BF16)
        yoff = sb("yoff", [128, 1024])
        ysb = sb("ysb", [128, 1024])
        ya_bf = sb("ya_bf", [128, 1024], BF16)
        Hst = sb("Hst", [128, 1024])
        Hbf = sb("Hbf", [128, 1024], BF16)
        ect = sb("ect", [128, 16])
        ssq = sb("ssq", [128, 2])
        rs2 = sb("rs2", [128, 2])
        bnst = sb("bnst", [128, 4, 2, 6])
        mv = sb("mv", [128, 4, 2])
        rv = sb("rv", [128, 4])

        S.add("dve", lambda e: e.memset(hist0[:, :, :], 0.0), writes=[hist0])
        S.add("dve", lambda e: e.memset(PK1[:, :], 0.0), writes=[PK1])
        S.add("dve", lambda e: e.memset(Hst[:, :], 0.0), writes=[Hst])
        S.add("dve", lambda e: e.memset(Hbf[:, :], 0.0), writes=[Hbf])

        def rmsnorm_fm(src, gcol, dst_bf, dst_view=None):
            S.add("act", lambda e: e.activation(out=mix[:, 0:8, :], in_=src[:, :, :], func=AF.Square), reads=[src], writes=[mix])
            pb = nextPA()
            for k in range(8):
                S.add("pe", lambda e, k=k: e.matmul(pb[:, :], lhsT=cONESB[:, :], rhs=mix[:, k, :], start=(k == 0), stop=(k == 7)),
                      reads=[cONESB, mix], writes=[pb])
            S.add("act", lambda e: e.activation(out=rstd[:, :], in_=pb[:, :], func=AF.Ln, scale=1.0 / 1024.0, bias=EPS),
                  reads=[pb], writes=[rstd])
            S.add("act", lambda e: e.activation(out=rstd[:, :], in_=rstd[:, :], func=AF.Exp, scale=-0.5), reads=[rstd], writes=[rstd])
            for k in range(8):
                S.add("dve", lambda e, k=k: e.scalar_tensor_tensor(
                    out=(dst_bf[:, k, :] if dst_view is None else dst_view[:, k, :]), in0=src[:, k, :], scalar=pfm[:, gcol + k:gcol + k + 1], in1=rstd[:, :],
                    op0=ALU.mult, op1=ALU.mult), reads=[src, pfm, rstd], pwrites=[dst_bf])

        def projA(wv, j, rhs_buf, pb):
            for k in range(8):
                S.add("pe", lambda e, k=k: e.matmul(pb[:, :], lhsT=wv[1][:, k, j * 128:(j + 1) * 128], rhs=rhs_buf[:, k, :],
                                                     start=(k == 0), stop=(k == 7)),
                      reads=[wv[0], rhs_buf], writes=[pb])


        def norm_sq(pieces, pbufs, sq_view, sq_buf):
            for k in range(8):
                S.add("act", lambda e, k=k: e.activation(out=sq_view[:, k, :], in_=pieces[k], func=AF.Square), reads=[pbufs[k]], pwrites=[sq_buf])

        def norm_rest(pieces, pbufs, sq_view, sq_buf, gcol):
            pb = nextPA()
            for k in range(8):
                S.add("pe", lambda e, k=k: e.matmul(pb[:, :], lhsT=cONESB[:, :], rhs=sq_view[:, k, :], start=(k == 0), stop=(k == 7)),
                      reads=[cONESB, sq_buf], writes=[pb])
            S.add("act", lambda e: e.activation(out=rstd[:, :], in_=pb[:, :], func=AF.Ln, scale=1.0 / 1024.0, bias=EPS), reads=[pb], writes=[rstd])
            S.add("act", lambda e: e.activation(out=rstd[:, :], in_=rstd[:, :], func=AF.Exp, scale=-0.5), reads=[rstd], writes=[rstd])
            for k in range(8):
                S.add("dve", lambda e, k=k: e.scalar_tensor_tensor(out=hn[:, k, :], in0=pieces[k], scalar=pfm[:, gcol + k:gcol + k + 1], in1=rstd[:, :],
                                                                     op0=ALU.mult, op1=ALU.mult), reads=[pbufs[k], pfm, rstd], pwrites=[hn])

        l0_pieces = [bigZ.t[:, :, :].rearrange("p a (c t) -> p (a c) t", t=512)[:, k, :] for k in range(8)]
        l0_pbufs = [bigZ] * 8
        l0_sq = xh_tm.t[:, :, :].rearrange("p a b -> p (a b)").bitcast(BF16)[:, 0:4096].rearrange("p (k t) -> p k t", k=8)

        def l0_load(ti):
            S.dma("sp", bigZ.t[:, :, :].rearrange("p a (c t) -> p (a c) t", t=512), xT[:, ti * TT:(ti + 1) * TT].rearrange("(k p) t -> p k t", p=128),
                  writes=[bigZ], group=bigZ)

        def layer0_tile(ti):
            t0 = ti * TT
            last = (ti == NT - 1) and stage >= 1
            set_pa([P0, P1, P4, P5])
            if ti == 0:
                l0_load(0)
                norm_sq(l0_pieces, l0_pbufs, l0_sq, xh_tm)
                norm_rest(l0_pieces, l0_pbufs, l0_sq, xh_tm, PC("ne"))
            if stage <= 0.1:
                return
            wdt = wload(w_in_e[:, 2560:2576], 8, 16)
            samp_dt(wdt)
            pb = nextPA()
            for k in range(8):
                S.add("pe", lambda e, k=k: e.matmul(pb[0:16, :], lhsT=wdt[1][:, k, :], rhs=hn[:, k, :], start=(k == 0), stop=(k == 7)),
                      reads=[wdt[0], hn], writes=[pb])
            S.add("act", lambda e: e.activation(out=tmpT[:, :], in_=pb[0:16, :], func=AF.Exp, bias=p16[:, 0:1]),
                  reads=[pb, p16], writes=[tmpT])
            S.add("act", lambda e: e.activation(out=dtT[:, :], in_=tmpT[:, :], func=AF.Ln, bias=1.0), reads=[tmpT], writes=[dtT])
            S.add("dve", lambda e: e.tensor_scalar(out=daT[:, :], in0=dtT[:, :], scalar1=ea16[:, 0:1], scalar2=-1.0,
                                                    op0=ALU.mult, op1=ALU.mult), reads=[dtT, ea16], writes=[daT])
            for c in range(4):
                S.add("dve", lambda e, c=c: e.tensor_tensor_scan(out=csT[:, c * 128:(c + 1) * 128], data0=cONES16[:, :],
                                                                  data1=daT[:, c * 128:(c + 1) * 128], initial=0.0,
                                                                  op0=ALU.mult, op1=ALU.add),
                      reads=[cONES16, daT], pwrites=[csT])
            S.add("act", lambda e: e.activation(out=PK1[0:16, :], in_=dtT[:, :], func=AF.Copy), reads=[dtT], pwrites=[PK1])
            S.add("act", lambda e: e.activation(out=PK1[32:48, :], in_=csT[:, :], func=AF.Copy), reads=[csT], pwrites=[PK1])
            for c in range(4):
                S.add("act", lambda e, c=c: e.activation(out=tmpT[:, c * 128:(c + 1) * 128], in_=csT[:, c * 128:(c + 1) * 128],
                                                          func=AF.Exp, scale=-1.0, bias=csT[:, c * 128 + 127:c * 128 + 128]),
                      reads=[csT], pwrites=[tmpT])
            S.add("dve", lambda e: e.tensor_tensor(out=PK1[64:80, :], in0=tmpT[:, :], in1=dtT[:, :], op=ALU.mult),
                  reads=[tmpT, dtT], pwrites=[PK1])
            S.add("act", lambda e: e.activation(out=PK2[:, :], in_=csT[:, :], func=AF.Exp), reads=[csT], writes=[PK2])

            if stage <= 0.2:
                return
            for blk3 in range(3):
                wv = wload(w_in_e[:, 1024 + blk3 * 512:1024 + (blk3 + 1) * 512], 8, 512)
                samp_cols(wv, 1024 + blk3 * 512, 512)
                for jj in range(4):
                    j = blk3 * 4 + jj
                    pb = nextPA()
                    projA(wv, jj, hn, pb)
                    raw = raws[j % 2]
                    PC2 = P2 if j % 2 == 0 else P6
                    PT3 = P3 if j % 2 == 0 else P7
                    PT3v = P3.t if j % 2 == 0 else P7t
                    S.add("dve", lambda e, j=j, raw=raw: e.tensor_copy(out=raw[:, 0:3], in_=hist0[:, j, :]), reads=[hist0], pwrites=[raw])
                    S.add("act", lambda e, raw=raw, pb=pb: e.activation(out=raw[:, 3:515], in_=pb[:, :], func=AF.Copy),
                          reads=[pb], pwrites=[raw])
                    if last:
                        S.add("act", lambda e, j=j, pb=pb: e.activation(out=lastraw[:, j, :], in_=pb[:, 508:512], func=AF.Copy), reads=[pb], pwrites=[lastraw])
                    S.add("dve", lambda e, j=j, raw=raw: e.tensor_copy(out=hist0[:, j, :], in_=raw[:, 512:515]), reads=[raw], pwrites=[hist0])
                    if stage <= 0.21:
                        continue
                    for k in range(4):
                        dg = diag(PC("scw") + k * 12 + j)
                        S.add("pe", lambda e, k=k, dg=dg, raw=raw, PC2=PC2: e.matmul(PC2[:, :], lhsT=dg[:, :], rhs=raw[:, k:k + 512],
                                                                             start=(k == 0), stop=(k == 3)),
                              reads=[dg, raw], writes=[PC2])
                    if stage <= 0.22:
                        continue
                    bcol = PC("scb") + j
                    if j < 10:
                        xc = xcf[j % 2]
                        S.add("act", lambda e, xc=xc, bcol=bcol, PC2=PC2: e.activation(out=xc[:, :], in_=PC2[:, :], func=AF.Silu,
                                                                                 bias=pfm[:, bcol:bcol + 1]),
                              reads=[PC2, pfm], writes=[xc])
                        if stage <= 0.23:
                            continue
                        for b4 in range(4):
                            S.add("pe", lambda e, b4=b4, xc=xc, PT3v=PT3v: e.transpose(PT3v[:, b4 * 128:(b4 + 1) * 128], xc[:, b4 * 128:(b4 + 1) * 128], cIDF[:, :]),
                                  reads=[xc, cIDF], writes=[PT3])
                        if stage <= 0.24:
                            continue
                        if j < 8:
                            S.add("dve", lambda e, j=j, PT3v=PT3v: e.tensor_copy(out=xh_tm[:, :, j * 128:(j + 1) * 128],
                                                                       in_=PT3v[:, :].rearrange("p (b c) -> p b c", b=4)),
                                  reads=[PT3], pwrites=[xh_tm])
                        else:
                            jb = j - 8
                            S.add("dve", lambda e, jb=jb, PT3v=PT3v: e.tensor_copy(out=B_tm[:, :, jb * 128:(jb + 1) * 128],
                                                                         in_=PT3v[:, :].rearrange("p (b c) -> p b c", b=4)),
                                  reads=[PT3], pwrites=[B_tm])
                            if stage <= 0.25:
                                continue
                            S.add("dve", lambda e, jb=jb, xc=xc: e.tensor_copy(out=BT_bf[:, jb, :], in_=xc[:, :]), reads=[xc], pwrites=[BT_bf])
                    else:
                        jc = j - 10
                        S.add("act", lambda e, jc=jc, bcol=bcol, PC2=PC2: e.activation(out=CT_bf[:, jc, :], in_=PC2[:, :], func=AF.Silu,
                                                                                 bias=pfm[:, bcol:bcol + 1]),
                              reads=[PC2, pfm], pwrites=[CT_bf])

            if stage <= 0.3:
                return
            for half in range(2):
                wv = wload(w_in_e[:, half * 512:(half + 1) * 512], 8, 512)
                samp_cols(wv, half * 512, 512)
                for b4 in range(4):
                    pb = nextPA()
                    for k in range(8):
                        S.add("pe", lambda e, k=k, b4=b4, wv=wv, pb=pb: e.matmul(pb[:, :], lhsT=hn[:, k, b4 * 128:(b4 + 1) * 128],
                                                                                  rhs=wv[1][:, k, :], start=(k == 0), stop=(k == 7)),
                              reads=[wv[0], hn], writes=[pb])
                    S.add("act", lambda e, b4=b4, half=half, pb=pb: e.activation(out=bigZ[:, b4, half * 512:(half + 1) * 512],
                                                                                  in_=pb[:, :], func=AF.Silu),
                          reads=[pb], pwrites=[bigZ])

            if stage <= 0.4:
                return
            set_pa([P0, P1])
            def gen_ssd():
                for c in range(4):
                    cs_ = slice(c * 128, (c + 1) * 128)
                    S.add("pe", lambda e, cs_=cs_: e.transpose(P7a[:, 0:128], PK1[:, cs_], cIDF[:, :]), reads=[PK1, cIDF], pwrites=[P7])
                    S.add("pe", lambda e, cs_=cs_: e.transpose(P7a[:, 128:144], PK2[:, cs_], cIDF[0:16, 0:16]), reads=[PK2, cIDF], pwrites=[P7])
                    S.add("dve", lambda e: e.tensor_copy(out=tmq[:, :], in_=P7a[:, :]), reads=[P7], writes=[tmq])
                    dt_tm = tmq[:, 0:16]
                    dd_tm = tmq[:, 64:80]
                    ecs_tm = tmq[:, 128:144]

                    def bc16(ap):
                        return ap.unsqueeze(2).broadcast_to([128, 16, 64])

                    xh3 = xh_tm[:, c, :].rearrange("p (h q) -> p h q", h=16)
                    S.add("dve", lambda e, xh3=xh3, dt_tm=dt_tm: e.tensor_tensor(out=xs_bf[:, :].rearrange("p (h q) -> p h q", h=16), in0=xh3,
                                                                                  in1=bc16(dt_tm), op=ALU.mult),
                          reads=[xh_tm, tmq], writes=[xs_bf])
                    S.add("dve", lambda e, xh3=xh3, dd_tm=dd_tm: e.tensor_tensor(out=xsd_bf[:, :].rearrange("p (h q) -> p h q", h=16), in0=xh3,
                                                                                   in1=bc16(dd_tm), op=ALU.mult),
                          reads=[xh_tm, tmq], writes=[xsd_bf])
                    S.add("dve", lambda e, xh3=xh3: e.tensor_tensor(out=ysb[:, :].rearrange("p (h q) -> p h q", h=16), in0=xh3,
                                                                     in1=bc16(Dbc[:, :]), op=ALU.mult),
                          reads=[xh_tm, Dbc], writes=[ysb])
                    yield
                    for g in range(2):
                        S.add("pe", lambda e, g=g, cs_=cs_: e.matmul(P7b[:, g * 128:(g + 1) * 128], lhsT=BT_bf[:, g, cs_], rhs=CT_bf[:, g, cs_],
                                                                      start=True, stop=True),
                              reads=[BT_bf, CT_bf], pwrites=[P7])
                    for g in range(2):
                        pg = P3 if g == 0 else P4
                        S.add("pe", lambda e, g=g, pg=pg, cs_=cs_: e.matmul(pg[:, :], lhsT=CT_bf[:, g, cs_], rhs=Hbf[:, g * 512:(g + 1) * 512],
                                                                             start=True, stop=True),
                              reads=[CT_bf, Hbf], writes=[pg])
                        S.add("dve", lambda e, g=g, pg=pg, ecs_tm=ecs_tm: e.tensor_tensor(
                            out=yoff[:, g * 512:(g + 1) * 512].rearrange("p (h q) -> p h q", h=8),
                            in0=pg[:, :].rearrange("p (h q) -> p h q", h=8),
                            in1=ecs_tm[:, g * 8:(g + 1) * 8].unsqueeze(2).broadcast_to([128, 8, 64]), op=ALU.mult),
                            reads=[pg, tmq], pwrites=[yoff])
                    S.add("dve", lambda e: e.tensor_tensor(out=yoff[:, :], in0=yoff[:, :], in1=ysb[:, :], op=ALU.add), reads=[yoff, ysb], writes=[yoff])
                    yield
                    for q in range(4):
                        pc = P5 if q % 2 == 0 else P6
                        for h4 in range(4):
                            h = q * 4 + h4
                            S.add("pe", lambda e, h=h, h4=h4, pc=pc, cs_=cs_: e.matmul(pc[:, h4 * 128:(h4 + 1) * 128], lhsT=cSEL16[:, h * 128:(h + 1) * 128],
                                                                                        rhs=csT[:, cs_], start=True, stop=True),
                                  reads=[cSEL16, csT], pwrites=[pc])
                        for h4 in range(4):
                            h = q * 4 + h4
                            S.add("dve", lambda e, h=h, h4=h4, pc=pc, q=q: e.scalar_tensor_tensor(
                                out=Lbuf[:, (q % 2) * 4 + h4, :], in0=pc[:, h4 * 128:(h4 + 1) * 128], scalar=tmq[:, 32 + h:33 + h],
                                in1=cNEGM[:, :], op0=ALU.subtract, op1=ALU.add),
                                reads=[pc, tmq, cNEGM], pwrites=[Lbuf])
                        if q % 2 == 1:
                            g = q // 2
                            S.add("act", lambda e: e.activation(out=Lbuf[:, :, :], in_=Lbuf[:, :, :], func=AF.Exp), reads=[Lbuf], writes=[Lbuf])
                            S.add("dve", lambda e, g=g: e.tensor_tensor(
                                out=MT_bf[:, g * 8:(g + 1) * 8, :], in0=Lbuf[:, :, :],
                                in1=P7b[:, g * 128:(g + 1) * 128].unsqueeze(1).broadcast_to([128, 8, 128]), op=ALU.mult),
                                reads=[Lbuf, P7], pwrites=[MT_bf])
                        yield
                    for h in range(16):
                        pg = P3 if h < 8 else P4
                        S.add("pe", lambda e, h=h, pg=pg: e.matmul(pg[:, (h % 8) * 64:(h % 8 + 1) * 64], lhsT=MT_bf[:, h, :],
                                                                     rhs=xs_bf[:, h * 64:(h + 1) * 64], start=True, stop=True),
                              reads=[MT_bf, xs_bf], pwrites=[pg])
                    yield
                    for g in range(2):
                        pg = P3 if g == 0 else P4
                        S.add("dve", lambda e, g=g, pg=pg: e.tensor_tensor(out=ysb[:, g * 512:(g + 1) * 512], in0=pg[:, :],
                                                                            in1=yoff[:, g * 512:(g + 1) * 512], op=ALU.add),
                              reads=[pg, yoff], pwrites=[ysb])
                    S.add("dve", lambda e, c=c: e.tensor_tensor(out=ysb[:, :], in0=ysb[:, :], in1=bigZ[:, c, :], op=ALU.mult),
                          reads=[ysb, bigZ], writes=[ysb])
                    for g in range(2):
                        S.add("act", lambda e, g=g: e.activation(out=yoff[:, g * 512:(g + 1) * 512], in_=ysb[:, g * 512:(g + 1) * 512],
                                                                  func=AF.Square, accum_out=ssq[:, g:g + 1]),
                              reads=[ysb], pwrites=[yoff, ssq])
                    S.add("act", lambda e: e.activation(out=rs2[:, :], in_=ssq[:, :], func=AF.Ln, scale=1.0 / 512.0, bias=EPS), reads=[ssq], writes=[rs2])
                    S.add("act", lambda e: e.activation(out=rs2[:, :], in_=rs2[:, :], func=AF.Exp, scale=-0.5), reads=[rs2], writes=[rs2])
                    for g in range(2):
                        S.add("act", lambda e, g=g: e.activation(out=ya_bf[:, g * 512:(g + 1) * 512], in_=ysb[:, g * 512:(g + 1) * 512],
                                                                  func=AF.Copy, scale=rs2[:, g:g + 1]),
                              reads=[ysb, rs2], pwrites=[ya_bf])
                    for j in range(8):
                        S.add("pe", lambda e, j=j: e.transpose(P2[:, j * 64:(j + 1) * 64].bitcast(BF16), ya_bf[:, j * 128:(j + 1) * 128], cIDB[:, :]),
                              reads=[ya_bf, cIDB], pwrites=[P2])
                    for j in range(8):
                        S.add("act", lambda e, j=j, cs_=cs_: e.activation(out=mix[:, j, cs_], in_=P2[:, j * 64:(j + 1) * 64].bitcast(BF16),
                                                                           func=AF.Copy, scale=pfm[:, PC("gn") + j:PC("gn") + j + 1]),
                              reads=[P2, pfm], pwrites=[mix])
                    yield
                    S.add("pe", lambda e: e.matmul(P7c[:, :], lhsT=cSELLAST[:, :], rhs=tmq[:, 32:48], start=True, stop=True),
                          reads=[cSELLAST, tmq], pwrites=[P7])
                    S.add("dve", lambda e: e.tensor_copy(out=ect[:, :], in_=P7c[:, :]), reads=[P7], writes=[ect])
                    S.add("act", lambda e: e.activation(out=ect[:, :], in_=ect[:, :], func=AF.Exp), reads=[ect], writes=[ect])
                    S.add("dve", lambda e: e.tensor_tensor(out=Hst[:, :].rearrange("p (h q) -> p h q", h=16),
                                                             in0=Hst[:, :].rearrange("p (h q) -> p h q", h=16), in1=bc16(ect[:, :]), op=ALU.mult),
                          reads=[Hst, ect], writes=[Hst])
                    for g in range(2):
                        pg = P3 if g == 0 else P4
                        S.add("pe", lambda e, g=g, pg=pg, c=c: e.matmul(pg[:, :], lhsT=B_tm[:, c, g * 128:(g + 1) * 128], rhs=xsd_bf[:, g * 512:(g + 1) * 512],
                                                                         start=True, stop=True),
                              reads=[B_tm, xsd_bf], writes=[pg])
                        S.add("dve", lambda e, g=g, pg=pg: e.tensor_tensor(out=Hst[:, g * 512:(g + 1) * 512], in0=pg[:, :],
                                                                            in1=Hst[:, g * 512:(g + 1) * 512], op=ALU.add),
                              reads=[pg, Hst], pwrites=[Hst])
                    S.add("act", lambda e: e.activation(out=Hbf[:, :], in_=Hst[:, :], func=AF.Copy), reads=[Hst], writes=[Hbf])
                    yield

            def gen_ug():
                for half in range(2):
                    wv = wload(w_in_e[:, 2576 + half * 512:2576 + (half + 1) * 512], 8, 512)
                    samp_cols(wv, 2576 + half * 512, 512)
                    for jj in range(4):
                        j = half * 4 + jj
                        pb = nextPA()
                        projA(wv, jj, hn, pb)
                        S.add("act", lambda e, j=j, pb=pb: e.activation(out=bigA[:, j, :], in_=pb[:, :], func=AF.Copy),
                              reads=[pb], pwrites=[bigA])
                        yield

            def gen_g():
                S.add("act", lambda e: e.activation(out=bigA[:, :, :], in_=bigA[:, :, :], func=AF.Gelu_apprx_tanh), reads=[bigA], writes=[bigA])
                for half in range(2):
                    wv = wload(w_in_e[:, 4624 + half * 512:4624 + (half + 1) * 512], 8, 512)
                    samp_cols(wv, 4624 + half * 512, 512)
                    for jj in range(4):
                        j = half * 4 + jj
                        pb = nextPA()
                        projA(wv, jj, hn, pb)
                        sgb = sg[j % 2]
                        S.add("act", lambda e, sgb=sgb, pb=pb: e.activation(out=sgb[:, :], in_=pb[:, :], func=AF.Silu), reads=[pb], writes=[sgb])
                        S.add("dve", lambda e, j=j, sgb=sgb: e.tensor_tensor(out=bigA[:, j, :], in0=bigA[:, j, :], in1=sgb[:, :], op=ALU.mult),
                              reads=[bigA, sgb], pwrites=[bigA])
                        yield
            g1, g2 = gen_ssd(), gen_ug()
            n1 = 0
            done2 = False
            for _ in g1:
                n1 += 1
                if n1 % 4 == 0 and not done2:
                    if next(g2, "END") == "END":
                        done2 = True
            if not done2:
                for _ in g2:
                    pass
            for _ in gen_g():
                pass
            set_pa([P0, P1, P3, P4, P5, P6])
            for half in range(2):
                wv = wload(w_in_e[:, 3600 + half * 512:3600 + (half + 1) * 512], 8, 512)
                samp_cols(wv, 3600 + half * 512, 512)
                for b4 in range(4):
                    pb = nextPA()
                    for k in range(8):
                        S.add("pe", lambda e, k=k, b4=b4, wv=wv, pb=pb: e.matmul(pb[:, :], lhsT=hn[:, k, b4 * 128:(b4 + 1) * 128],
                                                                                  rhs=wv[1][:, k, :], start=(k == 0), stop=(k == 7)),
                              reads=[wv[0], hn], writes=[pb])
                    S.add("act", lambda e, b4=b4, half=half, pb=pb: e.activation(out=bigZ[:, b4, half * 512:(half + 1) * 512],
                                                                                  in_=pb[:, :], func=AF.Gelu_apprx_tanh),
                          reads=[pb], pwrites=[bigZ])
            for b4 in range(4):
                for half in range(2):
                    S.add("dve", lambda e, b4=b4, half=half: e.bn_stats(out=bnst[:, b4, half, :], in_=bigZ[:, b4, half * 512:(half + 1) * 512]),
                          reads=[bigZ], pwrites=[bnst])
                S.add("dve", lambda e, b4=b4: e.bn_aggr(out=mv[:, b4, :], in_=bnst[:, b4, :, :].rearrange("p a b -> p (a b)")),
                      reads=[bnst], pwrites=[mv])
            S.add("act", lambda e: e.activation(out=rv[:, :], in_=mv[:, :, 1], func=AF.Ln, bias=EPS), reads=[mv], writes=[rv])
            S.add("act", lambda e: e.activation(out=rv[:, :], in_=rv[:, :], func=AF.Exp, scale=-0.5), reads=[rv], writes=[rv])
            for b4 in range(4):
                S.add("dve", lambda e, b4=b4: e.tensor_scalar(out=vhat[:, b4, :], in0=bigZ[:, b4, :], scalar1=mv[:, b4, 0:1], scalar2=rv[:, b4:b4 + 1],
                                                               op0=ALU.subtract, op1=ALU.mult),
                      reads=[bigZ, mv, rv], pwrites=[vhat])
            if stage <= 0.7:
                return
            for b4 in range(4):
                bs_ = slice(b4 * 128, (b4 + 1) * 128)
                for gh in range(2):
                    pb = nextPA()
                    for g4 in range(4):
                        g = gh * 4 + g4
                        S.add("pe", lambda e, g=g, g4=g4, b4=b4, pb=pb: e.matmul(pb[:, g4 * 128:(g4 + 1) * 128], lhsT=vhat[:, b4, g * 128:(g + 1) * 128],
                                                                                  rhs=WmT[:, g, :], start=True, stop=True),
                              reads=[vhat, WmT], pwrites=[pb])
                    if stage <= 0.71:
                        continue
                    sgb = sg[gh]
                    for g4 in range(4):
                        g = gh * 4 + g4
                        S.add("dve", lambda e, g=g, g4=g4, pb=pb, sgb=sgb: e.scalar_tensor_tensor(
                            out=sgb[:, g4 * 128:(g4 + 1) * 128], in0=pb[:, g4 * 128:(g4 + 1) * 128],
                            scalar=pfm[:, PC("lng") + g:PC("lng") + g + 1], in1=Rg[:, g, :], op0=ALU.mult, op1=ALU.add),
                            reads=[pb, pfm, Rg], pwrites=[sgb])
                    if stage <= 0.72:
                        continue
                    S.add("dve", lambda e, gh=gh, sgb=sgb, bs_=bs_: e.tensor_tensor(
                        out=mix[:, 8 + gh * 4:8 + gh * 4 + 4, bs_], in0=sgb[:, :].rearrange("p (g t) -> p g t", g=4),
                        in1=bigA[:, gh * 4:gh * 4 + 4, bs_], op=ALU.mult),
                        reads=[sgb, bigA], pwrites=[mix])
            if stage <= 0.8:
                return
            S.dma("sp", bigA[:, :, :], xT[:, t0:t0 + TT].rearrange("(k p) t -> p k t", p=128), writes=[bigA], group=bigA)
            if ti + 1 < NT:
                l0_load(ti + 1)
                norm_sq(l0_pieces, l0_pbufs, l0_sq, xh_tm)
            for ob in range(4):
                if ob == 2 and ti + 1 < NT:
                    norm_rest(l0_pieces, l0_pbufs, l0_sq, xh_tm, PC("ne"))
                wv = wload(w_out_e[:, ob * 256:(ob + 1) * 256], 16, 256)
                for dj2 in range(2):
                    dj = ob * 2 + dj2
                    pb = nextPA()
                    for ek in range(16):
                        S.add("pe", lambda e, ek=ek, dj2=dj2, wv=wv, pb=pb: e.matmul(pb[:, :], lhsT=wv[1][:, ek, dj2 * 128:(dj2 + 1) * 128],
                                                                                      rhs=mix[:, ek, :], start=(ek == 0), stop=(ek == 15)),
                              reads=[wv[0], mix], writes=[pb])
                    S.add("dve", lambda e, dj=dj, pb=pb: e.tensor_tensor(out=bigA[:, dj, :], in0=pb[:, :], in1=bigA[:, dj, :], op=ALU.add),
                          reads=[pb, bigA], pwrites=[bigA])
            S.dma("sp", x1T[:, t0:t0 + TT].rearrange("(k p) t -> p k t", p=128), bigA[:, :, :], reads=[bigA], pwrites=[x1buf], group=bigA)

        l1_barrier_reads = []
        SA = bigA.t[:, :, :].rearrange("p a b -> p (a b)")
        SM = mix.t[:, :, :].rearrange("p a b -> p (a b)").bitcast(F32)
        SZ = bigZ.t[:, :, :].rearrange("p a b -> p (a b)")
        SX = xh_tm.t[:, :, :].rearrange("p a b -> p (a b)")
        projS = tmpw.t[:, 0:45 * NB].rearrange("p (c b) -> p c b", b=NB)
        hallS = bsbc.t[:, 0:4 * 12 * NB].rearrange("p (k j b) -> p k j b", k=4, j=12)
        xsT_s = sb("xsT_s", [128, 8, NB])
        sqS = sb("sqS", [128, 8, NB], BF16)
        rstdS = sb("rstdS", [128, NB])
        hnS = sb("hnS", [128, 8, NB], BF16)
        convS = sb("convS", [128, 12, NB])
        xcS = sb("xcS", [128, 12, NB])
        dtS = sb("dtS", [16, 3, NB])
        dtE = sb("dtE", [128, 2, 8, NB])
        xsS = sb("xsS", [128, 8, NB])
        vhat32 = vhat.t[:, :, :].rearrange("p a b -> p (a b)").bitcast(F32)
        BC_tm = vhat32[0:16, 1024:1536]
        yS = sb("yS", [128, 8, NB])
        t1S = sb("t1S", [128, 8, NB])
        t2S = sb("t2S", [128, 8, NB])
        stS = sb("stS", [128, 4, NB])
        mixS = sb("mixS", [128, 16, NB], BF16)
        cEXP = vhat32[0:16, 0:1024]
        S.dma("sp", xsT_s[:, :, :], d_xsT.rearrange("(k p) b -> p k b", p=128), writes=[xsT_s], group=xsT_s)

        def bcb(ap2):
            return ap2.unsqueeze(2).broadcast_to([128, ap2.shape[1], NB])

        def bcj(ap2, n):
            return ap2.unsqueeze(1).broadcast_to([128, n, NB])

        def rmsnorm_s(gcol, dst, dst_buf):
            S.add("act", lambda e: e.activation(out=sqS[:, :, :], in_=xsT_s[:, :, :], func=AF.Square), reads=[xsT_s], writes=[sqS])
            pb = nextPA()
            for k in range(8):
                S.add("pe", lambda e, k=k: e.matmul(pb[:, 0:NB], lhsT=cONESB[:, :], rhs=sqS[:, k, :], start=(k == 0), stop=(k == 7)),
                      reads=[cONESB, sqS], writes=[pb])
            S.add("act", lambda e: e.activation(out=rstdS[:, :], in_=pb[:, 0:NB], func=AF.Ln, scale=1.0 / 1024.0, bias=EPS), reads=[pb], writes=[rstdS])
            S.add("act", lambda e: e.activation(out=rstdS[:, :], in_=rstdS[:, :], func=AF.Exp, scale=-0.5), reads=[rstdS], writes=[rstdS])
            S.add("dve", lambda e: e.tensor_tensor(out=t1S[:, :, :], in0=xsT_s[:, :, :], in1=bcj(rstdS[:, :], 8), op=ALU.mult),
                  reads=[xsT_s, rstdS], writes=[t1S])
            S.add("dve", lambda e: e.tensor_tensor(out=dst, in0=t1S[:, :, :], in1=bcb(pfm[:, gcol:gcol + 8]), op=ALU.mult),
                  reads=[t1S, pfm], writes=[dst_buf])

        def proj_s(wsrc, col0, nchunk, c0, func, nrows=128):
            done = 0
            while done < nchunk:
                nb_ = min(4, nchunk - done)
                wv = wload(wsrc[:, col0 + done * 128:col0 + (done + nb_) * 128], 8, nb_ * 128)
                for jj in range(nb_):
                    pb = nextPA()
                    for k in range(8):
                        S.add("pe", lambda e, k=k, jj=jj, wv=wv, pb=pb: e.matmul(pb[:, 0:NB], lhsT=wv[1][:, k, jj * 128:(jj + 1) * 128], rhs=hnS[:, k, :],
                                                                                  start=(k == 0), stop=(k == 7)),
                              reads=[wv[0], hnS], writes=[pb])
                    cc = c0 + done + jj
                    S.add("act", lambda e, cc=cc, pb=pb: e.activation(out=projS[:, cc, :], in_=pb[:, 0:NB], func=func), reads=[pb], pwrites=[tmpw])
                done += nb_

        def outproj_s(wsrc):
            for ob in range(4):
                wv = wload(wsrc[:, ob * 256:(ob + 1) * 256], 16, 256)
                for dj2 in range(2):
                    dj = ob * 2 + dj2
                    pb = nextPA()
                    for ek in range(16):
                        S.add("pe", lambda e, ek=ek, dj2=dj2, wv=wv, pb=pb: e.matmul(pb[:, 0:NB], lhsT=wv[1][:, ek, dj2 * 128:(dj2 + 1) * 128],
                                                                                      rhs=mixS[:, ek, :], start=(ek == 0), stop=(ek == 15)),
                              reads=[wv[0], mixS], writes=[pb])
                    S.add("dve", lambda e, dj=dj, pb=pb: e.tensor_tensor(out=xsT_s[:, dj, :], in0=pb[:, 0:NB], in1=xsT_s[:, dj, :], op=ALU.add),
                          reads=[pb, xsT_s], pwrites=[xsT_s])

        def hist_to_fm(stage_ap, stage_buf, ntap_cols, dst_fn, dst_buf, bulk=None):
            i = 0
            while i < ntap_cols:
                n = min(32, ntap_cols - i)
                pb = nextPA()
                for q in range(n):
                    S.add("pe", lambda e, q=q, i=i, pb=pb: e.transpose(pb[:, q * NB:(q + 1) * NB], stage_ap[0:NB, (i + q) * 128:(i + q + 1) * 128], cIDF[0:NB, 0:NB]),
                          reads=[stage_buf, cIDF], pwrites=[pb])
                if bulk is not None:
                    dst_ap, pat, kw = bulk(i, n)
                    S.add("dve", lambda e, pb=pb, n=n, dst_ap=dst_ap, pat=pat, kw=kw: e.tensor_copy(out=dst_ap, in_=pb[:, 0:n * NB].rearrange(pat, **kw)),
                          reads=[pb], pwrites=[dst_buf])
                else:
                    for q in range(n):
                        S.add("dve", lambda e, q=q, i=i, pb=pb: e.tensor_copy(out=dst_fn(i + q), in_=pb[:, q * NB:(q + 1) * NB]), reads=[pb], pwrites=[dst_buf])
                i += n

        def stats_s(src3, src_buf, nch, inv_n):
            S.add("act", lambda e: e.activation(out=sqS[:, 0:nch, :], in_=src3, func=AF.Square), reads=[src_buf], writes=[sqS])
            S.add("dve", lambda e: e.tensor_copy(out=hnS[:, 0:nch, :], in_=src3), reads=[src_buf], writes=[hnS])
            pb = nextPA()
            for k in range(nch):
                S.add("pe", lambda e, k=k: e.matmul(pb[:, 0:NB], lhsT=cONESB[:, :], rhs=hnS[:, k, :], start=(k == 0), stop=(k == nch - 1)),
                      reads=[cONESB, hnS], writes=[pb])
            pb2 = nextPA()
            for k in range(nch):
                S.add("pe", lambda e, k=k: e.matmul(pb2[:, 0:NB], lhsT=cONESB[:, :], rhs=sqS[:, k, :], start=(k == 0), stop=(k == nch - 1)),
                      reads=[cONESB, sqS], writes=[pb2])
            S.add("dve", lambda e: e.tensor_scalar(out=stS[:, 0, :], in0=pb[:, 0:NB], scalar1=inv_n, scalar2=None, op0=ALU.mult), reads=[pb], pwrites=[stS])
            S.add("dve", lambda e: e.tensor_tensor(out=stS[:, 1, :], in0=stS[:, 0, :], in1=stS[:, 0, :], op=ALU.mult), reads=[stS], pwrites=[stS])
            S.add("dve", lambda e: e.scalar_tensor_tensor(out=stS[:, 1, :], in0=pb2[:, 0:NB], scalar=inv_n, in1=stS[:, 1, :], op0=ALU.mult, op1=ALU.subtract),
                  reads=[pb2, stS], pwrites=[stS])
            S.add("act", lambda e: e.activation(out=stS[:, 2, :], in_=stS[:, 1, :], func=AF.Ln, bias=EPS), reads=[stS], pwrites=[stS])
            S.add("act", lambda e: e.activation(out=stS[:, 2, :], in_=stS[:, 2, :], func=AF.Exp, scale=-0.5), reads=[stS], pwrites=[stS])

        def sample_layer0():
            set_pa([P0, P1])
            S.dma("sp", cEXP, d_exp, pwrites=[vhat], group=vhat)
            S.dma("sp", SA[0:NB, 0:1536], st_sconv[:, 0, :], pwrites=[bigA], group=bigA)
            S.dma("sp", SA[0:NB, 1536:3072], st_sconv[:, 1, :], pwrites=[bigA], group=bigA)
            S.dma("sp", SM[0:NB, 0:1536], st_sconv[:, 2, :], pwrites=[mix], group=mix)
            hist_to_fm(SA, bigA, 24, None, bsbc, bulk=lambda i, n: (hallS[:, 0:2, :, :], "p (k j b) -> p k j b", dict(k=2, j=12)))
            hist_to_fm(SM, mix, 12, None, bsbc, bulk=lambda i, n: (hallS[:, 2, :, :], "p (j b) -> p j b", dict(j=12)))
            S.add("dve", lambda e: e.tensor_copy(out=hallS[:, 3, :, :], in_=projS[:, 8:20, :]), reads=[tmpw], pwrites=[bsbc])
            S.dma("sp", o_sconv_s_new, projS[:, 8:20, :].rearrange("p j b -> p (j b)"), reads=[tmpw], pwrites=[outbuf], group=tmpw)
            S.dma("sp", o_sconv_s_hist, st_sconv[:, 1:3, :], pwrites=[outbuf], group=outbuf)
            wv4 = pfm[:, PC("scw"):PC("scw") + 48].rearrange("p (k j) -> p k j", k=4).unsqueeze(3).broadcast_to([128, 4, 12, NB])
            S.add("dve", lambda e: e.tensor_tensor(out=hallS, in0=hallS, in1=wv4, op=ALU.mult), reads=[bsbc, pfm], writes=[bsbc])
            S.add("dve", lambda e: e.tensor_reduce(out=convS[:, :, :], in_=hallS.rearrange("p k j b -> p j b k"), axis=AX.X, op=ALU.add),
                  reads=[bsbc], writes=[convS])
            for j in range(12):
                bc_ = PC("scb") + j
                S.add("act", lambda e, j=j, bc_=bc_: e.activation(out=xcS[:, j, :], in_=convS[:, j, :], func=AF.Silu, bias=pfm[:, bc_:bc_ + 1]),
                      reads=[convS, pfm], pwrites=[xcS])
            for q in range(2):
                pb = nextPA()
                for j in range(8):
                    S.add("pe", lambda e, q=q, j=j, pb=pb: e.matmul(pb[:, j * NB:(j + 1) * NB], lhsT=cEXP[:, j * 128:(j + 1) * 128], rhs=dtS[:, q, :],
                                                                     start=True, stop=True), reads=[vhat, dtS], pwrites=[pb])
                S.add("dve", lambda e, q=q, pb=pb: e.tensor_copy(out=dtE[:, q, :, :], in_=pb[:, 0:8 * NB].rearrange("p (j b) -> p j b", j=8)),
                      reads=[pb], pwrites=[dtE])
            S.add("dve", lambda e: e.tensor_tensor(out=xsS[:, :, :], in0=xcS[:, 0:8, :], in1=dtE[:, 0, :, :], op=ALU.mult), reads=[xcS, dtE], writes=[xsS])
            pb = nextPA()
            for q in range(4):
                S.add("pe", lambda e, q=q, pb=pb: e.transpose(pb[0:NB, q * 128:(q + 1) * 128], xcS[:, 8 + q, :], cIDF[:, :]), reads=[xcS, cIDF], pwrites=[pb])
            S.add("dve", lambda e, pb=pb: e.tensor_copy(out=BC_tm, in_=pb[0:NB, :]), reads=[pb], pwrites=[vhat])
            hbs = [SX[:, 0:1024], SX[:, 1024:2048], SX[:, 3072:4096]]
            ob = SX[:, 2048:3072].rearrange("p (j n) -> p j n", j=8)
            hbB = [Buf(hbs[0], "hb0"), Buf(hbs[1], "hb1"), Buf(hbs[2], "hb2")]
            obB = Buf(SX[:, 2048:3072], "obS")
            l1_barrier_reads.extend(hbB + [obB])
            S.add("dve", lambda e: e.memset(SX[:, 2048:3072], 0.0), writes=[xh_tm] + hbB + [obB])
            for b in range(NB):
                hb = hbs[b % 3].rearrange("p (j n) -> p j n", j=8)
                hB = hbB[b % 3]
                pbc = P5 if b % 2 == 0 else P6
                S.add("pe", lambda e, b=b, pbc=pbc: e.matmul(pbc[:, :], lhsT=cSEL16[:, b * 128:(b + 1) * 128], rhs=BC_tm, start=True, stop=True),
                      reads=[cSEL16, vhat], writes=[pbc])
                S.dma("sp", hb, st_ssm[b].rearrange("(j p) n -> p j n", p=128), writes=[hB], group=hB)
                for j8 in range(8):
                    S.add("act", lambda e, b=b, hb=hb, j8=j8: e.activation(out=hb[:, j8, :], in_=hb[:, j8, :], func=AF.Copy, scale=dtE[:, 1, j8, b:b + 1]),
                          reads=[dtE], pwrites=[hB])
                for g in range(2):
                    S.add("dve", lambda e, b=b, g=g, pbc=pbc: e.tensor_tensor(
                        out=ob[:, 4 * g:4 * g + 4, :], in0=pbc[:, g * 128:(g + 1) * 128].unsqueeze(1).broadcast_to([128, 4, 128]),
                        in1=xsS[:, 4 * g:4 * g + 4, b:b + 1].broadcast_to([128, 4, 128]), op=ALU.mult),
                        reads=[pbc, xsS], pwrites=[obB])
                S.add("dve", lambda e, hb=hb: e.tensor_tensor(out=hb, in0=hb, in1=ob, op=ALU.add), reads=[hB, obB], writes=[hB])
                S.dma("act", o_ssm_s[b].rearrange("(j p) n -> p j n", p=128), hb, reads=[hB], pwrites=[outbuf], group=hB)
                for g in range(2):
                    S.add("dve", lambda e, g=g, pbc=pbc, hb=hb: e.tensor_tensor(
                        out=ob[:, 4 * g:4 * g + 4, :], in0=pbc[:, 256 + g * 128:256 + (g + 1) * 128].unsqueeze(1).broadcast_to([128, 4, 128]),
                        in1=hb[:, 4 * g:4 * g + 4, :], op=ALU.mult), reads=[pbc, hB], pwrites=[obB])
                S.add("dve", lambda e, b=b: e.tensor_reduce(out=yS[:, :, b], in_=ob, axis=AX.X, op=ALU.add), reads=[obB], pwrites=[yS])
            S.add("dve", lambda e: e.tensor_tensor(out=t1S[:, :, :], in0=xcS[:, 0:8, :], in1=bcb(pfm[:, PC("Dfm"):PC("Dfm") + 8]), op=ALU.mult),
                  reads=[xcS, pfm], writes=[t1S])
            S.add("dve", lambda e: e.tensor_tensor(out=yS[:, :, :], in0=yS[:, :, :], in1=t1S[:, :, :], op=ALU.add), reads=[yS, t1S], writes=[yS])
            S.add("dve", lambda e: e.tensor_tensor(out=yS[:, :, :], in0=yS[:, :, :], in1=projS[:, 0:8, :], op=ALU.mult), reads=[yS, tmpw], writes=[yS])
            S.add("act", lambda e: e.activation(out=sqS[:, :, :], in_=yS[:, :, :], func=AF.Square), reads=[yS], writes=[sqS])
            pb = nextPA()
            for g in range(2):
                for k in range(4):
                    S.add("pe", lambda e, g=g, k=k, pb=pb: e.matmul(pb[:, g * NB:(g + 1) * NB], lhsT=cONESB[:, :], rhs=sqS[:, 4 * g + k, :],
                                                                     start=(k == 0), stop=(k == 3)), reads=[cONESB, sqS], pwrites=[pb])
            S.add("act", lambda e, pb=pb: e.activation(out=stS[:, 0:2, :], in_=pb[:, 0:2 * NB].rearrange("p (g b) -> p g b", g=2), func=AF.Ln,
                                                        scale=1.0 / 512.0, bias=EPS), reads=[pb], pwrites=[stS])
            S.add("act", lambda e: e.activation(out=stS[:, 0:2, :], in_=stS[:, 0:2, :], func=AF.Exp, scale=-0.5), reads=[stS], pwrites=[stS])
            for g in range(2):
                S.add("dve", lambda e, g=g: e.tensor_tensor(out=t1S[:, 4 * g:4 * g + 4, :], in0=yS[:, 4 * g:4 * g + 4, :], in1=bcj(stS[:, g, :], 4), op=ALU.mult),
                      reads=[yS, stS], pwrites=[t1S])
            S.add("dve", lambda e: e.tensor_tensor(out=mixS[:, 0:8, :], in0=t1S[:, :, :], in1=bcb(pfm[:, PC("gn"):PC("gn") + 8]), op=ALU.mult),
                  reads=[t1S, pfm], pwrites=[mixS])
            stats_s(projS[:, 29:37, :], tmpw, 8, 1.0 / 1024.0)
            S.add("dve", lambda e: e.tensor_tensor(out=t1S[:, :, :], in0=projS[:, 29:37, :], in1=bcj(stS[:, 0, :], 8), op=ALU.subtract), reads=[tmpw, stS], writes=[t1S])
            S.add("dve", lambda e: e.tensor_tensor(out=t1S[:, :, :], in0=t1S[:, :, :], in1=bcj(stS[:, 2, :], 8), op=ALU.mult), reads=[t1S, stS], writes=[t1S])
            S.add("dve", lambda e: e.tensor_tensor(out=t1S[:, :, :], in0=t1S[:, :, :], in1=bcb(pfm[:, PC("lng"):PC("lng") + 8]), op=ALU.mult), reads=[t1S, pfm], writes=[t1S])
            S.add("dve", lambda e: e.tensor_tensor(out=t2S[:, :, :], in0=t1S[:, :, :], in1=bcb(pfm[:, PC("lnb"):PC("lnb") + 8]), op=ALU.add), reads=[t1S, pfm], writes=[t2S])
            S.dma("sp", o_gv_s, t2S[:, :, :].rearrange("p j b -> p (j b)"), reads=[t2S], pwrites=[outbuf], group=t2S)
            S.add("dve", lambda e: e.tensor_tensor(out=t1S[:, :, :], in0=t2S[:, :, :], in1=bcb(pfm[:, PC("w00"):PC("w00") + 8]), op=ALU.mult), reads=[t2S, pfm], writes=[t1S])
            S.add("dve", lambda e: e.tensor_tensor(out=t1S[:, :, :], in0=t1S[:, :, :], in1=bcb(pfm[:, PC("b0"):PC("b0") + 8]), op=ALU.add), reads=[t1S, pfm], writes=[t1S])
            S.add("dve", lambda e: e.tensor_tensor(out=t1S[:, :, :], in0=t1S[:, :, :], in1=projS[:, 21:29, :], op=ALU.mult), reads=[t1S, tmpw], writes=[t1S])
            S.add("dve", lambda e: e.tensor_tensor(out=mixS[:, 8:16, :], in0=t1S[:, :, :], in1=projS[:, 37:45, :], op=ALU.mult), reads=[t1S, tmpw], pwrites=[mixS])
            outproj_s(w_out_e)
            rmsnorm_s(PC("no"), hnS[:, :, :], hnS)

        def sample_layer1():
            set_pa([P0, P1])
            hall31 = SZ[:, 0:31 * 8 * NB].rearrange("p (k j b) -> p k j b", k=31, j=8)
            hall4 = hallS[:, :, 0:8, :]
            S.add("dve", lambda e: e.tensor_tensor(out=hall31[:, 30, :, :], in0=projS[:, 0:8, :], in1=projS[:, 8:16, :], op=ALU.mult), reads=[tmpw], pwrites=[bigZ])
            S.dma("sp", o_ccv_s_new, hall31[:, 30, :, :].rearrange("p j b -> p (j b)"), reads=[bigZ], pwrites=[outbuf], group=bigZ)
            S.dma("sp", o_ccv_s_hist, st_ccv[:, 1:30, :], pwrites=[outbuf], group=outbuf)
            for gi in range(8):
                k0 = gi * 4
                nk = min(4, 30 - k0)
                stg_ap, stg_buf = (SA, bigA) if gi % 2 == 0 else (SM, mix)
                S.dma("sp", stg_ap[0:NB, 0:nk * 1024].rearrange("b (k c) -> b k c", k=nk), st_ccv[:, k0:k0 + nk, :], pwrites=[stg_buf], group=stg_buf)
                hist_to_fm(stg_ap, stg_buf, nk * 8, None, bigZ, bulk=lambda i, n, k0=k0, nk=nk: (hall31[:, k0:k0 + nk, :, :], "p (k j b) -> p k j b", dict(k=nk, j=8)))
            wv31 = pfm[:, PC("ccw"):PC("ccw") + 248].rearrange("p (k j) -> p k j", k=31).unsqueeze(3).broadcast_to([128, 31, 8, NB])
            S.add("dve", lambda e: e.tensor_tensor(out=hall31, in0=hall31, in1=wv31, op=ALU.mult), reads=[bigZ, pfm], writes=[bigZ])
            S.add("dve", lambda e: e.tensor_reduce(out=convS[:, 0:8, :], in_=hall31.rearrange("p k j b -> p j b k"), axis=AX.X, op=ALU.add),
                  reads=[bigZ], writes=[convS])
            S.add("dve", lambda e: e.tensor_tensor(out=convS[:, 0:8, :], in0=convS[:, 0:8, :], in1=bcb(pfm[:, PC("ccb"):PC("ccb") + 8]), op=ALU.add),
                  reads=[convS, pfm], writes=[convS])
            stats_s(convS[:, 0:8, :], convS, 8, 1.0 / 1024.0)
            S.add("dve", lambda e: e.tensor_tensor(out=t1S[:, :, :], in0=convS[:, 0:8, :], in1=bcj(stS[:, 0, :], 8), op=ALU.subtract), reads=[convS, stS], writes=[t1S])
            S.add("dve", lambda e: e.tensor_tensor(out=t1S[:, :, :], in0=t1S[:, :, :], in1=bcj(stS[:, 2, :], 8), op=ALU.mult), reads=[t1S, stS], writes=[t1S])
            S.add("dve", lambda e: e.tensor_tensor(out=t1S[:, :, :], in0=t1S[:, :, :], in1=bcb(pfm[:, PC("cclg"):PC("cclg") + 8]), op=ALU.mult), reads=[t1S, pfm], writes=[t1S])
            S.add("dve", lambda e: e.tensor_tensor(out=t1S[:, :, :], in0=t1S[:, :, :], in1=bcb(pfm[:, PC("cclb"):PC("cclb") + 8]), op=ALU.add), reads=[t1S, pfm], writes=[t1S])
            S.add("act", lambda e: e.activation(out=t1S[:, :, :], in_=t1S[:, :, :], func=AF.Silu), reads=[t1S], writes=[t1S])
            S.add("dve", lambda e: e.tensor_tensor(out=mixS[:, 0:8, :], in0=t1S[:, :, :], in1=projS[:, 16:24, :], op=ALU.mult), reads=[t1S, tmpw], pwrites=[mixS])
            S.dma("sp", SA[0:NB, 0:3072].rearrange("b (k c) -> b k c", k=3), st_lconv[:, :, :], pwrites=[bigA], group=bigA)
            hist_to_fm(SA, bigA, 24, None, bsbc, bulk=lambda i, n: (hall4[:, 0:3, :, :], "p (k j b) -> p k j b", dict(k=3, j=8)))
            S.add("dve", lambda e: e.tensor_copy(out=hall4[:, 3, :, :], in_=projS[:, 24:32, :]), reads=[tmpw], pwrites=[bsbc])
            S.dma("sp", o_lconv_s_new, projS[:, 24:32, :].rearrange("p j b -> p (j b)"), reads=[tmpw], pwrites=[outbuf], group=tmpw)
            S.dma("sp", o_lconv_s_hist, st_lconv[:, 1:3, :], pwrites=[outbuf], group=outbuf)
            wl4 = pfm[:, PC("lcw"):PC("lcw") + 32].rearrange("p (k j) -> p k j", k=4).unsqueeze(3).broadcast_to([128, 4, 8, NB])
            S.add("dve", lambda e: e.tensor_tensor(out=hall4, in0=hall4, in1=wl4, op=ALU.mult), reads=[bsbc, pfm], writes=[bsbc])
            S.add("dve", lambda e: e.tensor_reduce(out=xcS[:, 0:8, :], in_=hall4.rearrange("p k j b -> p j b k"), axis=AX.X, op=ALU.add), reads=[bsbc], writes=[xcS])
            S.add("dve", lambda e: e.tensor_tensor(out=xcS[:, 0:8, :], in0=xcS[:, 0:8, :], in1=bcb(pfm[:, PC("lcb"):PC("lcb") + 8]), op=ALU.add), reads=[xcS, pfm], writes=[xcS])
            S.add("dve", lambda e: e.tensor_copy(out=hnS[:, :, :], in_=xcS[:, 0:8, :]), reads=[xcS], writes=[hnS])
            for q, (boff, bcol) in enumerate(((0, "lba"), (8, "lbx"))):
                pb = nextPA()
                for j in range(8):
                    S.add("pe", lambda e, j=j, boff=boff, pb=pb: e.matmul(pb[:, j * NB:(j + 1) * NB], lhsT=MT_bf[:, boff + j, :], rhs=hnS[:, j, :], start=True, stop=True),
                          reads=[MT_bf, hnS], pwrites=[pb])
                dst = t1S if q == 0 else t2S
                S.add("dve", lambda e, pb=pb, dst=dst, bcol=bcol: e.tensor_tensor(out=dst[:, :, :], in0=pb[:, 0:8 * NB].rearrange("p (j b) -> p j b", j=8),
                                                                                  in1=bcb(pfm[:, PC(bcol):PC(bcol) + 8]), op=ALU.add), reads=[pb, pfm], writes=[dst])
                S.add("act", lambda e, dst=dst: e.activation(out=dst[:, :, :], in_=dst[:, :, :], func=AF.Sigmoid), reads=[dst], writes=[dst])
            S.add("dve", lambda e: e.tensor_tensor(out=t1S[:, :, :], in0=t1S[:, :, :], in1=bcb(sp8[:, :]), op=ALU.mult), reads=[t1S, sp8], writes=[t1S])
            S.add("act", lambda e: e.activation(out=t1S[:, :, :], in_=t1S[:, :, :], func=AF.Exp), reads=[t1S], writes=[t1S])
            S.add("dve", lambda e: e.tensor_tensor(out=yS[:, :, :], in0=t1S[:, :, :], in1=t1S[:, :, :], op=ALU.mult), reads=[t1S], writes=[yS])
            S.add("act", lambda e: e.activation(out=yS[:, :, :], in_=yS[:, :, :], func=AF.Sqrt, scale=-1.0, bias=1.0), reads=[yS], writes=[yS])
            S.add("dve", lambda e: e.tensor_tensor(out=t2S[:, :, :], in0=t2S[:, :, :], in1=xcS[:, 0:8, :], op=ALU.mult), reads=[t2S, xcS], writes=[t2S])
            S.add("dve", lambda e: e.tensor_tensor(out=t2S[:, :, :], in0=t2S[:, :, :], in1=yS[:, :, :], op=ALU.mult), reads=[t2S, yS], writes=[t2S])
            S.dma("sp", SM[0:NB, 0:1024], st_lru[:, :], pwrites=[mix], group=mix)
            hist_to_fm(SM, mix, 8, None, xsS, bulk=lambda i, n: (xsS[:, :, :], "p (j b) -> p j b", dict(j=8)))
            S.add("dve", lambda e: e.tensor_tensor(out=xsS[:, :, :], in0=xsS[:, :, :], in1=t1S[:, :, :], op=ALU.mult), reads=[xsS, t1S], writes=[xsS])
            S.add("dve", lambda e: e.tensor_tensor(out=xsS[:, :, :], in0=xsS[:, :, :], in1=t2S[:, :, :], op=ALU.add), reads=[xsS, t2S], writes=[xsS])
            S.dma("sp", o_lru_s, xsS[:, :, :].rearrange("p j b -> p (j b)"), reads=[xsS], pwrites=[outbuf], group=xsS)
            S.add("dve", lambda e: e.tensor_tensor(out=mixS[:, 8:16, :], in0=xsS[:, :, :], in1=projS[:, 32:40, :], op=ALU.mult), reads=[xsS, tmpw], pwrites=[mixS])
            outproj_s(w_out_o)
            rmsnorm_s(PC("nf"), t2S[:, :, :], t2S)
            S.dma("sp", o_y_s, t2S[:, :, :].rearrange("p j b -> p (j b)"), reads=[t2S], pwrites=[outbuf], group=t2S)


        samp_tab = [None]

        def samp_cols(wv, col0, ncols):
            tab = samp_tab[0]
            if tab is None:
                return
            for jj in range(ncols // 128):
                c = col0 + jj * 128
                for (s_, e_, base, func) in tab:
                    if s_ <= c < e_:
                        cc = base + (c - s_) // 128
                        pb = nextPA()
                        for k in range(8):
                            S.add("pe", lambda e, k=k, jj=jj, wv=wv, pb=pb: e.matmul(pb[:, 0:NB], lhsT=wv[1][:, k, jj * 128:(jj + 1) * 128], rhs=hnS[:, k, :],
                                                                                      start=(k == 0), stop=(k == 7)), reads=[wv[0], hnS], writes=[pb])
                        S.add("act", lambda e, cc=cc, pb=pb, func=func: e.activation(out=projS[:, cc, :], in_=pb[:, 0:NB], func=func), reads=[pb], pwrites=[tmpw])

        def samp_dt(wv):
            if samp_tab[0] is None:
                return
            pb = nextPA()
            for k in range(8):
                S.add("pe", lambda e, k=k, wv=wv, pb=pb: e.matmul(pb[0:16, 0:NB], lhsT=wv[1][:, k, :], rhs=hnS[:, k, :], start=(k == 0), stop=(k == 7)),
                      reads=[wv[0], hnS], writes=[pb])
            S.add("act", lambda e, pb=pb: e.activation(out=dtS[:, 0, :], in_=pb[0:16, 0:NB], func=AF.Exp, bias=p16[:, 0:1]), reads=[pb, p16], pwrites=[dtS])
            S.add("act", lambda e: e.activation(out=dtS[:, 0, :], in_=dtS[:, 0, :], func=AF.Ln, bias=1.0), reads=[dtS], pwrites=[dtS])
            S.add("dve", lambda e: e.tensor_scalar(out=dtS[:, 1, :], in0=dtS[:, 0, :], scalar1=ea16[:, 0:1], scalar2=-1.0, op0=ALU.mult, op1=ALU.mult),
                  reads=[dtS, ea16], pwrites=[dtS])
            S.add("act", lambda e: e.activation(out=dtS[:, 1, :], in_=dtS[:, 1, :], func=AF.Exp), reads=[dtS], pwrites=[dtS])

        TAB_L0 = [(0, 1024, 0, AF.Silu), (1024, 2560, 8, AF.Copy), (2576, 3600, 21, AF.Gelu_apprx_tanh),
                  (3600, 4624, 29, AF.Gelu_apprx_tanh), (4624, 5648, 37, AF.Silu)]
        TAB_L1 = [(0, 1024, 0, AF.Copy), (1024, 2048, 8, AF.Sigmoid), (2048, 3072, 16, AF.Silu), (3072, 4096, 24, AF.Copy), (4096, 5120, 32, AF.Silu)]
        if stage >= 3:
            rmsnorm_s(PC("ne"), hnS[:, :, :], hnS)

        for ti in range(int(_os.environ.get('K_L0T', NT if stage >= 1 else 0))):
            samp_tab[0] = TAB_L0 if (stage >= 3 and ti == NT - 1) else None
            layer0_tile(ti)
            samp_tab[0] = None

        hT = sb("hT", [128, 8, 128])
        for j in range(8):
            pb = nextPA()
            S.add("pe", lambda e, j=j, pb=pb: e.transpose(pb[:, 0:128], Hst[:, j * 128:(j + 1) * 128], cIDF[:, :]), reads=[Hst, cIDF], writes=[pb])
            S.add("act", lambda e, j=j, pb=pb: e.activation(out=hT[:, j, :], in_=pb[:, 0:128], func=AF.Copy), reads=[pb], pwrites=[hT])
        S.dma("sp", o_ssm_p.rearrange("(j p) n -> p j n", p=128), hT[:, :, :], reads=[hT], pwrites=[outbuf], group=hT)
        S.dma("sp", o_sconv_p, lastraw[:, :, :].rearrange("p j k -> p (j k)"), reads=[lastraw], pwrites=[outbuf], group=lastraw)
        if stage >= 3:
            sample_layer0()
        GL = 544
        glu_view = xh_tm.t[:, :, :].rearrange("p a b -> p (a b)").bitcast(BF16)
        glu = [Buf(glu_view[:, j * GL:(j + 1) * GL], f"glu{j}") for j in range(8)]
        if not _os.environ.get('K_SKIP_BAR'):
            S.add("dve", lambda e: e.memset(glu_view[:, 0:8 * GL], 0.0), reads=[xh_tm], writes=glu + l1_barrier_reads)
        cvo = bigZ.t[:, :, :].rearrange("p a (c t) -> p (a c) t", t=512)
        v32 = vhat.t[:, :, :].rearrange("p a b -> p (a b)").bitcast(F32)
        mean_sb, var_sb, rstd_ln, mr_sb = v32[:, 0:512], v32[:, 512:1024], v32[:, 1024:1536], v32[:, 1536:2048]
        if not _os.environ.get('K_SKIP_MT'):
            S.dma("pool", MT_bf[:, 0:8, :], d_bda.rearrange("p (j q) -> p j q", j=8), pwrites=[MT_bf], group=MT_bf)
            S.dma("pool", MT_bf[:, 8:16, :], d_bdx.rearrange("p (j q) -> p j q", j=8), pwrites=[MT_bf], group=MT_bf)
        sp8 = sb("sp8", [128, 8])
        sp16 = sb("sp16", [128, 8])
        hist1 = sb("hist1", [128, 8, 3], BF16)
        hcarry = sb("hcarry", [128, 8])
        glu_last = sb("glu_last", [128, 8, 30])
        lastraw1 = sb("lastraw1", [128, 8, 4])
        lamc = PC("lam")
        S.add("act", lambda e: e.activation(out=sp8[:, :], in_=pfm[:, lamc:lamc + 8], func=AF.Exp, scale=-1.0), reads=[pfm], writes=[sp8])
        S.add("act", lambda e: e.activation(out=sp8[:, :], in_=sp8[:, :], func=AF.Ln, bias=1.0), reads=[sp8], writes=[sp8])
        S.add("dve", lambda e: e.tensor_scalar(out=sp16[:, :], in0=sp8[:, :], scalar1=-16.0, scalar2=None, op0=ALU.mult), reads=[sp8], writes=[sp16])
        S.add("dve", lambda e: e.tensor_scalar(out=sp8[:, :], in0=sp8[:, :], scalar1=-8.0, scalar2=None, op0=ALU.mult), reads=[sp8], writes=[sp8])
        S.add("dve", lambda e: e.memset(hist1[:, :, :], 0.0), writes=[hist1])
        S.add("dve", lambda e: e.memset(hcarry[:, :], 0.0), writes=[hcarry])
        Lflat = Lbuf.t[:, :, :].rearrange("p a b -> p (a b)")
        t_xc, t_r = yoff[:, 0:512], yoff[:, 512:1024]
        t_i, t_a = ysb[:, 0:512], ysb[:, 512:1024]
        t_b, t_h = Hst[:, 0:512], Hst[:, 512:1024]
        t_g, t_m = Lflat[:, 0:512], Lflat[:, 512:1024]
        xc_bf = xs_bf[:, 0:512]
        SZl = bigZ.t[:, :, :].rearrange("p a b -> p (a b)")
        lzv = [SZl[:, i * 512:(i + 1) * 512] for i in range(8)]
        lz = [Buf(lzv[i], f"lz{i}") for i in range(8)]
        lbar = sb("lbar", [128, 2])

        l1_pieces = [t_xc, t_r, t_i, t_a, t_b, t_h, t_g, t_m]
        l1_pbufs = [yoff, yoff, ysb, ysb, Hst, Hst, Lbuf, Lbuf]
        l1_sq = vhat.t[:, :, :].rearrange("p a (c t) -> p (a c) t", t=512)

        def l1_load(ti):
            for k in range(8):
                S.dma("sp", l1_pieces[k], x1T[k * 128:(k + 1) * 128, ti * TT:(ti + 1) * TT], reads=[x1buf], pwrites=[l1_pbufs[k]], group=l1_pbufs[k])

        def layer1_tile(ti):
            t0 = ti * TT
            last = (ti == NT - 1)
            S.dma("sp", bigA[:, :, :], (xT if _os.environ.get("K_NOX1") else x1T)[:, t0:t0 + TT].rearrange("(k p) t -> p k t", p=128), reads=[x1buf], writes=[bigA], group=bigA)
            set_pa([P0, P1, P7, P3, P4])
            if ti == 0:
                l1_load(0)
                norm_sq(l1_pieces, l1_pbufs, l1_sq, vhat)
                norm_rest(l1_pieces, l1_pbufs, l1_sq, vhat, PC("no"))
            if stage <= 1.1:
                return
            pend_hist = []
            for half in range(2):
                wa_ = wload(w_in_o[:, half * 512:(half + 1) * 512], 8, 512)
                samp_cols(wa_, half * 512, 512)
                wb_ = wload(w_in_o[:, 1024 + half * 512:1024 + (half + 1) * 512], 8, 512)
                samp_cols(wb_, 1024 + half * 512, 512)
                for jj in range(4):
                    j = half * 4 + jj
                    pA = nextPA()
                    projA(wa_, jj, hn, pA)
                    pB = nextPA()
                    projA(wb_, jj, hn, pB)
                    sgb = sg[j % 2]
                    gj = glu[j]
                    PC2 = (P2, P5, P6)[j % 3]
                    S.add("act", lambda e, sgb=sgb, pB=pB: e.activation(out=sgb[:, :], in_=pB[:, :], func=AF.Sigmoid), reads=[pB], writes=[sgb])
                    S.add("dve", lambda e, gj=gj, pA=pA, sgb=sgb: e.tensor_tensor(out=gj[:, 30:542], in0=pA[:, :], in1=sgb[:, :], op=ALU.mult),
                          reads=[pA, sgb], pwrites=[gj])
                    if last:
                        S.add("dve", lambda e, j=j, pA=pA, sgb=sgb: e.tensor_tensor(out=glu_last[:, j, :], in0=pA[:, 482:512], in1=sgb[:, 482:512], op=ALU.mult),
                              reads=[pA, sgb], pwrites=[glu_last])
                    for k in range(31):
                        dg = diag(PC("ccw") + k * 8 + j, "dve" if k % 4 != 3 else "act")
                        S.add("pe", lambda e, k=k, dg=dg, gj=gj, PC2=PC2: e.matmul(PC2[:, :], lhsT=dg[:, :], rhs=gj[:, k:k + 512], start=(k == 0), stop=(k == 30)),
                              reads=[dg, gj], writes=[PC2])
                    bc_ = PC("ccb") + j
                    S.add("act", lambda e, j=j, bc_=bc_, PC2=PC2: e.activation(out=cvo[:, j, :], in_=PC2[:, :], func=AF.Identity, bias=pfm[:, bc_:bc_ + 1]),
                          reads=[PC2, pfm], pwrites=[bigZ])
                    if pend_hist:
                        pend_hist.pop()()
                    pend_hist.append(lambda gj=gj: S.add("dve", lambda e: e.tensor_copy(out=gj[:, 0:30], in_=gj[:, 512:542]), reads=[gj], pwrites=[gj]))
            if stage <= 1.2:
                return
            while pend_hist:
                pend_hist.pop()()
            S.add("act", lambda e: e.activation(out=mix[:, 0:8, :], in_=cvo, func=AF.Square), reads=[bigZ], writes=[mix])
            S.add("dve", lambda e: e.tensor_copy(out=mix[:, 8:16, :], in_=cvo), reads=[bigZ], pwrites=[mix])
            for k in range(8):
                S.add("pe", lambda e, k=k: e.matmul(P5[:, :], lhsT=cONESB[:, :], rhs=mix[:, 8 + k, :], start=(k == 0), stop=(k == 7)),
                      reads=[cONESB, mix], writes=[P5])
            for k in range(8):
                S.add("pe", lambda e, k=k: e.matmul(P6[:, :], lhsT=cONESB[:, :], rhs=mix[:, k, :], start=(k == 0), stop=(k == 7)),
                      reads=[cONESB, mix], writes=[P6])
            S.add("dve", lambda e: e.tensor_scalar(out=mean_sb, in0=P5[:, :], scalar1=1.0 / 1024.0, scalar2=None, op0=ALU.mult), reads=[P5], pwrites=[vhat])
            S.add("dve", lambda e: e.tensor_tensor(out=var_sb, in0=mean_sb, in1=mean_sb, op=ALU.mult), reads=[vhat], pwrites=[vhat])
            S.add("dve", lambda e: e.scalar_tensor_tensor(out=var_sb, in0=P6[:, :], scalar=1.0 / 1024.0, in1=var_sb, op0=ALU.mult, op1=ALU.subtract),
                  reads=[P6, vhat], pwrites=[vhat])
            S.add("act", lambda e: e.activation(out=rstd_ln, in_=var_sb, func=AF.Ln, bias=EPS), reads=[vhat], pwrites=[vhat])
            S.add("act", lambda e: e.activation(out=rstd_ln, in_=rstd_ln, func=AF.Exp, scale=-0.5), reads=[vhat], pwrites=[vhat])
            S.add("dve", lambda e: e.tensor_tensor(out=mr_sb, in0=mean_sb, in1=rstd_ln, op=ALU.mult), reads=[vhat], pwrites=[vhat])
            if stage <= 1.3:
                return
            for half in range(2):
                wv = wload(w_in_o[:, 2048 + half * 512:2048 + (half + 1) * 512], 8, 512)
                samp_cols(wv, 2048 + half * 512, 512)
                for jj in range(4):
                    j = half * 4 + jj
                    pG = nextPA()
                    projA(wv, jj, hn, pG)
                    sgb = sg[j % 2]
                    tb = xcf[j % 2]
                    S.add("act", lambda e, sgb=sgb, pG=pG: e.activation(out=sgb[:, :], in_=pG[:, :], func=AF.Silu), reads=[pG], writes=[sgb])
                    S.add("dve", lambda e, j=j, tb=tb: e.tensor_tensor(out=tb[:, :], in0=cvo[:, j, :], in1=rstd_ln, op=ALU.mult), reads=[bigZ, vhat], writes=[tb])
                    S.add("dve", lambda e, tb=tb: e.tensor_tensor(out=tb[:, :], in0=tb[:, :], in1=mr_sb, op=ALU.subtract), reads=[tb, vhat], writes=[tb])
                    gc_, bc_ = PC("cclg") + j, PC("cclb") + j
                    S.add("act", lambda e, tb=tb, gc_=gc_, bc_=bc_: e.activation(out=tb[:, :], in_=tb[:, :], func=AF.Silu, scale=pfm[:, gc_:gc_ + 1],
                                                                                 bias=pfm[:, bc_:bc_ + 1]), reads=[tb, pfm], writes=[tb])
                    S.add("dve", lambda e, j=j, tb=tb, sgb=sgb: e.tensor_tensor(out=mix[:, j, :], in0=tb[:, :], in1=sgb[:, :], op=ALU.mult),
                          reads=[tb, sgb], pwrites=[mix])
            if stage <= 1.4:
                return
            set_pa([P0, P1])
            S.add("dve", lambda e: e.memset(lbar[:, :], 0.0), writes=[bigZ, lbar] + lz)
            for half in range(2):
                wx_ = wload(w_in_o[:, 3072 + half * 512:3072 + (half + 1) * 512], 8, 512)
                samp_cols(wx_, 3072 + half * 512, 512)
                wg_ = wload(w_in_o[:, 4096 + half * 512:4096 + (half + 1) * 512], 8, 512)
                samp_cols(wg_, 4096 + half * 512, 512)
                for jj in range(4):
                    j = half * 4 + jj
                    od = j % 2
                    if od == 0:
                        V = dict(xc=t_xc, r=t_r, i=t_i, a=t_a, b=t_b, h=t_h, g=t_g, m=t_m, xb=xc_bf)
                        Bf = dict(xc=yoff, r=yoff, i=ysb, a=ysb, b=Hst, h=Hst, g=Lbuf, m=Lbuf, xb=xs_bf)
                        PCV, PGA, PGX = P2, P3, P4
                    else:
                        V = dict(xc=lzv[0], r=lzv[1], i=lzv[2], a=lzv[3], b=lzv[4], h=lzv[5], g=lzv[6], m=lzv[7], xb=xsd_bf[:, 0:512])
                        Bf = dict(xc=lz[0], r=lz[1], i=lz[2], a=lz[3], b=lz[4], h=lz[5], g=lz[6], m=lz[7], xb=xsd_bf)
                        PCV, PGA, PGX = P5, P6, P7
                    PGAv = PGA.t if PGA is not P7 else P7t
                    PGXv = PGX.t if PGX is not P7 else P7t
                    pX = nextPA()
                    projA(wx_, jj, hn, pX)
                    raw = raws[j % 2]
                    S.add("dve", lambda e, j=j, raw=raw: e.tensor_copy(out=raw[:, 0:3], in_=hist1[:, j, :]), reads=[hist1], pwrites=[raw])
                    S.add("act", lambda e, raw=raw, pX=pX: e.activation(out=raw[:, 3:515], in_=pX[:, :], func=AF.Copy), reads=[pX], pwrites=[raw])
                    if last:
                        S.add("act", lambda e, j=j, pX=pX: e.activation(out=lastraw1[:, j, :], in_=pX[:, 508:512], func=AF.Copy), reads=[pX], pwrites=[lastraw1])
                    S.add("dve", lambda e, j=j, raw=raw: e.tensor_copy(out=hist1[:, j, :], in_=raw[:, 512:515]), reads=[raw], pwrites=[hist1])
                    for k in range(4):
                        dg = diag(PC("lcw") + k * 8 + j, "dve")
                        S.add("pe", lambda e, k=k, dg=dg, raw=raw, PCV=PCV: e.matmul(PCV[:, :], lhsT=dg[:, :], rhs=raw[:, k:k + 512], start=(k == 0), stop=(k == 3)),
                              reads=[dg, raw], writes=[PCV])
                    bc_ = PC("lcb") + j
                    S.add("act", lambda e, bc_=bc_, V=V, PCV=PCV: e.activation(out=V["xc"], in_=PCV[:, :], func=AF.Identity, bias=pfm[:, bc_:bc_ + 1]),
                          reads=[PCV, pfm], pwrites=[Bf["xc"]])
                    S.add("dve", lambda e, V=V: e.tensor_copy(out=V["xb"], in_=V["xc"]), reads=[Bf["xc"]], writes=[Bf["xb"]])
                    S.add("pe", lambda e, j=j, V=V, PGAv=PGAv: e.matmul(PGAv[:, :], lhsT=MT_bf[:, j, :], rhs=V["xb"], start=True, stop=True), reads=[MT_bf, Bf["xb"]], writes=[PGA])
                    S.add("pe", lambda e, j=j, V=V, PGXv=PGXv: e.matmul(PGXv[:, :], lhsT=MT_bf[:, 8 + j, :], rhs=V["xb"], start=True, stop=True), reads=[MT_bf, Bf["xb"]], writes=[PGX])
                    ca, cx = PC("lba") + j, PC("lbx") + j
                    pGd = nextPA()
                    projA(wg_, jj, hn, pGd)
                    S.add("act", lambda e, ca=ca, V=V, PGAv=PGAv: e.activation(out=V["r"], in_=PGAv[:, :], func=AF.Sigmoid, bias=pfm[:, ca:ca + 1]), reads=[PGA, pfm], pwrites=[Bf["r"]])
                    S.add("act", lambda e, cx=cx, V=V, PGXv=PGXv: e.activation(out=V["i"], in_=PGXv[:, :], func=AF.Sigmoid, bias=pfm[:, cx:cx + 1]), reads=[PGX, pfm], pwrites=[Bf["i"]])
                    S.add("act", lambda e, pGd=pGd, V=V: e.activation(out=V["g"], in_=pGd[:, :], func=AF.Sigmoid), reads=[pGd], pwrites=[Bf["g"]])
                    S.add("dve", lambda e, pGd=pGd, V=V: e.tensor_tensor(out=V["g"], in0=pGd[:, :], in1=V["g"], op=ALU.mult), reads=[pGd, Bf["g"]], pwrites=[Bf["g"]])
                    S.add("act", lambda e, j=j, V=V: e.activation(out=V["a"], in_=V["r"], func=AF.Exp, scale=sp8[:, j:j + 1]), reads=[Bf["r"], sp8], pwrites=[Bf["a"]])
                    S.add("act", lambda e, j=j, V=V: e.activation(out=V["m"], in_=V["r"], func=AF.Exp, scale=sp16[:, j:j + 1]), reads=[Bf["r"], sp16], pwrites=[Bf["m"]])
                    S.add("act", lambda e, V=V: e.activation(out=V["m"], in_=V["m"], func=AF.Ln, scale=-1.0, bias=1.0), reads=[Bf["m"]], pwrites=[Bf["m"]])
                    S.add("act", lambda e, V=V: e.activation(out=V["m"], in_=V["m"], func=AF.Exp, scale=0.5), reads=[Bf["m"]], pwrites=[Bf["m"]])
                    S.add("dve", lambda e, V=V: e.tensor_tensor(out=V["b"], in0=V["i"], in1=V["xc"], op=ALU.mult), reads=[Bf["i"], Bf["xc"]], pwrites=[Bf["b"]])
                    S.add("dve", lambda e, V=V: e.tensor_tensor(out=V["b"], in0=V["b"], in1=V["m"], op=ALU.mult), reads=[Bf["b"], Bf["m"]], pwrites=[Bf["b"]])
                    S.add("dve", lambda e, j=j, V=V: e.tensor_tensor_scan(out=V["h"], data0=V["a"], data1=V["b"], initial=hcarry[:, j:j + 1], op0=ALU.mult, op1=ALU.add),
                          reads=[Bf["a"], Bf["b"], hcarry], pwrites=[Bf["h"]])
                    S.add("dve", lambda e, j=j, V=V: e.tensor_copy(out=hcarry[:, j:j + 1], in_=V["h"][:, 511:512]), reads=[Bf["h"]], pwrites=[hcarry])
                    S.add("dve", lambda e, j=j, V=V: e.tensor_tensor(out=mix[:, 8 + j, :], in0=V["h"], in1=V["g"], op=ALU.mult), reads=[Bf["h"], Bf["g"]], pwrites=[mix])
            S.add("dve", lambda e: e.memset(lbar[:, :], 0.0), writes=[bigZ, lbar] + lz)
            if ti + 1 < NT:
                l1_load(ti + 1)
                norm_sq(l1_pieces, l1_pbufs, l1_sq, vhat)
            for ob in range(4):
                if ob == 2 and ti + 1 < NT:
                    norm_rest(l1_pieces, l1_pbufs, l1_sq, vhat, PC("no"))
                wv = wload(w_out_o[:, ob * 256:(ob + 1) * 256], 16, 256)
                for dj2 in range(2):
                    dj = ob * 2 + dj2
                    pb = nextPA()
                    for ek in range(16):
                        S.add("pe", lambda e, ek=ek, dj2=dj2, wv=wv, pb=pb: e.matmul(pb[:, :], lhsT=wv[1][:, ek, dj2 * 128:(dj2 + 1) * 128],
                                                                                      rhs=mix[:, ek, :], start=(ek == 0), stop=(ek == 15)),
                              reads=[wv[0], mix], writes=[pb])
                    S.add("dve", lambda e, dj=dj, pb=pb: e.tensor_tensor(out=bigA[:, dj, :], in0=pb[:, :], in1=bigA[:, dj, :], op=ALU.add),
                          reads=[pb, bigA], pwrites=[bigA])
            if stage <= 1.6:
                return
            rmsnorm_fm(bigA, PC("nf"), bigZ, dst_view=cvo)
            S.dma("sp", yT[:, t0:t0 + TT].rearrange("(k p) t -> p k t", p=128), cvo, reads=[bigZ], pwrites=[outbuf], group=bigZ)

        if stage > 1:
            for ti in range(int(_os.environ.get('K_L1T', NT))):
                samp_tab[0] = TAB_L1 if (stage >= 3 and ti == NT - 1) else None
                layer1_tile(ti)
                samp_tab[0] = None
            S.dma("sp", o_ccv_p, glu_last[:, :, :].rearrange("p j k -> p (j k)"), reads=[glu_last], pwrites=[outbuf], group=glu_last)
            S.dma("sp", o_lconv_p, lastraw1[:, :, :].rearrange("p j k -> p (j k)"), reads=[lastraw1], pwrites=[outbuf], group=lastraw1)
            S.dma("sp", o_lru_p, hcarry[:, :], reads=[hcarry], pwrites=[outbuf], group=hcarry)
        if stage >= 3:
            sample_layer1()
        final_reads = [outbuf, bigA]
        S.add("sp", lambda e: e.nop(), reads=final_reads, writes=final_reads)
        S.finalize_and_emit(es)
    return nc


def _consts():
    idf = np.eye(128, dtype=np.float32)
    s = np.arange(128)
    negm = np.where(s[None, :] >= s[:, None], 0.0, -1.0e5).astype(np.float32)
    triu = (s[:, None] <= s[None, :]).astype(np.float32)
    sel16 = np.zeros((16, 16, 128), np.float32)
    for h in range(16):
        sel16[h, h, :] = 1.0
    sellast = np.zeros((128, 128), np.float32)
    sellast[127, :] = 1.0
    return dict(c_idf=idf, c_negm=negm, c_triu=triu, c_sel16=sel16.reshape(16, 2048), c_sellast=sellast)


def _prepare(inp):
    f = lambda a: np.ascontiguousarray(np.asarray(a, np.float32))
    shared = dict(
        w_in_e=f(inp["w_in_even"][0]), w_out_e=f(inp["w_out_even"][0]),
        w_in_o=f(inp["w_in_odd"][0]), w_out_o=f(inp["w_out_odd"][0]),
        pfm=_build_pfm(inp),
        p16=f(np.stack([inp["ssd_dt_bias"][0], inp["ssd_a_log"][0]], 1)),
        drow=f(inp["ssd_d"][0][None, :]),
        wsT=f(np.transpose(inp["gmlp_w_s"][0], (2, 0, 1)).reshape(128, 1024)),
        bsrow=f(inp["gmlp_b_s"][0].reshape(1, 1024)),
    )
    def _bd(w):
        w = np.asarray(w, np.float32)
        o = np.zeros((128, 8, 128), np.float32)
        for j in range(8):
            o[0:64, j, 0:64] = w[2 * j]
            o[64:128, j, 64:128] = w[2 * j + 1]
        return np.ascontiguousarray(o.reshape(128, 1024))
    shared["bda"] = _bd(inp["lru_wa"][0])
    shared["bdx"] = _bd(inp["lru_wx"][0])
    cexp = np.zeros((16, 8, 128), np.float32)
    for j in range(8):
        cexp[2 * j, j, 0:64] = 1.0
        cexp[2 * j + 1, j, 64:128] = 1.0
    shared["c_exp"] = cexp.reshape(16, 1024)
    shared.update(_consts())
    maps = []
    for c in range(NCORES):
        m = dict(shared)
        m["xT"] = f(inp["x_prompt"][c].T)
        sl = slice(c * NB, (c + 1) * NB)
        m["xsT"] = f(inp["x_sample"][sl, 0, :].T)
        m["st_ssm"] = f(inp["state_ssm"][0, sl].reshape(NB, 1024, 128))
        m["st_sconv"] = f(inp["state_ssd_conv"][0, sl])
        m["st_ccv"] = f(inp["state_ccv"][0, sl])
        m["st_lconv"] = f(inp["state_lru_conv"][0, sl])
        m["st_lru"] = f(inp["state_lru"][0, sl])
        maps.append(m)
    return maps


_NC_CACHE = {}


def _run(inp, stage=99):
    maps = _prepare(inp)
    if stage not in _NC_CACHE:
        _NC_CACHE[stage] = build_program(stage)
    nc = _NC_CACHE[stage]
    res = run_bass_kernel_spmd(nc, maps, core_ids=list(range(NCORES)))
    return res.results


def _fm2tm(a, nch):
    return np.ascontiguousarray(a.reshape(128, nch, NB).transpose(2, 1, 0).reshape(NB, nch * 128))


def _fmlast(a, nch, k, keep):
    return np.ascontiguousarray(a.reshape(128, nch, k)[:, :, k - keep:].transpose(2, 1, 0).reshape(keep, nch * 128))


def kernel(**inputs):
    res = _run(inputs, 99)
    B = NCORES
    y_p = np.stack([np.ascontiguousarray(r["yT"].T) for r in res])
    y_s = np.concatenate([_fm2tm(r["o_y_s"], 8) for r in res])[:, None, :]
    ssm_p = np.stack([r["o_ssm_p"].reshape(16, 64, 128) for r in res])[None]
    ssm_s = np.concatenate([r["o_ssm_s"].reshape(NB, 16, 64, 128) for r in res])[None]
    sconv_p = np.stack([_fmlast(r["o_sconv_p"], 12, 4, 3) for r in res])[None]
    sconv_s = np.concatenate([np.concatenate([r["o_sconv_s_hist"], _fm2tm(r["o_sconv_s_new"], 12)[:, None, :]], 1) for r in res])[None]
    gv_s = np.concatenate([_fm2tm(r["o_gv_s"], 8) for r in res])[None, :, None, :]
    ccv_p = np.stack([_fmlast(r["o_ccv_p"], 8, 30, 30) for r in res])[None]
    ccv_s = np.concatenate([np.concatenate([r["o_ccv_s_hist"], _fm2tm(r["o_ccv_s_new"], 8)[:, None, :]], 1) for r in res])[None]
    lconv_p = np.stack([_fmlast(r["o_lconv_p"], 8, 4, 3) for r in res])[None]
    lconv_s = np.concatenate([np.concatenate([r["o_lconv_s_hist"], _fm2tm(r["o_lconv_s_new"], 8)[:, None, :]], 1) for r in res])[None]
    lru_p = np.stack([np.ascontiguousarray(r["o_lru_p"].T).reshape(1024) for r in res])[None]
    lru_s = np.concatenate([_fm2tm(r["o_lru_s"], 8) for r in res])[None]
    outs = (y_p, y_s, ssm_p, ssm_s, sconv_p, sconv_s, gv_s, ccv_p, ccv_s, lconv_p, lconv_s, lru_p, lru_s)
    return tuple(np.ascontiguousarray(o.astype(np.float32)) for o in outs)
```

```python
import numpy as np
import ml_dtypes
import concourse.bass as bass
import concourse.mybir as mybir
from concourse.bass_utils import run_bass_kernel_spmd
from contextlib import ExitStack

F32 = mybir.dt.float32
BF16 = mybir.dt.bfloat16
AF = mybir.ActivationFunctionType
ALU = mybir.AluOpType
AX = mybir.AxisListType
COMPUTE = ("pe", "act", "dve", "pool")
EPS = 1e-6
NCORES = 8
SEQ = 2048
TT = 512
NT = SEQ // TT
NB = 16


class Buf:
    _n = 0

    def __init__(self, t, name=None):
        self.t = t
        self.name = name or f"buf{Buf._n}"
        Buf._n += 1
        self.writers = []
        self.readers = []
        self.base = []
        self.dma_ops = []
        self.sem = None

    def __getitem__(self, idx):
        return self.t[idx]


class Op:
    __slots__ = ("eng", "fn", "reads", "writes", "pwrites", "is_dma", "group", "gidx",
                 "eidx", "deps", "waits", "signal", "vc", "gpos")


def _is_pw(w, b):
    return any(x is b for x in w.pwrites)


class Sched:
    def __init__(self, nc):
        self.nc = nc
        self.ops = []
        self.by_eng = {e: [] for e in ("pe", "act", "dve", "pool", "sp")}

    def add(self, eng, fn, reads=(), writes=(), pwrites=(), dma=False, group=None):
        op = Op()
        op.eng, op.fn = eng, fn
        op.reads, op.writes, op.pwrites = list(reads), list(writes), list(pwrites)
        op.is_dma, op.group = dma, group
        op.gpos = len(self.ops)
        op.deps, op.waits, op.signal, op.vc = [], [], False, None
        deps = []
        for b in op.reads:
            deps.extend(b.writers)
        for b in op.writes:
            deps.extend(b.writers)
            deps.extend(b.readers)
        for b in op.pwrites:
            if b.readers:
                b.base = list(b.readers) + list(b.writers)
                b.writers = []
                b.readers = []
            deps.extend(b.base)
            deps.extend(w for w in b.writers if not _is_pw(w, b))
        for b in op.reads:
            b.readers.append(op)
        for b in op.writes:
            b.writers = [op]
            b.readers = []
            b.base = []
        for b in op.pwrites:
            b.writers.append(op)
        seen = set()
        for d in deps:
            if d is op or id(d) in seen:
                continue
            seen.add(id(d))
            if d.eng == "pe" and eng == "pe" and not d.is_dma and not dma:
                continue
            op.deps.append(d)
        if dma:
            op.gidx = len(group.dma_ops)
            group.dma_ops.append(op)
        op.eidx = len(self.by_eng[eng])
        self.by_eng[eng].append(op)
        self.ops.append(op)
        return op

    def dma(self, eng, out_ap, in_ap, reads=(), writes=(), pwrites=(), group=None, **kw):
        return self.add(eng, lambda e: e.dma_start(out=out_ap, in_=in_ap, **kw),
                        reads=reads, writes=writes, pwrites=pwrites, dma=True, group=group)

    def finalize_and_emit(self, es):
        nc = self.nc
        know = {e: {} for e in self.by_eng}
        for op in self.ops:
            k = know[op.eng]
            for d in op.deps:
                if d.is_dma:
                    key = ("g", id(d.group))
                    val = sum(1 for x in d.group.dma_ops if x.gpos < op.gpos)
                    tok = (key, val, d.group)
                else:
                    key = d.eng
                    val = d.eidx + 1
                    tok = (key, val, None)
                if k.get(key, 0) >= val:
                    continue
                op.waits.append(tok)
                k[key] = val
                if d.vc is not None:
                    for kk, vv in d.vc.items():
                        if k.get(kk, 0) < vv:
                            k[kk] = vv
                if not d.is_dma:
                    d.signal = True
            best = {}
            for tok in op.waits:
                if tok[0] not in best or best[tok[0]][1] < tok[1]:
                    best[tok[0]] = tok
            op.waits = list(best.values())
            op.vc = dict(k)
            if not op.is_dma:
                op.vc[op.eng] = op.eidx + 1
        ordmap = {}
        for e in COMPUTE:
            c, m = 0, {}
            for op in self.by_eng[e]:
                if op.signal:
                    c += 1
                m[op.eidx + 1] = c
            ordmap[e] = m
        esem = {e: es.enter_context(nc.semaphore(f"s_{e}")) for e in COMPUTE}
        groups = {}
        for op in self.ops:
            if op.is_dma and id(op.group) not in groups:
                groups[id(op.group)] = op.group
        for g in groups.values():
            g.sem = es.enter_context(nc.semaphore(f"g_{g.name}"))
        self.n_sems = 4 + len(groups)

        def emit_stream(ename, engine):
            for op in self.by_eng[ename]:
                for (key, val, grp) in op.waits:
                    if grp is not None:
                        engine.wait_ge(grp.sem, 16 * val)
                    else:
                        engine.wait_ge(esem[key], ordmap[key][val])
                ins = op.fn(engine)
                if op.is_dma:
                    ins.then_inc(op.group.sem, 16)
                elif op.signal:
                    ins.then_inc(esem[ename], 1)

        block = es.enter_context(nc.Block())

        @block.tensor
        def _(eng):
            emit_stream("pe", eng)

        @block.scalar
        def _(eng):
            emit_stream("act", eng)

        @block.vector
        def _(eng):
            emit_stream("dve", eng)

        @block.gpsimd
        def _(eng):
            emit_stream("pool", eng)

        @block.sync
        def _(eng):
            emit_stream("sp", eng)


def _fm(v):
    v = np.asarray(v, np.float32).reshape(-1)
    return np.ascontiguousarray(v.reshape(-1, 128).T)


PCOLS = {}


def _build_pfm(inp):
    cols, off = [], 0

    def put(name, arr):
        nonlocal off
        PCOLS[name] = off
        cols.append(arr)
        off += arr.shape[1]

    put("ne", _fm(inp["norm_even"][0]))
    put("no", _fm(inp["norm_odd"][0]))
    put("nf", _fm(inp["final_norm"]))
    put("scb", _fm(inp["ssd_conv_b"][0]))
    put("scw", np.concatenate([_fm(inp["ssd_conv_w"][0][k]) for k in range(4)], 1))
    put("gn", _fm(inp["ssd_norm"][0]))
    put("Dfm", _fm(np.repeat(inp["ssd_d"][0], 64)))
    put("lng", _fm(inp["gmlp_ln_g"][0]))
    put("lnb", _fm(inp["gmlp_ln_b"][0]))
    put("ccb", _fm(inp["ccv_b"][0]))
    put("cclg", _fm(inp["ccv_ln_g"][0]))
    put("cclb", _fm(inp["ccv_ln_b"][0]))
    put("ccw", np.concatenate([_fm(inp["ccv_w"][0][k]) for k in range(31)], 1))
    put("lcw", np.concatenate([_fm(inp["lru_conv_w"][0][k]) for k in range(4)], 1))
    put("lcb", _fm(inp["lru_conv_b"][0]))
    put("lba", _fm(inp["lru_ba"][0]))
    put("lbx", _fm(inp["lru_bx"][0]))
    put("lam", _fm(inp["lru_lambda"][0]))
    put("w00", _fm(np.repeat(inp["gmlp_w_s"][0][:, 0, 0], 128)))
    put("b0", _fm(np.repeat(inp["gmlp_b_s"][0][:, 0], 128)))
    return np.ascontiguousarray(np.concatenate(cols, 1))


NPCOL = 468
E_EVEN = 5648
E_ODD = 5120


def build_program(stage=99):
    import os as _os
    nc = bass.Bass("TRN2", target_bir_lowering=False)

    def din(name, shape, dt=F32):
        return nc.dram_tensor(name, list(shape), dt, kind="ExternalInput").ap()

    def dout(name, shape, dt=F32):
        return nc.dram_tensor(name, list(shape), dt, kind="ExternalOutput").ap()

    xT = din("xT", [1024, SEQ])
    w_in_e = din("w_in_e", [1024, E_EVEN])
    w_out_e = din("w_out_e", [2048, 1024])
    w_in_o = din("w_in_o", [1024, E_ODD])
    w_out_o = din("w_out_o", [2048, 1024])
    d_pfm = din("pfm", [128, NPCOL])
    d_p16 = din("p16", [16, 2])
    d_drow = din("drow", [1, 16])
    d_wsT = din("wsT", [128, 1024])
    d_bsrow = din("bsrow", [1, 1024])
    d_idf = din("c_idf", [128, 128])
    d_negm = din("c_negm", [128, 128])
    d_triu = din("c_triu", [128, 128])
    d_sel16 = din("c_sel16", [16, 2048])
    d_sellast = din("c_sellast", [128, 128])

    d_xsT = din("xsT", [1024, NB])
    st_ssm = din("st_ssm", [NB, 1024, 128])
    st_sconv = din("st_sconv", [NB, 3, 1536])
    st_ccv = din("st_ccv", [NB, 30, 1024])
    st_lconv = din("st_lconv", [NB, 3, 1024])
    st_lru = din("st_lru", [NB, 1024])
    d_exp = din("c_exp", [16, 1024])
    o_y_s = dout("o_y_s", [128, 8 * NB])
    o_ssm_s = dout("o_ssm_s", [NB, 1024, 128])
    o_sconv_s_new = dout("o_sconv_s_new", [128, 12 * NB])
    o_sconv_s_hist = dout("o_sconv_s_hist", [NB, 2, 1536])
    o_gv_s = dout("o_gv_s", [128, 8 * NB])
    o_ccv_s_new = dout("o_ccv_s_new", [128, 8 * NB])
    o_ccv_s_hist = dout("o_ccv_s_hist", [NB, 29, 1024])
    o_lconv_s_new = dout("o_lconv_s_new", [128, 8 * NB])
    o_lconv_s_hist = dout("o_lconv_s_hist", [NB, 2, 1024])
    o_lru_s = dout("o_lru_s", [128, 8 * NB])
    d_bda = din("bda", [128, 1024])
    d_bdx = din("bdx", [128, 1024])
    yT = dout("yT", [1024, SEQ])
    o_ccv_p = dout("o_ccv_p", [128, 240])
    o_lconv_p = dout("o_lconv_p", [128, 32])
    o_lru_p = dout("o_lru_p", [128, 8])
    x1T = nc.dram_tensor("x1T", [1024, SEQ], F32, kind="Internal").ap()
    o_ssm_p = dout("o_ssm_p", [1024, 128])
    o_sconv_p = dout("o_sconv_p", [128, 48])

    es = ExitStack()
    with es:
        S = Sched(nc)
        x1buf = Buf(None, "x1buf")
        outbuf = Buf(None, "outs")

        def sb(name, shape, dt=F32):
            return Buf(es.enter_context(nc.sbuf_tensor(name, list(shape), dt)), name)

        def psum(name, shape, dt=F32):
            return Buf(es.enter_context(nc.psum_tensor(name, list(shape), dt)), name)

        def PC(c):
            return PCOLS[c]

        pfm = sb("pfm_sb", [128, NPCOL])
        cIDF = sb("cIDF", [128, 128])
        cIDB = sb("cIDB", [128, 128], BF16)
        cNEGM = sb("cNEGM", [128, 128])
        cSELLAST = sb("cSELLAST", [128, 128])
        cSEL16 = sb("cSEL16", [16, 2048])
        cONESB = sb("cONESB", [128, 128], BF16)
        cONES16 = sb("cONES16", [16, 128])
        p16 = sb("p16_sb", [16, 2])
        Dbc = sb("Dbc", [128, 16])
        WmT = sb("WmT", [128, 8, 128], BF16)
        Rg = sb("Rg", [128, 8, 128])
        ea16 = sb("ea16", [16, 1])

        S.dma("sp", pfm[:, :], d_pfm, writes=[pfm], group=pfm)
        S.dma("sp", cIDF[:, :], d_idf, writes=[cIDF], group=cIDF)
        S.dma("pool", cIDB[:, :], d_idf, writes=[cIDB], group=cIDB)
        S.dma("sp", cNEGM[:, :], d_negm, writes=[cNEGM], group=cNEGM)
        S.dma("sp", cSELLAST[:, :], d_sellast, writes=[cSELLAST], group=cSELLAST)
        S.dma("sp", cSEL16[:, :], d_sel16, writes=[cSEL16], group=cSEL16)
        S.dma("sp", p16[:, :], d_p16, writes=[p16], group=p16)
        S.dma("sp", Dbc[:, :], d_drow[0, :].partition_broadcast(128), writes=[Dbc], group=Dbc)
        S.add("dve", lambda e: e.memset(cONESB[:, :], 1.0), writes=[cONESB])
        S.add("dve", lambda e: e.memset(cONES16[:, :], 1.0), writes=[cONES16])
        S.add("act", lambda e: e.activation(out=ea16[:, :], in_=p16[:, 1:2], func=AF.Exp), reads=[p16], writes=[ea16])

        P0 = psum("P0", [128, 512])
        P1 = psum("P1", [128, 512])
        P2 = psum("P2", [128, 512])
        P3 = psum("P3", [128, 512])
        P4 = psum("P4", [128, 512])
        P5 = psum("P5", [128, 512])
        P6 = psum("P6", [128, 512])
        P7t = es.enter_context(nc.psum_tensor("P7", [128, 512], F32))
        P7 = Buf(P7t, "P7")
        P7a = P7t[:, 0:144]
        P7b = P7t[:, 144:400]
        P7c = P7t[:, 400:416]
        PA = [P0, P1]
        pa_i = [0]

        def set_pa(banks):
            PA[:] = banks

        def nextPA():
            b = PA[pa_i[0] % len(PA)]
            pa_i[0] += 1
            return b

        tmpw = sb("tmpw2", [128, 1024])
        ctriu = sb("ctriu", [128, 128])
        bsbc = sb("bsbc2", [128, 1024])
        S.dma("sp", tmpw[:, :], d_wsT, writes=[tmpw], group=tmpw)
        S.dma("sp", ctriu[:, :], d_triu, writes=[ctriu], group=ctriu)
        S.dma("sp", bsbc[:, :], d_bsrow[0, :].partition_broadcast(128), writes=[bsbc], group=bsbc)
        S.add("dve", lambda e: e.tensor_tensor(
            out=WmT[:, :, :], in0=tmpw[:, :].rearrange("p (g t) -> p g t", g=8),
            in1=ctriu[:, :].unsqueeze(1).broadcast_to([128, 8, 128]), op=ALU.mult),
            reads=[tmpw, ctriu], writes=[WmT])
        for hf in range(2):
            S.add("pe", lambda e, hf=hf: e.matmul(P3[:, :] if hf == 0 else P4[:, :], lhsT=cONESB[:, :],
                                                    rhs=WmT[:, 4 * hf:4 * hf + 4, :].rearrange("p g t -> p (g t)"),
                                                    start=True, stop=True),
                  reads=[cONESB, WmT], writes=[P3 if hf == 0 else P4])
        for g in range(8):
            pb = P3 if g < 4 else P4
            S.add("dve", lambda e, g=g, pb=pb: e.scalar_tensor_tensor(
                out=Rg[:, g, :], in0=pb[:, (g % 4) * 128:(g % 4 + 1) * 128], scalar=pfm[:, PC("lnb") + g:PC("lnb") + g + 1],
                in1=bsbc[:, g * 128:(g + 1) * 128], op0=ALU.mult, op1=ALU.add),
                reads=[pb, pfm, bsbc], pwrites=[Rg])

        NW = 3
        wbufs = [sb(f"wbuf{i}", [128, 4096], BF16) for i in range(NW)]
        w_i = [0]

        def wload(dram_ap, kk, ww):
            b = wbufs[w_i[0] % NW]
            w_i[0] += 1
            view = b.t[:, 0:kk * ww].rearrange("p (k e) -> p k e", k=kk)
            S.dma("pool", view, dram_ap.rearrange("(k p) e -> p k e", p=128), writes=[b], group=b)
            return b, view

        NDG = 8
        dgbufs = [sb(f"dg{i}", [128, 128], BF16) for i in range(NDG)]
        dg_i = [0]

        def diag(col, eng="act"):
            b = dgbufs[dg_i[0] % NDG]
            dg_i[0] += 1
            if eng == "act":
                S.add("act", lambda e: e.activation(out=b[:, :], in_=cIDB[:, :], func=AF.Copy, scale=pfm[:, col:col + 1]),
                      reads=[cIDB, pfm], writes=[b])
            else:
                S.add("dve", lambda e: e.tensor_scalar(out=b[:, :], in0=cIDB[:, :], scalar1=pfm[:, col:col + 1], scalar2=None, op0=ALU.mult),
                      reads=[cIDB, pfm], writes=[b])
            return b

        bigA = sb("bigA", [128, 8, 512])
        mix = sb("mix", [128, 16, 512], BF16)
        hn = sb("hn", [128, 8, 512], BF16)
        rstd = sb("rstd", [128, 512])
        raws = [sb(f"raw{i}", [128, 515], BF16) for i in range(2)]
        hist0 = sb("hist0", [128, 12, 3], BF16)
        lastraw = sb("lastraw", [128, 12, 4])
        xcf = [sb(f"xcf{i}", [128, 512]) for i in range(2)]
        xh_tm = sb("xh_tm", [128, 4, 1024])
        B_tm = sb("B_tm", [128, 4, 256], BF16)
        BT_bf = sb("BT_bf", [128, 2, 512], BF16)
        CT_bf = sb("CT_bf", [128, 2, 512], BF16)
        bigZ = sb("bigZ", [128, 4, 1024])
        vhat = sb("vhat", [128, 4, 1024], BF16)
        sg = [sb(f"sg{i}", [128, 512]) for i in range(2)]
        dtT = sb("dtT", [16, 512])
        daT = sb("daT", [16, 512])
        csT = sb("csT", [16, 512])
        tmpT = sb("tmpT", [16, 512])
        PK1 = sb("PK1", [128, 512])
        PK2 = sb("PK2", [16, 512])
        tmq = sb("tmq", [128, 144])
        xs_bf = sb("xs_bf", [128, 1024], BF16)
        xsd_bf = sb("xsd_bf", [128, 1024], BF16)
        Lbuf = sb("Lbuf", [128, 8, 128])
        MT_bf = sb("MT_bf", [128, 16, 128], BF16)
        yoff = sb("yoff", [128, 1024])
        ysb = sb("ysb", [128, 1024])
        ya_bf = sb("ya_bf", [128, 1024], BF16)
        Hst = sb("Hst", [128, 1024])
        Hbf = sb("Hbf", [128, 1024], BF16)
        ect = sb("ect", [128, 16])
        ssq = sb("ssq", [128, 2])
        rs2 = sb("rs2", [128, 2])
        bnst = sb("bnst", [128, 4, 2, 6])
        mv = sb("mv", [128, 4, 2])
        rv = sb("rv", [128, 4])

        S.add("dve", lambda e: e.memset(hist0[:, :, :], 0.0), writes=[hist0])
        S.add("dve", lambda e: e.memset(PK1[:, :], 0.0), writes=[PK1])
        S.add("dve", lambda e: e.memset(Hst[:, :], 0.0), writes=[Hst])
        S.add("dve", lambda e: e.memset(Hbf[:, :], 0.0), writes=[Hbf])

        def rmsnorm_fm(src, gcol, dst_bf, dst_view=None):
            S.add("act", lambda e: e.activation(out=mix[:, 0:8, :], in_=src[:, :, :], func=AF.Square), reads=[src], writes=[mix])
            pb = nextPA()
            for k in range(8):
                S.add("pe", lambda e, k=k: e.matmul(pb[:, :], lhsT=cONESB[:, :], rhs=mix[:, k, :], start=(k == 0), stop=(k == 7)),
                      reads=[cONESB, mix], writes=[pb])
            S.add("act", lambda e: e.activation(out=rstd[:, :], in_=pb[:, :], func=AF.Ln, scale=1.0 / 1024.0, bias=EPS),
                  reads=[pb], writes=[rstd])
            S.add("act", lambda e: e.activation(out=rstd[:, :], in_=rstd[:, :], func=AF.Exp, scale=-0.5), reads=[rstd], writes=[rstd])
            for k in range(8):
                S.add("dve", lambda e, k=k: e.scalar_tensor_tensor(
                    out=(dst_bf[:, k, :] if dst_view is None else dst_view[:, k, :]), in0=src[:, k, :], scalar=pfm[:, gcol + k:gcol + k + 1], in1=rstd[:, :],
                    op0=ALU.mult, op1=ALU.mult), reads=[src, pfm, rstd], pwrites=[dst_bf])

        def projA(wv, j, rhs_buf, pb):
            for k in range(8):
                S.add("pe", lambda e, k=k: e.matmul(pb[:, :], lhsT=wv[1][:, k, j * 128:(j + 1) * 128], rhs=rhs_buf[:, k, :],
                                                     start=(k == 0), stop=(k == 7)),
                      reads=[wv[0], rhs_buf], writes=[pb])


        def norm_sq(pieces, pbufs, sq_view, sq_buf):
            for k in range(8):
                S.add("act", lambda e, k=k: e.activation(out=sq_view[:, k, :], in_=pieces[k], func=AF.Square), reads=[pbufs[k]], pwrites=[sq_buf])

        def norm_rest(pieces, pbufs, sq_view, sq_buf, gcol):
            pb = nextPA()
            for k in range(8):
                S.add("pe", lambda e, k=k: e.matmul(pb[:, :], lhsT=cONESB[:, :], rhs=sq_view[:, k, :], start=(k == 0), stop=(k == 7)),
                      reads=[cONESB, sq_buf], writes=[pb])
            S.add("act", lambda e: e.activation(out=rstd[:, :], in_=pb[:, :], func=AF.Ln, scale=1.0 / 1024.0, bias=EPS), reads=[pb], writes=[rstd])
            S.add("act", lambda e: e.activation(out=rstd[:, :], in_=rstd[:, :], func=AF.Exp, scale=-0.5), reads=[rstd], writes=[rstd])
            for k in range(8):
                S.add("dve", lambda e, k=k: e.scalar_tensor_tensor(out=hn[:, k, :], in0=pieces[k], scalar=pfm[:, gcol + k:gcol + k + 1], in1=rstd[:, :],
                                                                     op0=ALU.mult, op1=ALU.mult), reads=[pbufs[k], pfm, rstd], pwrites=[hn])

        l0_pieces = [bigZ.t[:, :, :].rearrange("p a (c t) -> p (a c) t", t=512)[:, k, :] for k in range(8)]
        l0_pbufs = [bigZ] * 8
        l0_sq = xh_tm.t[:, :, :].rearrange("p a b -> p (a b)").bitcast(BF16)[:, 0:4096].rearrange("p (k t) -> p k t", k=8)

        def l0_load(ti):
            S.dma("sp", bigZ.t[:, :, :].rearrange("p a (c t) -> p (a c) t", t=512), xT[:, ti * TT:(ti + 1) * TT].rearrange("(k p) t -> p k t", p=128),
                  writes=[bigZ], group=bigZ)

        def layer0_tile(ti):
            t0 = ti * TT
            last = (ti == NT - 1) and stage >= 1
            set_pa([P0, P1, P4, P5])
            if ti == 0:
                l0_load(0)
                norm_sq(l0_pieces, l0_pbufs, l0_sq, xh_tm)
                norm_rest(l0_pieces, l0_pbufs, l0_sq, xh_tm, PC("ne"))
            if stage <= 0.1:
                return
            wdt = wload(w_in_e[:, 2560:2576], 8, 16)
            samp_dt(wdt)
            pb = nextPA()
            for k in range(8):
                S.add("pe", lambda e, k=k: e.matmul(pb[0:16, :], lhsT=wdt[1][:, k, :], rhs=hn[:, k, :], start=(k == 0), stop=(k == 7)),
                      reads=[wdt[0], hn], writes=[pb])
            S.add("act", lambda e: e.activation(out=tmpT[:, :], in_=pb[0:16, :], func=AF.Exp, bias=p16[:, 0:1]),
                  reads=[pb, p16], writes=[tmpT])
            S.add("act", lambda e: e.activation(out=dtT[:, :], in_=tmpT[:, :], func=AF.Ln, bias=1.0), reads=[tmpT], writes=[dtT])
            S.add("dve", lambda e: e.tensor_scalar(out=daT[:, :], in0=dtT[:, :], scalar1=ea16[:, 0:1], scalar2=-1.0,
                                                    op0=ALU.mult, op1=ALU.mult), reads=[dtT, ea16], writes=[daT])
            for c in range(4):
                S.add("dve", lambda e, c=c: e.tensor_tensor_scan(out=csT[:, c * 128:(c + 1) * 128], data0=cONES16[:, :],
                                                                  data1=daT[:, c * 128:(c + 1) * 128], initial=0.0,
                                                                  op0=ALU.mult, op1=ALU.add),
                      reads=[cONES16, daT], pwrites=[csT])
            S.add("act", lambda e: e.activation(out=PK1[0:16, :], in_=dtT[:, :], func=AF.Copy), reads=[dtT], pwrites=[PK1])
            S.add("act", lambda e: e.activation(out=PK1[32:48, :], in_=csT[:, :], func=AF.Copy), reads=[csT], pwrites=[PK1])
            for c in range(4):
                S.add("act", lambda e, c=c: e.activation(out=tmpT[:, c * 128:(c + 1) * 128], in_=csT[:, c * 128:(c + 1) * 128],
                                                          func=AF.Exp, scale=-1.0, bias=csT[:, c * 128 + 127:c * 128 + 128]),
                      reads=[csT], pwrites=[tmpT])
            S.add("dve", lambda e: e.tensor_tensor(out=PK1[64:80, :], in0=tmpT[:, :], in1=dtT[:, :], op=ALU.mult),
                  reads=[tmpT, dtT], pwrites=[PK1])
            S.add("act", lambda e: e.activation(out=PK2[:, :], in_=csT[:, :], func=AF.Exp), reads=[csT], writes=[PK2])

            if stage <= 0.2:
                return
            for blk3 in range(3):
                wv = wload(w_in_e[:, 1024 + blk3 * 512:1024 + (blk3 + 1) * 512], 8, 512)
                samp_cols(wv, 1024 + blk3 * 512, 512)
                for jj in range(4):
                    j = blk3 * 4 + jj
                    pb = nextPA()
                    projA(wv, jj, hn, pb)
                    raw = raws[j % 2]
                    PC2 = P2 if j % 2 == 0 else P6
                    PT3 = P3 if j % 2 == 0 else P7
                    PT3v = P3.t if j % 2 == 0 else P7t
                    S.add("dve", lambda e, j=j, raw=raw: e.tensor_copy(out=raw[:, 0:3], in_=hist0[:, j, :]), reads=[hist0], pwrites=[raw])
                    S.add("act", lambda e, raw=raw, pb=pb: e.activation(out=raw[:, 3:515], in_=pb[:, :], func=AF.Copy),
                          reads=[pb], pwrites=[raw])
                    if last:
                        S.add("act", lambda e, j=j, pb=pb: e.activation(out=lastraw[:, j, :], in_=pb[:, 508:512], func=AF.Copy), reads=[pb], pwrites=[lastraw])
                    S.add("dve", lambda e, j=j, raw=raw: e.tensor_copy(out=hist0[:, j, :], in_=raw[:, 512:515]), reads=[raw], pwrites=[hist0])
                    if stage <= 0.21:
                        continue
                    for k in range(4):
                        dg = diag(PC("scw") + k * 12 + j)
                        S.add("pe", lambda e, k=k, dg=dg, raw=raw, PC2=PC2: e.matmul(PC2[:, :], lhsT=dg[:, :], rhs=raw[:, k:k + 512],
                                                                             start=(k == 0), stop=(k == 3)),
                              reads=[dg, raw], writes=[PC2])
                    if stage <= 0.22:
                        continue
                    bcol = PC("scb") + j
                    if j < 10:
                        xc = xcf[j % 2]
                        S.add("act", lambda e, xc=xc, bcol=bcol, PC2=PC2: e.activation(out=xc[:, :], in_=PC2[:, :], func=AF.Silu,
                                                                                 bias=pfm[:, bcol:bcol + 1]),
                              reads=[PC2, pfm], writes=[xc])
                        if stage <= 0.23:
                            continue
                        for b4 in range(4):
                            S.add("pe", lambda e, b4=b4, xc=xc, PT3v=PT3v: e.transpose(PT3v[:, b4 * 128:(b4 + 1) * 128], xc[:, b4 * 128:(b4 + 1) * 128], cIDF[:, :]),
                                  reads=[xc, cIDF], writes=[PT3])
                        if stage <= 0.24:
                            continue
                        if j < 8:
                            S.add("dve", lambda e, j=j, PT3v=PT3v: e.tensor_copy(out=xh_tm[:, :, j * 128:(j + 1) * 128],
                                                                       in_=PT3v[:, :].rearrange("p (b c) -> p b c", b=4)),
                                  reads=[PT3], pwrites=[xh_tm])
                        else:
                            jb = j - 8
                            S.add("dve", lambda e, jb=jb, PT3v=PT3v: e.tensor_copy(out=B_tm[:, :, jb * 128:(jb + 1) * 128],
                                                                         in_=PT3v[:, :].rearrange("p (b c) -> p b c", b=4)),
                                  reads=[PT3], pwrites=[B_tm])
                            if stage <= 0.25:
                                continue
                            S.add("dve", lambda e, jb=jb, xc=xc: e.tensor_copy(out=BT_bf[:, jb, :], in_=xc[:, :]), reads=[xc], pwrites=[BT_bf])
                    else:
                        jc = j - 10
                        S.add("act", lambda e, jc=jc, bcol=bcol, PC2=PC2: e.activation(out=CT_bf[:, jc, :], in_=PC2[:, :], func=AF.Silu,
                                                                                 bias=pfm[:, bcol:bcol + 1]),
                              reads=[PC2, pfm], pwrites=[CT_bf])

            if stage <= 0.3:
                return
            for half in range(2):
                wv = wload(w_in_e[:, half * 512:(half + 1) * 512], 8, 512)
                samp_cols(wv, half * 512, 512)
                for b4 in range(4):
                    pb = nextPA()
                    for k in range(8):
                        S.add("pe", lambda e, k=k, b4=b4, wv=wv, pb=pb: e.matmul(pb[:, :], lhsT=hn[:, k, b4 * 128:(b4 + 1) * 128],
                                                                                  rhs=wv[1][:, k, :], start=(k == 0), stop=(k == 7)),
                              reads=[wv[0], hn], writes=[pb])
                    S.add("act", lambda e, b4=b4, half=half, pb=pb: e.activation(out=bigZ[:, b4, half * 512:(half + 1) * 512],
                                                                                  in_=pb[:, :], func=AF.Silu),
                          reads=[pb], pwrites=[bigZ])

            if stage <= 0.4:
                return
            set_pa([P0, P1])
            def gen_ssd():
                for c in range(4):
                    cs_ = slice(c * 128, (c + 1) * 128)
                    S.add("pe", lambda e, cs_=cs_: e.transpose(P7a[:, 0:128], PK1[:, cs_], cIDF[:, :]), reads=[PK1, cIDF], pwrites=[P7])
                    S.add("pe", lambda e, cs_=cs_: e.transpose(P7a[:, 128:144], PK2[:, cs_], cIDF[0:16, 0:16]), reads=[PK2, cIDF], pwrites=[P7])
                    S.add("dve", lambda e: e.tensor_copy(out=tmq[:, :], in_=P7a[:, :]), reads=[P7], writes=[tmq])
                    dt_tm = tmq[:, 0:16]
                    dd_tm = tmq[:, 64:80]
                    ecs_tm = tmq[:, 128:144]

                    def bc16(ap):
                        return ap.unsqueeze(2).broadcast_to([128, 16, 64])

                    xh3 = xh_tm[:, c, :].rearrange("p (h q) -> p h q", h=16)
                    S.add("dve", lambda e, xh3=xh3, dt_tm=dt_tm: e.tensor_tensor(out=xs_bf[:, :].rearrange("p (h q) -> p h q", h=16), in0=xh3,
                                                                                  in1=bc16(dt_tm), op=ALU.mult),
                          reads=[xh_tm, tmq], writes=[xs_bf])
                    S.add("dve", lambda e, xh3=xh3, dd_tm=dd_tm: e.tensor_tensor(out=xsd_bf[:, :].rearrange("p (h q) -> p h q", h=16), in0=xh3,
                                                                                   in1=bc16(dd_tm), op=ALU.mult),
                          reads=[xh_tm, tmq], writes=[xsd_bf])
                    S.add("dve", lambda e, xh3=xh3: e.tensor_tensor(out=ysb[:, :].rearrange("p (h q) -> p h q", h=16), in0=xh3,
                                                                     in1=bc16(Dbc[:, :]), op=ALU.mult),
                          reads=[xh_tm, Dbc], writes=[ysb])
                    yield
                    for g in range(2):
                        S.add("pe", lambda e, g=g, cs_=cs_: e.matmul(P7b[:, g * 128:(g + 1) * 128], lhsT=BT_bf[:, g, cs_], rhs=CT_bf[:, g, cs_],
                                                                      start=True, stop=True),
                              reads=[BT_bf, CT_bf], pwrites=[P7])
                    for g in range(2):
                        pg = P3 if g == 0 else P4
                        S.add("pe", lambda e, g=g, pg=pg, cs_=cs_: e.matmul(pg[:, :], lhsT=CT_bf[:, g, cs_], rhs=Hbf[:, g * 512:(g + 1) * 512],
                                                                             start=True, stop=True),
                              reads=[CT_bf, Hbf], writes=[pg])
                        S.add("dve", lambda e, g=g, pg=pg, ecs_tm=ecs_tm: e.tensor_tensor(
                            out=yoff[:, g * 512:(g + 1) * 512].rearrange("p (h q) -> p h q", h=8),
                            in0=pg[:, :].rearrange("p (h q) -> p h q", h=8),
                            in1=ecs_tm[:, g * 8:(g + 1) * 8].unsqueeze(2).broadcast_to([128, 8, 64]), op=ALU.mult),
                            reads=[pg, tmq], pwrites=[yoff])
                    S.add("dve", lambda e: e.tensor_tensor(out=yoff[:, :], in0=yoff[:, :], in1=ysb[:, :], op=ALU.add), reads=[yoff, ysb], writes=[yoff])
                    yield
                    for q in range(4):
                        pc = P5 if q % 2 == 0 else P6
                        for h4 in range(4):
                            h = q * 4 + h4
                            S.add("pe", lambda e, h=h, h4=h4, pc=pc, cs_=cs_: e.matmul(pc[:, h4 * 128:(h4 + 1) * 128], lhsT=cSEL16[:, h * 128:(h + 1) * 128],
                                                                                        rhs=csT[:, cs_], start=True, stop=True),
                                  reads=[cSEL16, csT], pwrites=[pc])
                        for h4 in range(4):
                            h = q * 4 + h4
                            S.add("dve", lambda e, h=h, h4=h4, pc=pc, q=q: e.scalar_tensor_tensor(
                                out=Lbuf[:, (q % 2) * 4 + h4, :], in0=pc[:, h4 * 128:(h4 + 1) * 128], scalar=tmq[:, 32 + h:33 + h],
                                in1=cNEGM[:, :], op0=ALU.subtract, op1=ALU.add),
                                reads=[pc, tmq, cNEGM], pwrites=[Lbuf])
                        if q % 2 == 1:
                            g = q // 2
                            S.add("act", lambda e: e.activation(out=Lbuf[:, :, :], in_=Lbuf[:, :, :], func=AF.Exp), reads=[Lbuf], writes=[Lbuf])
                            S.add("dve", lambda e, g=g: e.tensor_tensor(
                                out=MT_bf[:, g * 8:(g + 1) * 8, :], in0=Lbuf[:, :, :],
                                in1=P7b[:, g * 128:(g + 1) * 128].unsqueeze(1).broadcast_to([128, 8, 128]), op=ALU.mult),
                                reads=[Lbuf, P7], pwrites=[MT_bf])
                        yield
                    for h in range(16):
                        pg = P3 if h < 8 else P4
                        S.add("pe", lambda e, h=h, pg=pg: e.matmul(pg[:, (h % 8) * 64:(h % 8 + 1) * 64], lhsT=MT_bf[:, h, :],
                                                                     rhs=xs_bf[:, h * 64:(h + 1) * 64], start=True, stop=True),
                              reads=[MT_bf, xs_bf], pwrites=[pg])
                    yield
                    for g in range(2):
                        pg = P3 if g == 0 else P4
                        S.add("dve", lambda e, g=g, pg=pg: e.tensor_tensor(out=ysb[:, g * 512:(g + 1) * 512], in0=pg[:, :],
                                                                            in1=yoff[:, g * 512:(g + 1) * 512], op=ALU.add),
                              reads=[pg, yoff], pwrites=[ysb])
                    S.add("dve", lambda e, c=c: e.tensor_tensor(out=ysb[:, :], in0=ysb[:, :], in1=bigZ[:, c, :], op=ALU.mult),
                          reads=[ysb, bigZ], writes=[ysb])
                    for g in range(2):
                        S.add("act", lambda e, g=g: e.activation(out=yoff[:, g * 512:(g + 1) * 512], in_=ysb[:, g * 512:(g + 1) * 512],
                                                                  func=AF.Square, accum_out=ssq[:, g:g + 1]),
                              reads=[ysb], pwrites=[yoff, ssq])
                    S.add("act", lambda e: e.activation(out=rs2[:, :], in_=ssq[:, :], func=AF.Ln, scale=1.0 / 512.0, bias=EPS), reads=[ssq], writes=[rs2])
                    S.add("act", lambda e: e.activation(out=rs2[:, :], in_=rs2[:, :], func=AF.Exp, scale=-0.5), reads=[rs2], writes=[rs2])
                    for g in range(2):
                        S.add("act", lambda e, g=g: e.activation(out=ya_bf[:, g * 512:(g + 1) * 512], in_=ysb[:, g * 512:(g + 1) * 512],
                                                                  func=AF.Copy, scale=rs2[:, g:g + 1]),
                              reads=[ysb, rs2], pwrites=[ya_bf])
                    for j in range(8):
                        S.add("pe", lambda e, j=j: e.transpose(P2[:, j * 64:(j + 1) * 64].bitcast(BF16), ya_bf[:, j * 128:(j + 1) * 128], cIDB[:, :]),
                              reads=[ya_bf, cIDB], pwrites=[P2])
                    for j in range(8):
                        S.add("act", lambda e, j=j, cs_=cs_: e.activation(out=mix[:, j, cs_], in_=P2[:, j * 64:(j + 1) * 64].bitcast(BF16),
                                                                           func=AF.Copy, scale=pfm[:, PC("gn") + j:PC("gn") + j + 1]),
                              reads=[P2, pfm], pwrites=[mix])
                    yield
                    S.add("pe", lambda e: e.matmul(P7c[:, :], lhsT=cSELLAST[:, :], rhs=tmq[:, 32:48], start=True, stop=True),
                          reads=[cSELLAST, tmq], pwrites=[P7])
                    S.add("dve", lambda e: e.tensor_copy(out=ect[:, :], in_=P7c[:, :]), reads=[P7], writes=[ect])
                    S.add("act", lambda e: e.activation(out=ect[:, :], in_=ect[:, :], func=AF.Exp), reads=[ect], writes=[ect])
                    S.add("dve", lambda e: e.tensor_tensor(out=Hst[:, :].rearrange("p (h q) -> p h q", h=16),
                                                             in0=Hst[:, :].rearrange("p (h q) -> p h q", h=16), in1=bc16(ect[:, :]), op=ALU.mult),
                          reads=[Hst, ect], writes=[Hst])
                    for g in range(2):
                        pg = P3 if g == 0 else P4
                        S.add("pe", lambda e, g=g, pg=pg, c=c: e.matmul(pg[:, :], lhsT=B_tm[:, c, g * 128:(g + 1) * 128], rhs=xsd_bf[:, g * 512:(g + 1) * 512],
                                                                         start=True, stop=True),
                              reads=[B_tm, xsd_bf], writes=[pg])
                        S.add("dve", lambda e, g=g, pg=pg: e.tensor_tensor(out=Hst[:, g * 512:(g + 1) * 512], in0=pg[:, :],
                                                                            in1=Hst[:, g * 512:(g + 1) * 512], op=ALU.add),
                              reads=[pg, Hst], pwrites=[Hst])
                    S.add("act", lambda e: e.activation(out=Hbf[:, :], in_=Hst[:, :], func=AF.Copy), reads=[Hst], writes=[Hbf])
                    yield

            def gen_ug():
                for half in range(2):
                    wv = wload(w_in_e[:, 2576 + half * 512:2576 + (half + 1) * 512], 8, 512)
                    samp_cols(wv, 2576 + half * 512, 512)
                    for jj in range(4):
                        j = half * 4 + jj
                        pb = nextPA()
                        projA(wv, jj, hn, pb)
                        S.add("act", lambda e, j=j, pb=pb: e.activation(out=bigA[:, j, :], in_=pb[:, :], func=AF.Copy),
                              reads=[pb], pwrites=[bigA])
                        yield

            def gen_g():
                S.add("act", lambda e: e.activation(out=bigA[:, :, :], in_=bigA[:, :, :], func=AF.Gelu_apprx_tanh), reads=[bigA], writes=[bigA])
                for half in range(2):
                    wv = wload(w_in_e[:, 4624 + half * 512:4624 + (half + 1) * 512], 8, 512)
                    samp_cols(wv, 4624 + half * 512, 512)
                    for jj in range(4):
                        j = half * 4 + jj
                        pb = nextPA()
                        projA(wv, jj, hn, pb)
                        sgb = sg[j % 2]
                        S.add("act", lambda e, sgb=sgb, pb=pb: e.activation(out=sgb[:, :], in_=pb[:, :], func=AF.Silu), reads=[pb], writes=[sgb])
                        S.add("dve", lambda e, j=j, sgb=sgb: e.tensor_tensor(out=bigA[:, j, :], in0=bigA[:, j, :], in1=sgb[:, :], op=ALU.mult),
                              reads=[bigA, sgb], pwrites=[bigA])
                        yield
            g1, g2 = gen_ssd(), gen_ug()
            n1 = 0
            done2 = False
            for _ in g1:
                n1 += 1
                if n1 % 4 == 0 and not done2:
                    if next(g2, "END") == "END":
                        done2 = True
            if not done2:
                for _ in g2:
                    pass
            for _ in gen_g():
                pass
            set_pa([P0, P1, P3, P4, P5, P6])
            for half in range(2):
                wv = wload(w_in_e[:, 3600 + half * 512:3600 + (half + 1) * 512], 8, 512)
                samp_cols(wv, 3600 + half * 512, 512)
                for b4 in range(4):
                    pb = nextPA()
                    for k in range(8):
                        S.add("pe", lambda e, k=k, b4=b4, wv=wv, pb=pb: e.matmul(pb[:, :], lhsT=hn[:, k, b4 * 128:(b4 + 1) * 128],
                                                                                  rhs=wv[1][:, k, :], start=(k == 0), stop=(k == 7)),
                              reads=[wv[0], hn], writes=[pb])
                    S.add("act", lambda e, b4=b4, half=half, pb=pb: e.activation(out=bigZ[:, b4, half * 512:(half + 1) * 512],
                                                                                  in_=pb[:, :], func=AF.Gelu_apprx_tanh),
                          reads=[pb], pwrites=[bigZ])
            for b4 in range(4):
                for half in range(2):
                    S.add("dve", lambda e, b4=b4, half=half: e.bn_stats(out=bnst[:, b4, half, :], in_=bigZ[:, b4, half * 512:(half + 1) * 512]),
                          reads=[bigZ], pwrites=[bnst])
                S.add("dve", lambda e, b4=b4: e.bn_aggr(out=mv[:, b4, :], in_=bnst[:, b4, :, :].rearrange("p a b -> p (a b)")),
                      reads=[bnst], pwrites=[mv])
            S.add("act", lambda e: e.activation(out=rv[:, :], in_=mv[:, :, 1], func=AF.Ln, bias=EPS), reads=[mv], writes=[rv])
            S.add("act", lambda e: e.activation(out=rv[:, :], in_=rv[:, :], func=AF.Exp, scale=-0.5), reads=[rv], writes=[rv])
            for b4 in range(4):
                S.add("dve", lambda e, b4=b4: e.tensor_scalar(out=vhat[:, b4, :], in0=bigZ[:, b4, :], scalar1=mv[:, b4, 0:1], scalar2=rv[:, b4:b4 + 1],
                                                               op0=ALU.subtract, op1=ALU.mult),
                      reads=[bigZ, mv, rv], pwrites=[vhat])
            if stage <= 0.7:
                return
            for b4 in range(4):
                bs_ = slice(b4 * 128, (b4 + 1) * 128)
                for gh in range(2):
                    pb = nextPA()
                    for g4 in range(4):
                        g = gh * 4 + g4
                        S.add("pe", lambda e, g=g, g4=g4, b4=b4, pb=pb: e.matmul(pb[:, g4 * 128:(g4 + 1) * 128], lhsT=vhat[:, b4, g * 128:(g + 1) * 128],
                                                                                  rhs=WmT[:, g, :], start=True, stop=True),
                              reads=[vhat, WmT], pwrites=[pb])
                    if stage <= 0.71:
                        continue
                    sgb = sg[gh]
                    for g4 in range(4):
                        g = gh * 4 + g4
                        S.add("dve", lambda e, g=g, g4=g4, pb=pb, sgb=sgb: e.scalar_tensor_tensor(
                            out=sgb[:, g4 * 128:(g4 + 1) * 128], in0=pb[:, g4 * 128:(g4 + 1) * 128],
                            scalar=pfm[:, PC("lng") + g:PC("lng") + g + 1], in1=Rg[:, g, :], op0=ALU.mult, op1=ALU.add),
                            reads=[pb, pfm, Rg], pwrites=[sgb])
                    if stage <= 0.72:
                        continue
                    S.add("dve", lambda e, gh=gh, sgb=sgb, bs_=bs_: e.tensor_tensor(
                        out=mix[:, 8 + gh * 4:8 + gh * 4 + 4, bs_], in0=sgb[:, :].rearrange("p (g t) -> p g t", g=4),
                        in1=bigA[:, gh * 4:gh * 4 + 4, bs_], op=ALU.mult),
                        reads=[sgb, bigA], pwrites=[mix])
            if stage <= 0.8:
                return
            S.dma("sp", bigA[:, :, :], xT[:, t0:t0 + TT].rearrange("(k p) t -> p k t", p=128), writes=[bigA], group=bigA)
            if ti + 1 < NT:
                l0_load(ti + 1)
                norm_sq(l0_pieces, l0_pbufs, l0_sq, xh_tm)
            for ob in range(4):
                if ob == 2 and ti + 1 < NT:
                    norm_rest(l0_pieces, l0_pbufs, l0_sq, xh_tm, PC("ne"))
                wv = wload(w_out_e[:, ob * 256:(ob + 1) * 256], 16, 256)
                for dj2 in range(2):
                    dj = ob * 2 + dj2
                    pb = nextPA()
                    for ek in range(16):
                        S.add("pe", lambda e, ek=ek, dj2=dj2, wv=wv, pb=pb: e.matmul(pb[:, :], lhsT=wv[1][:, ek, dj2 * 128:(dj2 + 1) * 128],
                                                                                      rhs=mix[:, ek, :], start=(ek == 0), stop=(ek == 15)),
                              reads=[wv[0], mix], writes=[pb])
                    S.add("dve", lambda e, dj=dj, pb=pb: e.tensor_tensor(out=bigA[:, dj, :], in0=pb[:, :], in1=bigA[:, dj, :], op=ALU.add),
                          reads=[pb, bigA], pwrites=[bigA])
            S.dma("sp", x1T[:, t0:t0 + TT].rearrange("(k p) t -> p k t", p=128), bigA[:, :, :], reads=[bigA], pwrites=[x1buf], group=bigA)

        l1_barrier_reads = []
        SA = bigA.t[:, :, :].rearrange("p a b -> p (a b)")
        SM = mix.t[:, :, :].rearrange("p a b -> p (a b)").bitcast(F32)
        SZ = bigZ.t[:, :, :].rearrange("p a b -> p (a b)")
        SX = xh_tm.t[:, :, :].rearrange("p a b -> p (a b)")
        projS = tmpw.t[:, 0:45 * NB].rearrange("p (c b) -> p c b", b=NB)
        hallS = bsbc.t[:, 0:4 * 12 * NB].rearrange("p (k j b) -> p k j b", k=4, j=12)
        xsT_s = sb("xsT_s", [128, 8, NB])
        sqS = sb("sqS", [128, 8, NB], BF16)
        rstdS = sb("rstdS", [128, NB])
        hnS = sb("hnS", [128, 8, NB], BF16)
        convS = sb("convS", [128, 12, NB])
        xcS = sb("xcS", [128, 12, NB])
        dtS = sb("dtS", [16, 3, NB])
        dtE = sb("dtE", [128, 2, 8, NB])
        xsS = sb("xsS", [128, 8, NB])
        vhat32 = vhat.t[:, :, :].rearrange("p a b -> p (a b)").bitcast(F32)
        BC_tm = vhat32[0:16, 1024:1536]
        yS = sb("yS", [128, 8, NB])
        t1S = sb("t1S", [128, 8, NB])
        t2S = sb("t2S", [128, 8, NB])
        stS = sb("stS", [128, 4, NB])
        mixS = sb("mixS", [128, 16, NB], BF16)
        cEXP = vhat32[0:16, 0:1024]
        S.dma("sp", xsT_s[:, :, :], d_xsT.rearrange("(k p) b -> p k b", p=128), writes=[xsT_s], group=xsT_s)

        def bcb(ap2):
            return ap2.unsqueeze(2).broadcast_to([128, ap2.shape[1], NB])

        def bcj(ap2, n):
            return ap2.unsqueeze(1).broadcast_to([128, n, NB])

        def rmsnorm_s(gcol, dst, dst_buf):
            S.add("act", lambda e: e.activation(out=sqS[:, :, :], in_=xsT_s[:, :, :], func=AF.Square), reads=[xsT_s], writes=[sqS])
            pb = nextPA()
            for k in range(8):
                S.add("pe", lambda e, k=k: e.matmul(pb[:, 0:NB], lhsT=cONESB[:, :], rhs=sqS[:, k, :], start=(k == 0), stop=(k == 7)),
                      reads=[cONESB, sqS], writes=[pb])
            S.add("act", lambda e: e.activation(out=rstdS[:, :], in_=pb[:, 0:NB], func=AF.Ln, scale=1.0 / 1024.0, bias=EPS), reads=[pb], writes=[rstdS])
            S.add("act", lambda e: e.activation(out=rstdS[:, :], in_=rstdS[:, :], func=AF.Exp, scale=-0.5), reads=[rstdS], writes=[rstdS])
            S.add("dve", lambda e: e.tensor_tensor(out=t1S[:, :, :], in0=xsT_s[:, :, :], in1=bcj(rstdS[:, :], 8), op=ALU.mult),
                  reads=[xsT_s, rstdS], writes=[t1S])
            S.add("dve", lambda e: e.tensor_tensor(out=dst, in0=t1S[:, :, :], in1=bcb(pfm[:, gcol:gcol + 8]), op=ALU.mult),
                  reads=[t1S, pfm], writes=[dst_buf])

        def proj_s(wsrc, col0, nchunk, c0, func, nrows=128):
            done = 0
            while done < nchunk:
                nb_ = min(4, nchunk - done)
                wv = wload(wsrc[:, col0 + done * 128:col0 + (done + nb_) * 128], 8, nb_ * 128)
                for jj in range(nb_):
                    pb = nextPA()
                    for k in range(8):
                        S.add("pe", lambda e, k=k, jj=jj, wv=wv, pb=pb: e.matmul(pb[:, 0:NB], lhsT=wv[1][:, k, jj * 128:(jj + 1) * 128], rhs=hnS[:, k, :],
                                                                                  start=(k == 0), stop=(k == 7)),
                              reads=[wv[0], hnS], writes=[pb])
                    cc = c0 + done + jj
                    S.add("act", lambda e, cc=cc, pb=pb: e.activation(out=projS[:, cc, :], in_=pb[:, 0:NB], func=func), reads=[pb], pwrites=[tmpw])
                done += nb_

        def outproj_s(wsrc):
            for ob in range(4):
                wv = wload(wsrc[:, ob * 256:(ob + 1) * 256], 16, 256)
                for dj2 in range(2):
                    dj = ob * 2 + dj2
                    pb = nextPA()
                    for ek in range(16):
                        S.add("pe", lambda e, ek=ek, dj2=dj2, wv=wv, pb=pb: e.matmul(pb[:, 0:NB], lhsT=wv[1][:, ek, dj2 * 128:(dj2 + 1) * 128],
                                                                                      rhs=mixS[:, ek, :], start=(ek == 0), stop=(ek == 15)),
                              reads=[wv[0], mixS], writes=[pb])
                    S.add("dve", lambda e, dj=dj, pb=pb: e.tensor_tensor(out=xsT_s[:, dj, :], in0=pb[:, 0:NB], in1=xsT_s[:, dj, :], op=ALU.add),
                          reads=[pb, xsT_s], pwrites=[xsT_s])

        def hist_to_fm(stage_ap, stage_buf, ntap_cols, dst_fn, dst_buf, bulk=None):
            i = 0
            while i < ntap_cols:
                n = min(32, ntap_cols - i)
                pb = nextPA()
                for q in range(n):
                    S.add("pe", lambda e, q=q, i=i, pb=pb: e.transpose(pb[:, q * NB:(q + 1) * NB], stage_ap[0:NB, (i + q) * 128:(i + q + 1) * 128], cIDF[0:NB, 0:NB]),
                          reads=[stage_buf, cIDF], pwrites=[pb])
                if bulk is not None:
                    dst_ap, pat, kw = bulk(i, n)
                    S.add("dve", lambda e, pb=pb, n=n, dst_ap=dst_ap, pat=pat, kw=kw: e.tensor_copy(out=dst_ap, in_=pb[:, 0:n * NB].rearrange(pat, **kw)),
                          reads=[pb], pwrites=[dst_buf])
                else:
                    for q in range(n):
                        S.add("dve", lambda e, q=q, i=i, pb=pb: e.tensor_copy(out=dst_fn(i + q), in_=pb[:, q * NB:(q + 1) * NB]), reads=[pb], pwrites=[dst_buf])
                i += n

        def stats_s(src3, src_buf, nch, inv_n):
            S.add("act", lambda e: e.activation(out=sqS[:, 0:nch, :], in_=src3, func=AF.Square), reads=[src_buf], writes=[sqS])
            S.add("dve", lambda e: e.tensor_copy(out=hnS[:, 0:nch, :], in_=src3), reads=[src_buf], writes=[hnS])
            pb = nextPA()
            for k in range(nch):
                S.add("pe", lambda e, k=k: e.matmul(pb[:, 0:NB], lhsT=cONESB[:, :], rhs=hnS[:, k, :], start=(k == 0), stop=(k == nch - 1)),
                      reads=[cONESB, hnS], writes=[pb])
            pb2 = nextPA()
            for k in range(nch):
                S.add("pe", lambda e, k=k: e.matmul(pb2[:, 0:NB], lhsT=cONESB[:, :], rhs=sqS[:, k, :], start=(k == 0), stop=(k == nch - 1)),
                      reads=[cONESB, sqS], writes=[pb2])
            S.add("dve", lambda e: e.tensor_scalar(out=stS[:, 0, :], in0=pb[:, 0:NB], scalar1=inv_n, scalar2=None, op0=ALU.mult), reads=[pb], pwrites=[stS])
            S.add("dve", lambda e: e.tensor_tensor(out=stS[:, 1, :], in0=stS[:, 0, :], in1=stS[:, 0, :], op=ALU.mult), reads=[stS], pwrites=[stS])
            S.add("dve", lambda e: e.scalar_tensor_tensor(out=stS[:, 1, :], in0=pb2[:, 0:NB], scalar=inv_n, in1=stS[:, 1, :], op0=ALU.mult, op1=ALU.subtract),
                  reads=[pb2, stS], pwrites=[stS])
            S.add("act", lambda e: e.activation(out=stS[:, 2, :], in_=stS[:, 1, :], func=AF.Ln, bias=EPS), reads=[stS], pwrites=[stS])
            S.add("act", lambda e: e.activation(out=stS[:, 2, :], in_=stS[:, 2, :], func=AF.Exp, scale=-0.5), reads=[stS], pwrites=[stS])

        def sample_layer0():
            set_pa([P0, P1])
            S.dma("sp", cEXP, d_exp, pwrites=[vhat], group=vhat)
            S.dma("sp", SA[0:NB, 0:1536], st_sconv[:, 0, :], pwrites=[bigA], group=bigA)
            S.dma("sp", SA[0:NB, 1536:3072], st_sconv[:, 1, :], pwrites=[bigA], group=bigA)
            S.dma("sp", SM[0:NB, 0:1536], st_sconv[:, 2, :], pwrites=[mix], group=mix)
            hist_to_fm(SA, bigA, 24, None, bsbc, bulk=lambda i, n: (hallS[:, 0:2, :, :], "p (k j b) -> p k j b", dict(k=2, j=12)))
            hist_to_fm(SM, mix, 12, None, bsbc, bulk=lambda i, n: (hallS[:, 2, :, :], "p (j b) -> p j b", dict(j=12)))
            S.add("dve", lambda e: e.tensor_copy(out=hallS[:, 3, :, :], in_=projS[:, 8:20, :]), reads=[tmpw], pwrites=[bsbc])
            S.dma("sp", o_sconv_s_new, projS[:, 8:20, :].rearrange("p j b -> p (j b)"), reads=[tmpw], pwrites=[outbuf], group=tmpw)
            S.dma("sp", o_sconv_s_hist, st_sconv[:, 1:3, :], pwrites=[outbuf], group=outbuf)
            wv4 = pfm[:, PC("scw"):PC("scw") + 48].rearrange("p (k j) -> p k j", k=4).unsqueeze(3).broadcast_to([128, 4, 12, NB])
            S.add("dve", lambda e: e.tensor_tensor(out=hallS, in0=hallS, in1=wv4, op=ALU.mult), reads=[bsbc, pfm], writes=[bsbc])
            S.add("dve", lambda e: e.tensor_reduce(out=convS[:, :, :], in_=hallS.rearrange("p k j b -> p j b k"), axis=AX.X, op=ALU.add),
                  reads=[bsbc], writes=[convS])
            for j in range(12):
                bc_ = PC("scb") + j
                S.add("act", lambda e, j=j, bc_=bc_: e.activation(out=xcS[:, j, :], in_=convS[:, j, :], func=AF.Silu, bias=pfm[:, bc_:bc_ + 1]),
                      reads=[convS, pfm], pwrites=[xcS])
            for q in range(2):
                pb = nextPA()
                for j in range(8):
                    S.add("pe", lambda e, q=q, j=j, pb=pb: e.matmul(pb[:, j * NB:(j + 1) * NB], lhsT=cEXP[:, j * 128:(j + 1) * 128], rhs=dtS[:, q, :],
                                                                     start=True, stop=True), reads=[vhat, dtS], pwrites=[pb])
                S.add("dve", lambda e, q=q, pb=pb: e.tensor_copy(out=dtE[:, q, :, :], in_=pb[:, 0:8 * NB].rearrange("p (j b) -> p j b", j=8)),
                      reads=[pb], pwrites=[dtE])
            S.add("dve", lambda e: e.tensor_tensor(out=xsS[:, :, :], in0=xcS[:, 0:8, :], in1=dtE[:, 0, :, :], op=ALU.mult), reads=[xcS, dtE], writes=[xsS])
            pb = nextPA()
            for q in range(4):
                S.add("pe", lambda e, q=q, pb=pb: e.transpose(pb[0:NB, q * 128:(q + 1) * 128], xcS[:, 8 + q, :], cIDF[:, :]), reads=[xcS, cIDF], pwrites=[pb])
            S.add("dve", lambda e, pb=pb: e.tensor_copy(out=BC_tm, in_=pb[0:NB, :]), reads=[pb], pwrites=[vhat])
            hbs = [SX[:, 0:1024], SX[:, 1024:2048], SX[:, 3072:4096]]
            ob = SX[:, 2048:3072].rearrange("p (j n) -> p j n", j=8)
            hbB = [Buf(hbs[0], "hb0"), Buf(hbs[1], "hb1"), Buf(hbs[2], "hb2")]
            obB = Buf(SX[:, 2048:3072], "obS")
            l1_barrier_reads.extend(hbB + [obB])
            S.add("dve", lambda e: e.memset(SX[:, 2048:3072], 0.0), writes=[xh_tm] + hbB + [obB])
            for b in range(NB):
                hb = hbs[b % 3].rearrange("p (j n) -> p j n", j=8)
                hB = hbB[b % 3]
                pbc = P5 if b % 2 == 0 else P6
                S.add("pe", lambda e, b=b, pbc=pbc: e.matmul(pbc[:, :], lhsT=cSEL16[:, b * 128:(b + 1) * 128], rhs=BC_tm, start=True, stop=True),
                      reads=[cSEL16, vhat], writes=[pbc])
                S.dma("sp", hb, st_ssm[b].rearrange("(j p) n -> p j n", p=128), writes=[hB], group=hB)
                for j8 in range(8):
                    S.add("act", lambda e, b=b, hb=hb, j8=j8: e.activation(out=hb[:, j8, :], in_=hb[:, j8, :], func=AF.Copy, scale=dtE[:, 1, j8, b:b + 1]),
                          reads=[dtE], pwrites=[hB])
                for g in range(2):
                    S.add("dve", lambda e, b=b, g=g, pbc=pbc: e.tensor_tensor(
                        out=ob[:, 4 * g:4 * g + 4, :], in0=pbc[:, g * 128:(g + 1) * 128].unsqueeze(1).broadcast_to([128, 4, 128]),
                        in1=xsS[:, 4 * g:4 * g + 4, b:b + 1].broadcast_to([128, 4, 128]), op=ALU.mult),
                        reads=[pbc, xsS], pwrites=[obB])
                S.add("dve", lambda e, hb=hb: e.tensor_tensor(out=hb, in0=hb, in1=ob, op=ALU.add), reads=[hB, obB], writes=[hB])
                S.dma("act", o_ssm_s[b].rearrange("(j p) n -> p j n", p=128), hb, reads=[hB], pwrites=[outbuf], group=hB)
                for g in range(2):
                    S.add("dve", lambda e, g=g, pbc=pbc, hb=hb: e.tensor_tensor(
                        out=ob[:, 4 * g:4 * g + 4, :], in0=pbc[:, 256 + g * 128:256 + (g + 1) * 128].unsqueeze(1).broadcast_to([128, 4, 128]),
                        in1=hb[:, 4 * g:4 * g + 4, :], op=ALU.mult), reads=[pbc, hB], pwrites=[obB])
                S.add("dve", lambda e, b=b: e.tensor_reduce(out=yS[:, :, b], in_=ob, axis=AX.X, op=ALU.add), reads=[obB], pwrites=[yS])
            S.add("dve", lambda e: e.tensor_tensor(out=t1S[:, :, :], in0=xcS[:, 0:8, :], in1=bcb(pfm[:, PC("Dfm"):PC("Dfm") + 8]), op=ALU.mult),
                  reads=[xcS, pfm], writes=[t1S])
            S.add("dve", lambda e: e.tensor_tensor(out=yS[:, :, :], in0=yS[:, :, :], in1=t1S[:, :, :], op=ALU.add), reads=[yS, t1S], writes=[yS])
            S.add("dve", lambda e: e.tensor_tensor(out=yS[:, :, :], in0=yS[:, :, :], in1=projS[:, 0:8, :], op=ALU.mult), reads=[yS, tmpw], writes=[yS])
            S.add("act", lambda e: e.activation(out=sqS[:, :, :], in_=yS[:, :, :], func=AF.Square), reads=[yS], writes=[sqS])
            pb = nextPA()
            for g in range(2):
                for k in range(4):
                    S.add("pe", lambda e, g=g, k=k, pb=pb: e.matmul(pb[:, g * NB:(g + 1) * NB], lhsT=cONESB[:, :], rhs=sqS[:, 4 * g + k, :],
                                                                     start=(k == 0), stop=(k == 3)), reads=[cONESB, sqS], pwrites=[pb])
            S.add("act", lambda e, pb=pb: e.activation(out=stS[:, 0:2, :], in_=pb[:, 0:2 * NB].rearrange("p (g b) -> p g b", g=2), func=AF.Ln,
                                                        scale=1.0 / 512.0, bias=EPS), reads=[pb], pwrites=[stS])
            S.add("act", lambda e: e.activation(out=stS[:, 0:2, :], in_=stS[:, 0:2, :], func=AF.Exp, scale=-0.5), reads=[stS], pwrites=[stS])
            for g in range(2):
                S.add("dve", lambda e, g=g: e.tensor_tensor(out=t1S[:, 4 * g:4 * g + 4, :], in0=yS[:, 4 * g:4 * g + 4, :], in1=bcj(stS[:, g, :], 4), op=ALU.mult),
                      reads=[yS, stS], pwrites=[t1S])
            S.add("dve", lambda e: e.tensor_tensor(out=mixS[:, 0:8, :], in0=t1S[:, :, :], in1=bcb(pfm[:, PC("gn"):PC("gn") + 8]), op=ALU.mult),
                  reads=[t1S, pfm], pwrites=[mixS])
            stats_s(projS[:, 29:37, :], tmpw, 8, 1.0 / 1024.0)
            S.add("dve", lambda e: e.tensor_tensor(out=t1S[:, :, :], in0=projS[:, 29:37, :], in1=bcj(stS[:, 0, :], 8), op=ALU.subtract), reads=[tmpw, stS], writes=[t1S])
            S.add("dve", lambda e: e.tensor_tensor(out=t1S[:, :, :], in0=t1S[:, :, :], in1=bcj(stS[:, 2, :], 8), op=ALU.mult), reads=[t1S, stS], writes=[t1S])
            S.add("dve", lambda e: e.tensor_tensor(out=t1S[:, :, :], in0=t1S[:, :, :], in1=bcb(pfm[:, PC("lng"):PC("lng") + 8]), op=ALU.mult), reads=[t1S, pfm], writes=[t1S])
            S.add("dve", lambda e: e.tensor_tensor(out=t2S[:, :, :], in0=t1S[:, :, :], in1=bcb(pfm[:, PC("lnb"):PC("lnb") + 8]), op=ALU.add), reads=[t1S, pfm], writes=[t2S])
            S.dma("sp", o_gv_s, t2S[:, :, :].rearrange("p j b -> p (j b)"), reads=[t2S], pwrites=[outbuf], group=t2S)
            S.add("dve", lambda e: e.tensor_tensor(out=t1S[:, :, :], in0=t2S[:, :, :], in1=bcb(pfm[:, PC("w00"):PC("w00") + 8]), op=ALU.mult), reads=[t2S, pfm], writes=[t1S])
            S.add("dve", lambda e: e.tensor_tensor(out=t1S[:, :, :], in0=t1S[:, :, :], in1=bcb(pfm[:, PC("b0"):PC("b0") + 8]), op=ALU.add), reads=[t1S, pfm], writes=[t1S])
            S.add("dve", lambda e: e.tensor_tensor(out=t1S[:, :, :], in0=t1S[:, :, :], in1=projS[:, 21:29, :], op=ALU.mult), reads=[t1S, tmpw], writes=[t1S])
            S.add("dve", lambda e: e.tensor_tensor(out=mixS[:, 8:16, :], in0=t1S[:, :, :], in1=projS[:, 37:45, :], op=ALU.mult), reads=[t1S, tmpw], pwrites=[mixS])
            outproj_s(w_out_e)
            rmsnorm_s(PC("no"), hnS[:, :, :], hnS)

        def sample_layer1():
            set_pa([P0, P1])
            hall31 = SZ[:, 0:31 * 8 * NB].rearrange("p (k j b) -> p k j b", k=31, j=8)
            hall4 = hallS[:, :, 0:8, :]
            S.add("dve", lambda e: e.tensor_tensor(out=hall31[:, 30, :, :], in0=projS[:, 0:8, :], in1=projS[:, 8:16, :], op=ALU.mult), reads=[tmpw], pwrites=[bigZ])
            S.dma("sp", o_ccv_s_new, hall31[:, 30, :, :].rearrange("p j b -> p (j b)"), reads=[bigZ], pwrites=[outbuf], group=bigZ)
            S.dma("sp", o_ccv_s_hist, st_ccv[:, 1:30, :], pwrites=[outbuf], group=outbuf)
            for gi in range(8):
                k0 = gi * 4
                nk = min(4, 30 - k0)
                stg_ap, stg_buf = (SA, bigA) if gi % 2 == 0 else (SM, mix)
                S.dma("sp", stg_ap[0:NB, 0:nk * 1024].rearrange("b (k c) -> b k c", k=nk), st_ccv[:, k0:k0 + nk, :], pwrites=[stg_buf], group=stg_buf)
                hist_to_fm(stg_ap, stg_buf, nk * 8, None, bigZ, bulk=lambda i, n, k0=k0, nk=nk: (hall31[:, k0:k0 + nk, :, :], "p (k j b) -> p k j b", dict(k=nk, j=8)))
            wv31 = pfm[:, PC("ccw"):PC("ccw") + 248].rearrange("p (k j) -> p k j", k=31).unsqueeze(3).broadcast_to([128, 31, 8, NB])
            S.add("dve", lambda e: e.tensor_tensor(out=hall31, in0=hall31, in1=wv31, op=ALU.mult), reads=[bigZ, pfm], writes=[bigZ])
            S.add("dve", lambda e: e.tensor_reduce(out=convS[:, 0:8, :], in_=hall31.rearrange("p k j b -> p j b k"), axis=AX.X, op=ALU.add),
                  reads=[bigZ], writes=[convS])
            S.add("dve", lambda e: e.tensor_tensor(out=convS[:, 0:8, :], in0=convS[:, 0:8, :], in1=bcb(pfm[:, PC("ccb"):PC("ccb") + 8]), op=ALU.add),
                  reads=[convS, pfm], writes=[convS])
            stats_s(convS[:, 0:8, :], convS, 8, 1.0 / 1024.0)
            S.add("dve", lambda e: e.tensor_tensor(out=t1S[:, :, :], in0=convS[:, 0:8, :], in1=bcj(stS[:, 0, :], 8), op=ALU.subtract), reads=[convS, stS], writes=[t1S])
            S.add("dve", lambda e: e.tensor_tensor(out=t1S[:, :, :], in0=t1S[:, :, :], in1=bcj(stS[:, 2, :], 8), op=ALU.mult), reads=[t1S, stS], writes=[t1S])
            S.add("dve", lambda e: e.tensor_tensor(out=t1S[:, :, :], in0=t1S[:, :, :], in1=bcb(pfm[:, PC("cclg"):PC("cclg") + 8]), op=ALU.mult), reads=[t1S, pfm], writes=[t1S])
            S.add("dve", lambda e: e.tensor_tensor(out=t1S[:, :, :], in0=t1S[:, :, :], in1=bcb(pfm[:, PC("cclb"):PC("cclb") + 8]), op=ALU.add), reads=[t1S, pfm], writes=[t1S])
            S.add("act", lambda e: e.activation(out=t1S[:, :, :], in_=t1S[:, :, :], func=AF.Silu), reads=[t1S], writes=[t1S])
            S.add("dve", lambda e: e.tensor_tensor(out=mixS[:, 0:8, :], in0=t1S[:, :, :], in1=projS[:, 16:24, :], op=ALU.mult), reads=[t1S, tmpw], pwrites=[mixS])
            S.dma("sp", SA[0:NB, 0:3072].rearrange("b (k c) -> b k c", k=3), st_lconv[:, :, :], pwrites=[bigA], group=bigA)
            hist_to_fm(SA, bigA, 24, None, bsbc, bulk=lambda i, n: (hall4[:, 0:3, :, :], "p (k j b) -> p k j b", dict(k=3, j=8)))
            S.add("dve", lambda e: e.tensor_copy(out=hall4[:, 3, :, :], in_=projS[:, 24:32, :]), reads=[tmpw], pwrites=[bsbc])
            S.dma("sp", o_lconv_s_new, projS[:, 24:32, :].rearrange("p j b -> p (j b)"), reads=[tmpw], pwrites=[outbuf], group=tmpw)
            S.dma("sp", o_lconv_s_hist, st_lconv[:, 1:3, :], pwrites=[outbuf], group=outbuf)
            wl4 = pfm[:, PC("lcw"):PC("lcw") + 32].rearrange("p (k j) -> p k j", k=4).unsqueeze(3).broadcast_to([128, 4, 8, NB])
            S.add("dve", lambda e: e.tensor_tensor(out=hall4, in0=hall4, in1=wl4, op=ALU.mult), reads=[bsbc, pfm], writes=[bsbc])
            S.add("dve", lambda e: e.tensor_reduce(out=xcS[:, 0:8, :], in_=hall4.rearrange("p k j b -> p j b k"), axis=AX.X, op=ALU.add), reads=[bsbc], writes=[xcS])
            S.add("dve", lambda e: e.tensor_tensor(out=xcS[:, 0:8, :], in0=xcS[:, 0:8, :], in1=bcb(pfm[:, PC("lcb"):PC("lcb") + 8]), op=ALU.add), reads=[xcS, pfm], writes=[xcS])
            S.add("dve", lambda e: e.tensor_copy(out=hnS[:, :, :], in_=xcS[:, 0:8, :]), reads=[xcS], writes=[hnS])
            for q, (boff, bcol) in enumerate(((0, "lba"), (8, "lbx"))):
                pb = nextPA()
                for j in range(8):
                    S.add("pe", lambda e, j=j, boff=boff, pb=pb: e.matmul(pb[:, j * NB:(j + 1) * NB], lhsT=MT_bf[:, boff + j, :], rhs=hnS[:, j, :], start=True, stop=True),
                          reads=[MT_bf, hnS], pwrites=[pb])
                dst = t1S if q == 0 else t2S
                S.add("dve", lambda e, pb=pb, dst=dst, bcol=bcol: e.tensor_tensor(out=dst[:, :, :], in0=pb[:, 0:8 * NB].rearrange("p (j b) -> p j b", j=8),
                                                                                  in1=bcb(pfm[:, PC(bcol):PC(bcol) + 8]), op=ALU.add), reads=[pb, pfm], writes=[dst])
                S.add("act", lambda e, dst=dst: e.activation(out=dst[:, :, :], in_=dst[:, :, :], func=AF.Sigmoid), reads=[dst], writes=[dst])
            S.add("dve", lambda e: e.tensor_tensor(out=t1S[:, :, :], in0=t1S[:, :, :], in1=bcb(sp8[:, :]), op=ALU.mult), reads=[t1S, sp8], writes=[t1S])
            S.add("act", lambda e: e.activation(out=t1S[:, :, :], in_=t1S[:, :, :], func=AF.Exp), reads=[t1S], writes=[t1S])
            S.add("dve", lambda e: e.tensor_tensor(out=yS[:, :, :], in0=t1S[:, :, :], in1=t1S[:, :, :], op=ALU.mult), reads=[t1S], writes=[yS])
            S.add("act", lambda e: e.activation(out=yS[:, :, :], in_=yS[:, :, :], func=AF.Sqrt, scale=-1.0, bias=1.0), reads=[yS], writes=[yS])
            S.add("dve", lambda e: e.tensor_tensor(out=t2S[:, :, :], in0=t2S[:, :, :], in1=xcS[:, 0:8, :], op=ALU.mult), reads=[t2S, xcS], writes=[t2S])
            S.add("dve", lambda e: e.tensor_tensor(out=t2S[:, :, :], in0=t2S[:, :, :], in1=yS[:, :, :], op=ALU.mult), reads=[t2S, yS], writes=[t2S])
            S.dma("sp", SM[0:NB, 0:1024], st_lru[:, :], pwrites=[mix], group=mix)
            hist_to_fm(SM, mix, 8, None, xsS, bulk=lambda i, n: (xsS[:, :, :], "p (j b) -> p j b", dict(j=8)))
            S.add("dve", lambda e: e.tensor_tensor(out=xsS[:, :, :], in0=xsS[:, :, :], in1=t1S[:, :, :], op=ALU.mult), reads=[xsS, t1S], writes=[xsS])
            S.add("dve", lambda e: e.tensor_tensor(out=xsS[:, :, :], in0=xsS[:, :, :], in1=t2S[:, :, :], op=ALU.add), reads=[xsS, t2S], writes=[xsS])
            S.dma("sp", o_lru_s, xsS[:, :, :].rearrange("p j b -> p (j b)"), reads=[xsS], pwrites=[outbuf], group=xsS)
            S.add("dve", lambda e: e.tensor_tensor(out=mixS[:, 8:16, :], in0=xsS[:, :, :], in1=projS[:, 32:40, :], op=ALU.mult), reads=[xsS, tmpw], pwrites=[mixS])
            outproj_s(w_out_o)
            rmsnorm_s(PC("nf"), t2S[:, :, :], t2S)
            S.dma("sp", o_y_s, t2S[:, :, :].rearrange("p j b -> p (j b)"), reads=[t2S], pwrites=[outbuf], group=t2S)


        samp_tab = [None]

        def samp_cols(wv, col0, ncols):
            tab = samp_tab[0]
            if tab is None:
                return
            for jj in range(ncols // 128):
                c = col0 + jj * 128
                for (s_, e_, base, func) in tab:
                    if s_ <= c < e_:
                        cc = base + (c - s_) // 128
                        pb = nextPA()
                        for k in range(8):
                            S.add("pe", lambda e, k=k, jj=jj, wv=wv, pb=pb: e.matmul(pb[:, 0:NB], lhsT=wv[1][:, k, jj * 128:(jj + 1) * 128], rhs=hnS[:, k, :],
                                                                                      start=(k == 0), stop=(k == 7)), reads=[wv[0], hnS], writes=[pb])
                        S.add("act", lambda e, cc=cc, pb=pb, func=func: e.activation(out=projS[:, cc, :], in_=pb[:, 0:NB], func=func), reads=[pb], pwrites=[tmpw])

        def samp_dt(wv):
            if samp_tab[0] is None:
                return
            pb = nextPA()
            for k in range(8):
                S.add("pe", lambda e, k=k, wv=wv, pb=pb: e.matmul(pb[0:16, 0:NB], lhsT=wv[1][:, k, :], rhs=hnS[:, k, :], start=(k == 0), stop=(k == 7)),
                      reads=[wv[0], hnS], writes=[pb])
            S.add("act", lambda e, pb=pb: e.activation(out=dtS[:, 0, :], in_=pb[0:16, 0:NB], func=AF.Exp, bias=p16[:, 0:1]), reads=[pb, p16], pwrites=[dtS])
            S.add("act", lambda e: e.activation(out=dtS[:, 0, :], in_=dtS[:, 0, :], func=AF.Ln, bias=1.0), reads=[dtS], pwrites=[dtS])
            S.add("dve", lambda e: e.tensor_scalar(out=dtS[:, 1, :], in0=dtS[:, 0, :], scalar1=ea16[:, 0:1], scalar2=-1.0, op0=ALU.mult, op1=ALU.mult),
                  reads=[dtS, ea16], pwrites=[dtS])
            S.add("act", lambda e: e.activation(out=dtS[:, 1, :], in_=dtS[:, 1, :], func=AF.Exp), reads=[dtS], pwrites=[dtS])

        TAB_L0 = [(0, 1024, 0, AF.Silu), (1024, 2560, 8, AF.Copy), (2576, 3600, 21, AF.Gelu_apprx_tanh),
                  (3600, 4624, 29, AF.Gelu_apprx_tanh), (4624, 5648, 37, AF.Silu)]
        TAB_L1 = [(0, 1024, 0, AF.Copy), (1024, 2048, 8, AF.Sigmoid), (2048, 3072, 16, AF.Silu), (3072, 4096, 24, AF.Copy), (4096, 5120, 32, AF.Silu)]
        if stage >= 3:
            rmsnorm_s(PC("ne"), hnS[:, :, :], hnS)

        for ti in range(int(_os.environ.get('K_L0T', NT if stage >= 1 else 0))):
            samp_tab[0] = TAB_L0 if (stage >= 3 and ti == NT - 1) else None
            layer0_tile(ti)
            samp_tab[0] = None

        hT = sb("hT", [128, 8, 128])
        for j in range(8):
            pb = nextPA()
            S.add("pe", lambda e, j=j, pb=pb: e.transpose(pb[:, 0:128], Hst[:, j * 128:(j + 1) * 128], cIDF[:, :]), reads=[Hst, cIDF], writes=[pb])
            S.add("act", lambda e, j=j, pb=pb: e.activation(out=hT[:, j, :], in_=pb[:, 0:128], func=AF.Copy), reads=[pb], pwrites=[hT])
        S.dma("sp", o_ssm_p.rearrange("(j p) n -> p j n", p=128), hT[:, :, :], reads=[hT], pwrites=[outbuf], group=hT)
        S.dma("sp", o_sconv_p, lastraw[:, :, :].rearrange("p j k -> p (j k)"), reads=[lastraw], pwrites=[outbuf], group=lastraw)
        if stage >= 3:
            sample_layer0()
        GL = 544
        glu_view = xh_tm.t[:, :, :].rearrange("p a b -> p (a b)").bitcast(BF16)
        glu = [Buf(glu_view[:, j * GL:(j + 1) * GL], f"glu{j}") for j in range(8)]
        if not _os.environ.get('K_SKIP_BAR'):
            S.add("dve", lambda e: e.memset(glu_view[:, 0:8 * GL], 0.0), reads=[xh_tm], writes=glu + l1_barrier_reads)
        cvo = bigZ.t[:, :, :].rearrange("p a (c t) -> p (a c) t", t=512)
        v32 = vhat.t[:, :, :].rearrange("p a b -> p (a b)").bitcast(F32)
        mean_sb, var_sb, rstd_ln, mr_sb = v32[:, 0:512], v32[:, 512:1024], v32[:, 1024:1536], v32[:, 1536:2048]
        if not _os.environ.get('K_SKIP_MT'):
            S.dma("pool", MT_bf[:, 0:8, :], d_bda.rearrange("p (j q) -> p j q", j=8), pwrites=[MT_bf], group=MT_bf)
            S.dma("pool", MT_bf[:, 8:16, :], d_bdx.rearrange("p (j q) -> p j q", j=8), pwrites=[MT_bf], group=MT_bf)
        sp8 = sb("sp8", [128, 8])
        sp16 = sb("sp16", [128, 8])
        hist1 = sb("hist1", [128, 8, 3], BF16)
        hcarry = sb("hcarry", [128, 8])
        glu_last = sb("glu_last", [128, 8, 30])
        lastraw1 = sb("lastraw1", [128, 8, 4])
        lamc = PC("lam")
        S.add("act", lambda e: e.activation(out=sp8[:, :], in_=pfm[:, lamc:lamc + 8], func=AF.Exp, scale=-1.0), reads=[pfm], writes=[sp8])
        S.add("act", lambda e: e.activation(out=sp8[:, :], in_=sp8[:, :], func=AF.Ln, bias=1.0), reads=[sp8], writes=[sp8])
        S.add("dve", lambda e: e.tensor_scalar(out=sp16[:, :], in0=sp8[:, :], scalar1=-16.0, scalar2=None, op0=ALU.mult), reads=[sp8], writes=[sp16])
        S.add("dve", lambda e: e.tensor_scalar(out=sp8[:, :], in0=sp8[:, :], scalar1=-8.0, scalar2=None, op0=ALU.mult), reads=[sp8], writes=[sp8])
        S.add("dve", lambda e: e.memset(hist1[:, :, :], 0.0), writes=[hist1])
        S.add("dve", lambda e: e.memset(hcarry[:, :], 0.0), writes=[hcarry])
        Lflat = Lbuf.t[:, :, :].rearrange("p a b -> p (a b)")
        t_xc, t_r = yoff[:, 0:512], yoff[:, 512:1024]
        t_i, t_a = ysb[:, 0:512], ysb[:, 512:1024]
        t_b, t_h = Hst[:, 0:512], Hst[:, 512:1024]
        t_g, t_m = Lflat[:, 0:512], Lflat[:, 512:1024]
        xc_bf = xs_bf[:, 0:512]
        SZl = bigZ.t[:, :, :].rearrange("p a b -> p (a b)")
        lzv = [SZl[:, i * 512:(i + 1) * 512] for i in range(8)]
        lz = [Buf(lzv[i], f"lz{i}") for i in range(8)]
        lbar = sb("lbar", [128, 2])

        l1_pieces = [t_xc, t_r, t_i, t_a, t_b, t_h, t_g, t_m]
        l1_pbufs = [yoff, yoff, ysb, ysb, Hst, Hst, Lbuf, Lbuf]
        l1_sq = vhat.t[:, :, :].rearrange("p a (c t) -> p (a c) t", t=512)

        def l1_load(ti):
            for k in range(8):
                S.dma("sp", l1_pieces[k], x1T[k * 128:(k + 1) * 128, ti * TT:(ti + 1) * TT], reads=[x1buf], pwrites=[l1_pbufs[k]], group=l1_pbufs[k])

        def layer1_tile(ti):
            t0 = ti * TT
            last = (ti == NT - 1)
            S.dma("sp", bigA[:, :, :], (xT if _os.environ.get("K_NOX1") else x1T)[:, t0:t0 + TT].rearrange("(k p) t -> p k t", p=128), reads=[x1buf], writes=[bigA], group=bigA)
            set_pa([P0, P1, P7, P3, P4])
            if ti == 0:
                l1_load(0)
                norm_sq(l1_pieces, l1_pbufs, l1_sq, vhat)
                norm_rest(l1_pieces, l1_pbufs, l1_sq, vhat, PC("no"))
            if stage <= 1.1:
                return
            pend_hist = []
            for half in range(2):
                wa_ = wload(w_in_o[:, half * 512:(half + 1) * 512], 8, 512)
                samp_cols(wa_, half * 512, 512)
                wb_ = wload(w_in_o[:, 1024 + half * 512:1024 + (half + 1) * 512], 8, 512)
                samp_cols(wb_, 1024 + half * 512, 512)
                for jj in range(4):
                    j = half * 4 + jj
                    pA = nextPA()
                    projA(wa_, jj, hn, pA)
                    pB = nextPA()
                    projA(wb_, jj, hn, pB)
                    sgb = sg[j % 2]
                    gj = glu[j]
                    PC2 = (P2, P5, P6)[j % 3]
                    S.add("act", lambda e, sgb=sgb, pB=pB: e.activation(out=sgb[:, :], in_=pB[:, :], func=AF.Sigmoid), reads=[pB], writes=[sgb])
                    S.add("dve", lambda e, gj=gj, pA=pA, sgb=sgb: e.tensor_tensor(out=gj[:, 30:542], in0=pA[:, :], in1=sgb[:, :], op=ALU.mult),
                          reads=[pA, sgb], pwrites=[gj])
                    if last:
                        S.add("dve", lambda e, j=j, pA=pA, sgb=sgb: e.tensor_tensor(out=glu_last[:, j, :], in0=pA[:, 482:512], in1=sgb[:, 482:512], op=ALU.mult),
                              reads=[pA, sgb], pwrites=[glu_last])
                    for k in range(31):
                        dg = diag(PC("ccw") + k * 8 + j, "dve" if k % 4 != 3 else "act")
                        S.add("pe", lambda e, k=k, dg=dg, gj=gj, PC2=PC2: e.matmul(PC2[:, :], lhsT=dg[:, :], rhs=gj[:, k:k + 512], start=(k == 0), stop=(k == 30)),
                              reads=[dg, gj], writes=[PC2])
                    bc_ = PC("ccb") + j
                    S.add("act", lambda e, j=j, bc_=bc_, PC2=PC2: e.activation(out=cvo[:, j, :], in_=PC2[:, :], func=AF.Identity, bias=pfm[:, bc_:bc_ + 1]),
                          reads=[PC2, pfm], pwrites=[bigZ])
                    if pend_hist:
                        pend_hist.pop()()
                    pend_hist.append(lambda gj=gj: S.add("dve", lambda e: e.tensor_copy(out=gj[:, 0:30], in_=gj[:, 512:542]), reads=[gj], pwrites=[gj]))
            if stage <= 1.2:
                return
            while pend_hist:
                pend_hist.pop()()
            S.add("act", lambda e: e.activation(out=mix[:, 0:8, :], in_=cvo, func=AF.Square), reads=[bigZ], writes=[mix])
            S.add("dve", lambda e: e.tensor_copy(out=mix[:, 8:16, :], in_=cvo), reads=[bigZ], pwrites=[mix])
            for k in range(8):
                S.add("pe", lambda e, k=k: e.matmul(P5[:, :], lhsT=cONESB[:, :], rhs=mix[:, 8 + k, :], start=(k == 0), stop=(k == 7)),
                      reads=[cONESB, mix], writes=[P5])
            for k in range(8):
                S.add("pe", lambda e, k=k: e.matmul(P6[:, :], lhsT=cONESB[:, :], rhs=mix[:, k, :], start=(k == 0), stop=(k == 7)),
                      reads=[cONESB, mix], writes=[P6])
            S.add("dve", lambda e: e.tensor_scalar(out=mean_sb, in0=P5[:, :], scalar1=1.0 / 1024.0, scalar2=None, op0=ALU.mult), reads=[P5], pwrites=[vhat])
            S.add("dve", lambda e: e.tensor_tensor(out=var_sb, in0=mean_sb, in1=mean_sb, op=ALU.mult), reads=[vhat], pwrites=[vhat])
            S.add("dve", lambda e: e.scalar_tensor_tensor(out=var_sb, in0=P6[:, :], scalar=1.0 / 1024.0, in1=var_sb, op0=ALU.mult, op1=ALU.subtract),
                  reads=[P6, vhat], pwrites=[vhat])
            S.add("act", lambda e: e.activation(out=rstd_ln, in_=var_sb, func=AF.Ln, bias=EPS), reads=[vhat], pwrites=[vhat])
            S.add("act", lambda e: e.activation(out=rstd_ln, in_=rstd_ln, func=AF.Exp, scale=-0.5), reads=[vhat], pwrites=[vhat])
            S.add("dve", lambda e: e.tensor_tensor(out=mr_sb, in0=mean_sb, in1=rstd_ln, op=ALU.mult), reads=[vhat], pwrites=[vhat])
            if stage <= 1.3:
                return
            for half in range(2):
                wv = wload(w_in_o[:, 2048 + half * 512:2048 + (half + 1) * 512], 8, 512)
                samp_cols(wv, 2048 + half * 512, 512)
                for jj in range(4):
                    j = half * 4 + jj
                    pG = nextPA()
                    projA(wv, jj, hn, pG)
                    sgb = sg[j % 2]
                    tb = xcf[j % 2]
                    S.add("act", lambda e, sgb=sgb, pG=pG: e.activation(out=sgb[:, :], in_=pG[:, :], func=AF.Silu), reads=[pG], writes=[sgb])
                    S.add("dve", lambda e, j=j, tb=tb: e.tensor_tensor(out=tb[:, :], in0=cvo[:, j, :], in1=rstd_ln, op=ALU.mult), reads=[bigZ, vhat], writes=[tb])
                    S.add("dve", lambda e, tb=tb: e.tensor_tensor(out=tb[:, :], in0=tb[:, :], in1=mr_sb, op=ALU.subtract), reads=[tb, vhat], writes=[tb])
                    gc_, bc_ = PC("cclg") + j, PC("cclb") + j
                    S.add("act", lambda e, tb=tb, gc_=gc_, bc_=bc_: e.activation(out=tb[:, :], in_=tb[:, :], func=AF.Silu, scale=pfm[:, gc_:gc_ + 1],
                                                                                 bias=pfm[:, bc_:bc_ + 1]), reads=[tb, pfm], writes=[tb])
                    S.add("dve", lambda e, j=j, tb=tb, sgb=sgb: e.tensor_tensor(out=mix[:, j, :], in0=tb[:, :], in1=sgb[:, :], op=ALU.mult),
                          reads=[tb, sgb], pwrites=[mix])
            if stage <= 1.4:
                return
            set_pa([P0, P1])
            S.add("dve", lambda e: e.memset(lbar[:, :], 0.0), writes=[bigZ, lbar] + lz)
            def lru_chunk(j, jj, wx_, wg_):
                od = j % 2
                if od == 0:
                    V = dict(xc=t_xc, r=t_r, i=t_i, a=t_a, b=t_b, h=t_h, g=t_g, m=t_m, xb=xc_bf)
                    Bf = dict(xc=yoff, r=yoff, i=ysb, a=ysb, b=Hst, h=Hst, g=Lbuf, m=Lbuf, xb=xs_bf)
                    PCV, PGA, PGX = P2, P3, P4
                else:
                    V = dict(xc=lzv[0], r=lzv[1], i=lzv[2], a=lzv[3], b=lzv[4], h=lzv[5], g=lzv[6], m=lzv[7], xb=xsd_bf[:, 0:512])
                    Bf = dict(xc=lz[0], r=lz[1], i=lz[2], a=lz[3], b=lz[4], h=lz[5], g=lz[6], m=lz[7], xb=xsd_bf)
                    PCV, PGA, PGX = P5, P6, P7
                PGAv = PGA.t if PGA is not P7 else P7t
                PGXv = PGX.t if PGX is not P7 else P7t
                pX = nextPA()
                projA(wx_, jj, hn, pX)
                raw = raws[j % 2]
                S.add("dve", lambda e, j=j, raw=raw: e.tensor_copy(out=raw[:, 0:3], in_=hist1[:, j, :]), reads=[hist1], pwrites=[raw])
                S.add("act", lambda e, raw=raw, pX=pX: e.activation(out=raw[:, 3:515], in_=pX[:, :], func=AF.Copy), reads=[pX], pwrites=[raw])
                if last:
                    S.add("act", lambda e, j=j, pX=pX: e.activation(out=lastraw1[:, j, :], in_=pX[:, 508:512], func=AF.Copy), reads=[pX], pwrites=[lastraw1])
                S.add("dve", lambda e, j=j, raw=raw: e.tensor_copy(out=hist1[:, j, :], in_=raw[:, 512:515]), reads=[raw], pwrites=[hist1])
                yield
                for k in range(4):
                    dg = diag(PC("lcw") + k * 8 + j, "dve")
                    S.add("pe", lambda e, k=k, dg=dg, raw=raw, PCV=PCV: e.matmul(PCV[:, :], lhsT=dg[:, :], rhs=raw[:, k:k + 512], start=(k == 0), stop=(k == 3)),
                          reads=[dg, raw], writes=[PCV])
                bc_ = PC("lcb") + j
                S.add("act", lambda e, bc_=bc_, V=V, PCV=PCV: e.activation(out=V["xc"], in_=PCV[:, :], func=AF.Identity, bias=pfm[:, bc_:bc_ + 1]),
                      reads=[PCV, pfm], pwrites=[Bf["xc"]])
                S.add("dve", lambda e, V=V: e.tensor_copy(out=V["xb"], in_=V["xc"]), reads=[Bf["xc"]], writes=[Bf["xb"]])
                yield
                S.add("pe", lambda e, j=j, V=V, PGAv=PGAv: e.matmul(PGAv[:, :], lhsT=MT_bf[:, j, :], rhs=V["xb"], start=True, stop=True), reads=[MT_bf, Bf["xb"]], writes=[PGA])
                S.add("pe", lambda e, j=j, V=V, PGXv=PGXv: e.matmul(PGXv[:, :], lhsT=MT_bf[:, 8 + j, :], rhs=V["xb"], start=True, stop=True), reads=[MT_bf, Bf["xb"]], writes=[PGX])
                ca, cx = PC("lba") + j, PC("lbx") + j
                pGd = nextPA()
                projA(wg_, jj, hn, pGd)
                yield
                S.add("act", lambda e, ca=ca, V=V, PGAv=PGAv: e.activation(out=V["r"], in_=PGAv[:, :], func=AF.Sigmoid, bias=pfm[:, ca:ca + 1]), reads=[PGA, pfm], pwrites=[Bf["r"]])
                S.add("act", lambda e, cx=cx, V=V, PGXv=PGXv: e.activation(out=V["i"], in_=PGXv[:, :], func=AF.Sigmoid, bias=pfm[:, cx:cx + 1]), reads=[PGX, pfm], pwrites=[Bf["i"]])
                S.add("act", lambda e, pGd=pGd, V=V: e.activation(out=V["g"], in_=pGd[:, :], func=AF.Sigmoid), reads=[pGd], pwrites=[Bf["g"]])
                S.add("dve", lambda e, pGd=pGd, V=V: e.tensor_tensor(out=V["g"], in0=pGd[:, :], in1=V["g"], op=ALU.mult), reads=[pGd, Bf["g"]], pwrites=[Bf["g"]])
                yield
                S.add("act", lambda e, j=j, V=V: e.activation(out=V["a"], in_=V["r"], func=AF.Exp, scale=sp8[:, j:j + 1]), reads=[Bf["r"], sp8], pwrites=[Bf["a"]])
                S.add("act", lambda e, j=j, V=V: e.activation(out=V["m"], in_=V["r"], func=AF.Exp, scale=sp16[:, j:j + 1]), reads=[Bf["r"], sp16], pwrites=[Bf["m"]])
                S.add("act", lambda e, V=V: e.activation(out=V["m"], in_=V["m"], func=AF.Ln, scale=-1.0, bias=1.0), reads=[Bf["m"]], pwrites=[Bf["m"]])
                S.add("act", lambda e, V=V: e.activation(out=V["m"], in_=V["m"], func=AF.Exp, scale=0.5), reads=[Bf["m"]], pwrites=[Bf["m"]])
                yield
                S.add("dve", lambda e, V=V: e.tensor_tensor(out=V["b"], in0=V["i"], in1=V["xc"], op=ALU.mult), reads=[Bf["i"], Bf["xc"]], pwrites=[Bf["b"]])
                S.add("dve", lambda e, V=V: e.tensor_tensor(out=V["b"], in0=V["b"], in1=V["m"], op=ALU.mult), reads=[Bf["b"], Bf["m"]], pwrites=[Bf["b"]])
                S.add("dve", lambda e, j=j, V=V: e.tensor_tensor_scan(out=V["h"], data0=V["a"], data1=V["b"], initial=hcarry[:, j:j + 1], op0=ALU.mult, op1=ALU.add),
                      reads=[Bf["a"], Bf["b"], hcarry], pwrites=[Bf["h"]])
                S.add("dve", lambda e, j=j, V=V: e.tensor_copy(out=hcarry[:, j:j + 1], in_=V["h"][:, 511:512]), reads=[Bf["h"]], pwrites=[hcarry])
                S.add("dve", lambda e, j=j, V=V: e.tensor_tensor(out=mix[:, 8 + j, :], in0=V["h"], in1=V["g"], op=ALU.mult), reads=[Bf["h"], Bf["g"]], pwrites=[mix])

            lru_w = {}

            def lru_weights(half):
                if half not in lru_w:
                    wx_ = wload(w_in_o[:, 3072 + half * 512:3072 + (half + 1) * 512], 8, 512)
                    samp_cols(wx_, 3072 + half * 512, 512)
                    wg_ = wload(w_in_o[:, 4096 + half * 512:4096 + (half + 1) * 512], 8, 512)
                    samp_cols(wg_, 4096 + half * 512, 512)
                    lru_w[half] = (wx_, wg_)
                return lru_w[half]

            active, nextj = [], 0
            while active or nextj < 8:
                if len(active) < 2 and nextj < 8:
                    wx_, wg_ = lru_weights(nextj // 4)
                    active.append(lru_chunk(nextj, nextj % 4, wx_, wg_))
                    nextj += 1
                for g_ in list(active):
                    if next(g_, "END") == "END":
                        active.remove(g_)
            S.add("dve", lambda e: e.memset(lbar[:, :], 0.0), writes=[bigZ, lbar] + lz)
            if ti + 1 < NT:
                l1_load(ti + 1)
                norm_sq(l1_pieces, l1_pbufs, l1_sq, vhat)
            for ob in range(4):
                if ob == 2 and ti + 1 < NT:
                    norm_rest(l1_pieces, l1_pbufs, l1_sq, vhat, PC("no"))
                wv = wload(w_out_o[:, ob * 256:(ob + 1) * 256], 16, 256)
                for dj2 in range(2):
                    dj = ob * 2 + dj2
                    pb = nextPA()
                    for ek in range(16):
                        S.add("pe", lambda e, ek=ek, dj2=dj2, wv=wv, pb=pb: e.matmul(pb[:, :], lhsT=wv[1][:, ek, dj2 * 128:(dj2 + 1) * 128],
                                                                                      rhs=mix[:, ek, :], start=(ek == 0), stop=(ek == 15)),
                              reads=[wv[0], mix], writes=[pb])
                    S.add("dve", lambda e, dj=dj, pb=pb: e.tensor_tensor(out=bigA[:, dj, :], in0=pb[:, :], in1=bigA[:, dj, :], op=ALU.add),
                          reads=[pb, bigA], pwrites=[bigA])
            if stage <= 1.6:
                return
            rmsnorm_fm(bigA, PC("nf"), bigZ, dst_view=cvo)
            S.dma("sp", yT[:, t0:t0 + TT].rearrange("(k p) t -> p k t", p=128), cvo, reads=[bigZ], pwrites=[outbuf], group=bigZ)

        if stage > 1:
            for ti in range(int(_os.environ.get('K_L1T', NT))):
                samp_tab[0] = TAB_L1 if (stage >= 3 and ti == NT - 1) else None
                layer1_tile(ti)
                samp_tab[0] = None
            S.dma("sp", o_ccv_p, glu_last[:, :, :].rearrange("p j k -> p (j k)"), reads=[glu_last], pwrites=[outbuf], group=glu_last)
            S.dma("sp", o_lconv_p, lastraw1[:, :, :].rearrange("p j k -> p (j k)"), reads=[lastraw1], pwrites=[outbuf], group=lastraw1)
            S.dma("sp", o_lru_p, hcarry[:, :], reads=[hcarry], pwrites=[outbuf], group=hcarry)
        if stage >= 3:
            sample_layer1()
        final_reads = [outbuf, bigA]
        S.add("sp", lambda e: e.nop(), reads=final_reads, writes=final_reads)
        S.finalize_and_emit(es)
    return nc


def _consts():
    idf = np.eye(128, dtype=np.float32)
    s = np.arange(128)
    negm = np.where(s[None, :] >= s[:, None], 0.0, -1.0e5).astype(np.float32)
    triu = (s[:, None] <= s[None, :]).astype(np.float32)
    sel16 = np.zeros((16, 16, 128), np.float32)
    for h in range(16):
        sel16[h, h, :] = 1.0
    sellast = np.zeros((128, 128), np.float32)
    sellast[127, :] = 1.0
    return dict(c_idf=idf, c_negm=negm, c_triu=triu, c_sel16=sel16.reshape(16, 2048), c_sellast=sellast)


def _prepare(inp):
    f = lambda a: np.ascontiguousarray(np.asarray(a, np.float32))
    shared = dict(
        w_in_e=f(inp["w_in_even"][0]), w_out_e=f(inp["w_out_even"][0]),
        w_in_o=f(inp["w_in_odd"][0]), w_out_o=f(inp["w_out_odd"][0]),
        pfm=_build_pfm(inp),
        p16=f(np.stack([inp["ssd_dt_bias"][0], inp["ssd_a_log"][0]], 1)),
        drow=f(inp["ssd_d"][0][None, :]),
        wsT=f(np.transpose(inp["gmlp_w_s"][0], (2, 0, 1)).reshape(128, 1024)),
        bsrow=f(inp["gmlp_b_s"][0].reshape(1, 1024)),
    )
    def _bd(w):
        w = np.asarray(w, np.float32)
        o = np.zeros((128, 8, 128), np.float32)
        for j in range(8):
            o[0:64, j, 0:64] = w[2 * j]
            o[64:128, j, 64:128] = w[2 * j + 1]
        return np.ascontiguousarray(o.reshape(128, 1024))
    shared["bda"] = _bd(inp["lru_wa"][0])
    shared["bdx"] = _bd(inp["lru_wx"][0])
    cexp = np.zeros((16, 8, 128), np.float32)
    for j in range(8):
        cexp[2 * j, j, 0:64] = 1.0
        cexp[2 * j + 1, j, 64:128] = 1.0
    shared["c_exp"] = cexp.reshape(16, 1024)
    shared.update(_consts())
    maps = []
    for c in range(NCORES):
        m = dict(shared)
        m["xT"] = f(inp["x_prompt"][c].T)
        sl = slice(c * NB, (c + 1) * NB)
        m["xsT"] = f(inp["x_sample"][sl, 0, :].T)
        m["st_ssm"] = f(inp["state_ssm"][0, sl].reshape(NB, 1024, 128))
        m["st_sconv"] = f(inp["state_ssd_conv"][0, sl])
        m["st_ccv"] = f(inp["state_ccv"][0, sl])
        m["st_lconv"] = f(inp["state_lru_conv"][0, sl])
        m["st_lru"] = f(inp["state_lru"][0, sl])
        maps.append(m)
    return maps


_NC_CACHE = {}


def _run(inp, stage=99):
    maps = _prepare(inp)
    if stage not in _NC_CACHE:
        _NC_CACHE[stage] = build_program(stage)
    nc = _NC_CACHE[stage]
    res = run_bass_kernel_spmd(nc, maps, core_ids=list(range(NCORES)))
    return res.results


def _fm2tm(a, nch):
    return np.ascontiguousarray(a.reshape(128, nch, NB).transpose(2, 1, 0).reshape(NB, nch * 128))


def _fmlast(a, nch, k, keep):
    return np.ascontiguousarray(a.reshape(128, nch, k)[:, :, k - keep:].transpose(2, 1, 0).reshape(keep, nch * 128))


def kernel(**inputs):
    res = _run(inputs, 99)
    B = NCORES
    y_p = np.stack([np.ascontiguousarray(r["yT"].T) for r in res])
    y_s = np.concatenate([_fm2tm(r["o_y_s"], 8) for r in res])[:, None, :]
    ssm_p = np.stack([r["o_ssm_p"].reshape(16, 64, 128) for r in res])[None]
    ssm_s = np.concatenate([r["o_ssm_s"].reshape(NB, 16, 64, 128) for r in res])[None]
    sconv_p = np.stack([_fmlast(r["o_sconv_p"], 12, 4, 3) for r in res])[None]
    sconv_s = np.concatenate([np.concatenate([r["o_sconv_s_hist"], _fm2tm(r["o_sconv_s_new"], 12)[:, None, :]], 1) for r in res])[None]
    gv_s = np.concatenate([_fm2tm(r["o_gv_s"], 8) for r in res])[None, :, None, :]
    ccv_p = np.stack([_fmlast(r["o_ccv_p"], 8, 30, 30) for r in res])[None]
    ccv_s = np.concatenate([np.concatenate([r["o_ccv_s_hist"], _fm2tm(r["o_ccv_s_new"], 8)[:, None, :]], 1) for r in res])[None]
    lconv_p = np.stack([_fmlast(r["o_lconv_p"], 8, 4, 3) for r in res])[None]
    lconv_s = np.concatenate([np.concatenate([r["o_lconv_s_hist"], _fm2tm(r["o_lconv_s_new"], 8)[:, None, :]], 1) for r in res])[None]
    lru_p = np.stack([np.ascontiguousarray(r["o_lru_p"].T).reshape(1024) for r in res])[None]
    lru_s = np.concatenate([_fm2tm(r["o_lru_s"], 8) for r in res])[None]
    outs = (y_p, y_s, ssm_p, ssm_s, sconv_p, sconv_s, gv_s, ccv_p, ccv_s, lconv_p, lconv_s, lru_p, lru_s)
    return tuple(np.ascontiguousarray(o.astype(np.float32)) for o in outs)
```

```python
import numpy as np
import ml_dtypes
import concourse.bass as bass
import concourse.mybir as mybir
from concourse.bass_utils import run_bass_kernel_spmd
from contextlib import ExitStack

F32 = mybir.dt.float32
BF16 = mybir.dt.bfloat16
AF = mybir.ActivationFunctionType
ALU = mybir.AluOpType
AX = mybir.AxisListType
COMPUTE = ("pe", "act", "dve", "pool")
EPS = 1e-6
NCORES = 8
SEQ = 2048
TT = 512
NT = SEQ // TT
NB = 16


class Buf:
    _n = 0

    def __init__(self, t, name=None):
        self.t = t
        self.name = name or f"buf{Buf._n}"
        Buf._n += 1
        self.writers = []
        self.readers = []
        self.base = []
        self.dma_ops = []
        self.sem = None

    def __getitem__(self, idx):
        return self.t[idx]


class Op:
    __slots__ = ("eng", "fn", "reads", "writes", "pwrites", "is_dma", "group", "gidx",
                 "eidx", "deps", "waits", "signal", "vc", "gpos")


def _is_pw(w, b):
    return any(x is b for x in w.pwrites)


class Sched:
    def __init__(self, nc):
        self.nc = nc
        self.ops = []
        self.by_eng = {e: [] for e in ("pe", "act", "dve", "pool", "sp")}

    def add(self, eng, fn, reads=(), writes=(), pwrites=(), dma=False, group=None):
        op = Op()
        op.eng, op.fn = eng, fn
        op.reads, op.writes, op.pwrites = list(reads), list(writes), list(pwrites)
        op.is_dma, op.group = dma, group
        op.gpos = len(self.ops)
        op.deps, op.waits, op.signal, op.vc = [], [], False, None
        deps = []
        for b in op.reads:
            deps.extend(b.writers)
        for b in op.writes:
            deps.extend(b.writers)
            deps.extend(b.readers)
        for b in op.pwrites:
            if b.readers:
                b.base = list(b.readers) + list(b.writers)
                b.writers = []
                b.readers = []
            deps.extend(b.base)
            deps.extend(w for w in b.writers if not _is_pw(w, b))
        for b in op.reads:
            b.readers.append(op)
        for b in op.writes:
            b.writers = [op]
            b.readers = []
            b.base = []
        for b in op.pwrites:
            b.writers.append(op)
        seen = set()
        for d in deps:
            if d is op or id(d) in seen:
                continue
            seen.add(id(d))
            if d.eng == "pe" and eng == "pe" and not d.is_dma and not dma:
                continue
            op.deps.append(d)
        if dma:
            op.gidx = len(group.dma_ops)
            group.dma_ops.append(op)
        op.eidx = len(self.by_eng[eng])
        self.by_eng[eng].append(op)
        self.ops.append(op)
        return op

    def dma(self, eng, out_ap, in_ap, reads=(), writes=(), pwrites=(), group=None, **kw):
        return self.add(eng, lambda e: e.dma_start(out=out_ap, in_=in_ap, **kw),
                        reads=reads, writes=writes, pwrites=pwrites, dma=True, group=group)

    def finalize_and_emit(self, es):
        nc = self.nc
        know = {e: {} for e in self.by_eng}
        for op in self.ops:
            k = know[op.eng]
            for d in op.deps:
                if d.is_dma:
                    key = ("g", id(d.group))
                    val = sum(1 for x in d.group.dma_ops if x.gpos < op.gpos)
                    tok = (key, val, d.group)
                else:
                    key = d.eng
                    val = d.eidx + 1
                    tok = (key, val, None)
                if k.get(key, 0) >= val:
                    continue
                op.waits.append(tok)
                k[key] = val
                if d.vc is not None:
                    for kk, vv in d.vc.items():
                        if k.get(kk, 0) < vv:
                            k[kk] = vv
                if not d.is_dma:
                    d.signal = True
            best = {}
            for tok in op.waits:
                if tok[0] not in best or best[tok[0]][1] < tok[1]:
                    best[tok[0]] = tok
            op.waits = list(best.values())
            op.vc = dict(k)
            if not op.is_dma:
                op.vc[op.eng] = op.eidx + 1
        ordmap = {}
        for e in COMPUTE:
            c, m = 0, {}
            for op in self.by_eng[e]:
                if op.signal:
                    c += 1
                m[op.eidx + 1] = c
            ordmap[e] = m
        esem = {e: es.enter_context(nc.semaphore(f"s_{e}")) for e in COMPUTE}
        groups = {}
        for op in self.ops:
            if op.is_dma and id(op.group) not in groups:
                groups[id(op.group)] = op.group
        for g in groups.values():
            g.sem = es.enter_context(nc.semaphore(f"g_{g.name}"))
        self.n_sems = 4 + len(groups)

        def emit_stream(ename, engine):
            for op in self.by_eng[ename]:
                for (key, val, grp) in op.waits:
                    if grp is not None:
                        engine.wait_ge(grp.sem, 16 * val)
                    else:
                        engine.wait_ge(esem[key], ordmap[key][val])
                ins = op.fn(engine)
                if op.is_dma:
                    ins.then_inc(op.group.sem, 16)
                elif op.signal:
                    ins.then_inc(esem[ename], 1)

        block = es.enter_context(nc.Block())

        @block.tensor
        def _(eng):
            emit_stream("pe", eng)

        @block.scalar
        def _(eng):
            emit_stream("act", eng)

        @block.vector
        def _(eng):
            emit_stream("dve", eng)

        @block.gpsimd
        def _(eng):
            emit_stream("pool", eng)

        @block.sync
        def _(eng):
            emit_stream("sp", eng)


def _fm(v):
    v = np.asarray(v, np.float32).reshape(-1)
    return np.ascontiguousarray(v.reshape(-1, 128).T)


PCOLS = {}


def _build_pfm(inp):
    cols, off = [], 0

    def put(name, arr):
        nonlocal off
        PCOLS[name] = off
        cols.append(arr)
        off += arr.shape[1]

    put("ne", _fm(inp["norm_even"][0]))
    put("no", _fm(inp["norm_odd"][0]))
    put("nf", _fm(inp["final_norm"]))
    put("scb", _fm(inp["ssd_conv_b"][0]))
    put("scw", np.concatenate([_fm(inp["ssd_conv_w"][0][k]) for k in range(4)], 1))
    put("gn", _fm(inp["ssd_norm"][0]))
    put("Dfm", _fm(np.repeat(inp["ssd_d"][0], 64)))
    put("lng", _fm(inp["gmlp_ln_g"][0]))
    put("lnb", _fm(inp["gmlp_ln_b"][0]))
    put("ccb", _fm(inp["ccv_b"][0]))
    put("cclg", _fm(inp["ccv_ln_g"][0]))
    put("cclb", _fm(inp["ccv_ln_b"][0]))
    put("ccw", np.concatenate([_fm(inp["ccv_w"][0][k]) for k in range(31)], 1))
    put("lcw", np.concatenate([_fm(inp["lru_conv_w"][0][k]) for k in range(4)], 1))
    put("lcb", _fm(inp["lru_conv_b"][0]))
    put("lba", _fm(inp["lru_ba"][0]))
    put("lbx", _fm(inp["lru_bx"][0]))
    put("lam", _fm(inp["lru_lambda"][0]))
    put("w00", _fm(np.repeat(inp["gmlp_w_s"][0][:, 0, 0], 128)))
    put("b0", _fm(np.repeat(inp["gmlp_b_s"][0][:, 0], 128)))
    return np.ascontiguousarray(np.concatenate(cols, 1))


NPCOL = 468
E_EVEN = 5648
E_ODD = 5120


def build_program(stage=99):
    import os as _os
    nc = bass.Bass("TRN2", target_bir_lowering=False)

    def din(name, shape, dt=F32):
        return nc.dram_tensor(name, list(shape), dt, kind="ExternalInput").ap()

    def dout(name, shape, dt=F32):
        return nc.dram_tensor(name, list(shape), dt, kind="ExternalOutput").ap()

    xT = din("xT", [1024, SEQ])
    w_in_e = din("w_in_e", [1024, E_EVEN])
    w_out_e = din("w_out_e", [2048, 1024])
    w_in_o = din("w_in_o", [1024, E_ODD])
    w_out_o = din("w_out_o", [2048, 1024])
    d_pfm = din("pfm", [128, NPCOL])
    d_p16 = din("p16", [16, 2])
    d_drow = din("drow", [1, 16])
    d_wsT = din("wsT", [128, 1024])
    d_bsrow = din("bsrow", [1, 1024])
    d_idf = din("c_idf", [128, 128])
    d_negm = din("c_negm", [128, 128])
    d_triu = din("c_triu", [128, 128])
    d_sel16 = din("c_sel16", [16, 2048])
    d_sellast = din("c_sellast", [128, 128])

    d_xsT = din("xsT", [1024, NB])
    st_ssm = din("st_ssm", [NB, 1024, 128])
    st_sconv = din("st_sconv", [NB, 3, 1536])
    st_ccv = din("st_ccv", [NB, 30, 1024])
    st_lconv = din("st_lconv", [NB, 3, 1024])
    st_lru = din("st_lru", [NB, 1024])
    d_exp = din("c_exp", [16, 1024])
    o_y_s = dout("o_y_s", [128, 8 * NB])
    o_ssm_s = dout("o_ssm_s", [NB, 1024, 128])
    o_sconv_s_new = dout("o_sconv_s_new", [128, 12 * NB])
    o_sconv_s_hist = dout("o_sconv_s_hist", [NB, 2, 1536])
    o_gv_s = dout("o_gv_s", [128, 8 * NB])
    o_ccv_s_new = dout("o_ccv_s_new", [128, 8 * NB])
    o_ccv_s_hist = dout("o_ccv_s_hist", [NB, 29, 1024])
    o_lconv_s_new = dout("o_lconv_s_new", [128, 8 * NB])
    o_lconv_s_hist = dout("o_lconv_s_hist", [NB, 2, 1024])
    o_lru_s = dout("o_lru_s", [128, 8 * NB])
    d_bda = din("bda", [128, 1024])
    d_bdx = din("bdx", [128, 1024])
    yT = dout("yT", [1024, SEQ])
    o_ccv_p = dout("o_ccv_p", [128, 240])
    o_lconv_p = dout("o_lconv_p", [128, 32])
    o_lru_p = dout("o_lru_p", [128, 8])
    x1T = nc.dram_tensor("x1T", [1024, SEQ], F32, kind="Internal").ap()
    o_ssm_p = dout("o_ssm_p", [1024, 128])
    o_sconv_p = dout("o_sconv_p", [128, 48])

    es = ExitStack()
    with es:
        S = Sched(nc)
        x1buf = Buf(None, "x1buf")
        outbuf = Buf(None, "outs")

        def sb(name, shape, dt=F32):
            return Buf(es.enter_context(nc.sbuf_tensor(name, list(shape), dt)), name)

        def psum(name, shape, dt=F32):
            return Buf(es.enter_context(nc.psum_tensor(name, list(shape), dt)), name)

        def PC(c):
            return PCOLS[c]

        pfm = sb("pfm_sb", [128, NPCOL])
        cIDF = sb("cIDF", [128, 128])
        cIDB = sb("cIDB", [128, 128], BF16)
        cNEGM = sb("cNEGM", [128, 128])
        cSELLAST = sb("cSELLAST", [128, 128])
        cSEL16 = sb("cSEL16", [16, 2048])
        cONESB = sb("cONESB", [128, 128], BF16)
        cONES16 = sb("cONES16", [16, 128])
        p16 = sb("p16_sb", [16, 2])
        Dbc = sb("Dbc", [128, 16])
        WmT = sb("WmT", [128, 8, 128], BF16)
        Rg = sb("Rg", [128, 8, 128])
        ea16 = sb("ea16", [16, 1])

        S.dma("sp", pfm[:, :], d_pfm, writes=[pfm], group=pfm)
        S.dma("sp", cIDF[:, :], d_idf, writes=[cIDF], group=cIDF)
        S.dma("pool", cIDB[:, :], d_idf, writes=[cIDB], group=cIDB)
        S.dma("sp", cNEGM[:, :], d_negm, writes=[cNEGM], group=cNEGM)
        S.dma("sp", cSELLAST[:, :], d_sellast, writes=[cSELLAST], group=cSELLAST)
        S.dma("sp", cSEL16[:, :], d_sel16, writes=[cSEL16], group=cSEL16)
        S.dma("sp", p16[:, :], d_p16, writes=[p16], group=p16)
        S.dma("sp", Dbc[:, :], d_drow[0, :].partition_broadcast(128), writes=[Dbc], group=Dbc)
        S.add("dve", lambda e: e.memset(cONESB[:, :], 1.0), writes=[cONESB])
        S.add("dve", lambda e: e.memset(cONES16[:, :], 1.0), writes=[cONES16])
        S.add("act", lambda e: e.activation(out=ea16[:, :], in_=p16[:, 1:2], func=AF.Exp), reads=[p16], writes=[ea16])

        P0 = psum("P0", [128, 512])
        P1 = psum("P1", [128, 512])
        P2 = psum("P2", [128, 512])
        P3 = psum("P3", [128, 512])
        P4 = psum("P4", [128, 512])
        P5 = psum("P5", [128, 512])
        P6 = psum("P6", [128, 512])
        P7t = es.enter_context(nc.psum_tensor("P7", [128, 512], F32))
        P7 = Buf(P7t, "P7")
        P7a = P7t[:, 0:144]
        P7b = P7t[:, 144:400]
        P7c = P7t[:, 400:416]
        PA = [P0, P1]
        pa_i = [0]

        def set_pa(banks):
            PA[:] = banks

        def nextPA():
            b = PA[pa_i[0] % len(PA)]
            pa_i[0] += 1
            return b

        tmpw = sb("tmpw2", [128, 1024])
        ctriu = sb("ctriu", [128, 128])
        bsbc = sb("bsbc2", [128, 1024])
        S.dma("sp", tmpw[:, :], d_wsT, writes=[tmpw], group=tmpw)
        S.dma("sp", ctriu[:, :], d_triu, writes=[ctriu], group=ctriu)
        S.dma("sp", bsbc[:, :], d_bsrow[0, :].partition_broadcast(128), writes=[bsbc], group=bsbc)
        S.add("dve", lambda e: e.tensor_tensor(
            out=WmT[:, :, :], in0=tmpw[:, :].rearrange("p (g t) -> p g t", g=8),
            in1=ctriu[:, :].unsqueeze(1).broadcast_to([128, 8, 128]), op=ALU.mult),
            reads=[tmpw, ctriu], writes=[WmT])
        for hf in range(2):
            S.add("pe", lambda e, hf=hf: e.matmul(P3[:, :] if hf == 0 else P4[:, :], lhsT=cONESB[:, :],
                                                    rhs=WmT[:, 4 * hf:4 * hf + 4, :].rearrange("p g t -> p (g t)"),
                                                    start=True, stop=True),
                  reads=[cONESB, WmT], writes=[P3 if hf == 0 else P4])
        for g in range(8):
            pb = P3 if g < 4 else P4
            S.add("dve", lambda e, g=g, pb=pb: e.scalar_tensor_tensor(
                out=Rg[:, g, :], in0=pb[:, (g % 4) * 128:(g % 4 + 1) * 128], scalar=pfm[:, PC("lnb") + g:PC("lnb") + g + 1],
                in1=bsbc[:, g * 128:(g + 1) * 128], op0=ALU.mult, op1=ALU.add),
                reads=[pb, pfm, bsbc], pwrites=[Rg])

        NW = 3
        wbufs = [sb(f"wbuf{i}", [128, 4096], BF16) for i in range(NW)]
        w_i = [0]

        def wload(dram_ap, kk, ww):
            b = wbufs[w_i[0] % NW]
            w_i[0] += 1
            view = b.t[:, 0:kk * ww].rearrange("p (k e) -> p k e", k=kk)
            S.dma("pool", view, dram_ap.rearrange("(k p) e -> p k e", p=128), writes=[b], group=b)
            return b, view

        NDG = 8
        dgbufs = [sb(f"dg{i}", [128, 128], BF16) for i in range(NDG)]
        dg_i = [0]

        def diag(col, eng="act"):
            b = dgbufs[dg_i[0] % NDG]
            dg_i[0] += 1
            if eng == "act":
                S.add("act", lambda e: e.activation(out=b[:, :], in_=cIDB[:, :], func=AF.Copy, scale=pfm[:, col:col + 1]),
                      reads=[cIDB, pfm], writes=[b])
            else:
                S.add("dve", lambda e: e.tensor_scalar(out=b[:, :], in0=cIDB[:, :], scalar1=pfm[:, col:col + 1], scalar2=None, op0=ALU.mult),
                      reads=[cIDB, pfm], writes=[b])
            return b

        bigA = sb("bigA", [128, 8, 512])
        mix = sb("mix", [128, 16, 512], BF16)
        hn = sb("hn", [128, 8, 512], BF16)
        rstd = sb("rstd", [128, 512])
        raws = [sb(f"raw{i}", [128, 515], BF16) for i in range(2)]
        hist0 = sb("hist0", [128, 12, 3], BF16)
        lastraw = sb("lastraw", [128, 12, 4])
        xcf = [sb(f"xcf{i}", [128, 512]) for i in range(2)]
        xh_tm = sb("xh_tm", [128, 4, 1024])
        B_tm = sb("B_tm", [128, 4, 256], BF16)
        BT_bf = sb("BT_bf", [128, 2, 512], BF16)
        CT_bf = sb("CT_bf", [128, 2, 512], BF16)
        bigZ = sb("bigZ", [128, 4, 1024])
        vhat = sb("vhat", [128, 4, 1024], BF16)
        sg = [sb(f"sg{i}", [128, 512]) for i in range(2)]
        dtT = sb("dtT", [16, 512])
        daT = sb("daT", [16, 512])
        csT = sb("csT", [16, 512])
        tmpT = sb("tmpT", [16, 512])
        PK1 = sb("PK1", [128, 512])
        PK2 = sb("PK2", [16, 512])
        tmq = sb("tmq", [128, 144])
        xs_bf = sb("xs_bf", [128, 1024], BF16)
        xsd_bf = sb("xsd_bf", [128, 1024], BF16)
        Lbuf = sb("Lbuf", [128, 8, 128])
        MT_bf = sb("MT_bf", [128, 16, 128], BF16)
        yoff = sb("yoff", [128, 1024])
        ysb = sb("ysb", [128, 1024])
        ya_bf = sb("ya_bf", [128, 1024], BF16)
        Hst = sb("Hst", [128, 1024])
        Hbf = sb("Hbf", [128, 1024], BF16)
        ect = sb("ect", [128, 16])
        ssq = sb("ssq", [128, 2])
        rs2 = sb("rs2", [128, 2])
        bnst = sb("bnst", [128, 4, 2, 6])
        mv = sb("mv", [128, 4, 2])
        rv = sb("rv", [128, 4])

        S.add("dve", lambda e: e.memset(hist0[:, :, :], 0.0), writes=[hist0])
        S.add("dve", lambda e: e.memset(PK1[:, :], 0.0), writes=[PK1])
        S.add("dve", lambda e: e.memset(Hst[:, :], 0.0), writes=[Hst])
        S.add("dve", lambda e: e.memset(Hbf[:, :], 0.0), writes=[Hbf])

        def rmsnorm_fm(src, gcol, dst_bf, dst_view=None):
            S.add("act", lambda e: e.activation(out=mix[:, 0:8, :], in_=src[:, :, :], func=AF.Square), reads=[src], writes=[mix])
            pb = nextPA()
            for k in range(8):
                S.add("pe", lambda e, k=k: e.matmul(pb[:, :], lhsT=cONESB[:, :], rhs=mix[:, k, :], start=(k == 0), stop=(k == 7)),
                      reads=[cONESB, mix], writes=[pb])
            S.add("act", lambda e: e.activation(out=rstd[:, :], in_=pb[:, :], func=AF.Ln, scale=1.0 / 1024.0, bias=EPS),
                  reads=[pb], writes=[rstd])
            S.add("act", lambda e: e.activation(out=rstd[:, :], in_=rstd[:, :], func=AF.Exp, scale=-0.5), reads=[rstd], writes=[rstd])
            for k in range(8):
                S.add("dve", lambda e, k=k: e.scalar_tensor_tensor(
                    out=(dst_bf[:, k, :] if dst_view is None else dst_view[:, k, :]), in0=src[:, k, :], scalar=pfm[:, gcol + k:gcol + k + 1], in1=rstd[:, :],
                    op0=ALU.mult, op1=ALU.mult), reads=[src, pfm, rstd], pwrites=[dst_bf])

        def projA(wv, j, rhs_buf, pb):
            for k in range(8):
                S.add("pe", lambda e, k=k: e.matmul(pb[:, :], lhsT=wv[1][:, k, j * 128:(j + 1) * 128], rhs=rhs_buf[:, k, :],
                                                     start=(k == 0), stop=(k == 7)),
                      reads=[wv[0], rhs_buf], writes=[pb])


        def norm_sq(pieces, pbufs, sq_view, sq_buf):
            for k in range(8):
                S.add("act", lambda e, k=k: e.activation(out=sq_view[:, k, :], in_=pieces[k], func=AF.Square), reads=[pbufs[k]], pwrites=[sq_buf])

        def norm_rest(pieces, pbufs, sq_view, sq_buf, gcol):
            pb = nextPA()
            for k in range(8):
                S.add("pe", lambda e, k=k: e.matmul(pb[:, :], lhsT=cONESB[:, :], rhs=sq_view[:, k, :], start=(k == 0), stop=(k == 7)),
                      reads=[cONESB, sq_buf], writes=[pb])
            S.add("act", lambda e: e.activation(out=rstd[:, :], in_=pb[:, :], func=AF.Ln, scale=1.0 / 1024.0, bias=EPS), reads=[pb], writes=[rstd])
            S.add("act", lambda e: e.activation(out=rstd[:, :], in_=rstd[:, :], func=AF.Exp, scale=-0.5), reads=[rstd], writes=[rstd])
            for k in range(8):
                S.add("dve", lambda e, k=k: e.scalar_tensor_tensor(out=hn[:, k, :], in0=pieces[k], scalar=pfm[:, gcol + k:gcol + k + 1], in1=rstd[:, :],
                                                                     op0=ALU.mult, op1=ALU.mult), reads=[pbufs[k], pfm, rstd], pwrites=[hn])

        l0_pieces = [bigZ.t[:, :, :].rearrange("p a (c t) -> p (a c) t", t=512)[:, k, :] for k in range(8)]
        l0_pbufs = [bigZ] * 8
        l0_sq = xh_tm.t[:, :, :].rearrange("p a b -> p (a b)").bitcast(BF16)[:, 0:4096].rearrange("p (k t) -> p k t", k=8)

        def l0_load(ti):
            S.dma("sp", bigZ.t[:, :, :].rearrange("p a (c t) -> p (a c) t", t=512), xT[:, ti * TT:(ti + 1) * TT].rearrange("(k p) t -> p k t", p=128),
                  writes=[bigZ], group=bigZ)

        def layer0_tile(ti):
            t0 = ti * TT
            last = (ti == NT - 1) and stage >= 1
            set_pa([P0, P1, P4, P5])
            if ti == 0:
                l0_load(0)
                norm_sq(l0_pieces, l0_pbufs, l0_sq, xh_tm)
                norm_rest(l0_pieces, l0_pbufs, l0_sq, xh_tm, PC("ne"))
            if stage <= 0.1:
                return
            wdt = wload(w_in_e[:, 2560:2576], 8, 16)
            samp_dt(wdt)
            pb = nextPA()
            for k in range(8):
                S.add("pe", lambda e, k=k: e.matmul(pb[0:16, :], lhsT=wdt[1][:, k, :], rhs=hn[:, k, :], start=(k == 0), stop=(k == 7)),
                      reads=[wdt[0], hn], writes=[pb])
            S.add("act", lambda e: e.activation(out=tmpT[:, :], in_=pb[0:16, :], func=AF.Exp, bias=p16[:, 0:1]),
                  reads=[pb, p16], writes=[tmpT])
            S.add("act", lambda e: e.activation(out=dtT[:, :], in_=tmpT[:, :], func=AF.Ln, bias=1.0), reads=[tmpT], writes=[dtT])
            S.add("dve", lambda e: e.tensor_scalar(out=daT[:, :], in0=dtT[:, :], scalar1=ea16[:, 0:1], scalar2=-1.0,
                                                    op0=ALU.mult, op1=ALU.mult), reads=[dtT, ea16], writes=[daT])
            for c in range(4):
                S.add("dve", lambda e, c=c: e.tensor_tensor_scan(out=csT[:, c * 128:(c + 1) * 128], data0=cONES16[:, :],
                                                                  data1=daT[:, c * 128:(c + 1) * 128], initial=0.0,
                                                                  op0=ALU.mult, op1=ALU.add),
                      reads=[cONES16, daT], pwrites=[csT])
            S.add("act", lambda e: e.activation(out=PK1[0:16, :], in_=dtT[:, :], func=AF.Copy), reads=[dtT], pwrites=[PK1])
            S.add("act", lambda e: e.activation(out=PK1[32:48, :], in_=csT[:, :], func=AF.Copy), reads=[csT], pwrites=[PK1])
            for c in range(4):
                S.add("act", lambda e, c=c: e.activation(out=tmpT[:, c * 128:(c + 1) * 128], in_=csT[:, c * 128:(c + 1) * 128],
                                                          func=AF.Exp, scale=-1.0, bias=csT[:, c * 128 + 127:c * 128 + 128]),
                      reads=[csT], pwrites=[tmpT])
            S.add("dve", lambda e: e.tensor_tensor(out=PK1[64:80, :], in0=tmpT[:, :], in1=dtT[:, :], op=ALU.mult),
                  reads=[tmpT, dtT], pwrites=[PK1])
            S.add("act", lambda e: e.activation(out=PK2[:, :], in_=csT[:, :], func=AF.Exp), reads=[csT], writes=[PK2])

            if stage <= 0.2:
                return
            for blk3 in range(3):
                wv = wload(w_in_e[:, 1024 + blk3 * 512:1024 + (blk3 + 1) * 512], 8, 512)
                samp_cols(wv, 1024 + blk3 * 512, 512)
                for jj in range(4):
                    j = blk3 * 4 + jj
                    pb = nextPA()
                    projA(wv, jj, hn, pb)
                    raw = raws[j % 2]
                    PC2 = P2 if j % 2 == 0 else P6
                    PT3 = P3 if j % 2 == 0 else P7
                    PT3v = P3.t if j % 2 == 0 else P7t
                    S.add("dve", lambda e, j=j, raw=raw: e.tensor_copy(out=raw[:, 0:3], in_=hist0[:, j, :]), reads=[hist0], pwrites=[raw])
                    S.add("act", lambda e, raw=raw, pb=pb: e.activation(out=raw[:, 3:515], in_=pb[:, :], func=AF.Copy),
                          reads=[pb], pwrites=[raw])
                    if last:
                        S.add("act", lambda e, j=j, pb=pb: e.activation(out=lastraw[:, j, :], in_=pb[:, 508:512], func=AF.Copy), reads=[pb], pwrites=[lastraw])
                    S.add("dve", lambda e, j=j, raw=raw: e.tensor_copy(out=hist0[:, j, :], in_=raw[:, 512:515]), reads=[raw], pwrites=[hist0])
                    if stage <= 0.21:
                        continue
                    for k in range(4):
                        dg = diag(PC("scw") + k * 12 + j)
                        S.add("pe", lambda e, k=k, dg=dg, raw=raw, PC2=PC2: e.matmul(PC2[:, :], lhsT=dg[:, :], rhs=raw[:, k:k + 512],
                                                                             start=(k == 0), stop=(k == 3)),
                              reads=[dg, raw], writes=[PC2])
                    if stage <= 0.22:
                        continue
                    bcol = PC("scb") + j
                    if j < 10:
                        xc = xcf[j % 2]
                        S.add("act", lambda e, xc=xc, bcol=bcol, PC2=PC2: e.activation(out=xc[:, :], in_=PC2[:, :], func=AF.Silu,
                                                                                 bias=pfm[:, bcol:bcol + 1]),
                              reads=[PC2, pfm], writes=[xc])
                        if stage <= 0.23:
                            continue
                        for b4 in range(4):
                            S.add("pe", lambda e, b4=b4, xc=xc, PT3v=PT3v: e.transpose(PT3v[:, b4 * 128:(b4 + 1) * 128], xc[:, b4 * 128:(b4 + 1) * 128], cIDF[:, :]),
                                  reads=[xc, cIDF], writes=[PT3])
                        if stage <= 0.24:
                            continue
                        if j < 8:
                            S.add("dve", lambda e, j=j, PT3v=PT3v: e.tensor_copy(out=xh_tm[:, :, j * 128:(j + 1) * 128],
                                                                       in_=PT3v[:, :].rearrange("p (b c) -> p b c", b=4)),
                                  reads=[PT3], pwrites=[xh_tm])
                        else:
                            jb = j - 8
                            S.add("dve", lambda e, jb=jb, PT3v=PT3v: e.tensor_copy(out=B_tm[:, :, jb * 128:(jb + 1) * 128],
                                                                         in_=PT3v[:, :].rearrange("p (b c) -> p b c", b=4)),
                                  reads=[PT3], pwrites=[B_tm])
                            if stage <= 0.25:
                                continue
                            S.add("dve", lambda e, jb=jb, xc=xc: e.tensor_copy(out=BT_bf[:, jb, :], in_=xc[:, :]), reads=[xc], pwrites=[BT_bf])
                    else:
                        jc = j - 10
                        S.add("act", lambda e, jc=jc, bcol=bcol, PC2=PC2: e.activation(out=CT_bf[:, jc, :], in_=PC2[:, :], func=AF.Silu,
                                                                                 bias=pfm[:, bcol:bcol + 1]),
                              reads=[PC2, pfm], pwrites=[CT_bf])

            if stage <= 0.3:
                return
            for half in range(2):
                wv = wload(w_in_e[:, half * 512:(half + 1) * 512], 8, 512)
                samp_cols(wv, half * 512, 512)
                for b4 in range(4):
                    pb = nextPA()
                    for k in range(8):
                        S.add("pe", lambda e, k=k, b4=b4, wv=wv, pb=pb: e.matmul(pb[:, :], lhsT=hn[:, k, b4 * 128:(b4 + 1) * 128],
                                                                                  rhs=wv[1][:, k, :], start=(k == 0), stop=(k == 7)),
                              reads=[wv[0], hn], writes=[pb])
                    S.add("act", lambda e, b4=b4, half=half, pb=pb: e.activation(out=bigZ[:, b4, half * 512:(half + 1) * 512],
                                                                                  in_=pb[:, :], func=AF.Silu),
                          reads=[pb], pwrites=[bigZ])

            if stage <= 0.4:
                return
            set_pa([P0, P1])
            def gen_ssd():
                for c in range(4):
                    cs_ = slice(c * 128, (c + 1) * 128)
                    S.add("pe", lambda e, cs_=cs_: e.transpose(P7a[:, 0:128], PK1[:, cs_], cIDF[:, :]), reads=[PK1, cIDF], pwrites=[P7])
                    S.add("pe", lambda e, cs_=cs_: e.transpose(P7a[:, 128:144], PK2[:, cs_], cIDF[0:16, 0:16]), reads=[PK2, cIDF], pwrites=[P7])
                    S.add("dve", lambda e: e.tensor_copy(out=tmq[:, :], in_=P7a[:, :]), reads=[P7], writes=[tmq])
                    dt_tm = tmq[:, 0:16]
                    dd_tm = tmq[:, 64:80]
                    ecs_tm = tmq[:, 128:144]

                    def bc16(ap):
                        return ap.unsqueeze(2).broadcast_to([128, 16, 64])

                    xh3 = xh_tm[:, c, :].rearrange("p (h q) -> p h q", h=16)
                    S.add("dve", lambda e, xh3=xh3, dt_tm=dt_tm: e.tensor_tensor(out=xs_bf[:, :].rearrange("p (h q) -> p h q", h=16), in0=xh3,
                                                                                  in1=bc16(dt_tm), op=ALU.mult),
                          reads=[xh_tm, tmq], writes=[xs_bf])
                    S.add("dve", lambda e, xh3=xh3, dd_tm=dd_tm: e.tensor_tensor(out=xsd_bf[:, :].rearrange("p (h q) -> p h q", h=16), in0=xh3,
                                                                                   in1=bc16(dd_tm), op=ALU.mult),
                          reads=[xh_tm, tmq], writes=[xsd_bf])
                    S.add("dve", lambda e, xh3=xh3: e.tensor_tensor(out=ysb[:, :].rearrange("p (h q) -> p h q", h=16), in0=xh3,
                                                                     in1=bc16(Dbc[:, :]), op=ALU.mult),
                          reads=[xh_tm, Dbc], writes=[ysb])
                    yield
                    for g in range(2):
                        S.add("pe", lambda e, g=g, cs_=cs_: e.matmul(P7b[:, g * 128:(g + 1) * 128], lhsT=BT_bf[:, g, cs_], rhs=CT_bf[:, g, cs_],
                                                                      start=True, stop=True),
                              reads=[BT_bf, CT_bf], pwrites=[P7])
                    for g in range(2):
                        pg = P3 if g == 0 else P4
                        S.add("pe", lambda e, g=g, pg=pg, cs_=cs_: e.matmul(pg[:, :], lhsT=CT_bf[:, g, cs_], rhs=Hbf[:, g * 512:(g + 1) * 512],
                                                                             start=True, stop=True),
                              reads=[CT_bf, Hbf], writes=[pg])
                        S.add("dve", lambda e, g=g, pg=pg, ecs_tm=ecs_tm: e.tensor_tensor(
                            out=yoff[:, g * 512:(g + 1) * 512].rearrange("p (h q) -> p h q", h=8),
                            in0=pg[:, :].rearrange("p (h q) -> p h q", h=8),
                            in1=ecs_tm[:, g * 8:(g + 1) * 8].unsqueeze(2).broadcast_to([128, 8, 64]), op=ALU.mult),
                            reads=[pg, tmq], pwrites=[yoff])
                    S.add("dve", lambda e: e.tensor_tensor(out=yoff[:, :], in0=yoff[:, :], in1=ysb[:, :], op=ALU.add), reads=[yoff, ysb], writes=[yoff])
                    yield
                    for q in range(4):
                        pc = P5 if q % 2 == 0 else P6
                        for h4 in range(4):
                            h = q * 4 + h4
                            S.add("pe", lambda e, h=h, h4=h4, pc=pc, cs_=cs_: e.matmul(pc[:, h4 * 128:(h4 + 1) * 128], lhsT=cSEL16[:, h * 128:(h + 1) * 128],
                                                                                        rhs=csT[:, cs_], start=True, stop=True),
                                  reads=[cSEL16, csT], pwrites=[pc])
                        for h4 in range(4):
                            h = q * 4 + h4
                            S.add("dve", lambda e, h=h, h4=h4, pc=pc, q=q: e.scalar_tensor_tensor(
                                out=Lbuf[:, (q % 2) * 4 + h4, :], in0=pc[:, h4 * 128:(h4 + 1) * 128], scalar=tmq[:, 32 + h:33 + h],
                                in1=cNEGM[:, :], op0=ALU.subtract, op1=ALU.add),
                                reads=[pc, tmq, cNEGM], pwrites=[Lbuf])
                        if q % 2 == 1:
                            g = q // 2
                            S.add("act", lambda e: e.activation(out=Lbuf[:, :, :], in_=Lbuf[:, :, :], func=AF.Exp), reads=[Lbuf], writes=[Lbuf])
                            S.add("dve", lambda e, g=g: e.tensor_tensor(
                                out=MT_bf[:, g * 8:(g + 1) * 8, :], in0=Lbuf[:, :, :],
                                in1=P7b[:, g * 128:(g + 1) * 128].unsqueeze(1).broadcast_to([128, 8, 128]), op=ALU.mult),
                                reads=[Lbuf, P7], pwrites=[MT_bf])
                        yield
                    for h in range(16):
                        pg = P3 if h < 8 else P4
                        S.add("pe", lambda e, h=h, pg=pg: e.matmul(pg[:, (h % 8) * 64:(h % 8 + 1) * 64], lhsT=MT_bf[:, h, :],
                                                                     rhs=xs_bf[:, h * 64:(h + 1) * 64], start=True, stop=True),
                              reads=[MT_bf, xs_bf], pwrites=[pg])
                    yield
                    for g in range(2):
                        pg = P3 if g == 0 else P4
                        S.add("dve", lambda e, g=g, pg=pg: e.tensor_tensor(out=ysb[:, g * 512:(g + 1) * 512], in0=pg[:, :],
                                                                            in1=yoff[:, g * 512:(g + 1) * 512], op=ALU.add),
                              reads=[pg, yoff], pwrites=[ysb])
                    S.add("dve", lambda e, c=c: e.tensor_tensor(out=ysb[:, :], in0=ysb[:, :], in1=bigZ[:, c, :], op=ALU.mult),
                          reads=[ysb, bigZ], writes=[ysb])
                    for g in range(2):
                        S.add("act", lambda e, g=g: e.activation(out=yoff[:, g * 512:(g + 1) * 512], in_=ysb[:, g * 512:(g + 1) * 512],
                                                                  func=AF.Square, accum_out=ssq[:, g:g + 1]),
                              reads=[ysb], pwrites=[yoff, ssq])
                    S.add("act", lambda e: e.activation(out=rs2[:, :], in_=ssq[:, :], func=AF.Ln, scale=1.0 / 512.0, bias=EPS), reads=[ssq], writes=[rs2])
                    S.add("act", lambda e: e.activation(out=rs2[:, :], in_=rs2[:, :], func=AF.Exp, scale=-0.5), reads=[rs2], writes=[rs2])
                    for g in range(2):
                        S.add("act", lambda e, g=g: e.activation(out=ya_bf[:, g * 512:(g + 1) * 512], in_=ysb[:, g * 512:(g + 1) * 512],
                                                                  func=AF.Copy, scale=rs2[:, g:g + 1]),
                              reads=[ysb, rs2], pwrites=[ya_bf])
                    for j in range(8):
                        S.add("pe", lambda e, j=j: e.transpose(P2[:, j * 64:(j + 1) * 64].bitcast(BF16), ya_bf[:, j * 128:(j + 1) * 128], cIDB[:, :]),
                              reads=[ya_bf, cIDB], pwrites=[P2])
                    for j in range(8):
                        S.add("act", lambda e, j=j, cs_=cs_: e.activation(out=mix[:, j, cs_], in_=P2[:, j * 64:(j + 1) * 64].bitcast(BF16),
                                                                           func=AF.Copy, scale=pfm[:, PC("gn") + j:PC("gn") + j + 1]),
                              reads=[P2, pfm], pwrites=[mix])
                    yield
                    S.add("pe", lambda e: e.matmul(P7c[:, :], lhsT=cSELLAST[:, :], rhs=tmq[:, 32:48], start=True, stop=True),
                          reads=[cSELLAST, tmq], pwrites=[P7])
                    S.add("dve", lambda e: e.tensor_copy(out=ect[:, :], in_=P7c[:, :]), reads=[P7], writes=[ect])
                    S.add("act", lambda e: e.activation(out=ect[:, :], in_=ect[:, :], func=AF.Exp), reads=[ect], writes=[ect])
                    S.add("dve", lambda e: e.tensor_tensor(out=Hst[:, :].rearrange("p (h q) -> p h q", h=16),
                                                             in0=Hst[:, :].rearrange("p (h q) -> p h q", h=16), in1=bc16(ect[:, :]), op=ALU.mult),
                          reads=[Hst, ect], writes=[Hst])
                    for g in range(2):
                        pg = P3 if g == 0 else P4
                        S.add("pe", lambda e, g=g, pg=pg, c=c: e.matmul(pg[:, :], lhsT=B_tm[:, c, g * 128:(g + 1) * 128], rhs=xsd_bf[:, g * 512:(g + 1) * 512],
                                                                         start=True, stop=True),
                              reads=[B_tm, xsd_bf], writes=[pg])
                        S.add("dve", lambda e, g=g, pg=pg: e.tensor_tensor(out=Hst[:, g * 512:(g + 1) * 512], in0=pg[:, :],
                                                                            in1=Hst[:, g * 512:(g + 1) * 512], op=ALU.add),
                              reads=[pg, Hst], pwrites=[Hst])
                    S.add("act", lambda e: e.activation(out=Hbf[:, :], in_=Hst[:, :], func=AF.Copy), reads=[Hst], writes=[Hbf])
                    yield

            def gen_ug():
                for half in range(2):
                    wv = wload(w_in_e[:, 2576 + half * 512:2576 + (half + 1) * 512], 8, 512)
                    samp_cols(wv, 2576 + half * 512, 512)
                    for jj in range(4):
                        j = half * 4 + jj
                        pb = nextPA()
                        projA(wv, jj, hn, pb)
                        S.add("act", lambda e, j=j, pb=pb: e.activation(out=bigA[:, j, :], in_=pb[:, :], func=AF.Copy),
                              reads=[pb], pwrites=[bigA])
                        yield

            def gen_g():
                S.add("act", lambda e: e.activation(out=bigA[:, :, :], in_=bigA[:, :, :], func=AF.Gelu_apprx_tanh), reads=[bigA], writes=[bigA])
                for half in range(2):
                    wv = wload(w_in_e[:, 4624 + half * 512:4624 + (half + 1) * 512], 8, 512)
                    samp_cols(wv, 4624 + half * 512, 512)
                    for jj in range(4):
                        j = half * 4 + jj
                        pb = nextPA()
                        projA(wv, jj, hn, pb)
                        sgb = sg[j % 2]
                        S.add("act", lambda e, sgb=sgb, pb=pb: e.activation(out=sgb[:, :], in_=pb[:, :], func=AF.Silu), reads=[pb], writes=[sgb])
                        S.add("dve", lambda e, j=j, sgb=sgb: e.tensor_tensor(out=bigA[:, j, :], in0=bigA[:, j, :], in1=sgb[:, :], op=ALU.mult),
                              reads=[bigA, sgb], pwrites=[bigA])
                        yield
            g1, g2 = gen_ssd(), gen_ug()
            n1 = 0
            done2 = False
            for _ in g1:
                n1 += 1
                if n1 % 4 == 0 and not done2:
                    if next(g2, "END") == "END":
                        done2 = True
            if not done2:
                for _ in g2:
                    pass
            for _ in gen_g():
                pass
            set_pa([P0, P1, P3, P4, P5, P6])
            for half in range(2):
                wv = wload(w_in_e[:, 3600 + half * 512:3600 + (half + 1) * 512], 8, 512)
                samp_cols(wv, 3600 + half * 512, 512)
                for b4 in range(4):
                    pb = nextPA()
                    for k in range(8):
                        S.add("pe", lambda e, k=k, b4=b4, wv=wv, pb=pb: e.matmul(pb[:, :], lhsT=hn[:, k, b4 * 128:(b4 + 1) * 128],
                                                                                  rhs=wv[1][:, k, :], start=(k == 0), stop=(k == 7)),
                              reads=[wv[0], hn], writes=[pb])
                    S.add("act", lambda e, b4=b4, half=half, pb=pb: e.activation(out=bigZ[:, b4, half * 512:(half + 1) * 512],
                                                                                  in_=pb[:, :], func=AF.Gelu_apprx_tanh),
                          reads=[pb], pwrites=[bigZ])
            for b4 in range(4):
                for half in range(2):
                    S.add("dve", lambda e, b4=b4, half=half: e.bn_stats(out=bnst[:, b4, half, :], in_=bigZ[:, b4, half * 512:(half + 1) * 512]),
                          reads=[bigZ], pwrites=[bnst])
                S.add("dve", lambda e, b4=b4: e.bn_aggr(out=mv[:, b4, :], in_=bnst[:, b4, :, :].rearrange("p a b -> p (a b)")),
                      reads=[bnst], pwrites=[mv])
            S.add("act", lambda e: e.activation(out=rv[:, :], in_=mv[:, :, 1], func=AF.Ln, bias=EPS), reads=[mv], writes=[rv])
            S.add("act", lambda e: e.activation(out=rv[:, :], in_=rv[:, :], func=AF.Exp, scale=-0.5), reads=[rv], writes=[rv])
            for b4 in range(4):
                S.add("dve", lambda e, b4=b4: e.tensor_scalar(out=vhat[:, b4, :], in0=bigZ[:, b4, :], scalar1=mv[:, b4, 0:1], scalar2=rv[:, b4:b4 + 1],
                                                               op0=ALU.subtract, op1=ALU.mult),
                      reads=[bigZ, mv, rv], pwrites=[vhat])
            if stage <= 0.7:
                return
            for b4 in range(4):
                bs_ = slice(b4 * 128, (b4 + 1) * 128)
                for gh in range(2):
                    pb = nextPA()
                    for g4 in range(4):
                        g = gh * 4 + g4
                        S.add("pe", lambda e, g=g, g4=g4, b4=b4, pb=pb: e.matmul(pb[:, g4 * 128:(g4 + 1) * 128], lhsT=vhat[:, b4, g * 128:(g + 1) * 128],
                                                                                  rhs=WmT[:, g, :], start=True, stop=True),
                              reads=[vhat, WmT], pwrites=[pb])
                    if stage <= 0.71:
                        continue
                    sgb = sg[gh]
                    for g4 in range(4):
                        g = gh * 4 + g4
                        S.add("dve", lambda e, g=g, g4=g4, pb=pb, sgb=sgb: e.scalar_tensor_tensor(
                            out=sgb[:, g4 * 128:(g4 + 1) * 128], in0=pb[:, g4 * 128:(g4 + 1) * 128],
                            scalar=pfm[:, PC("lng") + g:PC("lng") + g + 1], in1=Rg[:, g, :], op0=ALU.mult, op1=ALU.add),
                            reads=[pb, pfm, Rg], pwrites=[sgb])
                    if stage <= 0.72:
                        continue
                    S.add("dve", lambda e, gh=gh, sgb=sgb, bs_=bs_: e.tensor_tensor(
                        out=mix[:, 8 + gh * 4:8 + gh * 4 + 4, bs_], in0=sgb[:, :].rearrange("p (g t) -> p g t", g=4),
                        in1=bigA[:, gh * 4:gh * 4 + 4, bs_], op=ALU.mult),
                        reads=[sgb, bigA], pwrites=[mix])
            if stage <= 0.8:
                return
            S.dma("sp", bigA[:, :, :], xT[:, t0:t0 + TT].rearrange("(k p) t -> p k t", p=128), writes=[bigA], group=bigA)
            if ti + 1 < NT:
                l0_load(ti + 1)
                norm_sq(l0_pieces, l0_pbufs, l0_sq, xh_tm)
            for ob in range(4):
                if ob == 2 and ti + 1 < NT:
                    norm_rest(l0_pieces, l0_pbufs, l0_sq, xh_tm, PC("ne"))
                wv = wload(w_out_e[:, ob * 256:(ob + 1) * 256], 16, 256)
                for dj2 in range(2):
                    dj = ob * 2 + dj2
                    pb = nextPA()
                    for ek in range(16):
                        S.add("pe", lambda e, ek=ek, dj2=dj2, wv=wv, pb=pb: e.matmul(pb[:, :], lhsT=wv[1][:, ek, dj2 * 128:(dj2 + 1) * 128],
                                                                                      rhs=mix[:, ek, :], start=(ek == 0), stop=(ek == 15)),
                              reads=[wv[0], mix], writes=[pb])
                    S.add("dve", lambda e, dj=dj, pb=pb: e.tensor_tensor(out=bigA[:, dj, :], in0=pb[:, :], in1=bigA[:, dj, :], op=ALU.add),
                          reads=[pb, bigA], pwrites=[bigA])
            S.dma("sp", x1T[:, t0:t0 + TT].rearrange("(k p) t -> p k t", p=128), bigA[:, :, :], reads=[bigA], pwrites=[x1buf], group=bigA)

        l1_barrier_reads = []
        SA = bigA.t[:, :, :].rearrange("p a b -> p (a b)")
        SM = mix.t[:, :, :].rearrange("p a b -> p (a b)").bitcast(F32)
        SZ = bigZ.t[:, :, :].rearrange("p a b -> p (a b)")
        SX = xh_tm.t[:, :, :].rearrange("p a b -> p (a b)")
        projS = tmpw.t[:, 0:45 * NB].rearrange("p (c b) -> p c b", b=NB)
        hallS = bsbc.t[:, 0:4 * 12 * NB].rearrange("p (k j b) -> p k j b", k=4, j=12)
        xsT_s = sb("xsT_s", [128, 8, NB])
        sqS = sb("sqS", [128, 8, NB], BF16)
        rstdS = sb("rstdS", [128, NB])
        hnS = sb("hnS", [128, 8, NB], BF16)
        convS = sb("convS", [128, 12, NB])
        xcS = sb("xcS", [128, 12, NB])
        dtS = sb("dtS", [16, 3, NB])
        dtE = sb("dtE", [128, 2, 8, NB])
        xsS = sb("xsS", [128, 8, NB])
        vhat32 = vhat.t[:, :, :].rearrange("p a b -> p (a b)").bitcast(F32)
        BC_tm = vhat32[0:16, 1024:1536]
        yS = sb("yS", [128, 8, NB])
        t1S = sb("t1S", [128, 8, NB])
        t2S = sb("t2S", [128, 8, NB])
        stS = sb("stS", [128, 4, NB])
        mixS = sb("mixS", [128, 16, NB], BF16)
        cEXP = vhat32[0:16, 0:1024]
        S.dma("sp", xsT_s[:, :, :], d_xsT.rearrange("(k p) b -> p k b", p=128), writes=[xsT_s], group=xsT_s)

        def bcb(ap2):
            return ap2.unsqueeze(2).broadcast_to([128, ap2.shape[1], NB])

        def bcj(ap2, n):
            return ap2.unsqueeze(1).broadcast_to([128, n, NB])

        def rmsnorm_s(gcol, dst, dst_buf):
            S.add("act", lambda e: e.activation(out=sqS[:, :, :], in_=xsT_s[:, :, :], func=AF.Square), reads=[xsT_s], writes=[sqS])
            pb = nextPA()
            for k in range(8):
                S.add("pe", lambda e, k=k: e.matmul(pb[:, 0:NB], lhsT=cONESB[:, :], rhs=sqS[:, k, :], start=(k == 0), stop=(k == 7)),
                      reads=[cONESB, sqS], writes=[pb])
            S.add("act", lambda e: e.activation(out=rstdS[:, :], in_=pb[:, 0:NB], func=AF.Ln, scale=1.0 / 1024.0, bias=EPS), reads=[pb], writes=[rstdS])
            S.add("act", lambda e: e.activation(out=rstdS[:, :], in_=rstdS[:, :], func=AF.Exp, scale=-0.5), reads=[rstdS], writes=[rstdS])
            S.add("dve", lambda e: e.tensor_tensor(out=t1S[:, :, :], in0=xsT_s[:, :, :], in1=bcj(rstdS[:, :], 8), op=ALU.mult),
                  reads=[xsT_s, rstdS], writes=[t1S])
            S.add("dve", lambda e: e.tensor_tensor(out=dst, in0=t1S[:, :, :], in1=bcb(pfm[:, gcol:gcol + 8]), op=ALU.mult),
                  reads=[t1S, pfm], writes=[dst_buf])

        def proj_s(wsrc, col0, nchunk, c0, func, nrows=128):
            done = 0
            while done < nchunk:
                nb_ = min(4, nchunk - done)
                wv = wload(wsrc[:, col0 + done * 128:col0 + (done + nb_) * 128], 8, nb_ * 128)
                for jj in range(nb_):
                    pb = nextPA()
                    for k in range(8):
                        S.add("pe", lambda e, k=k, jj=jj, wv=wv, pb=pb: e.matmul(pb[:, 0:NB], lhsT=wv[1][:, k, jj * 128:(jj + 1) * 128], rhs=hnS[:, k, :],
                                                                                  start=(k == 0), stop=(k == 7)),
                              reads=[wv[0], hnS], writes=[pb])
                    cc = c0 + done + jj
                    S.add("act", lambda e, cc=cc, pb=pb: e.activation(out=projS[:, cc, :], in_=pb[:, 0:NB], func=func), reads=[pb], pwrites=[tmpw])
                done += nb_

        def outproj_s(wsrc):
            for ob in range(4):
                wv = wload(wsrc[:, ob * 256:(ob + 1) * 256], 16, 256)
                for dj2 in range(2):
                    dj = ob * 2 + dj2
                    pb = nextPA()
                    for ek in range(16):
                        S.add("pe", lambda e, ek=ek, dj2=dj2, wv=wv, pb=pb: e.matmul(pb[:, 0:NB], lhsT=wv[1][:, ek, dj2 * 128:(dj2 + 1) * 128],
                                                                                      rhs=mixS[:, ek, :], start=(ek == 0), stop=(ek == 15)),
                              reads=[wv[0], mixS], writes=[pb])
                    S.add("dve", lambda e, dj=dj, pb=pb: e.tensor_tensor(out=xsT_s[:, dj, :], in0=pb[:, 0:NB], in1=xsT_s[:, dj, :], op=ALU.add),
                          reads=[pb, xsT_s], pwrites=[xsT_s])

        def hist_to_fm(stage_ap, stage_buf, ntap_cols, dst_fn, dst_buf, bulk=None):
            i = 0
            while i < ntap_cols:
                n = min(32, ntap_cols - i)
                pb = nextPA()
                for q in range(n):
                    S.add("pe", lambda e, q=q, i=i, pb=pb: e.transpose(pb[:, q * NB:(q + 1) * NB], stage_ap[0:NB, (i + q) * 128:(i + q + 1) * 128], cIDF[0:NB, 0:NB]),
                          reads=[stage_buf, cIDF], pwrites=[pb])
                if bulk is not None:
                    dst_ap, pat, kw = bulk(i, n)
                    S.add("dve", lambda e, pb=pb, n=n, dst_ap=dst_ap, pat=pat, kw=kw: e.tensor_copy(out=dst_ap, in_=pb[:, 0:n * NB].rearrange(pat, **kw)),
                          reads=[pb], pwrites=[dst_buf])
                else:
                    for q in range(n):
                        S.add("dve", lambda e, q=q, i=i, pb=pb: e.tensor_copy(out=dst_fn(i + q), in_=pb[:, q * NB:(q + 1) * NB]), reads=[pb], pwrites=[dst_buf])
                i += n

        def stats_s(src3, src_buf, nch, inv_n):
            S.add("act", lambda e: e.activation(out=sqS[:, 0:nch, :], in_=src3, func=AF.Square), reads=[src_buf], writes=[sqS])
            S.add("dve", lambda e: e.tensor_copy(out=hnS[:, 0:nch, :], in_=src3), reads=[src_buf], writes=[hnS])
            pb = nextPA()
            for k in range(nch):
                S.add("pe", lambda e, k=k: e.matmul(pb[:, 0:NB], lhsT=cONESB[:, :], rhs=hnS[:, k, :], start=(k == 0), stop=(k == nch - 1)),
                      reads=[cONESB, hnS], writes=[pb])
            pb2 = nextPA()
            for k in range(nch):
                S.add("pe", lambda e, k=k: e.matmul(pb2[:, 0:NB], lhsT=cONESB[:, :], rhs=sqS[:, k, :], start=(k == 0), stop=(k == nch - 1)),
                      reads=[cONESB, sqS], writes=[pb2])
            S.add("dve", lambda e: e.tensor_scalar(out=stS[:, 0, :], in0=pb[:, 0:NB], scalar1=inv_n, scalar2=None, op0=ALU.mult), reads=[pb], pwrites=[stS])
            S.add("dve", lambda e: e.tensor_tensor(out=stS[:, 1, :], in0=stS[:, 0, :], in1=stS[:, 0, :], op=ALU.mult), reads=[stS], pwrites=[stS])
            S.add("dve", lambda e: e.scalar_tensor_tensor(out=stS[:, 1, :], in0=pb2[:, 0:NB], scalar=inv_n, in1=stS[:, 1, :], op0=ALU.mult, op1=ALU.subtract),
                  reads=[pb2, stS], pwrites=[stS])
            S.add("act", lambda e: e.activation(out=stS[:, 2, :], in_=stS[:, 1, :], func=AF.Ln, bias=EPS), reads=[stS], pwrites=[stS])
            S.add("act", lambda e: e.activation(out=stS[:, 2, :], in_=stS[:, 2, :], func=AF.Exp, scale=-0.5), reads=[stS], pwrites=[stS])

        def sample_layer0():
            set_pa([P0, P1])
            S.dma("sp", cEXP, d_exp, pwrites=[vhat], group=vhat)
            S.dma("sp", SA[0:NB, 0:1536], st_sconv[:, 0, :], pwrites=[bigA], group=bigA)
            S.dma("sp", SA[0:NB, 1536:3072], st_sconv[:, 1, :], pwrites=[bigA], group=bigA)
            S.dma("sp", SM[0:NB, 0:1536], st_sconv[:, 2, :], pwrites=[mix], group=mix)
            hist_to_fm(SA, bigA, 24, None, bsbc, bulk=lambda i, n: (hallS[:, 0:2, :, :], "p (k j b) -> p k j b", dict(k=2, j=12)))
            hist_to_fm(SM, mix, 12, None, bsbc, bulk=lambda i, n: (hallS[:, 2, :, :], "p (j b) -> p j b", dict(j=12)))
            S.add("dve", lambda e: e.tensor_copy(out=hallS[:, 3, :, :], in_=projS[:, 8:20, :]), reads=[tmpw], pwrites=[bsbc])
            S.dma("sp", o_sconv_s_new, projS[:, 8:20, :].rearrange("p j b -> p (j b)"), reads=[tmpw], pwrites=[outbuf], group=tmpw)
            S.dma("sp", o_sconv_s_hist, st_sconv[:, 1:3, :], pwrites=[outbuf], group=outbuf)
            wv4 = pfm[:, PC("scw"):PC("scw") + 48].rearrange("p (k j) -> p k j", k=4).unsqueeze(3).broadcast_to([128, 4, 12, NB])
            S.add("dve", lambda e: e.tensor_tensor(out=hallS, in0=hallS, in1=wv4, op=ALU.mult), reads=[bsbc, pfm], writes=[bsbc])
            S.add("dve", lambda e: e.tensor_reduce(out=convS[:, :, :], in_=hallS.rearrange("p k j b -> p j b k"), axis=AX.X, op=ALU.add),
                  reads=[bsbc], writes=[convS])
            for j in range(12):
                bc_ = PC("scb") + j
                S.add("act", lambda e, j=j, bc_=bc_: e.activation(out=xcS[:, j, :], in_=convS[:, j, :], func=AF.Silu, bias=pfm[:, bc_:bc_ + 1]),
                      reads=[convS, pfm], pwrites=[xcS])
            for q in range(2):
                pb = nextPA()
                for j in range(8):
                    S.add("pe", lambda e, q=q, j=j, pb=pb: e.matmul(pb[:, j * NB:(j + 1) * NB], lhsT=cEXP[:, j * 128:(j + 1) * 128], rhs=dtS[:, q, :],
                                                                     start=True, stop=True), reads=[vhat, dtS], pwrites=[pb])
                S.add("dve", lambda e, q=q, pb=pb: e.tensor_copy(out=dtE[:, q, :, :], in_=pb[:, 0:8 * NB].rearrange("p (j b) -> p j b", j=8)),
                      reads=[pb], pwrites=[dtE])
            S.add("dve", lambda e: e.tensor_tensor(out=xsS[:, :, :], in0=xcS[:, 0:8, :], in1=dtE[:, 0, :, :], op=ALU.mult), reads=[xcS, dtE], writes=[xsS])
            pb = nextPA()
            for q in range(4):
                S.add("pe", lambda e, q=q, pb=pb: e.transpose(pb[0:NB, q * 128:(q + 1) * 128], xcS[:, 8 + q, :], cIDF[:, :]), reads=[xcS, cIDF], pwrites=[pb])
            S.add("dve", lambda e, pb=pb: e.tensor_copy(out=BC_tm, in_=pb[0:NB, :]), reads=[pb], pwrites=[vhat])
            hbs = [SX[:, 0:1024], SX[:, 1024:2048], SX[:, 3072:4096]]
            ob = SX[:, 2048:3072].rearrange("p (j n) -> p j n", j=8)
            hbB = [Buf(hbs[0], "hb0"), Buf(hbs[1], "hb1"), Buf(hbs[2], "hb2")]
            obB = Buf(SX[:, 2048:3072], "obS")
            l1_barrier_reads.extend(hbB + [obB])
            S.add("dve", lambda e: e.memset(SX[:, 2048:3072], 0.0), writes=[xh_tm] + hbB + [obB])
            for b in range(NB):
                hb = hbs[b % 3].rearrange("p (j n) -> p j n", j=8)
                hB = hbB[b % 3]
                pbc = P5 if b % 2 == 0 else P6
                S.add("pe", lambda e, b=b, pbc=pbc: e.matmul(pbc[:, :], lhsT=cSEL16[:, b * 128:(b + 1) * 128], rhs=BC_tm, start=True, stop=True),
                      reads=[cSEL16, vhat], writes=[pbc])
                S.dma("sp", hb, st_ssm[b].rearrange("(j p) n -> p j n", p=128), writes=[hB], group=hB)
                for j8 in range(8):
                    S.add("act", lambda e, b=b, hb=hb, j8=j8: e.activation(out=hb[:, j8, :], in_=hb[:, j8, :], func=AF.Copy, scale=dtE[:, 1, j8, b:b + 1]),
                          reads=[dtE], pwrites=[hB])
                for g in range(2):
                    S.add("dve", lambda e, b=b, g=g, pbc=pbc: e.tensor_tensor(
                        out=ob[:, 4 * g:4 * g + 4, :], in0=pbc[:, g * 128:(g + 1) * 128].unsqueeze(1).broadcast_to([128, 4, 128]),
                        in1=xsS[:, 4 * g:4 * g + 4, b:b + 1].broadcast_to([128, 4, 128]), op=ALU.mult),
                        reads=[pbc, xsS], pwrites=[obB])
                S.add("dve", lambda e, hb=hb: e.tensor_tensor(out=hb, in0=hb, in1=ob, op=ALU.add), reads=[hB, obB], writes=[hB])
                S.dma("act", o_ssm_s[b].rearrange("(j p) n -> p j n", p=128), hb, reads=[hB], pwrites=[outbuf], group=hB)
                for g in range(2):
                    S.add("dve", lambda e, g=g, pbc=pbc, hb=hb: e.tensor_tensor(
                        out=ob[:, 4 * g:4 * g + 4, :], in0=pbc[:, 256 + g * 128:256 + (g + 1) * 128].unsqueeze(1).broadcast_to([128, 4, 128]),
                        in1=hb[:, 4 * g:4 * g + 4, :], op=ALU.mult), reads=[pbc, hB], pwrites=[obB])
                S.add("dve", lambda e, b=b: e.tensor_reduce(out=yS[:, :, b], in_=ob, axis=AX.X, op=ALU.add), reads=[obB], pwrites=[yS])
            S.add("dve", lambda e: e.tensor_tensor(out=t1S[:, :, :], in0=xcS[:, 0:8, :], in1=bcb(pfm[:, PC("Dfm"):PC("Dfm") + 8]), op=ALU.mult),
                  reads=[xcS, pfm], writes=[t1S])
            S.add("dve", lambda e: e.tensor_tensor(out=yS[:, :, :], in0=yS[:, :, :], in1=t1S[:, :, :], op=ALU.add), reads=[yS, t1S], writes=[yS])
            S.add("dve", lambda e: e.tensor_tensor(out=yS[:, :, :], in0=yS[:, :, :], in1=projS[:, 0:8, :], op=ALU.mult), reads=[yS, tmpw], writes=[yS])
            S.add("act", lambda e: e.activation(out=sqS[:, :, :], in_=yS[:, :, :], func=AF.Square), reads=[yS], writes=[sqS])
            pb = nextPA()
            for g in range(2):
                for k in range(4):
                    S.add("pe", lambda e, g=g, k=k, pb=pb: e.matmul(pb[:, g * NB:(g + 1) * NB], lhsT=cONESB[:, :], rhs=sqS[:, 4 * g + k, :],
                                                                     start=(k == 0), stop=(k == 3)), reads=[cONESB, sqS], pwrites=[pb])
            S.add("act", lambda e, pb=pb: e.activation(out=stS[:, 0:2, :], in_=pb[:, 0:2 * NB].rearrange("p (g b) -> p g b", g=2), func=AF.Ln,
                                                        scale=1.0 / 512.0, bias=EPS), reads=[pb], pwrites=[stS])
            S.add("act", lambda e: e.activation(out=stS[:, 0:2, :], in_=stS[:, 0:2, :], func=AF.Exp, scale=-0.5), reads=[stS], pwrites=[stS])
            for g in range(2):
                S.add("dve", lambda e, g=g: e.tensor_tensor(out=t1S[:, 4 * g:4 * g + 4, :], in0=yS[:, 4 * g:4 * g + 4, :], in1=bcj(stS[:, g, :], 4), op=ALU.mult),
                      reads=[yS, stS], pwrites=[t1S])
            S.add("dve", lambda e: e.tensor_tensor(out=mixS[:, 0:8, :], in0=t1S[:, :, :], in1=bcb(pfm[:, PC("gn"):PC("gn") + 8]), op=ALU.mult),
                  reads=[t1S, pfm], pwrites=[mixS])
            stats_s(projS[:, 29:37, :], tmpw, 8, 1.0 / 1024.0)
            S.add("dve", lambda e: e.tensor_tensor(out=t1S[:, :, :], in0=projS[:, 29:37, :], in1=bcj(stS[:, 0, :], 8), op=ALU.subtract), reads=[tmpw, stS], writes=[t1S])
            S.add("dve", lambda e: e.tensor_tensor(out=t1S[:, :, :], in0=t1S[:, :, :], in1=bcj(stS[:, 2, :], 8), op=ALU.mult), reads=[t1S, stS], writes=[t1S])
            S.add("dve", lambda e: e.tensor_tensor(out=t1S[:, :, :], in0=t1S[:, :, :], in1=bcb(pfm[:, PC("lng"):PC("lng") + 8]), op=ALU.mult), reads=[t1S, pfm], writes=[t1S])
            S.add("dve", lambda e: e.tensor_tensor(out=t2S[:, :, :], in0=t1S[:, :, :], in1=bcb(pfm[:, PC("lnb"):PC("lnb") + 8]), op=ALU.add), reads=[t1S, pfm], writes=[t2S])
            S.dma("sp", o_gv_s, t2S[:, :, :].rearrange("p j b -> p (j b)"), reads=[t2S], pwrites=[outbuf], group=t2S)
            S.add("dve", lambda e: e.tensor_tensor(out=t1S[:, :, :], in0=t2S[:, :, :], in1=bcb(pfm[:, PC("w00"):PC("w00") + 8]), op=ALU.mult), reads=[t2S, pfm], writes=[t1S])
            S.add("dve", lambda e: e.tensor_tensor(out=t1S[:, :, :], in0=t1S[:, :, :], in1=bcb(pfm[:, PC("b0"):PC("b0") + 8]), op=ALU.add), reads=[t1S, pfm], writes=[t1S])
            S.add("dve", lambda e: e.tensor_tensor(out=t1S[:, :, :], in0=t1S[:, :, :], in1=projS[:, 21:29, :], op=ALU.mult), reads=[t1S, tmpw], writes=[t1S])
            S.add("dve", lambda e: e.tensor_tensor(out=mixS[:, 8:16, :], in0=t1S[:, :, :], in1=projS[:, 37:45, :], op=ALU.mult), reads=[t1S, tmpw], pwrites=[mixS])
            outproj_s(w_out_e)
            rmsnorm_s(PC("no"), hnS[:, :, :], hnS)

        def sample_layer1():
            set_pa([P0, P1])
            hall31 = SZ[:, 0:31 * 8 * NB].rearrange("p (k j b) -> p k j b", k=31, j=8)
            hall4 = hallS[:, :, 0:8, :]
            S.add("dve", lambda e: e.tensor_tensor(out=hall31[:, 30, :, :], in0=projS[:, 0:8, :], in1=projS[:, 8:16, :], op=ALU.mult), reads=[tmpw], pwrites=[bigZ])
            S.dma("sp", o_ccv_s_new, hall31[:, 30, :, :].rearrange("p j b -> p (j b)"), reads=[bigZ], pwrites=[outbuf], group=bigZ)
            S.dma("sp", o_ccv_s_hist, st_ccv[:, 1:30, :], pwrites=[outbuf], group=outbuf)
            for gi in range(8):
                k0 = gi * 4
                nk = min(4, 30 - k0)
                stg_ap, stg_buf = (SA, bigA) if gi % 2 == 0 else (SM, mix)
                S.dma("sp", stg_ap[0:NB, 0:nk * 1024].rearrange("b (k c) -> b k c", k=nk), st_ccv[:, k0:k0 + nk, :], pwrites=[stg_buf], group=stg_buf)
                hist_to_fm(stg_ap, stg_buf, nk * 8, None, bigZ, bulk=lambda i, n, k0=k0, nk=nk: (hall31[:, k0:k0 + nk, :, :], "p (k j b) -> p k j b", dict(k=nk, j=8)))
            wv31 = pfm[:, PC("ccw"):PC("ccw") + 248].rearrange("p (k j) -> p k j", k=31).unsqueeze(3).broadcast_to([128, 31, 8, NB])
            S.add("dve", lambda e: e.tensor_tensor(out=hall31, in0=hall31, in1=wv31, op=ALU.mult), reads=[bigZ, pfm], writes=[bigZ])
            S.add("dve", lambda e: e.tensor_reduce(out=convS[:, 0:8, :], in_=hall31.rearrange("p k j b -> p j b k"), axis=AX.X, op=ALU.add),
                  reads=[bigZ], writes=[convS])
            S.add("dve", lambda e: e.tensor_tensor(out=convS[:, 0:8, :], in0=convS[:, 0:8, :], in1=bcb(pfm[:, PC("ccb"):PC("ccb") + 8]), op=ALU.add),
                  reads=[convS, pfm], writes=[convS])
            stats_s(convS[:, 0:8, :], convS, 8, 1.0 / 1024.0)
            S.add("dve", lambda e: e.tensor_tensor(out=t1S[:, :, :], in0=convS[:, 0:8, :], in1=bcj(stS[:, 0, :], 8), op=ALU.subtract), reads=[convS, stS], writes=[t1S])
            S.add("dve", lambda e: e.tensor_tensor(out=t1S[:, :, :], in0=t1S[:, :, :], in1=bcj(stS[:, 2, :], 8), op=ALU.mult), reads=[t1S, stS], writes=[t1S])
            S.add("dve", lambda e: e.tensor_tensor(out=t1S[:, :, :], in0=t1S[:, :, :], in1=bcb(pfm[:, PC("cclg"):PC("cclg") + 8]), op=ALU.mult), reads=[t1S, pfm], writes=[t1S])
            S.add("dve", lambda e: e.tensor_tensor(out=t1S[:, :, :], in0=t1S[:, :, :], in1=bcb(pfm[:, PC("cclb"):PC("cclb") + 8]), op=ALU.add), reads=[t1S, pfm], writes=[t1S])
            S.add("act", lambda e: e.activation(out=t1S[:, :, :], in_=t1S[:, :, :], func=AF.Silu), reads=[t1S], writes=[t1S])
            S.add("dve", lambda e: e.tensor_tensor(out=mixS[:, 0:8, :], in0=t1S[:, :, :], in1=projS[:, 16:24, :], op=ALU.mult), reads=[t1S, tmpw], pwrites=[mixS])
            S.dma("sp", SA[0:NB, 0:3072].rearrange("b (k c) -> b k c", k=3), st_lconv[:, :, :], pwrites=[bigA], group=bigA)
            hist_to_fm(SA, bigA, 24, None, bsbc, bulk=lambda i, n: (hall4[:, 0:3, :, :], "p (k j b) -> p k j b", dict(k=3, j=8)))
            S.add("dve", lambda e: e.tensor_copy(out=hall4[:, 3, :, :], in_=projS[:, 24:32, :]), reads=[tmpw], pwrites=[bsbc])
            S.dma("sp", o_lconv_s_new, projS[:, 24:32, :].rearrange("p j b -> p (j b)"), reads=[tmpw], pwrites=[outbuf], group=tmpw)
            S.dma("sp", o_lconv_s_hist, st_lconv[:, 1:3, :], pwrites=[outbuf], group=outbuf)
            wl4 = pfm[:, PC("lcw"):PC("lcw") + 32].rearrange("p (k j) -> p k j", k=4).unsqueeze(3).broadcast_to([128, 4, 8, NB])
            S.add("dve", lambda e: e.tensor_tensor(out=hall4, in0=hall4, in1=wl4, op=ALU.mult), reads=[bsbc, pfm], writes=[bsbc])
            S.add("dve", lambda e: e.tensor_reduce(out=xcS[:, 0:8, :], in_=hall4.rearrange("p k j b -> p j b k"), axis=AX.X, op=ALU.add), reads=[bsbc], writes=[xcS])
            S.add("dve", lambda e: e.tensor_tensor(out=xcS[:, 0:8, :], in0=xcS[:, 0:8, :], in1=bcb(pfm[:, PC("lcb"):PC("lcb") + 8]), op=ALU.add), reads=[xcS, pfm], writes=[xcS])
            S.add("dve", lambda e: e.tensor_copy(out=hnS[:, :, :], in_=xcS[:, 0:8, :]), reads=[xcS], writes=[hnS])
            for q, (boff, bcol) in enumerate(((0, "lba"), (8, "lbx"))):
                pb = nextPA()
                for j in range(8):
                    S.add("pe", lambda e, j=j, boff=boff, pb=pb: e.matmul(pb[:, j * NB:(j + 1) * NB], lhsT=MT_bf[:, boff + j, :], rhs=hnS[:, j, :], start=True, stop=True),
                          reads=[MT_bf, hnS], pwrites=[pb])
                dst = t1S if q == 0 else t2S
                S.add("dve", lambda e, pb=pb, dst=dst, bcol=bcol: e.tensor_tensor(out=dst[:, :, :], in0=pb[:, 0:8 * NB].rearrange("p (j b) -> p j b", j=8),
                                                                                  in1=bcb(pfm[:, PC(bcol):PC(bcol) + 8]), op=ALU.add), reads=[pb, pfm], writes=[dst])
                S.add("act", lambda e, dst=dst: e.activation(out=dst[:, :, :], in_=dst[:, :, :], func=AF.Sigmoid), reads=[dst], writes=[dst])
            S.add("dve", lambda e: e.tensor_tensor(out=t1S[:, :, :], in0=t1S[:, :, :], in1=bcb(sp8[:, :]), op=ALU.mult), reads=[t1S, sp8], writes=[t1S])
            S.add("act", lambda e: e.activation(out=t1S[:, :, :], in_=t1S[:, :, :], func=AF.Exp), reads=[t1S], writes=[t1S])
            S.add("dve", lambda e: e.tensor_tensor(out=yS[:, :, :], in0=t1S[:, :, :], in1=t1S[:, :, :], op=ALU.mult), reads=[t1S], writes=[yS])
            S.add("act", lambda e: e.activation(out=yS[:, :, :], in_=yS[:, :, :], func=AF.Sqrt, scale=-1.0, bias=1.0), reads=[yS], writes=[yS])
            S.add("dve", lambda e: e.tensor_tensor(out=t2S[:, :, :], in0=t2S[:, :, :], in1=xcS[:, 0:8, :], op=ALU.mult), reads=[t2S, xcS], writes=[t2S])
            S.add("dve", lambda e: e.tensor_tensor(out=t2S[:, :, :], in0=t2S[:, :, :], in1=yS[:, :, :], op=ALU.mult), reads=[t2S, yS], writes=[t2S])
            S.dma("sp", SM[0:NB, 0:1024], st_lru[:, :], pwrites=[mix], group=mix)
            hist_to_fm(SM, mix, 8, None, xsS, bulk=lambda i, n: (xsS[:, :, :], "p (j b) -> p j b", dict(j=8)))
            S.add("dve", lambda e: e.tensor_tensor(out=xsS[:, :, :], in0=xsS[:, :, :], in1=t1S[:, :, :], op=ALU.mult), reads=[xsS, t1S], writes=[xsS])
            S.add("dve", lambda e: e.tensor_tensor(out=xsS[:, :, :], in0=xsS[:, :, :], in1=t2S[:, :, :], op=ALU.add), reads=[xsS, t2S], writes=[xsS])
            S.dma("sp", o_lru_s, xsS[:, :, :].rearrange("p j b -> p (j b)"), reads=[xsS], pwrites=[outbuf], group=xsS)
            S.add("dve", lambda e: e.tensor_tensor(out=mixS[:, 8:16, :], in0=xsS[:, :, :], in1=projS[:, 32:40, :], op=ALU.mult), reads=[xsS, tmpw], pwrites=[mixS])
            outproj_s(w_out_o)
            rmsnorm_s(PC("nf"), t2S[:, :, :], t2S)
            S.dma("sp", o_y_s, t2S[:, :, :].rearrange("p j b -> p (j b)"), reads=[t2S], pwrites=[outbuf], group=t2S)


        samp_tab = [None]

        def samp_cols(wv, col0, ncols):
            tab = samp_tab[0]
            if tab is None:
                return
            for jj in range(ncols // 128):
                c = col0 + jj * 128
                for (s_, e_, base, func) in tab:
                    if s_ <= c < e_:
                        cc = base + (c - s_) // 128
                        pb = nextPA()
                        for k in range(8):
                            S.add("pe", lambda e, k=k, jj=jj, wv=wv, pb=pb: e.matmul(pb[:, 0:NB], lhsT=wv[1][:, k, jj * 128:(jj + 1) * 128], rhs=hnS[:, k, :],
                                                                                      start=(k == 0), stop=(k == 7)), reads=[wv[0], hnS], writes=[pb])
                        S.add("act", lambda e, cc=cc, pb=pb, func=func: e.activation(out=projS[:, cc, :], in_=pb[:, 0:NB], func=func), reads=[pb], pwrites=[tmpw])

        def samp_dt(wv):
            if samp_tab[0] is None:
                return
            pb = nextPA()
            for k in range(8):
                S.add("pe", lambda e, k=k, wv=wv, pb=pb: e.matmul(pb[0:16, 0:NB], lhsT=wv[1][:, k, :], rhs=hnS[:, k, :], start=(k == 0), stop=(k == 7)),
                      reads=[wv[0], hnS], writes=[pb])
            S.add("act", lambda e, pb=pb: e.activation(out=dtS[:, 0, :], in_=pb[0:16, 0:NB], func=AF.Exp, bias=p16[:, 0:1]), reads=[pb, p16], pwrites=[dtS])
            S.add("act", lambda e: e.activation(out=dtS[:, 0, :], in_=dtS[:, 0, :], func=AF.Ln, bias=1.0), reads=[dtS], pwrites=[dtS])
            S.add("dve", lambda e: e.tensor_scalar(out=dtS[:, 1, :], in0=dtS[:, 0, :], scalar1=ea16[:, 0:1], scalar2=-1.0, op0=ALU.mult, op1=ALU.mult),
                  reads=[dtS, ea16], pwrites=[dtS])
            S.add("act", lambda e: e.activation(out=dtS[:, 1, :], in_=dtS[:, 1, :], func=AF.Exp), reads=[dtS], pwrites=[dtS])

        TAB_L0 = [(0, 1024, 0, AF.Silu), (1024, 2560, 8, AF.Copy), (2576, 3600, 21, AF.Gelu_apprx_tanh),
                  (3600, 4624, 29, AF.Gelu_apprx_tanh), (4624, 5648, 37, AF.Silu)]
        TAB_L1 = [(0, 1024, 0, AF.Copy), (1024, 2048, 8, AF.Sigmoid), (2048, 3072, 16, AF.Silu), (3072, 4096, 24, AF.Copy), (4096, 5120, 32, AF.Silu)]
        if stage >= 3:
            rmsnorm_s(PC("ne"), hnS[:, :, :], hnS)

        for ti in range(int(_os.environ.get('K_L0T', NT if stage >= 1 else 0))):
            samp_tab[0] = TAB_L0 if (stage >= 3 and ti == NT - 1) else None
            layer0_tile(ti)
            samp_tab[0] = None

        hT = sb("hT", [128, 8, 128])
        for j in range(8):
            pb = nextPA()
            S.add("pe", lambda e, j=j, pb=pb: e.transpose(pb[:, 0:128], Hst[:, j * 128:(j + 1) * 128], cIDF[:, :]), reads=[Hst, cIDF], writes=[pb])
            S.add("act", lambda e, j=j, pb=pb: e.activation(out=hT[:, j, :], in_=pb[:, 0:128], func=AF.Copy), reads=[pb], pwrites=[hT])
        S.dma("sp", o_ssm_p.rearrange("(j p) n -> p j n", p=128), hT[:, :, :], reads=[hT], pwrites=[outbuf], group=hT)
        S.dma("sp", o_sconv_p, lastraw[:, :, :].rearrange("p j k -> p (j k)"), reads=[lastraw], pwrites=[outbuf], group=lastraw)
        if stage >= 3:
            sample_layer0()
        GL = 544
        glu_view = xh_tm.t[:, :, :].rearrange("p a b -> p (a b)").bitcast(BF16)
        glu = [Buf(glu_view[:, j * GL:(j + 1) * GL], f"glu{j}") for j in range(8)]
        if not _os.environ.get('K_SKIP_BAR'):
            S.add("dve", lambda e: e.memset(glu_view[:, 0:8 * GL], 0.0), reads=[xh_tm], writes=glu + l1_barrier_reads)
        cvo = bigZ.t[:, :, :].rearrange("p a (c t) -> p (a c) t", t=512)
        v32 = vhat.t[:, :, :].rearrange("p a b -> p (a b)").bitcast(F32)
        mean_sb, var_sb, rstd_ln, mr_sb = v32[:, 0:512], v32[:, 512:1024], v32[:, 1024:1536], v32[:, 1536:2048]
        if not _os.environ.get('K_SKIP_MT'):
            S.dma("pool", MT_bf[:, 0:8, :], d_bda.rearrange("p (j q) -> p j q", j=8), pwrites=[MT_bf], group=MT_bf)
            S.dma("pool", MT_bf[:, 8:16, :], d_bdx.rearrange("p (j q) -> p j q", j=8), pwrites=[MT_bf], group=MT_bf)
        sp8 = sb("sp8", [128, 8])
        sp16 = sb("sp16", [128, 8])
        hist1 = sb("hist1", [128, 8, 3], BF16)
        hcarry = sb("hcarry", [128, 8])
        glu_last = sb("glu_last", [128, 8, 30])
        lastraw1 = sb("lastraw1", [128, 8, 4])
        lamc = PC("lam")
        S.add("act", lambda e: e.activation(out=sp8[:, :], in_=pfm[:, lamc:lamc + 8], func=AF.Exp, scale=-1.0), reads=[pfm], writes=[sp8])
        S.add("act", lambda e: e.activation(out=sp8[:, :], in_=sp8[:, :], func=AF.Ln, bias=1.0), reads=[sp8], writes=[sp8])
        S.add("dve", lambda e: e.tensor_scalar(out=sp16[:, :], in0=sp8[:, :], scalar1=-16.0, scalar2=None, op0=ALU.mult), reads=[sp8], writes=[sp16])
        S.add("dve", lambda e: e.tensor_scalar(out=sp8[:, :], in0=sp8[:, :], scalar1=-8.0, scalar2=None, op0=ALU.mult), reads=[sp8], writes=[sp8])
        S.add("dve", lambda e: e.memset(hist1[:, :, :], 0.0), writes=[hist1])
        S.add("dve", lambda e: e.memset(hcarry[:, :], 0.0), writes=[hcarry])
        Lflat = Lbuf.t[:, :, :].rearrange("p a b -> p (a b)")
        t_xc, t_r = yoff[:, 0:512], yoff[:, 512:1024]
        t_i, t_a = ysb[:, 0:512], ysb[:, 512:1024]
        t_b, t_h = Hst[:, 0:512], Hst[:, 512:1024]
        t_g, t_m = Lflat[:, 0:512], Lflat[:, 512:1024]
        xc_bf = xs_bf[:, 0:512]
        SZl = bigZ.t[:, :, :].rearrange("p a b -> p (a b)")
        lzv = [SZl[:, i * 512:(i + 1) * 512] for i in range(8)]
        lz = [Buf(lzv[i], f"lz{i}") for i in range(8)]
        lbar = sb("lbar", [128, 2])

        l1_pieces = [t_xc, t_r, t_i, t_a, t_b, t_h, t_g, t_m]
        l1_pbufs = [yoff, yoff, ysb, ysb, Hst, Hst, Lbuf, Lbuf]
        l1_sq = vhat.t[:, :, :].rearrange("p a (c t) -> p (a c) t", t=512)

        def l1_load(ti):
            for k in range(8):
                S.dma("sp", l1_pieces[k], x1T[k * 128:(k + 1) * 128, ti * TT:(ti + 1) * TT], reads=[x1buf], pwrites=[l1_pbufs[k]], group=l1_pbufs[k])

        def layer1_tile(ti):
            t0 = ti * TT
            last = (ti == NT - 1)
            S.dma("sp", bigA[:, :, :], (xT if _os.environ.get("K_NOX1") else x1T)[:, t0:t0 + TT].rearrange("(k p) t -> p k t", p=128), reads=[x1buf], writes=[bigA], group=bigA)
            set_pa([P0, P1, P7, P3, P4])
            if ti == 0:
                l1_load(0)
                norm_sq(l1_pieces, l1_pbufs, l1_sq, vhat)
                norm_rest(l1_pieces, l1_pbufs, l1_sq, vhat, PC("no"))
            if stage <= 1.1:
                return
            def conv_chunk(j, jj, wa_, wb_):
                pA = nextPA()
                projA(wa_, jj, hn, pA)
                pB = nextPA()
                projA(wb_, jj, hn, pB)
                sgb = sg[j % 2]
                gj = glu[j]
                PC2 = (P2, P5, P6)[j % 3]
                yield
                S.add("act", lambda e, sgb=sgb, pB=pB: e.activation(out=sgb[:, :], in_=pB[:, :], func=AF.Sigmoid), reads=[pB], writes=[sgb])
                S.add("dve", lambda e, gj=gj, pA=pA, sgb=sgb: e.tensor_tensor(out=gj[:, 30:542], in0=pA[:, :], in1=sgb[:, :], op=ALU.mult),
                      reads=[pA, sgb], pwrites=[gj])
                if last:
                    S.add("dve", lambda e, j=j, pA=pA, sgb=sgb: e.tensor_tensor(out=glu_last[:, j, :], in0=pA[:, 482:512], in1=sgb[:, 482:512], op=ALU.mult),
                          reads=[pA, sgb], pwrites=[glu_last])
                yield
                for k in range(31):
                    dg = diag(PC("ccw") + k * 8 + j, "dve" if k % 4 != 3 else "act")
                    S.add("pe", lambda e, k=k, dg=dg, gj=gj, PC2=PC2: e.matmul(PC2[:, :], lhsT=dg[:, :], rhs=gj[:, k:k + 512], start=(k == 0), stop=(k == 30)),
                          reads=[dg, gj], writes=[PC2])
                    if k in (9, 19):
                        yield
                yield
                bc_ = PC("ccb") + j
                S.add("act", lambda e, j=j, bc_=bc_, PC2=PC2: e.activation(out=cvo[:, j, :], in_=PC2[:, :], func=AF.Identity, bias=pfm[:, bc_:bc_ + 1]),
                      reads=[PC2, pfm], pwrites=[bigZ])
                S.add("dve", lambda e, gj=gj: e.tensor_copy(out=gj[:, 0:30], in_=gj[:, 512:542]), reads=[gj], pwrites=[gj])

            conv_w = {}

            def conv_weights(half):
                if half not in conv_w:
                    wa_ = wload(w_in_o[:, half * 512:(half + 1) * 512], 8, 512)
                    samp_cols(wa_, half * 512, 512)
                    wb_ = wload(w_in_o[:, 1024 + half * 512:1024 + (half + 1) * 512], 8, 512)
                    samp_cols(wb_, 1024 + half * 512, 512)
                    conv_w[half] = (wa_, wb_)
                return conv_w[half]

            active, nextj = [], 0
            while active or nextj < 8:
                if len(active) < 2 and nextj < 8:
                    wa_, wb_ = conv_weights(nextj // 4)
                    active.append(conv_chunk(nextj, nextj % 4, wa_, wb_))
                    nextj += 1
                for g_ in list(active):
                    if next(g_, "END") == "END":
                        active.remove(g_)
            S.add("act", lambda e: e.activation(out=mix[:, 0:8, :], in_=cvo, func=AF.Square), reads=[bigZ], writes=[mix])
            S.add("dve", lambda e: e.tensor_copy(out=mix[:, 8:16, :], in_=cvo), reads=[bigZ], pwrites=[mix])
            for k in range(8):
                S.add("pe", lambda e, k=k: e.matmul(P5[:, :], lhsT=cONESB[:, :], rhs=mix[:, 8 + k, :], start=(k == 0), stop=(k == 7)),
                      reads=[cONESB, mix], writes=[P5])
            for k in range(8):
                S.add("pe", lambda e, k=k: e.matmul(P6[:, :], lhsT=cONESB[:, :], rhs=mix[:, k, :], start=(k == 0), stop=(k == 7)),
                      reads=[cONESB, mix], writes=[P6])
            S.add("dve", lambda e: e.tensor_scalar(out=mean_sb, in0=P5[:, :], scalar1=1.0 / 1024.0, scalar2=None, op0=ALU.mult), reads=[P5], pwrites=[vhat])
            S.add("dve", lambda e: e.tensor_tensor(out=var_sb, in0=mean_sb, in1=mean_sb, op=ALU.mult), reads=[vhat], pwrites=[vhat])
            S.add("dve", lambda e: e.scalar_tensor_tensor(out=var_sb, in0=P6[:, :], scalar=1.0 / 1024.0, in1=var_sb, op0=ALU.mult, op1=ALU.subtract),
                  reads=[P6, vhat], pwrites=[vhat])
            S.add("act", lambda e: e.activation(out=rstd_ln, in_=var_sb, func=AF.Ln, bias=EPS), reads=[vhat], pwrites=[vhat])
            S.add("act", lambda e: e.activation(out=rstd_ln, in_=rstd_ln, func=AF.Exp, scale=-0.5), reads=[vhat], pwrites=[vhat])
            S.add("dve", lambda e: e.tensor_tensor(out=mr_sb, in0=mean_sb, in1=rstd_ln, op=ALU.mult), reads=[vhat], pwrites=[vhat])
            if stage <= 1.3:
                return
            for half in range(2):
                wv = wload(w_in_o[:, 2048 + half * 512:2048 + (half + 1) * 512], 8, 512)
                samp_cols(wv, 2048 + half * 512, 512)
                for jj in range(4):
                    j = half * 4 + jj
                    pG = nextPA()
                    projA(wv, jj, hn, pG)
                    sgb = sg[j % 2]
                    tb = xcf[j % 2]
                    S.add("act", lambda e, sgb=sgb, pG=pG: e.activation(out=sgb[:, :], in_=pG[:, :], func=AF.Silu), reads=[pG], writes=[sgb])
                    S.add("dve", lambda e, j=j, tb=tb: e.tensor_tensor(out=tb[:, :], in0=cvo[:, j, :], in1=rstd_ln, op=ALU.mult), reads=[bigZ, vhat], writes=[tb])
                    S.add("dve", lambda e, tb=tb: e.tensor_tensor(out=tb[:, :], in0=tb[:, :], in1=mr_sb, op=ALU.subtract), reads=[tb, vhat], writes=[tb])
                    gc_, bc_ = PC("cclg") + j, PC("cclb") + j
                    S.add("act", lambda e, tb=tb, gc_=gc_, bc_=bc_: e.activation(out=tb[:, :], in_=tb[:, :], func=AF.Silu, scale=pfm[:, gc_:gc_ + 1],
                                                                                 bias=pfm[:, bc_:bc_ + 1]), reads=[tb, pfm], writes=[tb])
                    S.add("dve", lambda e, j=j, tb=tb, sgb=sgb: e.tensor_tensor(out=mix[:, j, :], in0=tb[:, :], in1=sgb[:, :], op=ALU.mult),
                          reads=[tb, sgb], pwrites=[mix])
            if stage <= 1.4:
                return
            set_pa([P0, P1])
            S.add("dve", lambda e: e.memset(lbar[:, :], 0.0), writes=[bigZ, lbar] + lz)
            def lru_chunk(j, jj, wx_, wg_):
                od = j % 2
                if od == 0:
                    V = dict(xc=t_xc, r=t_r, i=t_i, a=t_a, b=t_b, h=t_h, g=t_g, m=t_m, xb=xc_bf)
                    Bf = dict(xc=yoff, r=yoff, i=ysb, a=ysb, b=Hst, h=Hst, g=Lbuf, m=Lbuf, xb=xs_bf)
                    PCV, PGA, PGX = P2, P3, P4
                else:
                    V = dict(xc=lzv[0], r=lzv[1], i=lzv[2], a=lzv[3], b=lzv[4], h=lzv[5], g=lzv[6], m=lzv[7], xb=xsd_bf[:, 0:512])
                    Bf = dict(xc=lz[0], r=lz[1], i=lz[2], a=lz[3], b=lz[4], h=lz[5], g=lz[6], m=lz[7], xb=xsd_bf)
                    PCV, PGA, PGX = P5, P6, P7
                PGAv = PGA.t if PGA is not P7 else P7t
                PGXv = PGX.t if PGX is not P7 else P7t
                pX = nextPA()
                projA(wx_, jj, hn, pX)
                raw = raws[j % 2]
                S.add("dve", lambda e, j=j, raw=raw: e.tensor_copy(out=raw[:, 0:3], in_=hist1[:, j, :]), reads=[hist1], pwrites=[raw])
                S.add("act", lambda e, raw=raw, pX=pX: e.activation(out=raw[:, 3:515], in_=pX[:, :], func=AF.Copy), reads=[pX], pwrites=[raw])
                if last:
                    S.add("act", lambda e, j=j, pX=pX: e.activation(out=lastraw1[:, j, :], in_=pX[:, 508:512], func=AF.Copy), reads=[pX], pwrites=[lastraw1])
                S.add("dve", lambda e, j=j, raw=raw: e.tensor_copy(out=hist1[:, j, :], in_=raw[:, 512:515]), reads=[raw], pwrites=[hist1])
                yield
                for k in range(4):
                    dg = diag(PC("lcw") + k * 8 + j, "dve")
                    S.add("pe", lambda e, k=k, dg=dg, raw=raw, PCV=PCV: e.matmul(PCV[:, :], lhsT=dg[:, :], rhs=raw[:, k:k + 512], start=(k == 0), stop=(k == 3)),
                          reads=[dg, raw], writes=[PCV])
                bc_ = PC("lcb") + j
                S.add("act", lambda e, bc_=bc_, V=V, PCV=PCV: e.activation(out=V["xc"], in_=PCV[:, :], func=AF.Identity, bias=pfm[:, bc_:bc_ + 1]),
                      reads=[PCV, pfm], pwrites=[Bf["xc"]])
                S.add("dve", lambda e, V=V: e.tensor_copy(out=V["xb"], in_=V["xc"]), reads=[Bf["xc"]], writes=[Bf["xb"]])
                yield
                S.add("pe", lambda e, j=j, V=V, PGAv=PGAv: e.matmul(PGAv[:, :], lhsT=MT_bf[:, j, :], rhs=V["xb"], start=True, stop=True), reads=[MT_bf, Bf["xb"]], writes=[PGA])
                S.add("pe", lambda e, j=j, V=V, PGXv=PGXv: e.matmul(PGXv[:, :], lhsT=MT_bf[:, 8 + j, :], rhs=V["xb"], start=True, stop=True), reads=[MT_bf, Bf["xb"]], writes=[PGX])
                ca, cx = PC("lba") + j, PC("lbx") + j
                pGd = nextPA()
                projA(wg_, jj, hn, pGd)
                yield
                S.add("act", lambda e, ca=ca, V=V, PGAv=PGAv: e.activation(out=V["r"], in_=PGAv[:, :], func=AF.Sigmoid, bias=pfm[:, ca:ca + 1]), reads=[PGA, pfm], pwrites=[Bf["r"]])
                S.add("act", lambda e, cx=cx, V=V, PGXv=PGXv: e.activation(out=V["i"], in_=PGXv[:, :], func=AF.Sigmoid, bias=pfm[:, cx:cx + 1]), reads=[PGX, pfm], pwrites=[Bf["i"]])
                S.add("act", lambda e, pGd=pGd, V=V: e.activation(out=V["g"], in_=pGd[:, :], func=AF.Sigmoid), reads=[pGd], pwrites=[Bf["g"]])
                S.add("dve", lambda e, pGd=pGd, V=V: e.tensor_tensor(out=V["g"], in0=pGd[:, :], in1=V["g"], op=ALU.mult), reads=[pGd, Bf["g"]], pwrites=[Bf["g"]])
                yield
                S.add("act", lambda e, j=j, V=V: e.activation(out=V["a"], in_=V["r"], func=AF.Exp, scale=sp8[:, j:j + 1]), reads=[Bf["r"], sp8], pwrites=[Bf["a"]])
                S.add("act", lambda e, j=j, V=V: e.activation(out=V["m"], in_=V["r"], func=AF.Exp, scale=sp16[:, j:j + 1]), reads=[Bf["r"], sp16], pwrites=[Bf["m"]])
                S.add("act", lambda e, V=V: e.activation(out=V["m"], in_=V["m"], func=AF.Ln, scale=-1.0, bias=1.0), reads=[Bf["m"]], pwrites=[Bf["m"]])
                S.add("act", lambda e, V=V: e.activation(out=V["m"], in_=V["m"], func=AF.Exp, scale=0.5), reads=[Bf["m"]], pwrites=[Bf["m"]])
                yield
                S.add("dve", lambda e, V=V: e.tensor_tensor(out=V["b"], in0=V["i"], in1=V["xc"], op=ALU.mult), reads=[Bf["i"], Bf["xc"]], pwrites=[Bf["b"]])
                S.add("dve", lambda e, V=V: e.tensor_tensor(out=V["b"], in0=V["b"], in1=V["m"], op=ALU.mult), reads=[Bf["b"], Bf["m"]], pwrites=[Bf["b"]])
                S.add("dve", lambda e, j=j, V=V: e.tensor_tensor_scan(out=V["h"], data0=V["a"], data1=V["b"], initial=hcarry[:, j:j + 1], op0=ALU.mult, op1=ALU.add),
                      reads=[Bf["a"], Bf["b"], hcarry], pwrites=[Bf["h"]])
                S.add("dve", lambda e, j=j, V=V: e.tensor_copy(out=hcarry[:, j:j + 1], in_=V["h"][:, 511:512]), reads=[Bf["h"]], pwrites=[hcarry])
                S.add("dve", lambda e, j=j, V=V: e.tensor_tensor(out=mix[:, 8 + j, :], in0=V["h"], in1=V["g"], op=ALU.mult), reads=[Bf["h"], Bf["g"]], pwrites=[mix])

            lru_w = {}

            def lru_weights(half):
                if half not in lru_w:
                    wx_ = wload(w_in_o[:, 3072 + half * 512:3072 + (half + 1) * 512], 8, 512)
                    samp_cols(wx_, 3072 + half * 512, 512)
                    wg_ = wload(w_in_o[:, 4096 + half * 512:4096 + (half + 1) * 512], 8, 512)
                    samp_cols(wg_, 4096 + half * 512, 512)
                    lru_w[half] = (wx_, wg_)
                return lru_w[half]

            active, nextj = [], 0
            while active or nextj < 8:
                if len(active) < 2 and nextj < 8:
                    wx_, wg_ = lru_weights(nextj // 4)
                    active.append(lru_chunk(nextj, nextj % 4, wx_, wg_))
                    nextj += 1
                for g_ in list(active):
                    if next(g_, "END") == "END":
                        active.remove(g_)
            S.add("dve", lambda e: e.memset(lbar[:, :], 0.0), writes=[bigZ, lbar] + lz)
            if ti + 1 < NT:
                l1_load(ti + 1)
                norm_sq(l1_pieces, l1_pbufs, l1_sq, vhat)
            for ob in range(4):
                if ob == 2 and ti + 1 < NT:
                    norm_rest(l1_pieces, l1_pbufs, l1_sq, vhat, PC("no"))
                wv = wload(w_out_o[:, ob * 256:(ob + 1) * 256], 16, 256)
                for dj2 in range(2):
                    dj = ob * 2 + dj2
                    pb = nextPA()
                    for ek in range(16):
                        S.add("pe", lambda e, ek=ek, dj2=dj2, wv=wv, pb=pb: e.matmul(pb[:, :], lhsT=wv[1][:, ek, dj2 * 128:(dj2 + 1) * 128],
                                                                                      rhs=mix[:, ek, :], start=(ek == 0), stop=(ek == 15)),
                              reads=[wv[0], mix], writes=[pb])
                    S.add("dve", lambda e, dj=dj, pb=pb: e.tensor_tensor(out=bigA[:, dj, :], in0=pb[:, :], in1=bigA[:, dj, :], op=ALU.add),
                          reads=[pb, bigA], pwrites=[bigA])
            if stage <= 1.6:
                return
            rmsnorm_fm(bigA, PC("nf"), bigZ, dst_view=cvo)
            S.dma("sp", yT[:, t0:t0 + TT].rearrange("(k p) t -> p k t", p=128), cvo, reads=[bigZ], pwrites=[outbuf], group=bigZ)

        if stage > 1:
            for ti in range(int(_os.environ.get('K_L1T', NT))):
                samp_tab[0] = TAB_L1 if (stage >= 3 and ti == NT - 1) else None
                layer1_tile(ti)
                samp_tab[0] = None
            S.dma("sp", o_ccv_p, glu_last[:, :, :].rearrange("p j k -> p (j k)"), reads=[glu_last], pwrites=[outbuf], group=glu_last)
            S.dma("sp", o_lconv_p, lastraw1[:, :, :].rearrange("p j k -> p (j k)"), reads=[lastraw1], pwrites=[outbuf], group=lastraw1)
            S.dma("sp", o_lru_p, hcarry[:, :], reads=[hcarry], pwrites=[outbuf], group=hcarry)
        if stage >= 3:
            sample_layer1()
        final_reads = [outbuf, bigA]
        S.add("sp", lambda e: e.nop(), reads=final_reads, writes=final_reads)
        S.finalize_and_emit(es)
    return nc


def _consts():
    idf = np.eye(128, dtype=np.float32)
    s = np.arange(128)
    negm = np.where(s[None, :] >= s[:, None], 0.0, -1.0e5).astype(np.float32)
    triu = (s[:, None] <= s[None, :]).astype(np.float32)
    sel16 = np.zeros((16, 16, 128), np.float32)
    for h in range(16):
        sel16[h, h, :] = 1.0
    sellast = np.zeros((128, 128), np.float32)
    sellast[127, :] = 1.0
    return dict(c_idf=idf, c_negm=negm, c_triu=triu, c_sel16=sel16.reshape(16, 2048), c_sellast=sellast)


def _prepare(inp):
    f = lambda a: np.ascontiguousarray(np.asarray(a, np.float32))
    shared = dict(
        w_in_e=f(inp["w_in_even"][0]), w_out_e=f(inp["w_out_even"][0]),
        w_in_o=f(inp["w_in_odd"][0]), w_out_o=f(inp["w_out_odd"][0]),
        pfm=_build_pfm(inp),
        p16=f(np.stack([inp["ssd_dt_bias"][0], inp["ssd_a_log"][0]], 1)),
        drow=f(inp["ssd_d"][0][None, :]),
        wsT=f(np.transpose(inp["gmlp_w_s"][0], (2, 0, 1)).reshape(128, 1024)),
        bsrow=f(inp["gmlp_b_s"][0].reshape(1, 1024)),
    )
    def _bd(w):
        w = np.asarray(w, np.float32)
        o = np.zeros((128, 8, 128), np.float32)
        for j in range(8):
            o[0:64, j, 0:64] = w[2 * j]
            o[64:128, j, 64:128] = w[2 * j + 1]
        return np.ascontiguousarray(o.reshape(128, 1024))
    shared["bda"] = _bd(inp["lru_wa"][0])
    shared["bdx"] = _bd(inp["lru_wx"][0])
    cexp = np.zeros((16, 8, 128), np.float32)
    for j in range(8):
        cexp[2 * j, j, 0:64] = 1.0
        cexp[2 * j + 1, j, 64:128] = 1.0
    shared["c_exp"] = cexp.reshape(16, 1024)
    shared.update(_consts())
    maps = []
    for c in range(NCORES):
        m = dict(shared)
        m["xT"] = f(inp["x_prompt"][c].T)
        sl = slice(c * NB, (c + 1) * NB)
        m["xsT"] = f(inp["x_sample"][sl, 0, :].T)
        m["st_ssm"] = f(inp["state_ssm"][0, sl].reshape(NB, 1024, 128))
        m["st_sconv"] = f(inp["state_ssd_conv"][0, sl])
        m["st_ccv"] = f(inp["state_ccv"][0, sl])
        m["st_lconv"] = f(inp["state_lru_conv"][0, sl])
        m["st_lru"] = f(inp["state_lru"][0, sl])
        maps.append(m)
    return maps


_NC_CACHE = {}


def _run(inp, stage=99):
    maps = _prepare(inp)
    if stage not in _NC_CACHE:
        _NC_CACHE[stage] = build_program(stage)
    nc = _NC_CACHE[stage]
    res = run_bass_kernel_spmd(nc, maps, core_ids=list(range(NCORES)))
    return res.results


def _fm2tm(a, nch):
    return np.ascontiguousarray(a.reshape(128, nch, NB).transpose(2, 1, 0).reshape(NB, nch * 128))


def _fmlast(a, nch, k, keep):
    return np.ascontiguousarray(a.reshape(128, nch, k)[:, :, k - keep:].transpose(2, 1, 0).reshape(keep, nch * 128))


def kernel(**inputs):
    res = _run(inputs, 99)
    B = NCORES
    y_p = np.stack([np.ascontiguousarray(r["yT"].T) for r in res])
    y_s = np.concatenate([_fm2tm(r["o_y_s"], 8) for r in res])[:, None, :]
    ssm_p = np.stack([r["o_ssm_p"].reshape(16, 64, 128) for r in res])[None]
    ssm_s = np.concatenate([r["o_ssm_s"].reshape(NB, 16, 64, 128) for r in res])[None]
    sconv_p = np.stack([_fmlast(r["o_sconv_p"], 12, 4, 3) for r in res])[None]
    sconv_s = np.concatenate([np.concatenate([r["o_sconv_s_hist"], _fm2tm(r["o_sconv_s_new"], 12)[:, None, :]], 1) for r in res])[None]
    gv_s = np.concatenate([_fm2tm(r["o_gv_s"], 8) for r in res])[None, :, None, :]
    ccv_p = np.stack([_fmlast(r["o_ccv_p"], 8, 30, 30) for r in res])[None]
    ccv_s = np.concatenate([np.concatenate([r["o_ccv_s_hist"], _fm2tm(r["o_ccv_s_new"], 8)[:, None, :]], 1) for r in res])[None]
    lconv_p = np.stack([_fmlast(r["o_lconv_p"], 8, 4, 3) for r in res])[None]
    lconv_s = np.concatenate([np.concatenate([r["o_lconv_s_hist"], _fm2tm(r["o_lconv_s_new"], 8)[:, None, :]], 1) for r in res])[None]
    lru_p = np.stack([np.ascontiguousarray(r["o_lru_p"].T).reshape(1024) for r in res])[None]
    lru_s = np.concatenate([_fm2tm(r["o_lru_s"], 8) for r in res])[None]
    outs = (y_p, y_s, ssm_p, ssm_s, sconv_p, sconv_s, gv_s, ccv_p, ccv_s, lconv_p, lconv_s, lru_p, lru_s)
    return tuple(np.ascontiguousarray(o.astype(np.float32)) for o in outs)
```

```python
import numpy as np
import ml_dtypes
import concourse.bass as bass
import concourse.mybir as mybir
from concourse.bass_utils import run_bass_kernel_spmd
from contextlib import ExitStack

F32 = mybir.dt.float32
BF16 = mybir.dt.bfloat16
AF = mybir.ActivationFunctionType
ALU = mybir.AluOpType
AX = mybir.AxisListType
COMPUTE = ("pe", "act", "dve", "pool")
EPS = 1e-6
NCORES = 8
SEQ = 2048
TT = 512
NT = SEQ // TT
NB = 16


class Buf:
    _n = 0

    def __init__(self, t, name=None):
        self.t = t
        self.name = name or f"buf{Buf._n}"
        Buf._n += 1
        self.writers = []
        self.readers = []
        self.base = []
        self.dma_ops = []
        self.sem = None

    def __getitem__(self, idx):
        return self.t[idx]


class Op:
    __slots__ = ("eng", "fn", "reads", "writes", "pwrites", "is_dma", "group", "gidx",
                 "eidx", "deps", "waits", "signal", "vc", "gpos")


def _is_pw(w, b):
    return any(x is b for x in w.pwrites)


class Sched:
    def __init__(self, nc):
        self.nc = nc
        self.ops = []
        self.by_eng = {e: [] for e in ("pe", "act", "dve", "pool", "sp")}

    def add(self, eng, fn, reads=(), writes=(), pwrites=(), dma=False, group=None):
        op = Op()
        op.eng, op.fn = eng, fn
        op.reads, op.writes, op.pwrites = list(reads), list(writes), list(pwrites)
        op.is_dma, op.group = dma, group
        op.gpos = len(self.ops)
        op.deps, op.waits, op.signal, op.vc = [], [], False, None
        deps = []
        for b in op.reads:
            deps.extend(b.writers)
        for b in op.writes:
            deps.extend(b.writers)
            deps.extend(b.readers)
        for b in op.pwrites:
            if b.readers:
                b.base = list(b.readers) + list(b.writers)
                b.writers = []
                b.readers = []
            deps.extend(b.base)
            deps.extend(w for w in b.writers if not _is_pw(w, b))
        for b in op.reads:
            b.readers.append(op)
        for b in op.writes:
            b.writers = [op]
            b.readers = []
            b.base = []
        for b in op.pwrites:
            b.writers.append(op)
        seen = set()
        for d in deps:
            if d is op or id(d) in seen:
                continue
            seen.add(id(d))
            if d.eng == "pe" and eng == "pe" and not d.is_dma and not dma:
                continue
            op.deps.append(d)
        if dma:
            op.gidx = len(group.dma_ops)
            group.dma_ops.append(op)
        op.eidx = len(self.by_eng[eng])
        self.by_eng[eng].append(op)
        self.ops.append(op)
        return op

    def dma(self, eng, out_ap, in_ap, reads=(), writes=(), pwrites=(), group=None, **kw):
        return self.add(eng, lambda e: e.dma_start(out=out_ap, in_=in_ap, **kw),
                        reads=reads, writes=writes, pwrites=pwrites, dma=True, group=group)

    def finalize_and_emit(self, es):
        nc = self.nc
        know = {e: {} for e in self.by_eng}
        for op in self.ops:
            k = know[op.eng]
            for d in op.deps:
                if d.is_dma:
                    key = ("g", id(d.group))
                    val = sum(1 for x in d.group.dma_ops if x.gpos < op.gpos)
                    tok = (key, val, d.group)
                else:
                    key = d.eng
                    val = d.eidx + 1
                    tok = (key, val, None)
                if k.get(key, 0) >= val:
                    continue
                op.waits.append(tok)
                k[key] = val
                if d.vc is not None:
                    for kk, vv in d.vc.items():
                        if k.get(kk, 0) < vv:
                            k[kk] = vv
                if not d.is_dma:
                    d.signal = True
            best = {}
            for tok in op.waits:
                if tok[0] not in best or best[tok[0]][1] < tok[1]:
                    best[tok[0]] = tok
            op.waits = list(best.values())
            op.vc = dict(k)
            if not op.is_dma:
                op.vc[op.eng] = op.eidx + 1
        ordmap = {}
        for e in COMPUTE:
            c, m = 0, {}
            for op in self.by_eng[e]:
                if op.signal:
                    c += 1
                m[op.eidx + 1] = c
            ordmap[e] = m
        esem = {e: es.enter_context(nc.semaphore(f"s_{e}")) for e in COMPUTE}
        groups = {}
        for op in self.ops:
            if op.is_dma and id(op.group) not in groups:
                groups[id(op.group)] = op.group
        for g in groups.values():
            g.sem = es.enter_context(nc.semaphore(f"g_{g.name}"))
        self.n_sems = 4 + len(groups)

        def emit_stream(ename, engine):
            for op in self.by_eng[ename]:
                for (key, val, grp) in op.waits:
                    if grp is not None:
                        engine.wait_ge(grp.sem, 16 * val)
                    else:
                        engine.wait_ge(esem[key], ordmap[key][val])
                ins = op.fn(engine)
                if op.is_dma:
                    ins.then_inc(op.group.sem, 16)
                elif op.signal:
                    ins.then_inc(esem[ename], 1)

        block = es.enter_context(nc.Block())

        @block.tensor
        def _(eng):
            emit_stream("pe", eng)

        @block.scalar
        def _(eng):
            emit_stream("act", eng)

        @block.vector
        def _(eng):
            emit_stream("dve", eng)

        @block.gpsimd
        def _(eng):
            emit_stream("pool", eng)

        @block.sync
        def _(eng):
            emit_stream("sp", eng)


def _fm(v):
    v = np.asarray(v, np.float32).reshape(-1)
    return np.ascontiguousarray(v.reshape(-1, 128).T)


PCOLS = {}


def _build_pfm(inp):
    cols, off = [], 0

    def put(name, arr):
        nonlocal off
        PCOLS[name] = off
        cols.append(arr)
        off += arr.shape[1]

    put("ne", _fm(inp["norm_even"][0]))
    put("no", _fm(inp["norm_odd"][0]))
    put("nf", _fm(inp["final_norm"]))
    put("scb", _fm(inp["ssd_conv_b"][0]))
    put("scw", np.concatenate([_fm(inp["ssd_conv_w"][0][k]) for k in range(4)], 1))
    put("gn", _fm(inp["ssd_norm"][0]))
    put("Dfm", _fm(np.repeat(inp["ssd_d"][0], 64)))
    put("lng", _fm(inp["gmlp_ln_g"][0]))
    put("lnb", _fm(inp["gmlp_ln_b"][0]))
    put("ccb", _fm(inp["ccv_b"][0]))
    put("cclg", _fm(inp["ccv_ln_g"][0]))
    put("cclb", _fm(inp["ccv_ln_b"][0]))
    put("ccw", np.concatenate([_fm(inp["ccv_w"][0][k]) for k in range(31)], 1))
    put("lcw", np.concatenate([_fm(inp["lru_conv_w"][0][k]) for k in range(4)], 1))
    put("lcb", _fm(inp["lru_conv_b"][0]))
    put("lba", _fm(inp["lru_ba"][0]))
    put("lbx", _fm(inp["lru_bx"][0]))
    put("lam", _fm(inp["lru_lambda"][0]))
    put("w00", _fm(np.repeat(inp["gmlp_w_s"][0][:, 0, 0], 128)))
    put("b0", _fm(np.repeat(inp["gmlp_b_s"][0][:, 0], 128)))
    return np.ascontiguousarray(np.concatenate(cols, 1))


NPCOL = 468
E_EVEN = 5648
E_ODD = 5120


def build_program(stage=99):
    import os as _os
    nc = bass.Bass("TRN2", target_bir_lowering=False)

    def din(name, shape, dt=F32):
        return nc.dram_tensor(name, list(shape), dt, kind="ExternalInput").ap()

    def dout(name, shape, dt=F32):
        return nc.dram_tensor(name, list(shape), dt, kind="ExternalOutput").ap()

    xT = din("xT", [1024, SEQ])
    w_in_e = din("w_in_e", [1024, E_EVEN])
    w_out_e = din("w_out_e", [2048, 1024])
    w_in_o = din("w_in_o", [1024, E_ODD])
    w_out_o = din("w_out_o", [2048, 1024])
    d_pfm = din("pfm", [128, NPCOL])
    d_p16 = din("p16", [16, 2])
    d_drow = din("drow", [1, 16])
    d_wsT = din("wsT", [128, 1024])
    d_bsrow = din("bsrow", [1, 1024])
    d_idf = din("c_idf", [128, 128])
    d_negm = din("c_negm", [128, 128])
    d_triu = din("c_triu", [128, 128])
    d_sel16 = din("c_sel16", [16, 2048])
    d_sellast = din("c_sellast", [128, 128])

    d_xsT = din("xsT", [1024, NB])
    st_ssm = din("st_ssm", [NB, 1024, 128])
    st_sconv = din("st_sconv", [NB, 3, 1536])
    st_ccv = din("st_ccv", [NB, 30, 1024])
    st_lconv = din("st_lconv", [NB, 3, 1024])
    st_lru = din("st_lru", [NB, 1024])
    d_exp = din("c_exp", [16, 1024])
    o_y_s = dout("o_y_s", [128, 8 * NB])
    o_ssm_s = dout("o_ssm_s", [NB, 1024, 128])
    o_sconv_s_new = dout("o_sconv_s_new", [128, 12 * NB])
    o_sconv_s_hist = dout("o_sconv_s_hist", [NB, 2, 1536])
    o_gv_s = dout("o_gv_s", [128, 8 * NB])
    o_ccv_s_new = dout("o_ccv_s_new", [128, 8 * NB])
    o_ccv_s_hist = dout("o_ccv_s_hist", [NB, 29, 1024])
    o_lconv_s_new = dout("o_lconv_s_new", [128, 8 * NB])
    o_lconv_s_hist = dout("o_lconv_s_hist", [NB, 2, 1024])
    o_lru_s = dout("o_lru_s", [128, 8 * NB])
    d_bda = din("bda", [128, 1024])
    d_bdx = din("bdx", [128, 1024])
    yT = dout("yT", [1024, SEQ])
    o_ccv_p = dout("o_ccv_p", [128, 240])
    o_lconv_p = dout("o_lconv_p", [128, 32])
    o_lru_p = dout("o_lru_p", [128, 8])
    x1T = nc.dram_tensor("x1T", [1024, SEQ], F32, kind="Internal").ap()
    o_ssm_p = dout("o_ssm_p", [1024, 128])
    o_sconv_p = dout("o_sconv_p", [128, 48])

    es = ExitStack()
    with es:
        S = Sched(nc)
        x1buf = Buf(None, "x1buf")
        outbuf = Buf(None, "outs")

        def sb(name, shape, dt=F32):
            return Buf(es.enter_context(nc.sbuf_tensor(name, list(shape), dt)), name)

        def psum(name, shape, dt=F32):
            return Buf(es.enter_context(nc.psum_tensor(name, list(shape), dt)), name)

        def PC(c):
            return PCOLS[c]

        pfm = sb("pfm_sb", [128, NPCOL])
        cIDF = sb("cIDF", [128, 128])
        cIDB = sb("cIDB", [128, 128], BF16)
        cNEGM = sb("cNEGM", [128, 128])
        cSELLAST = sb("cSELLAST", [128, 128])
        cSEL16 = sb("cSEL16", [16, 2048])
        cONESB = sb("cONESB", [128, 128], BF16)
        cONES16 = sb("cONES16", [16, 128])
        p16 = sb("p16_sb", [16, 2])
        Dbc = sb("Dbc", [128, 16])
        WmT = sb("WmT", [128, 8, 128], BF16)
        Rg = sb("Rg", [128, 8, 128])
        ea16 = sb("ea16", [16, 1])

        S.dma("sp", pfm[:, :], d_pfm, writes=[pfm], group=pfm)
        S.dma("sp", cIDF[:, :], d_idf, writes=[cIDF], group=cIDF)
        S.dma("pool", cIDB[:, :], d_idf, writes=[cIDB], group=cIDB)
        S.dma("sp", cNEGM[:, :], d_negm, writes=[cNEGM], group=cNEGM)
        S.dma("sp", cSELLAST[:, :], d_sellast, writes=[cSELLAST], group=cSELLAST)
        S.dma("sp", cSEL16[:, :], d_sel16, writes=[cSEL16], group=cSEL16)
        S.dma("sp", p16[:, :], d_p16, writes=[p16], group=p16)
        S.dma("sp", Dbc[:, :], d_drow[0, :].partition_broadcast(128), writes=[Dbc], group=Dbc)
        S.add("dve", lambda e: e.memset(cONESB[:, :], 1.0), writes=[cONESB])
        S.add("dve", lambda e: e.memset(cONES16[:, :], 1.0), writes=[cONES16])
        S.add("act", lambda e: e.activation(out=ea16[:, :], in_=p16[:, 1:2], func=AF.Exp), reads=[p16], writes=[ea16])

        P0 = psum("P0", [128, 512])
        P1 = psum("P1", [128, 512])
        P2 = psum("P2", [128, 512])
        P3 = psum("P3", [128, 512])
        P4 = psum("P4", [128, 512])
        P5 = psum("P5", [128, 512])
        P6 = psum("P6", [128, 512])
        P7t = es.enter_context(nc.psum_tensor("P7", [128, 512], F32))
        P7 = Buf(P7t, "P7")
        P7a = P7t[:, 0:144]
        P7b = P7t[:, 144:400]
        P7c = P7t[:, 400:416]
        PA = [P0, P1]
        pa_i = [0]

        def set_pa(banks):
            PA[:] = banks

        def nextPA():
            b = PA[pa_i[0] % len(PA)]
            pa_i[0] += 1
            return b

        tmpw = sb("tmpw2", [128, 1024])
        ctriu = sb("ctriu", [128, 128])
        bsbc = sb("bsbc2", [128, 1024])
        S.dma("sp", tmpw[:, :], d_wsT, writes=[tmpw], group=tmpw)
        S.dma("sp", ctriu[:, :], d_triu, writes=[ctriu], group=ctriu)
        S.dma("sp", bsbc[:, :], d_bsrow[0, :].partition_broadcast(128), writes=[bsbc], group=bsbc)
        S.add("dve", lambda e: e.tensor_tensor(
            out=WmT[:, :, :], in0=tmpw[:, :].rearrange("p (g t) -> p g t", g=8),
            in1=ctriu[:, :].unsqueeze(1).broadcast_to([128, 8, 128]), op=ALU.mult),
            reads=[tmpw, ctriu], writes=[WmT])
        for hf in range(2):
            S.add("pe", lambda e, hf=hf: e.matmul(P3[:, :] if hf == 0 else P4[:, :], lhsT=cONESB[:, :],
                                                    rhs=WmT[:, 4 * hf:4 * hf + 4, :].rearrange("p g t -> p (g t)"),
                                                    start=True, stop=True),
                  reads=[cONESB, WmT], writes=[P3 if hf == 0 else P4])
        for g in range(8):
            pb = P3 if g < 4 else P4
            S.add("dve", lambda e, g=g, pb=pb: e.scalar_tensor_tensor(
                out=Rg[:, g, :], in0=pb[:, (g % 4) * 128:(g % 4 + 1) * 128], scalar=pfm[:, PC("lnb") + g:PC("lnb") + g + 1],
                in1=bsbc[:, g * 128:(g + 1) * 128], op0=ALU.mult, op1=ALU.add),
                reads=[pb, pfm, bsbc], pwrites=[Rg])

        NW = 3
        wbufs = [sb(f"wbuf{i}", [128, 4096], BF16) for i in range(NW)]
        w_i = [0]

        def wload(dram_ap, kk, ww):
            b = wbufs[w_i[0] % NW]
            w_i[0] += 1
            view = b.t[:, 0:kk * ww].rearrange("p (k e) -> p k e", k=kk)
            S.dma("pool", view, dram_ap.rearrange("(k p) e -> p k e", p=128), writes=[b], group=b)
            return b, view

        NDG = 8
        dgbufs = [sb(f"dg{i}", [128, 128], BF16) for i in range(NDG)]
        dg_i = [0]

        def diag(col, eng="act"):
            b = dgbufs[dg_i[0] % NDG]
            dg_i[0] += 1
            if eng == "act":
                S.add("act", lambda e: e.activation(out=b[:, :], in_=cIDB[:, :], func=AF.Copy, scale=pfm[:, col:col + 1]),
                      reads=[cIDB, pfm], writes=[b])
            else:
                S.add("dve", lambda e: e.tensor_scalar(out=b[:, :], in0=cIDB[:, :], scalar1=pfm[:, col:col + 1], scalar2=None, op0=ALU.mult),
                      reads=[cIDB, pfm], writes=[b])
            return b

        bigA = sb("bigA", [128, 8, 512])
        mix = sb("mix", [128, 16, 512], BF16)
        hn = sb("hn", [128, 8, 512], BF16)
        rstd = sb("rstd", [128, 512])
        raws = [sb(f"raw{i}", [128, 515], BF16) for i in range(2)]
        hist0 = sb("hist0", [128, 12, 3], BF16)
        lastraw = sb("lastraw", [128, 12, 4])
        xcf = [sb(f"xcf{i}", [128, 512]) for i in range(2)]
        xh_tm = sb("xh_tm", [128, 4, 1024])
        B_tm = sb("B_tm", [128, 4, 256], BF16)
        BT_bf = sb("BT_bf", [128, 2, 512], BF16)
        CT_bf = sb("CT_bf", [128, 2, 512], BF16)
        bigZ = sb("bigZ", [128, 4, 1024])
        vhat = sb("vhat", [128, 4, 1024], BF16)
        sg = [sb(f"sg{i}", [128, 512]) for i in range(2)]
        dtT = sb("dtT", [16, 512])
        daT = sb("daT", [16, 512])
        csT = sb("csT", [16, 512])
        tmpT = sb("tmpT", [16, 512])
        PK1 = sb("PK1", [128, 512])
        PK2 = sb("PK2", [16, 512])
        tmq = sb("tmq", [128, 144])
        xs_bf = sb("xs_bf", [128, 1024], BF16)
        xsd_bf = sb("xsd_bf", [128, 1024], BF16)
        Lbuf = sb("Lbuf", [128, 8, 128])
        MT_bf = sb("MT_bf", [128, 16, 128], BF16)
        yoff = sb("yoff", [128, 1024])
        ysb = sb("ysb", [128, 1024])
        ya_bf = sb("ya_bf", [128, 1024], BF16)
        Hst = sb("Hst", [128, 1024])
        Hbf = sb("Hbf", [128, 1024], BF16)
        ect = sb("ect", [128, 16])
        ssq = sb("ssq", [128, 2])
        rs2 = sb("rs2", [128, 2])
        bnst = sb("bnst", [128, 4, 2, 6])
        mv = sb("mv", [128, 4, 2])
        rv = sb("rv", [128, 4])

        S.add("dve", lambda e: e.memset(hist0[:, :, :], 0.0), writes=[hist0])
        S.add("dve", lambda e: e.memset(PK1[:, :], 0.0), writes=[PK1])
        S.add("dve", lambda e: e.memset(Hst[:, :], 0.0), writes=[Hst])
        S.add("dve", lambda e: e.memset(Hbf[:, :], 0.0), writes=[Hbf])

        def rmsnorm_fm(src, gcol, dst_bf, dst_view=None):
            S.add("act", lambda e: e.activation(out=mix[:, 0:8, :], in_=src[:, :, :], func=AF.Square), reads=[src], writes=[mix])
            pb = nextPA()
            for k in range(8):
                S.add("pe", lambda e, k=k: e.matmul(pb[:, :], lhsT=cONESB[:, :], rhs=mix[:, k, :], start=(k == 0), stop=(k == 7)),
                      reads=[cONESB, mix], writes=[pb])
            S.add("act", lambda e: e.activation(out=rstd[:, :], in_=pb[:, :], func=AF.Ln, scale=1.0 / 1024.0, bias=EPS),
                  reads=[pb], writes=[rstd])
            S.add("act", lambda e: e.activation(out=rstd[:, :], in_=rstd[:, :], func=AF.Exp, scale=-0.5), reads=[rstd], writes=[rstd])
            for k in range(8):
                S.add("dve", lambda e, k=k: e.scalar_tensor_tensor(
                    out=(dst_bf[:, k, :] if dst_view is None else dst_view[:, k, :]), in0=src[:, k, :], scalar=pfm[:, gcol + k:gcol + k + 1], in1=rstd[:, :],
                    op0=ALU.mult, op1=ALU.mult), reads=[src, pfm, rstd], pwrites=[dst_bf])

        def projA(wv, j, rhs_buf, pb):
            for k in range(8):
                S.add("pe", lambda e, k=k: e.matmul(pb[:, :], lhsT=wv[1][:, k, j * 128:(j + 1) * 128], rhs=rhs_buf[:, k, :],
                                                     start=(k == 0), stop=(k == 7)),
                      reads=[wv[0], rhs_buf], writes=[pb])


        def norm_sq(pieces, pbufs, sq_view, sq_buf):
            for k in range(8):
                S.add("act", lambda e, k=k: e.activation(out=sq_view[:, k, :], in_=pieces[k], func=AF.Square), reads=[pbufs[k]], pwrites=[sq_buf])

        def norm_rest(pieces, pbufs, sq_view, sq_buf, gcol):
            pb = nextPA()
            for k in range(8):
                S.add("pe", lambda e, k=k: e.matmul(pb[:, :], lhsT=cONESB[:, :], rhs=sq_view[:, k, :], start=(k == 0), stop=(k == 7)),
                      reads=[cONESB, sq_buf], writes=[pb])
            S.add("act", lambda e: e.activation(out=rstd[:, :], in_=pb[:, :], func=AF.Ln, scale=1.0 / 1024.0, bias=EPS), reads=[pb], writes=[rstd])
            S.add("act", lambda e: e.activation(out=rstd[:, :], in_=rstd[:, :], func=AF.Exp, scale=-0.5), reads=[rstd], writes=[rstd])
            for k in range(8):
                S.add("dve", lambda e, k=k: e.scalar_tensor_tensor(out=hn[:, k, :], in0=pieces[k], scalar=pfm[:, gcol + k:gcol + k + 1], in1=rstd[:, :],
                                                                     op0=ALU.mult, op1=ALU.mult), reads=[pbufs[k], pfm, rstd], pwrites=[hn])

        l0_pieces = [bigZ.t[:, :, :].rearrange("p a (c t) -> p (a c) t", t=512)[:, k, :] for k in range(8)]
        l0_pbufs = [bigZ] * 8
        l0_sq = xh_tm.t[:, :, :].rearrange("p a b -> p (a b)").bitcast(BF16)[:, 0:4096].rearrange("p (k t) -> p k t", k=8)

        def l0_load(ti):
            S.dma("sp", bigZ.t[:, :, :].rearrange("p a (c t) -> p (a c) t", t=512), xT[:, ti * TT:(ti + 1) * TT].rearrange("(k p) t -> p k t", p=128),
                  writes=[bigZ], group=bigZ)

        def layer0_tile(ti):
            t0 = ti * TT
            last = (ti == NT - 1) and stage >= 1
            set_pa([P0, P1, P4, P5])
            if ti == 0:
                l0_load(0)
                norm_sq(l0_pieces, l0_pbufs, l0_sq, xh_tm)
                norm_rest(l0_pieces, l0_pbufs, l0_sq, xh_tm, PC("ne"))
            if stage <= 0.1:
                return
            wdt = wload(w_in_e[:, 2560:2576], 8, 16)
            samp_dt(wdt)
            pb = nextPA()
            for k in range(8):
                S.add("pe", lambda e, k=k: e.matmul(pb[0:16, :], lhsT=wdt[1][:, k, :], rhs=hn[:, k, :], start=(k == 0), stop=(k == 7)),
                      reads=[wdt[0], hn], writes=[pb])
            S.add("act", lambda e: e.activation(out=tmpT[:, :], in_=pb[0:16, :], func=AF.Exp, bias=p16[:, 0:1]),
                  reads=[pb, p16], writes=[tmpT])
            S.add("act", lambda e: e.activation(out=dtT[:, :], in_=tmpT[:, :], func=AF.Ln, bias=1.0), reads=[tmpT], writes=[dtT])
            S.add("dve", lambda e: e.tensor_scalar(out=daT[:, :], in0=dtT[:, :], scalar1=ea16[:, 0:1], scalar2=-1.0,
                                                    op0=ALU.mult, op1=ALU.mult), reads=[dtT, ea16], writes=[daT])
            for c in range(4):
                S.add("dve", lambda e, c=c: e.tensor_tensor_scan(out=csT[:, c * 128:(c + 1) * 128], data0=cONES16[:, :],
                                                                  data1=daT[:, c * 128:(c + 1) * 128], initial=0.0,
                                                                  op0=ALU.mult, op1=ALU.add),
                      reads=[cONES16, daT], pwrites=[csT])
            S.add("act", lambda e: e.activation(out=PK1[0:16, :], in_=dtT[:, :], func=AF.Copy), reads=[dtT], pwrites=[PK1])
            S.add("act", lambda e: e.activation(out=PK1[32:48, :], in_=csT[:, :], func=AF.Copy), reads=[csT], pwrites=[PK1])
            for c in range(4):
                S.add("act", lambda e, c=c: e.activation(out=tmpT[:, c * 128:(c + 1) * 128], in_=csT[:, c * 128:(c + 1) * 128],
                                                          func=AF.Exp, scale=-1.0, bias=csT[:, c * 128 + 127:c * 128 + 128]),
                      reads=[csT], pwrites=[tmpT])
            S.add("dve", lambda e: e.tensor_tensor(out=PK1[64:80, :], in0=tmpT[:, :], in1=dtT[:, :], op=ALU.mult),
                  reads=[tmpT, dtT], pwrites=[PK1])
            S.add("act", lambda e: e.activation(out=PK2[:, :], in_=csT[:, :], func=AF.Exp), reads=[csT], writes=[PK2])

            if stage <= 0.2:
                return
            def xbc_chunk(j, jj, wv):
                pb = nextPA()
                projA(wv, jj, hn, pb)
                raw = raws[j % 2]
                PC2 = P2 if j % 2 == 0 else P6
                PT3 = P3 if j % 2 == 0 else P7
                PT3v = P3.t if j % 2 == 0 else P7t
                S.add("dve", lambda e, j=j, raw=raw: e.tensor_copy(out=raw[:, 0:3], in_=hist0[:, j, :]), reads=[hist0], pwrites=[raw])
                S.add("act", lambda e, raw=raw, pb=pb: e.activation(out=raw[:, 3:515], in_=pb[:, :], func=AF.Copy),
                      reads=[pb], pwrites=[raw])
                if last:
                    S.add("act", lambda e, j=j, pb=pb: e.activation(out=lastraw[:, j, :], in_=pb[:, 508:512], func=AF.Copy), reads=[pb], pwrites=[lastraw])
                S.add("dve", lambda e, j=j, raw=raw: e.tensor_copy(out=hist0[:, j, :], in_=raw[:, 512:515]), reads=[raw], pwrites=[hist0])
                yield
                for k in range(4):
                    dg = diag(PC("scw") + k * 12 + j)
                    S.add("pe", lambda e, k=k, dg=dg, raw=raw, PC2=PC2: e.matmul(PC2[:, :], lhsT=dg[:, :], rhs=raw[:, k:k + 512],
                                                                         start=(k == 0), stop=(k == 3)),
                          reads=[dg, raw], writes=[PC2])
                yield
                bcol = PC("scb") + j
                if j < 10:
                    xc = xcf[j % 2]
                    S.add("act", lambda e, xc=xc, bcol=bcol, PC2=PC2: e.activation(out=xc[:, :], in_=PC2[:, :], func=AF.Silu,
                                                                             bias=pfm[:, bcol:bcol + 1]),
                          reads=[PC2, pfm], writes=[xc])
                    yield
                    for b4 in range(4):
                        S.add("pe", lambda e, b4=b4, xc=xc, PT3v=PT3v: e.transpose(PT3v[:, b4 * 128:(b4 + 1) * 128], xc[:, b4 * 128:(b4 + 1) * 128], cIDF[:, :]),
                              reads=[xc, cIDF], writes=[PT3])
                    yield
                    if j < 8:
                        S.add("dve", lambda e, j=j, PT3v=PT3v: e.tensor_copy(out=xh_tm[:, :, j * 128:(j + 1) * 128],
                                                                   in_=PT3v[:, :].rearrange("p (b c) -> p b c", b=4)),
                              reads=[PT3], pwrites=[xh_tm])
                    else:
                        jb = j - 8
                        S.add("dve", lambda e, jb=jb, PT3v=PT3v: e.tensor_copy(out=B_tm[:, :, jb * 128:(jb + 1) * 128],
                                                                     in_=PT3v[:, :].rearrange("p (b c) -> p b c", b=4)),
                              reads=[PT3], pwrites=[B_tm])
                        S.add("dve", lambda e, jb=jb, xc=xc: e.tensor_copy(out=BT_bf[:, jb, :], in_=xc[:, :]), reads=[xc], pwrites=[BT_bf])
                else:
                    jc = j - 10
                    S.add("act", lambda e, jc=jc, bcol=bcol, PC2=PC2: e.activation(out=CT_bf[:, jc, :], in_=PC2[:, :], func=AF.Silu,
                                                                             bias=pfm[:, bcol:bcol + 1]),
                          reads=[PC2, pfm], pwrites=[CT_bf])

            xbc_w = {}

            def xbc_weights(blk3):
                if blk3 not in xbc_w:
                    wv = wload(w_in_e[:, 1024 + blk3 * 512:1024 + (blk3 + 1) * 512], 8, 512)
                    samp_cols(wv, 1024 + blk3 * 512, 512)
                    xbc_w[blk3] = wv
                return xbc_w[blk3]

            active, nextj = [], 0
            while active or nextj < 12:
                if len(active) < 2 and nextj < 12:
                    active.append(xbc_chunk(nextj, nextj % 4, xbc_weights(nextj // 4)))
                    nextj += 1
                for g_ in list(active):
                    if next(g_, "END") == "END":
                        active.remove(g_)

            if stage <= 0.3:
                return
            for half in range(2):
                wv = wload(w_in_e[:, half * 512:(half + 1) * 512], 8, 512)
                samp_cols(wv, half * 512, 512)
                for b4 in range(4):
                    pb = nextPA()
                    for k in range(8):
                        S.add("pe", lambda e, k=k, b4=b4, wv=wv, pb=pb: e.matmul(pb[:, :], lhsT=hn[:, k, b4 * 128:(b4 + 1) * 128],
                                                                                  rhs=wv[1][:, k, :], start=(k == 0), stop=(k == 7)),
                              reads=[wv[0], hn], writes=[pb])
                    S.add("act", lambda e, b4=b4, half=half, pb=pb: e.activation(out=bigZ[:, b4, half * 512:(half + 1) * 512],
                                                                                  in_=pb[:, :], func=AF.Silu),
                          reads=[pb], pwrites=[bigZ])

            if stage <= 0.4:
                return
            set_pa([P0, P1])
            def gen_ssd():
                for c in range(4):
                    cs_ = slice(c * 128, (c + 1) * 128)
                    S.add("pe", lambda e, cs_=cs_: e.transpose(P7a[:, 0:128], PK1[:, cs_], cIDF[:, :]), reads=[PK1, cIDF], pwrites=[P7])
                    S.add("pe", lambda e, cs_=cs_: e.transpose(P7a[:, 128:144], PK2[:, cs_], cIDF[0:16, 0:16]), reads=[PK2, cIDF], pwrites=[P7])
                    S.add("dve", lambda e: e.tensor_copy(out=tmq[:, :], in_=P7a[:, :]), reads=[P7], writes=[tmq])
                    dt_tm = tmq[:, 0:16]
                    dd_tm = tmq[:, 64:80]
                    ecs_tm = tmq[:, 128:144]

                    def bc16(ap):
                        return ap.unsqueeze(2).broadcast_to([128, 16, 64])

                    xh3 = xh_tm[:, c, :].rearrange("p (h q) -> p h q", h=16)
                    S.add("dve", lambda e, xh3=xh3, dt_tm=dt_tm: e.tensor_tensor(out=xs_bf[:, :].rearrange("p (h q) -> p h q", h=16), in0=xh3,
                                                                                  in1=bc16(dt_tm), op=ALU.mult),
                          reads=[xh_tm, tmq], writes=[xs_bf])
                    S.add("dve", lambda e, xh3=xh3, dd_tm=dd_tm: e.tensor_tensor(out=xsd_bf[:, :].rearrange("p (h q) -> p h q", h=16), in0=xh3,
                                                                                   in1=bc16(dd_tm), op=ALU.mult),
                          reads=[xh_tm, tmq], writes=[xsd_bf])
                    S.add("dve", lambda e, xh3=xh3: e.tensor_tensor(out=ysb[:, :].rearrange("p (h q) -> p h q", h=16), in0=xh3,
                                                                     in1=bc16(Dbc[:, :]), op=ALU.mult),
                          reads=[xh_tm, Dbc], writes=[ysb])
                    yield
                    for g in range(2):
                        S.add("pe", lambda e, g=g, cs_=cs_: e.matmul(P7b[:, g * 128:(g + 1) * 128], lhsT=BT_bf[:, g, cs_], rhs=CT_bf[:, g, cs_],
                                                                      start=True, stop=True),
                              reads=[BT_bf, CT_bf], pwrites=[P7])
                    for g in range(2):
                        pg = P3 if g == 0 else P4
                        S.add("pe", lambda e, g=g, pg=pg, cs_=cs_: e.matmul(pg[:, :], lhsT=CT_bf[:, g, cs_], rhs=Hbf[:, g * 512:(g + 1) * 512],
                                                                             start=True, stop=True),
                              reads=[CT_bf, Hbf], writes=[pg])
                        S.add("dve", lambda e, g=g, pg=pg, ecs_tm=ecs_tm: e.tensor_tensor(
                            out=yoff[:, g * 512:(g + 1) * 512].rearrange("p (h q) -> p h q", h=8),
                            in0=pg[:, :].rearrange("p (h q) -> p h q", h=8),
                            in1=ecs_tm[:, g * 8:(g + 1) * 8].unsqueeze(2).broadcast_to([128, 8, 64]), op=ALU.mult),
                            reads=[pg, tmq], pwrites=[yoff])
                    S.add("dve", lambda e: e.tensor_tensor(out=yoff[:, :], in0=yoff[:, :], in1=ysb[:, :], op=ALU.add), reads=[yoff, ysb], writes=[yoff])
                    yield
                    for q in range(4):
                        pc = P5 if q % 2 == 0 else P6
                        for h4 in range(4):
                            h = q * 4 + h4
                            S.add("pe", lambda e, h=h, h4=h4, pc=pc, cs_=cs_: e.matmul(pc[:, h4 * 128:(h4 + 1) * 128], lhsT=cSEL16[:, h * 128:(h + 1) * 128],
                                                                                        rhs=csT[:, cs_], start=True, stop=True),
                                  reads=[cSEL16, csT], pwrites=[pc])
                        for h4 in range(4):
                            h = q * 4 + h4
                            S.add("dve", lambda e, h=h, h4=h4, pc=pc, q=q: e.scalar_tensor_tensor(
                                out=Lbuf[:, (q % 2) * 4 + h4, :], in0=pc[:, h4 * 128:(h4 + 1) * 128], scalar=tmq[:, 32 + h:33 + h],
                                in1=cNEGM[:, :], op0=ALU.subtract, op1=ALU.add),
                                reads=[pc, tmq, cNEGM], pwrites=[Lbuf])
                        if q % 2 == 1:
                            g = q // 2
                            S.add("act", lambda e: e.activation(out=Lbuf[:, :, :], in_=Lbuf[:, :, :], func=AF.Exp), reads=[Lbuf], writes=[Lbuf])
                            S.add("dve", lambda e, g=g: e.tensor_tensor(
                                out=MT_bf[:, g * 8:(g + 1) * 8, :], in0=Lbuf[:, :, :],
                                in1=P7b[:, g * 128:(g + 1) * 128].unsqueeze(1).broadcast_to([128, 8, 128]), op=ALU.mult),
                                reads=[Lbuf, P7], pwrites=[MT_bf])
                        yield
                    for h in range(16):
                        pg = P3 if h < 8 else P4
                        S.add("pe", lambda e, h=h, pg=pg: e.matmul(pg[:, (h % 8) * 64:(h % 8 + 1) * 64], lhsT=MT_bf[:, h, :],
                                                                     rhs=xs_bf[:, h * 64:(h + 1) * 64], start=True, stop=True),
                              reads=[MT_bf, xs_bf], pwrites=[pg])
                    yield
                    for g in range(2):
                        pg = P3 if g == 0 else P4
                        S.add("dve", lambda e, g=g, pg=pg: e.tensor_tensor(out=ysb[:, g * 512:(g + 1) * 512], in0=pg[:, :],
                                                                            in1=yoff[:, g * 512:(g + 1) * 512], op=ALU.add),
                              reads=[pg, yoff], pwrites=[ysb])
                    S.add("dve", lambda e, c=c: e.tensor_tensor(out=ysb[:, :], in0=ysb[:, :], in1=bigZ[:, c, :], op=ALU.mult),
                          reads=[ysb, bigZ], writes=[ysb])
                    for g in range(2):
                        S.add("act", lambda e, g=g: e.activation(out=yoff[:, g * 512:(g + 1) * 512], in_=ysb[:, g * 512:(g + 1) * 512],
                                                                  func=AF.Square, accum_out=ssq[:, g:g + 1]),
                              reads=[ysb], pwrites=[yoff, ssq])
                    S.add("act", lambda e: e.activation(out=rs2[:, :], in_=ssq[:, :], func=AF.Ln, scale=1.0 / 512.0, bias=EPS), reads=[ssq], writes=[rs2])
                    S.add("act", lambda e: e.activation(out=rs2[:, :], in_=rs2[:, :], func=AF.Exp, scale=-0.5), reads=[rs2], writes=[rs2])
                    for g in range(2):
                        S.add("act", lambda e, g=g: e.activation(out=ya_bf[:, g * 512:(g + 1) * 512], in_=ysb[:, g * 512:(g + 1) * 512],
                                                                  func=AF.Copy, scale=rs2[:, g:g + 1]),
                              reads=[ysb, rs2], pwrites=[ya_bf])
                    for j in range(8):
                        S.add("pe", lambda e, j=j: e.transpose(P2[:, j * 64:(j + 1) * 64].bitcast(BF16), ya_bf[:, j * 128:(j + 1) * 128], cIDB[:, :]),
                              reads=[ya_bf, cIDB], pwrites=[P2])
                    for j in range(8):
                        S.add("act", lambda e, j=j, cs_=cs_: e.activation(out=mix[:, j, cs_], in_=P2[:, j * 64:(j + 1) * 64].bitcast(BF16),
                                                                           func=AF.Copy, scale=pfm[:, PC("gn") + j:PC("gn") + j + 1]),
                              reads=[P2, pfm], pwrites=[mix])
                    yield
                    S.add("pe", lambda e: e.matmul(P7c[:, :], lhsT=cSELLAST[:, :], rhs=tmq[:, 32:48], start=True, stop=True),
                          reads=[cSELLAST, tmq], pwrites=[P7])
                    S.add("dve", lambda e: e.tensor_copy(out=ect[:, :], in_=P7c[:, :]), reads=[P7], writes=[ect])
                    S.add("act", lambda e: e.activation(out=ect[:, :], in_=ect[:, :], func=AF.Exp), reads=[ect], writes=[ect])
                    S.add("dve", lambda e: e.tensor_tensor(out=Hst[:, :].rearrange("p (h q) -> p h q", h=16),
                                                             in0=Hst[:, :].rearrange("p (h q) -> p h q", h=16), in1=bc16(ect[:, :]), op=ALU.mult),
                          reads=[Hst, ect], writes=[Hst])
                    for g in range(2):
                        pg = P3 if g == 0 else P4
                        S.add("pe", lambda e, g=g, pg=pg, c=c: e.matmul(pg[:, :], lhsT=B_tm[:, c, g * 128:(g + 1) * 128], rhs=xsd_bf[:, g * 512:(g + 1) * 512],
                                                                         start=True, stop=True),
                              reads=[B_tm, xsd_bf], writes=[pg])
                        S.add("dve", lambda e, g=g, pg=pg: e.tensor_tensor(out=Hst[:, g * 512:(g + 1) * 512], in0=pg[:, :],
                                                                            in1=Hst[:, g * 512:(g + 1) * 512], op=ALU.add),
                              reads=[pg, Hst], pwrites=[Hst])
                    S.add("act", lambda e: e.activation(out=Hbf[:, :], in_=Hst[:, :], func=AF.Copy), reads=[Hst], writes=[Hbf])
                    yield

            def gen_ug():
                for half in range(2):
                    wv = wload(w_in_e[:, 2576 + half * 512:2576 + (half + 1) * 512], 8, 512)
                    samp_cols(wv, 2576 + half * 512, 512)
                    for jj in range(4):
                        j = half * 4 + jj
                        pb = nextPA()
                        projA(wv, jj, hn, pb)
                        S.add("act", lambda e, j=j, pb=pb: e.activation(out=bigA[:, j, :], in_=pb[:, :], func=AF.Copy),
                              reads=[pb], pwrites=[bigA])
                        yield

            def gen_g():
                S.add("act", lambda e: e.activation(out=bigA[:, :, :], in_=bigA[:, :, :], func=AF.Gelu_apprx_tanh), reads=[bigA], writes=[bigA])
                for half in range(2):
                    wv = wload(w_in_e[:, 4624 + half * 512:4624 + (half + 1) * 512], 8, 512)
                    samp_cols(wv, 4624 + half * 512, 512)
                    for jj in range(4):
                        j = half * 4 + jj
                        pb = nextPA()
                        projA(wv, jj, hn, pb)
                        sgb = sg[j % 2]
                        S.add("act", lambda e, sgb=sgb, pb=pb: e.activation(out=sgb[:, :], in_=pb[:, :], func=AF.Silu), reads=[pb], writes=[sgb])
                        S.add("dve", lambda e, j=j, sgb=sgb: e.tensor_tensor(out=bigA[:, j, :], in0=bigA[:, j, :], in1=sgb[:, :], op=ALU.mult),
                              reads=[bigA, sgb], pwrites=[bigA])
                        yield
            g1, g2 = gen_ssd(), gen_ug()
            n1 = 0
            done2 = False
            for _ in g1:
                n1 += 1
                if n1 % 4 == 0 and not done2:
                    if next(g2, "END") == "END":
                        done2 = True
            if not done2:
                for _ in g2:
                    pass
            for _ in gen_g():
                pass
            set_pa([P0, P1, P3, P4, P5, P6])
            for half in range(2):
                wv = wload(w_in_e[:, 3600 + half * 512:3600 + (half + 1) * 512], 8, 512)
                samp_cols(wv, 3600 + half * 512, 512)
                for b4 in range(4):
                    pb = nextPA()
                    for k in range(8):
                        S.add("pe", lambda e, k=k, b4=b4, wv=wv, pb=pb: e.matmul(pb[:, :], lhsT=hn[:, k, b4 * 128:(b4 + 1) * 128],
                                                                                  rhs=wv[1][:, k, :], start=(k == 0), stop=(k == 7)),
                              reads=[wv[0], hn], writes=[pb])
                    S.add("act", lambda e, b4=b4, half=half, pb=pb: e.activation(out=bigZ[:, b4, half * 512:(half + 1) * 512],
                                                                                  in_=pb[:, :], func=AF.Gelu_apprx_tanh),
                          reads=[pb], pwrites=[bigZ])
            for b4 in range(4):
                for half in range(2):
                    S.add("dve", lambda e, b4=b4, half=half: e.bn_stats(out=bnst[:, b4, half, :], in_=bigZ[:, b4, half * 512:(half + 1) * 512]),
                          reads=[bigZ], pwrites=[bnst])
                S.add("dve", lambda e, b4=b4: e.bn_aggr(out=mv[:, b4, :], in_=bnst[:, b4, :, :].rearrange("p a b -> p (a b)")),
                      reads=[bnst], pwrites=[mv])
            S.add("act", lambda e: e.activation(out=rv[:, :], in_=mv[:, :, 1], func=AF.Ln, bias=EPS), reads=[mv], writes=[rv])
            S.add("act", lambda e: e.activation(out=rv[:, :], in_=rv[:, :], func=AF.Exp, scale=-0.5), reads=[rv], writes=[rv])
            for b4 in range(4):
                S.add("dve", lambda e, b4=b4: e.tensor_scalar(out=vhat[:, b4, :], in0=bigZ[:, b4, :], scalar1=mv[:, b4, 0:1], scalar2=rv[:, b4:b4 + 1],
                                                               op0=ALU.subtract, op1=ALU.mult),
                      reads=[bigZ, mv, rv], pwrites=[vhat])
            if stage <= 0.7:
                return
            for b4 in range(4):
                bs_ = slice(b4 * 128, (b4 + 1) * 128)
                for gh in range(2):
                    pb = nextPA()
                    for g4 in range(4):
                        g = gh * 4 + g4
                        S.add("pe", lambda e, g=g, g4=g4, b4=b4, pb=pb: e.matmul(pb[:, g4 * 128:(g4 + 1) * 128], lhsT=vhat[:, b4, g * 128:(g + 1) * 128],
                                                                                  rhs=WmT[:, g, :], start=True, stop=True),
                              reads=[vhat, WmT], pwrites=[pb])
                    if stage <= 0.71:
                        continue
                    sgb = sg[gh]
                    for g4 in range(4):
                        g = gh * 4 + g4
                        S.add("dve", lambda e, g=g, g4=g4, pb=pb, sgb=sgb: e.scalar_tensor_tensor(
                            out=sgb[:, g4 * 128:(g4 + 1) * 128], in0=pb[:, g4 * 128:(g4 + 1) * 128],
                            scalar=pfm[:, PC("lng") + g:PC("lng") + g + 1], in1=Rg[:, g, :], op0=ALU.mult, op1=ALU.add),
                            reads=[pb, pfm, Rg], pwrites=[sgb])
                    if stage <= 0.72:
                        continue
                    S.add("dve", lambda e, gh=gh, sgb=sgb, bs_=bs_: e.tensor_tensor(
                        out=mix[:, 8 + gh * 4:8 + gh * 4 + 4, bs_], in0=sgb[:, :].rearrange("p (g t) -> p g t", g=4),
                        in1=bigA[:, gh * 4:gh * 4 + 4, bs_], op=ALU.mult),
                        reads=[sgb, bigA], pwrites=[mix])
            if stage <= 0.8:
                return
            S.dma("sp", bigA[:, :, :], xT[:, t0:t0 + TT].rearrange("(k p) t -> p k t", p=128), writes=[bigA], group=bigA)
            if ti + 1 < NT:
                l0_load(ti + 1)
                norm_sq(l0_pieces, l0_pbufs, l0_sq, xh_tm)
            for ob in range(4):
                if ob == 2 and ti + 1 < NT:
                    norm_rest(l0_pieces, l0_pbufs, l0_sq, xh_tm, PC("ne"))
                wv = wload(w_out_e[:, ob * 256:(ob + 1) * 256], 16, 256)
                for dj2 in range(2):
                    dj = ob * 2 + dj2
                    pb = nextPA()
                    for ek in range(16):
                        S.add("pe", lambda e, ek=ek, dj2=dj2, wv=wv, pb=pb: e.matmul(pb[:, :], lhsT=wv[1][:, ek, dj2 * 128:(dj2 + 1) * 128],
                                                                                      rhs=mix[:, ek, :], start=(ek == 0), stop=(ek == 15)),
                              reads=[wv[0], mix], writes=[pb])
                    S.add("dve", lambda e, dj=dj, pb=pb: e.tensor_tensor(out=bigA[:, dj, :], in0=pb[:, :], in1=bigA[:, dj, :], op=ALU.add),
                          reads=[pb, bigA], pwrites=[bigA])
            S.dma("sp", x1T[:, t0:t0 + TT].rearrange("(k p) t -> p k t", p=128), bigA[:, :, :], reads=[bigA], pwrites=[x1buf], group=bigA)

        l1_barrier_reads = []
        SA = bigA.t[:, :, :].rearrange("p a b -> p (a b)")
        SM = mix.t[:, :, :].rearrange("p a b -> p (a b)").bitcast(F32)
        SZ = bigZ.t[:, :, :].rearrange("p a b -> p (a b)")
        SX = xh_tm.t[:, :, :].rearrange("p a b -> p (a b)")
        projS = tmpw.t[:, 0:45 * NB].rearrange("p (c b) -> p c b", b=NB)
        hallS = bsbc.t[:, 0:4 * 12 * NB].rearrange("p (k j b) -> p k j b", k=4, j=12)
        xsT_s = sb("xsT_s", [128, 8, NB])
        sqS = sb("sqS", [128, 8, NB], BF16)
        rstdS = sb("rstdS", [128, NB])
        hnS = sb("hnS", [128, 8, NB], BF16)
        convS = sb("convS", [128, 12, NB])
        xcS = sb("xcS", [128, 12, NB])
        dtS = sb("dtS", [16, 3, NB])
        dtE = sb("dtE", [128, 2, 8, NB])
        xsS = sb("xsS", [128, 8, NB])
        vhat32 = vhat.t[:, :, :].rearrange("p a b -> p (a b)").bitcast(F32)
        BC_tm = vhat32[0:16, 1024:1536]
        yS = sb("yS", [128, 8, NB])
        t1S = sb("t1S", [128, 8, NB])
        t2S = sb("t2S", [128, 8, NB])
        stS = sb("stS", [128, 4, NB])
        mixS = sb("mixS", [128, 16, NB], BF16)
        cEXP = vhat32[0:16, 0:1024]
        S.dma("sp", xsT_s[:, :, :], d_xsT.rearrange("(k p) b -> p k b", p=128), writes=[xsT_s], group=xsT_s)

        def bcb(ap2):
            return ap2.unsqueeze(2).broadcast_to([128, ap2.shape[1], NB])

        def bcj(ap2, n):
            return ap2.unsqueeze(1).broadcast_to([128, n, NB])

        def rmsnorm_s(gcol, dst, dst_buf):
            S.add("act", lambda e: e.activation(out=sqS[:, :, :], in_=xsT_s[:, :, :], func=AF.Square), reads=[xsT_s], writes=[sqS])
            pb = nextPA()
            for k in range(8):
                S.add("pe", lambda e, k=k: e.matmul(pb[:, 0:NB], lhsT=cONESB[:, :], rhs=sqS[:, k, :], start=(k == 0), stop=(k == 7)),
                      reads=[cONESB, sqS], writes=[pb])
            S.add("act", lambda e: e.activation(out=rstdS[:, :], in_=pb[:, 0:NB], func=AF.Ln, scale=1.0 / 1024.0, bias=EPS), reads=[pb], writes=[rstdS])
            S.add("act", lambda e: e.activation(out=rstdS[:, :], in_=rstdS[:, :], func=AF.Exp, scale=-0.5), reads=[rstdS], writes=[rstdS])
            S.add("dve", lambda e: e.tensor_tensor(out=t1S[:, :, :], in0=xsT_s[:, :, :], in1=bcj(rstdS[:, :], 8), op=ALU.mult),
                  reads=[xsT_s, rstdS], writes=[t1S])
            S.add("dve", lambda e: e.tensor_tensor(out=dst, in0=t1S[:, :, :], in1=bcb(pfm[:, gcol:gcol + 8]), op=ALU.mult),
                  reads=[t1S, pfm], writes=[dst_buf])

        def proj_s(wsrc, col0, nchunk, c0, func, nrows=128):
            done = 0
            while done < nchunk:
                nb_ = min(4, nchunk - done)
                wv = wload(wsrc[:, col0 + done * 128:col0 + (done + nb_) * 128], 8, nb_ * 128)
                for jj in range(nb_):
                    pb = nextPA()
                    for k in range(8):
                        S.add("pe", lambda e, k=k, jj=jj, wv=wv, pb=pb: e.matmul(pb[:, 0:NB], lhsT=wv[1][:, k, jj * 128:(jj + 1) * 128], rhs=hnS[:, k, :],
                                                                                  start=(k == 0), stop=(k == 7)),
                              reads=[wv[0], hnS], writes=[pb])
                    cc = c0 + done + jj
                    S.add("act", lambda e, cc=cc, pb=pb: e.activation(out=projS[:, cc, :], in_=pb[:, 0:NB], func=func), reads=[pb], pwrites=[tmpw])
                done += nb_

        def outproj_s(wsrc):
            for ob in range(4):
                wv = wload(wsrc[:, ob * 256:(ob + 1) * 256], 16, 256)
                for dj2 in range(2):
                    dj = ob * 2 + dj2
                    pb = nextPA()
                    for ek in range(16):
                        S.add("pe", lambda e, ek=ek, dj2=dj2, wv=wv, pb=pb: e.matmul(pb[:, 0:NB], lhsT=wv[1][:, ek, dj2 * 128:(dj2 + 1) * 128],
                                                                                      rhs=mixS[:, ek, :], start=(ek == 0), stop=(ek == 15)),
                              reads=[wv[0], mixS], writes=[pb])
                    S.add("dve", lambda e, dj=dj, pb=pb: e.tensor_tensor(out=xsT_s[:, dj, :], in0=pb[:, 0:NB], in1=xsT_s[:, dj, :], op=ALU.add),
                          reads=[pb, xsT_s], pwrites=[xsT_s])

        def hist_to_fm(stage_ap, stage_buf, ntap_cols, dst_fn, dst_buf, bulk=None):
            i = 0
            while i < ntap_cols:
                n = min(32, ntap_cols - i)
                pb = nextPA()
                for q in range(n):
                    S.add("pe", lambda e, q=q, i=i, pb=pb: e.transpose(pb[:, q * NB:(q + 1) * NB], stage_ap[0:NB, (i + q) * 128:(i + q + 1) * 128], cIDF[0:NB, 0:NB]),
                          reads=[stage_buf, cIDF], pwrites=[pb])
                if bulk is not None:
                    dst_ap, pat, kw = bulk(i, n)
                    S.add("dve", lambda e, pb=pb, n=n, dst_ap=dst_ap, pat=pat, kw=kw: e.tensor_copy(out=dst_ap, in_=pb[:, 0:n * NB].rearrange(pat, **kw)),
                          reads=[pb], pwrites=[dst_buf])
                else:
                    for q in range(n):
                        S.add("dve", lambda e, q=q, i=i, pb=pb: e.tensor_copy(out=dst_fn(i + q), in_=pb[:, q * NB:(q + 1) * NB]), reads=[pb], pwrites=[dst_buf])
                i += n

        def stats_s(src3, src_buf, nch, inv_n):
            S.add("act", lambda e: e.activation(out=sqS[:, 0:nch, :], in_=src3, func=AF.Square), reads=[src_buf], writes=[sqS])
            S.add("dve", lambda e: e.tensor_copy(out=hnS[:, 0:nch, :], in_=src3), reads=[src_buf], writes=[hnS])
            pb = nextPA()
            for k in range(nch):
                S.add("pe", lambda e, k=k: e.matmul(pb[:, 0:NB], lhsT=cONESB[:, :], rhs=hnS[:, k, :], start=(k == 0), stop=(k == nch - 1)),
                      reads=[cONESB, hnS], writes=[pb])
            pb2 = nextPA()
            for k in range(nch):
                S.add("pe", lambda e, k=k: e.matmul(pb2[:, 0:NB], lhsT=cONESB[:, :], rhs=sqS[:, k, :], start=(k == 0), stop=(k == nch - 1)),
                      reads=[cONESB, sqS], writes=[pb2])
            S.add("dve", lambda e: e.tensor_scalar(out=stS[:, 0, :], in0=pb[:, 0:NB], scalar1=inv_n, scalar2=None, op0=ALU.mult), reads=[pb], pwrites=[stS])
            S.add("dve", lambda e: e.tensor_tensor(out=stS[:, 1, :], in0=stS[:, 0, :], in1=stS[:, 0, :], op=ALU.mult), reads=[stS], pwrites=[stS])
            S.add("dve", lambda e: e.scalar_tensor_tensor(out=stS[:, 1, :], in0=pb2[:, 0:NB], scalar=inv_n, in1=stS[:, 1, :], op0=ALU.mult, op1=ALU.subtract),
                  reads=[pb2, stS], pwrites=[stS])
            S.add("act", lambda e: e.activation(out=stS[:, 2, :], in_=stS[:, 1, :], func=AF.Ln, bias=EPS), reads=[stS], pwrites=[stS])
            S.add("act", lambda e: e.activation(out=stS[:, 2, :], in_=stS[:, 2, :], func=AF.Exp, scale=-0.5), reads=[stS], pwrites=[stS])

        def sample_layer0():
            set_pa([P0, P1])
            S.dma("sp", cEXP, d_exp, pwrites=[vhat], group=vhat)
            S.dma("sp", SA[0:NB, 0:1536], st_sconv[:, 0, :], pwrites=[bigA], group=bigA)
            S.dma("sp", SA[0:NB, 1536:3072], st_sconv[:, 1, :], pwrites=[bigA], group=bigA)
            S.dma("sp", SM[0:NB, 0:1536], st_sconv[:, 2, :], pwrites=[mix], group=mix)
            hist_to_fm(SA, bigA, 24, None, bsbc, bulk=lambda i, n: (hallS[:, 0:2, :, :], "p (k j b) -> p k j b", dict(k=2, j=12)))
            hist_to_fm(SM, mix, 12, None, bsbc, bulk=lambda i, n: (hallS[:, 2, :, :], "p (j b) -> p j b", dict(j=12)))
            S.add("dve", lambda e: e.tensor_copy(out=hallS[:, 3, :, :], in_=projS[:, 8:20, :]), reads=[tmpw], pwrites=[bsbc])
            S.dma("sp", o_sconv_s_new, projS[:, 8:20, :].rearrange("p j b -> p (j b)"), reads=[tmpw], pwrites=[outbuf], group=tmpw)
            S.dma("sp", o_sconv_s_hist, st_sconv[:, 1:3, :], pwrites=[outbuf], group=outbuf)
            wv4 = pfm[:, PC("scw"):PC("scw") + 48].rearrange("p (k j) -> p k j", k=4).unsqueeze(3).broadcast_to([128, 4, 12, NB])
            S.add("dve", lambda e: e.tensor_tensor(out=hallS, in0=hallS, in1=wv4, op=ALU.mult), reads=[bsbc, pfm], writes=[bsbc])
            S.add("dve", lambda e: e.tensor_reduce(out=convS[:, :, :], in_=hallS.rearrange("p k j b -> p j b k"), axis=AX.X, op=ALU.add),
                  reads=[bsbc], writes=[convS])
            for j in range(12):
                bc_ = PC("scb") + j
                S.add("act", lambda e, j=j, bc_=bc_: e.activation(out=xcS[:, j, :], in_=convS[:, j, :], func=AF.Silu, bias=pfm[:, bc_:bc_ + 1]),
                      reads=[convS, pfm], pwrites=[xcS])
            for q in range(2):
                pb = nextPA()
                for j in range(8):
                    S.add("pe", lambda e, q=q, j=j, pb=pb: e.matmul(pb[:, j * NB:(j + 1) * NB], lhsT=cEXP[:, j * 128:(j + 1) * 128], rhs=dtS[:, q, :],
                                                                     start=True, stop=True), reads=[vhat, dtS], pwrites=[pb])
                S.add("dve", lambda e, q=q, pb=pb: e.tensor_copy(out=dtE[:, q, :, :], in_=pb[:, 0:8 * NB].rearrange("p (j b) -> p j b", j=8)),
                      reads=[pb], pwrites=[dtE])
            S.add("dve", lambda e: e.tensor_tensor(out=xsS[:, :, :], in0=xcS[:, 0:8, :], in1=dtE[:, 0, :, :], op=ALU.mult), reads=[xcS, dtE], writes=[xsS])
            pb = nextPA()
            for q in range(4):
                S.add("pe", lambda e, q=q, pb=pb: e.transpose(pb[0:NB, q * 128:(q + 1) * 128], xcS[:, 8 + q, :], cIDF[:, :]), reads=[xcS, cIDF], pwrites=[pb])
            S.add("dve", lambda e, pb=pb: e.tensor_copy(out=BC_tm, in_=pb[0:NB, :]), reads=[pb], pwrites=[vhat])
            hbs = [SX[:, 0:1024], SX[:, 1024:2048], SX[:, 3072:4096]]
            ob = SX[:, 2048:3072].rearrange("p (j n) -> p j n", j=8)
            hbB = [Buf(hbs[0], "hb0"), Buf(hbs[1], "hb1"), Buf(hbs[2], "hb2")]
            obB = Buf(SX[:, 2048:3072], "obS")
            l1_barrier_reads.extend(hbB + [obB])
            S.add("dve", lambda e: e.memset(SX[:, 2048:3072], 0.0), writes=[xh_tm] + hbB + [obB])
            for b in range(NB):
                hb = hbs[b % 3].rearrange("p (j n) -> p j n", j=8)
                hB = hbB[b % 3]
                pbc = P5 if b % 2 == 0 else P6
                S.add("pe", lambda e, b=b, pbc=pbc: e.matmul(pbc[:, :], lhsT=cSEL16[:, b * 128:(b + 1) * 128], rhs=BC_tm, start=True, stop=True),
                      reads=[cSEL16, vhat], writes=[pbc])
                S.dma("sp", hb, st_ssm[b].rearrange("(j p) n -> p j n", p=128), writes=[hB], group=hB)
                for j8 in range(8):
                    S.add("act", lambda e, b=b, hb=hb, j8=j8: e.activation(out=hb[:, j8, :], in_=hb[:, j8, :], func=AF.Copy, scale=dtE[:, 1, j8, b:b + 1]),
                          reads=[dtE], pwrites=[hB])
                for g in range(2):
                    S.add("dve", lambda e, b=b, g=g, pbc=pbc: e.tensor_tensor(
                        out=ob[:, 4 * g:4 * g + 4, :], in0=pbc[:, g * 128:(g + 1) * 128].unsqueeze(1).broadcast_to([128, 4, 128]),
                        in1=xsS[:, 4 * g:4 * g + 4, b:b + 1].broadcast_to([128, 4, 128]), op=ALU.mult),
                        reads=[pbc, xsS], pwrites=[obB])
                S.add("dve", lambda e, hb=hb: e.tensor_tensor(out=hb, in0=hb, in1=ob, op=ALU.add), reads=[hB, obB], writes=[hB])
                S.dma("act", o_ssm_s[b].rearrange("(j p) n -> p j n", p=128), hb, reads=[hB], pwrites=[outbuf], group=hB)
                for g in range(2):
                    S.add("dve", lambda e, g=g, pbc=pbc, hb=hb: e.tensor_tensor(
                        out=ob[:, 4 * g:4 * g + 4, :], in0=pbc[:, 256 + g * 128:256 + (g + 1) * 128].unsqueeze(1).broadcast_to([128, 4, 128]),
                        in1=hb[:, 4 * g:4 * g + 4, :], op=ALU.mult), reads=[pbc, hB], pwrites=[obB])
                S.add("dve", lambda e, b=b: e.tensor_reduce(out=yS[:, :, b], in_=ob, axis=AX.X, op=ALU.add), reads=[obB], pwrites=[yS])
            S.add("dve", lambda e: e.tensor_tensor(out=t1S[:, :, :], in0=xcS[:, 0:8, :], in1=bcb(pfm[:, PC("Dfm"):PC("Dfm") + 8]), op=ALU.mult),
                  reads=[xcS, pfm], writes=[t1S])
            S.add("dve", lambda e: e.tensor_tensor(out=yS[:, :, :], in0=yS[:, :, :], in1=t1S[:, :, :], op=ALU.add), reads=[yS, t1S], writes=[yS])
            S.add("dve", lambda e: e.tensor_tensor(out=yS[:, :, :], in0=yS[:, :, :], in1=projS[:, 0:8, :], op=ALU.mult), reads=[yS, tmpw], writes=[yS])
            S.add("act", lambda e: e.activation(out=sqS[:, :, :], in_=yS[:, :, :], func=AF.Square), reads=[yS], writes=[sqS])
            pb = nextPA()
            for g in range(2):
                for k in range(4):
                    S.add("pe", lambda e, g=g, k=k, pb=pb: e.matmul(pb[:, g * NB:(g + 1) * NB], lhsT=cONESB[:, :], rhs=sqS[:, 4 * g + k, :],
                                                                     start=(k == 0), stop=(k == 3)), reads=[cONESB, sqS], pwrites=[pb])
            S.add("act", lambda e, pb=pb: e.activation(out=stS[:, 0:2, :], in_=pb[:, 0:2 * NB].rearrange("p (g b) -> p g b", g=2), func=AF.Ln,
                                                        scale=1.0 / 512.0, bias=EPS), reads=[pb], pwrites=[stS])
            S.add("act", lambda e: e.activation(out=stS[:, 0:2, :], in_=stS[:, 0:2, :], func=AF.Exp, scale=-0.5), reads=[stS], pwrites=[stS])
            for g in range(2):
                S.add("dve", lambda e, g=g: e.tensor_tensor(out=t1S[:, 4 * g:4 * g + 4, :], in0=yS[:, 4 * g:4 * g + 4, :], in1=bcj(stS[:, g, :], 4), op=ALU.mult),
                      reads=[yS, stS], pwrites=[t1S])
            S.add("dve", lambda e: e.tensor_tensor(out=mixS[:, 0:8, :], in0=t1S[:, :, :], in1=bcb(pfm[:, PC("gn"):PC("gn") + 8]), op=ALU.mult),
                  reads=[t1S, pfm], pwrites=[mixS])
            stats_s(projS[:, 29:37, :], tmpw, 8, 1.0 / 1024.0)
            S.add("dve", lambda e: e.tensor_tensor(out=t1S[:, :, :], in0=projS[:, 29:37, :], in1=bcj(stS[:, 0, :], 8), op=ALU.subtract), reads=[tmpw, stS], writes=[t1S])
            S.add("dve", lambda e: e.tensor_tensor(out=t1S[:, :, :], in0=t1S[:, :, :], in1=bcj(stS[:, 2, :], 8), op=ALU.mult), reads=[t1S, stS], writes=[t1S])
            S.add("dve", lambda e: e.tensor_tensor(out=t1S[:, :, :], in0=t1S[:, :, :], in1=bcb(pfm[:, PC("lng"):PC("lng") + 8]), op=ALU.mult), reads=[t1S, pfm], writes=[t1S])
            S.add("dve", lambda e: e.tensor_tensor(out=t2S[:, :, :], in0=t1S[:, :, :], in1=bcb(pfm[:, PC("lnb"):PC("lnb") + 8]), op=ALU.add), reads=[t1S, pfm], writes=[t2S])
            S.dma("sp", o_gv_s, t2S[:, :, :].rearrange("p j b -> p (j b)"), reads=[t2S], pwrites=[outbuf], group=t2S)
            S.add("dve", lambda e: e.tensor_tensor(out=t1S[:, :, :], in0=t2S[:, :, :], in1=bcb(pfm[:, PC("w00"):PC("w00") + 8]), op=ALU.mult), reads=[t2S, pfm], writes=[t1S])
            S.add("dve", lambda e: e.tensor_tensor(out=t1S[:, :, :], in0=t1S[:, :, :], in1=bcb(pfm[:, PC("b0"):PC("b0") + 8]), op=ALU.add), reads=[t1S, pfm], writes=[t1S])
            S.add("dve", lambda e: e.tensor_tensor(out=t1S[:, :, :], in0=t1S[:, :, :], in1=projS[:, 21:29, :], op=ALU.mult), reads=[t1S, tmpw], writes=[t1S])
            S.add("dve", lambda e: e.tensor_tensor(out=mixS[:, 8:16, :], in0=t1S[:, :, :], in1=projS[:, 37:45, :], op=ALU.mult), reads=[t1S, tmpw], pwrites=[mixS])
            outproj_s(w_out_e)
            rmsnorm_s(PC("no"), hnS[:, :, :], hnS)

        def sample_layer1():
            set_pa([P0, P1])
            hall31 = SZ[:, 0:31 * 8 * NB].rearrange("p (k j b) -> p k j b", k=31, j=8)
            hall4 = hallS[:, :, 0:8, :]
            S.add("dve", lambda e: e.tensor_tensor(out=hall31[:, 30, :, :], in0=projS[:, 0:8, :], in1=projS[:, 8:16, :], op=ALU.mult), reads=[tmpw], pwrites=[bigZ])
            S.dma("sp", o_ccv_s_new, hall31[:, 30, :, :].rearrange("p j b -> p (j b)"), reads=[bigZ], pwrites=[outbuf], group=bigZ)
            S.dma("sp", o_ccv_s_hist, st_ccv[:, 1:30, :], pwrites=[outbuf], group=outbuf)
            for gi in range(8):
                k0 = gi * 4
                nk = min(4, 30 - k0)
                stg_ap, stg_buf = (SA, bigA) if gi % 2 == 0 else (SM, mix)
                S.dma("sp", stg_ap[0:NB, 0:nk * 1024].rearrange("b (k c) -> b k c", k=nk), st_ccv[:, k0:k0 + nk, :], pwrites=[stg_buf], group=stg_buf)
                hist_to_fm(stg_ap, stg_buf, nk * 8, None, bigZ, bulk=lambda i, n, k0=k0, nk=nk: (hall31[:, k0:k0 + nk, :, :], "p (k j b) -> p k j b", dict(k=nk, j=8)))
            wv31 = pfm[:, PC("ccw"):PC("ccw") + 248].rearrange("p (k j) -> p k j", k=31).unsqueeze(3).broadcast_to([128, 31, 8, NB])
            S.add("dve", lambda e: e.tensor_tensor(out=hall31, in0=hall31, in1=wv31, op=ALU.mult), reads=[bigZ, pfm], writes=[bigZ])
            S.add("dve", lambda e: e.tensor_reduce(out=convS[:, 0:8, :], in_=hall31.rearrange("p k j b -> p j b k"), axis=AX.X, op=ALU.add),
                  reads=[bigZ], writes=[convS])
            S.add("dve", lambda e: e.tensor_tensor(out=convS[:, 0:8, :], in0=convS[:, 0:8, :], in1=bcb(pfm[:, PC("ccb"):PC("ccb") + 8]), op=ALU.add),
                  reads=[convS, pfm], writes=[convS])
            stats_s(convS[:, 0:8, :], convS, 8, 1.0 / 1024.0)
            S.add("dve", lambda e: e.tensor_tensor(out=t1S[:, :, :], in0=convS[:, 0:8, :], in1=bcj(stS[:, 0, :], 8), op=ALU.subtract), reads=[convS, stS], writes=[t1S])
            S.add("dve", lambda e: e.tensor_tensor(out=t1S[:, :, :], in0=t1S[:, :, :], in1=bcj(stS[:, 2, :], 8), op=ALU.mult), reads=[t1S, stS], writes=[t1S])
            S.add("dve", lambda e: e.tensor_tensor(out=t1S[:, :, :], in0=t1S[:, :, :], in1=bcb(pfm[:, PC("cclg"):PC("cclg") + 8]), op=ALU.mult), reads=[t1S, pfm], writes=[t1S])
            S.add("dve", lambda e: e.tensor_tensor(out=t1S[:, :, :], in0=t1S[:, :, :], in1=bcb(pfm[:, PC("cclb"):PC("cclb") + 8]), op=ALU.add), reads=[t1S, pfm], writes=[t1S])
            S.add("act", lambda e: e.activation(out=t1S[:, :, :], in_=t1S[:, :, :], func=AF.Silu), reads=[t1S], writes=[t1S])
            S.add("dve", lambda e: e.tensor_tensor(out=mixS[:, 0:8, :], in0=t1S[:, :, :], in1=projS[:, 16:24, :], op=ALU.mult), reads=[t1S, tmpw], pwrites=[mixS])
            S.dma("sp", SA[0:NB, 0:3072].rearrange("b (k c) -> b k c", k=3), st_lconv[:, :, :], pwrites=[bigA], group=bigA)
            hist_to_fm(SA, bigA, 24, None, bsbc, bulk=lambda i, n: (hall4[:, 0:3, :, :], "p (k j b) -> p k j b", dict(k=3, j=8)))
            S.add("dve", lambda e: e.tensor_copy(out=hall4[:, 3, :, :], in_=projS[:, 24:32, :]), reads=[tmpw], pwrites=[bsbc])
            S.dma("sp", o_lconv_s_new, projS[:, 24:32, :].rearrange("p j b -> p (j b)"), reads=[tmpw], pwrites=[outbuf], group=tmpw)
            S.dma("sp", o_lconv_s_hist, st_lconv[:, 1:3, :], pwrites=[outbuf], group=outbuf)
            wl4 = pfm[:, PC("lcw"):PC("lcw") + 32].rearrange("p (k j) -> p k j", k=4).unsqueeze(3).broadcast_to([128, 4, 8, NB])
            S.add("dve", lambda e: e.tensor_tensor(out=hall4, in0=hall4, in1=wl4, op=ALU.mult), reads=[bsbc, pfm], writes=[bsbc])
            S.add("dve", lambda e: e.tensor_reduce(out=xcS[:, 0:8, :], in_=hall4.rearrange("p k j b -> p j b k"), axis=AX.X, op=ALU.add), reads=[bsbc], writes=[xcS])
            S.add("dve", lambda e: e.tensor_tensor(out=xcS[:, 0:8, :], in0=xcS[:, 0:8, :], in1=bcb(pfm[:, PC("lcb"):PC("lcb") + 8]), op=ALU.add), reads=[xcS, pfm], writes=[xcS])
            S.add("dve", lambda e: e.tensor_copy(out=hnS[:, :, :], in_=xcS[:, 0:8, :]), reads=[xcS], writes=[hnS])
            for q, (boff, bcol) in enumerate(((0, "lba"), (8, "lbx"))):
                pb = nextPA()
                for j in range(8):
                    S.add("pe", lambda e, j=j, boff=boff, pb=pb: e.matmul(pb[:, j * NB:(j + 1) * NB], lhsT=MT_bf[:, boff + j, :], rhs=hnS[:, j, :], start=True, stop=True),
                          reads=[MT_bf, hnS], pwrites=[pb])
                dst = t1S if q == 0 else t2S
                S.add("dve", lambda e, pb=pb, dst=dst, bcol=bcol: e.tensor_tensor(out=dst[:, :, :], in0=pb[:, 0:8 * NB].rearrange("p (j b) -> p j b", j=8),
                                                                                  in1=bcb(pfm[:, PC(bcol):PC(bcol) + 8]), op=ALU.add), reads=[pb, pfm], writes=[dst])
                S.add("act", lambda e, dst=dst: e.activation(out=dst[:, :, :], in_=dst[:, :, :], func=AF.Sigmoid), reads=[dst], writes=[dst])
            S.add("dve", lambda e: e.tensor_tensor(out=t1S[:, :, :], in0=t1S[:, :, :], in1=bcb(sp8[:, :]), op=ALU.mult), reads=[t1S, sp8], writes=[t1S])
            S.add("act", lambda e: e.activation(out=t1S[:, :, :], in_=t1S[:, :, :], func=AF.Exp), reads=[t1S], writes=[t1S])
            S.add("dve", lambda e: e.tensor_tensor(out=yS[:, :, :], in0=t1S[:, :, :], in1=t1S[:, :, :], op=ALU.mult), reads=[t1S], writes=[yS])
            S.add("act", lambda e: e.activation(out=yS[:, :, :], in_=yS[:, :, :], func=AF.Sqrt, scale=-1.0, bias=1.0), reads=[yS], writes=[yS])
            S.add("dve", lambda e: e.tensor_tensor(out=t2S[:, :, :], in0=t2S[:, :, :], in1=xcS[:, 0:8, :], op=ALU.mult), reads=[t2S, xcS], writes=[t2S])
            S.add("dve", lambda e: e.tensor_tensor(out=t2S[:, :, :], in0=t2S[:, :, :], in1=yS[:, :, :], op=ALU.mult), reads=[t2S, yS], writes=[t2S])
            S.dma("sp", SM[0:NB, 0:1024], st_lru[:, :], pwrites=[mix], group=mix)
            hist_to_fm(SM, mix, 8, None, xsS, bulk=lambda i, n: (xsS[:, :, :], "p (j b) -> p j b", dict(j=8)))
            S.add("dve", lambda e: e.tensor_tensor(out=xsS[:, :, :], in0=xsS[:, :, :], in1=t1S[:, :, :], op=ALU.mult), reads=[xsS, t1S], writes=[xsS])
            S.add("dve", lambda e: e.tensor_tensor(out=xsS[:, :, :], in0=xsS[:, :, :], in1=t2S[:, :, :], op=ALU.add), reads=[xsS, t2S], writes=[xsS])
            S.dma("sp", o_lru_s, xsS[:, :, :].rearrange("p j b -> p (j b)"), reads=[xsS], pwrites=[outbuf], group=xsS)
            S.add("dve", lambda e: e.tensor_tensor(out=mixS[:, 8:16, :], in0=xsS[:, :, :], in1=projS[:, 32:40, :], op=ALU.mult), reads=[xsS, tmpw], pwrites=[mixS])
            outproj_s(w_out_o)
            rmsnorm_s(PC("nf"), t2S[:, :, :], t2S)
            S.dma("sp", o_y_s, t2S[:, :, :].rearrange("p j b -> p (j b)"), reads=[t2S], pwrites=[outbuf], group=t2S)


        samp_tab = [None]

        def samp_cols(wv, col0, ncols):
            tab = samp_tab[0]
            if tab is None:
                return
            for jj in range(ncols // 128):
                c = col0 + jj * 128
                for (s_, e_, base, func) in tab:
                    if s_ <= c < e_:
                        cc = base + (c - s_) // 128
                        pb = nextPA()
                        for k in range(8):
                            S.add("pe", lambda e, k=k, jj=jj, wv=wv, pb=pb: e.matmul(pb[:, 0:NB], lhsT=wv[1][:, k, jj * 128:(jj + 1) * 128], rhs=hnS[:, k, :],
                                                                                      start=(k == 0), stop=(k == 7)), reads=[wv[0], hnS], writes=[pb])
                        S.add("act", lambda e, cc=cc, pb=pb, func=func: e.activation(out=projS[:, cc, :], in_=pb[:, 0:NB], func=func), reads=[pb], pwrites=[tmpw])

        def samp_dt(wv):
            if samp_tab[0] is None:
                return
            pb = nextPA()
            for k in range(8):
                S.add("pe", lambda e, k=k, wv=wv, pb=pb: e.matmul(pb[0:16, 0:NB], lhsT=wv[1][:, k, :], rhs=hnS[:, k, :], start=(k == 0), stop=(k == 7)),
                      reads=[wv[0], hnS], writes=[pb])
            S.add("act", lambda e, pb=pb: e.activation(out=dtS[:, 0, :], in_=pb[0:16, 0:NB], func=AF.Exp, bias=p16[:, 0:1]), reads=[pb, p16], pwrites=[dtS])
            S.add("act", lambda e: e.activation(out=dtS[:, 0, :], in_=dtS[:, 0, :], func=AF.Ln, bias=1.0), reads=[dtS], pwrites=[dtS])
            S.add("dve", lambda e: e.tensor_scalar(out=dtS[:, 1, :], in0=dtS[:, 0, :], scalar1=ea16[:, 0:1], scalar2=-1.0, op0=ALU.mult, op1=ALU.mult),
                  reads=[dtS, ea16], pwrites=[dtS])
            S.add("act", lambda e: e.activation(out=dtS[:, 1, :], in_=dtS[:, 1, :], func=AF.Exp), reads=[dtS], pwrites=[dtS])

        TAB_L0 = [(0, 1024, 0, AF.Silu), (1024, 2560, 8, AF.Copy), (2576, 3600, 21, AF.Gelu_apprx_tanh),
                  (3600, 4624, 29, AF.Gelu_apprx_tanh), (4624, 5648, 37, AF.Silu)]
        TAB_L1 = [(0, 1024, 0, AF.Copy), (1024, 2048, 8, AF.Sigmoid), (2048, 3072, 16, AF.Silu), (3072, 4096, 24, AF.Copy), (4096, 5120, 32, AF.Silu)]
        if stage >= 3:
            rmsnorm_s(PC("ne"), hnS[:, :, :], hnS)

        for ti in range(int(_os.environ.get('K_L0T', NT if stage >= 1 else 0))):
            samp_tab[0] = TAB_L0 if (stage >= 3 and ti == NT - 1) else None
            layer0_tile(ti)
            samp_tab[0] = None

        hT = sb("hT", [128, 8, 128])
        for j in range(8):
            pb = nextPA()
            S.add("pe", lambda e, j=j, pb=pb: e.transpose(pb[:, 0:128], Hst[:, j * 128:(j + 1) * 128], cIDF[:, :]), reads=[Hst, cIDF], writes=[pb])
            S.add("act", lambda e, j=j, pb=pb: e.activation(out=hT[:, j, :], in_=pb[:, 0:128], func=AF.Copy), reads=[pb], pwrites=[hT])
        S.dma("sp", o_ssm_p.rearrange("(j p) n -> p j n", p=128), hT[:, :, :], reads=[hT], pwrites=[outbuf], group=hT)
        S.dma("sp", o_sconv_p, lastraw[:, :, :].rearrange("p j k -> p (j k)"), reads=[lastraw], pwrites=[outbuf], group=lastraw)
        if stage >= 3:
            sample_layer0()
        GL = 544
        glu_view = xh_tm.t[:, :, :].rearrange("p a b -> p (a b)").bitcast(BF16)
        glu = [Buf(glu_view[:, j * GL:(j + 1) * GL], f"glu{j}") for j in range(8)]
        if not _os.environ.get('K_SKIP_BAR'):
            S.add("dve", lambda e: e.memset(glu_view[:, 0:8 * GL], 0.0), reads=[xh_tm], writes=glu + l1_barrier_reads)
        cvo = bigZ.t[:, :, :].rearrange("p a (c t) -> p (a c) t", t=512)
        v32 = vhat.t[:, :, :].rearrange("p a b -> p (a b)").bitcast(F32)
        mean_sb, var_sb, rstd_ln, mr_sb = v32[:, 0:512], v32[:, 512:1024], v32[:, 1024:1536], v32[:, 1536:2048]
        if not _os.environ.get('K_SKIP_MT'):
            S.dma("pool", MT_bf[:, 0:8, :], d_bda.rearrange("p (j q) -> p j q", j=8), pwrites=[MT_bf], group=MT_bf)
            S.dma("pool", MT_bf[:, 8:16, :], d_bdx.rearrange("p (j q) -> p j q", j=8), pwrites=[MT_bf], group=MT_bf)
        sp8 = sb("sp8", [128, 8])
        sp16 = sb("sp16", [128, 8])
        hist1 = sb("hist1", [128, 8, 3], BF16)
        hcarry = sb("hcarry", [128, 8])
        glu_last = sb("glu_last", [128, 8, 30])
        lastraw1 = sb("lastraw1", [128, 8, 4])
        lamc = PC("lam")
        S.add("act", lambda e: e.activation(out=sp8[:, :], in_=pfm[:, lamc:lamc + 8], func=AF.Exp, scale=-1.0), reads=[pfm], writes=[sp8])
        S.add("act", lambda e: e.activation(out=sp8[:, :], in_=sp8[:, :], func=AF.Ln, bias=1.0), reads=[sp8], writes=[sp8])
        S.add("dve", lambda e: e.tensor_scalar(out=sp16[:, :], in0=sp8[:, :], scalar1=-16.0, scalar2=None, op0=ALU.mult), reads=[sp8], writes=[sp16])
        S.add("dve", lambda e: e.tensor_scalar(out=sp8[:, :], in0=sp8[:, :], scalar1=-8.0, scalar2=None, op0=ALU.mult), reads=[sp8], writes=[sp8])
        S.add("dve", lambda e: e.memset(hist1[:, :, :], 0.0), writes=[hist1])
        S.add("dve", lambda e: e.memset(hcarry[:, :], 0.0), writes=[hcarry])
        Lflat = Lbuf.t[:, :, :].rearrange("p a b -> p (a b)")
        t_xc, t_r = yoff[:, 0:512], yoff[:, 512:1024]
        t_i, t_a = ysb[:, 0:512], ysb[:, 512:1024]
        t_b, t_h = Hst[:, 0:512], Hst[:, 512:1024]
        t_g, t_m = Lflat[:, 0:512], Lflat[:, 512:1024]
        xc_bf = xs_bf[:, 0:512]
        SZl = bigZ.t[:, :, :].rearrange("p a b -> p (a b)")
        lzv = [SZl[:, i * 512:(i + 1) * 512] for i in range(8)]
        lz = [Buf(lzv[i], f"lz{i}") for i in range(8)]
        lbar = sb("lbar", [128, 2])

        l1_pieces = [t_xc, t_r, t_i, t_a, t_b, t_h, t_g, t_m]
        l1_pbufs = [yoff, yoff, ysb, ysb, Hst, Hst, Lbuf, Lbuf]
        l1_sq = vhat.t[:, :, :].rearrange("p a (c t) -> p (a c) t", t=512)

        def l1_load(ti):
            for k in range(8):
                S.dma("sp", l1_pieces[k], x1T[k * 128:(k + 1) * 128, ti * TT:(ti + 1) * TT], reads=[x1buf], pwrites=[l1_pbufs[k]], group=l1_pbufs[k])

        def layer1_tile(ti):
            t0 = ti * TT
            last = (ti == NT - 1)
            S.dma("sp", bigA[:, :, :], (xT if _os.environ.get("K_NOX1") else x1T)[:, t0:t0 + TT].rearrange("(k p) t -> p k t", p=128), reads=[x1buf], writes=[bigA], group=bigA)
            set_pa([P0, P1, P7, P3, P4])
            if ti == 0:
                l1_load(0)
                norm_sq(l1_pieces, l1_pbufs, l1_sq, vhat)
                norm_rest(l1_pieces, l1_pbufs, l1_sq, vhat, PC("no"))
            if stage <= 1.1:
                return
            def conv_chunk(j, jj, wa_, wb_):
                pA = nextPA()
                projA(wa_, jj, hn, pA)
                pB = nextPA()
                projA(wb_, jj, hn, pB)
                sgb = sg[j % 2]
                gj = glu[j]
                PC2 = (P2, P5, P6)[j % 3]
                yield
                S.add("act", lambda e, sgb=sgb, pB=pB: e.activation(out=sgb[:, :], in_=pB[:, :], func=AF.Sigmoid), reads=[pB], writes=[sgb])
                S.add("dve", lambda e, gj=gj, pA=pA, sgb=sgb: e.tensor_tensor(out=gj[:, 30:542], in0=pA[:, :], in1=sgb[:, :], op=ALU.mult),
                      reads=[pA, sgb], pwrites=[gj])
                if last:
                    S.add("dve", lambda e, j=j, pA=pA, sgb=sgb: e.tensor_tensor(out=glu_last[:, j, :], in0=pA[:, 482:512], in1=sgb[:, 482:512], op=ALU.mult),
                          reads=[pA, sgb], pwrites=[glu_last])
                yield
                for k in range(31):
                    dg = diag(PC("ccw") + k * 8 + j, "dve" if k % 4 != 3 else "act")
                    S.add("pe", lambda e, k=k, dg=dg, gj=gj, PC2=PC2: e.matmul(PC2[:, :], lhsT=dg[:, :], rhs=gj[:, k:k + 512], start=(k == 0), stop=(k == 30)),
                          reads=[dg, gj], writes=[PC2])
                    if k in (9, 19):
                        yield
                yield
                bc_ = PC("ccb") + j
                S.add("act", lambda e, j=j, bc_=bc_, PC2=PC2: e.activation(out=cvo[:, j, :], in_=PC2[:, :], func=AF.Identity, bias=pfm[:, bc_:bc_ + 1]),
                      reads=[PC2, pfm], pwrites=[bigZ])
                S.add("dve", lambda e, gj=gj: e.tensor_copy(out=gj[:, 0:30], in_=gj[:, 512:542]), reads=[gj], pwrites=[gj])

            conv_w = {}

            def conv_weights(half):
                if half not in conv_w:
                    wa_ = wload(w_in_o[:, half * 512:(half + 1) * 512], 8, 512)
                    samp_cols(wa_, half * 512, 512)
                    wb_ = wload(w_in_o[:, 1024 + half * 512:1024 + (half + 1) * 512], 8, 512)
                    samp_cols(wb_, 1024 + half * 512, 512)
                    conv_w[half] = (wa_, wb_)
                return conv_w[half]

            active, nextj = [], 0
            while active or nextj < 8:
                if len(active) < 2 and nextj < 8:
                    wa_, wb_ = conv_weights(nextj // 4)
                    active.append(conv_chunk(nextj, nextj % 4, wa_, wb_))
                    nextj += 1
                for g_ in list(active):
                    if next(g_, "END") == "END":
                        active.remove(g_)
            S.add("act", lambda e: e.activation(out=mix[:, 0:8, :], in_=cvo, func=AF.Square), reads=[bigZ], writes=[mix])
            S.add("dve", lambda e: e.tensor_copy(out=mix[:, 8:16, :], in_=cvo), reads=[bigZ], pwrites=[mix])
            for k in range(8):
                S.add("pe", lambda e, k=k: e.matmul(P5[:, :], lhsT=cONESB[:, :], rhs=mix[:, 8 + k, :], start=(k == 0), stop=(k == 7)),
                      reads=[cONESB, mix], writes=[P5])
            for k in range(8):
                S.add("pe", lambda e, k=k: e.matmul(P6[:, :], lhsT=cONESB[:, :], rhs=mix[:, k, :], start=(k == 0), stop=(k == 7)),
                      reads=[cONESB, mix], writes=[P6])
            S.add("dve", lambda e: e.tensor_scalar(out=mean_sb, in0=P5[:, :], scalar1=1.0 / 1024.0, scalar2=None, op0=ALU.mult), reads=[P5], pwrites=[vhat])
            S.add("dve", lambda e: e.tensor_tensor(out=var_sb, in0=mean_sb, in1=mean_sb, op=ALU.mult), reads=[vhat], pwrites=[vhat])
            S.add("dve", lambda e: e.scalar_tensor_tensor(out=var_sb, in0=P6[:, :], scalar=1.0 / 1024.0, in1=var_sb, op0=ALU.mult, op1=ALU.subtract),
                  reads=[P6, vhat], pwrites=[vhat])
            S.add("act", lambda e: e.activation(out=rstd_ln, in_=var_sb, func=AF.Ln, bias=EPS), reads=[vhat], pwrites=[vhat])
            S.add("act", lambda e: e.activation(out=rstd_ln, in_=rstd_ln, func=AF.Exp, scale=-0.5), reads=[vhat], pwrites=[vhat])
            S.add("dve", lambda e: e.tensor_tensor(out=mr_sb, in0=mean_sb, in1=rstd_ln, op=ALU.mult), reads=[vhat], pwrites=[vhat])
            if stage <= 1.3:
                return
            for half in range(2):
                wv = wload(w_in_o[:, 2048 + half * 512:2048 + (half + 1) * 512], 8, 512)
                samp_cols(wv, 2048 + half * 512, 512)
                for jj in range(4):
                    j = half * 4 + jj
                    pG = nextPA()
                    projA(wv, jj, hn, pG)
                    sgb = sg[j % 2]
                    tb = xcf[j % 2]
                    S.add("act", lambda e, sgb=sgb, pG=pG: e.activation(out=sgb[:, :], in_=pG[:, :], func=AF.Silu), reads=[pG], writes=[sgb])
                    S.add("dve", lambda e, j=j, tb=tb: e.tensor_tensor(out=tb[:, :], in0=cvo[:, j, :], in1=rstd_ln, op=ALU.mult), reads=[bigZ, vhat], writes=[tb])
                    S.add("dve", lambda e, tb=tb: e.tensor_tensor(out=tb[:, :], in0=tb[:, :], in1=mr_sb, op=ALU.subtract), reads=[tb, vhat], writes=[tb])
                    gc_, bc_ = PC("cclg") + j, PC("cclb") + j
                    S.add("act", lambda e, tb=tb, gc_=gc_, bc_=bc_: e.activation(out=tb[:, :], in_=tb[:, :], func=AF.Silu, scale=pfm[:, gc_:gc_ + 1],
                                                                                 bias=pfm[:, bc_:bc_ + 1]), reads=[tb, pfm], writes=[tb])
                    S.add("dve", lambda e, j=j, tb=tb, sgb=sgb: e.tensor_tensor(out=mix[:, j, :], in0=tb[:, :], in1=sgb[:, :], op=ALU.mult),
                          reads=[tb, sgb], pwrites=[mix])
            if stage <= 1.4:
                return
            set_pa([P0, P1])
            S.add("dve", lambda e: e.memset(lbar[:, :], 0.0), writes=[bigZ, lbar] + lz)
            def lru_chunk(j, jj, wx_, wg_):
                od = j % 2
                if od == 0:
                    V = dict(xc=t_xc, r=t_r, i=t_i, a=t_a, b=t_b, h=t_h, g=t_g, m=t_m, xb=xc_bf)
                    Bf = dict(xc=yoff, r=yoff, i=ysb, a=ysb, b=Hst, h=Hst, g=Lbuf, m=Lbuf, xb=xs_bf)
                    PCV, PGA, PGX = P2, P3, P4
                else:
                    V = dict(xc=lzv[0], r=lzv[1], i=lzv[2], a=lzv[3], b=lzv[4], h=lzv[5], g=lzv[6], m=lzv[7], xb=xsd_bf[:, 0:512])
                    Bf = dict(xc=lz[0], r=lz[1], i=lz[2], a=lz[3], b=lz[4], h=lz[5], g=lz[6], m=lz[7], xb=xsd_bf)
                    PCV, PGA, PGX = P5, P6, P7
                PGAv = PGA.t if PGA is not P7 else P7t
                PGXv = PGX.t if PGX is not P7 else P7t
                pX = nextPA()
                projA(wx_, jj, hn, pX)
                raw = raws[j % 2]
                S.add("dve", lambda e, j=j, raw=raw: e.tensor_copy(out=raw[:, 0:3], in_=hist1[:, j, :]), reads=[hist1], pwrites=[raw])
                S.add("act", lambda e, raw=raw, pX=pX: e.activation(out=raw[:, 3:515], in_=pX[:, :], func=AF.Copy), reads=[pX], pwrites=[raw])
                if last:
                    S.add("act", lambda e, j=j, pX=pX: e.activation(out=lastraw1[:, j, :], in_=pX[:, 508:512], func=AF.Copy), reads=[pX], pwrites=[lastraw1])
                S.add("dve", lambda e, j=j, raw=raw: e.tensor_copy(out=hist1[:, j, :], in_=raw[:, 512:515]), reads=[raw], pwrites=[hist1])
                yield
                for k in range(4):
                    dg = diag(PC("lcw") + k * 8 + j, "dve")
                    S.add("pe", lambda e, k=k, dg=dg, raw=raw, PCV=PCV: e.matmul(PCV[:, :], lhsT=dg[:, :], rhs=raw[:, k:k + 512], start=(k == 0), stop=(k == 3)),
                          reads=[dg, raw], writes=[PCV])
                bc_ = PC("lcb") + j
                S.add("act", lambda e, bc_=bc_, V=V, PCV=PCV: e.activation(out=V["xc"], in_=PCV[:, :], func=AF.Identity, bias=pfm[:, bc_:bc_ + 1]),
                      reads=[PCV, pfm], pwrites=[Bf["xc"]])
                S.add("dve", lambda e, V=V: e.tensor_copy(out=V["xb"], in_=V["xc"]), reads=[Bf["xc"]], writes=[Bf["xb"]])
                yield
                S.add("pe", lambda e, j=j, V=V, PGAv=PGAv: e.matmul(PGAv[:, :], lhsT=MT_bf[:, j, :], rhs=V["xb"], start=True, stop=True), reads=[MT_bf, Bf["xb"]], writes=[PGA])
                S.add("pe", lambda e, j=j, V=V, PGXv=PGXv: e.matmul(PGXv[:, :], lhsT=MT_bf[:, 8 + j, :], rhs=V["xb"], start=True, stop=True), reads=[MT_bf, Bf["xb"]], writes=[PGX])
                ca, cx = PC("lba") + j, PC("lbx") + j
                pGd = nextPA()
                projA(wg_, jj, hn, pGd)
                yield
                S.add("act", lambda e, ca=ca, V=V, PGAv=PGAv: e.activation(out=V["r"], in_=PGAv[:, :], func=AF.Sigmoid, bias=pfm[:, ca:ca + 1]), reads=[PGA, pfm], pwrites=[Bf["r"]])
                S.add("act", lambda e, cx=cx, V=V, PGXv=PGXv: e.activation(out=V["i"], in_=PGXv[:, :], func=AF.Sigmoid, bias=pfm[:, cx:cx + 1]), reads=[PGX, pfm], pwrites=[Bf["i"]])
                S.add("act", lambda e, pGd=pGd, V=V: e.activation(out=V["g"], in_=pGd[:, :], func=AF.Sigmoid), reads=[pGd], pwrites=[Bf["g"]])
                S.add("dve", lambda e, pGd=pGd, V=V: e.tensor_tensor(out=V["g"], in0=pGd[:, :], in1=V["g"], op=ALU.mult), reads=[pGd, Bf["g"]], pwrites=[Bf["g"]])
                yield
                S.add("act", lambda e, j=j, V=V: e.activation(out=V["a"], in_=V["r"], func=AF.Exp, scale=sp8[:, j:j + 1]), reads=[Bf["r"], sp8], pwrites=[Bf["a"]])
                S.add("act", lambda e, j=j, V=V: e.activation(out=V["m"], in_=V["r"], func=AF.Exp, scale=sp16[:, j:j + 1]), reads=[Bf["r"], sp16], pwrites=[Bf["m"]])
                S.add("act", lambda e, V=V: e.activation(out=V["m"], in_=V["m"], func=AF.Ln, scale=-1.0, bias=1.0), reads=[Bf["m"]], pwrites=[Bf["m"]])
                S.add("act", lambda e, V=V: e.activation(out=V["m"], in_=V["m"], func=AF.Exp, scale=0.5), reads=[Bf["m"]], pwrites=[Bf["m"]])
                yield
                S.add("dve", lambda e, V=V: e.tensor_tensor(out=V["b"], in0=V["i"], in1=V["xc"], op=ALU.mult), reads=[Bf["i"], Bf["xc"]], pwrites=[Bf["b"]])
                S.add("dve", lambda e, V=V: e.tensor_tensor(out=V["b"], in0=V["b"], in1=V["m"], op=ALU.mult), reads=[Bf["b"], Bf["m"]], pwrites=[Bf["b"]])
                S.add("dve", lambda e, j=j, V=V: e.tensor_tensor_scan(out=V["h"], data0=V["a"], data1=V["b"], initial=hcarry[:, j:j + 1], op0=ALU.mult, op1=ALU.add),
                      reads=[Bf["a"], Bf["b"], hcarry], pwrites=[Bf["h"]])
                S.add("dve", lambda e, j=j, V=V: e.tensor_copy(out=hcarry[:, j:j + 1], in_=V["h"][:, 511:512]), reads=[Bf["h"]], pwrites=[hcarry])
                S.add("dve", lambda e, j=j, V=V: e.tensor_tensor(out=mix[:, 8 + j, :], in0=V["h"], in1=V["g"], op=ALU.mult), reads=[Bf["h"], Bf["g"]], pwrites=[mix])

            lru_w = {}

            def lru_weights(half):
                if half not in lru_w:
                    wx_ = wload(w_in_o[:, 3072 + half * 512:3072 + (half + 1) * 512], 8, 512)
                    samp_cols(wx_, 3072 + half * 512, 512)
                    wg_ = wload(w_in_o[:, 4096 + half * 512:4096 + (half + 1) * 512], 8, 512)
                    samp_cols(wg_, 4096 + half * 512, 512)
                    lru_w[half] = (wx_, wg_)
                return lru_w[half]

            active, nextj = [], 0
            while active or nextj < 8:
                if len(active) < 2 and nextj < 8:
                    wx_, wg_ = lru_weights(nextj // 4)
                    active.append(lru_chunk(nextj, nextj % 4, wx_, wg_))
                    nextj += 1
                for g_ in list(active):
                    if next(g_, "END") == "END":
                        active.remove(g_)
            S.add("dve", lambda e: e.memset(lbar[:, :], 0.0), writes=[bigZ, lbar] + lz)
            if ti + 1 < NT:
                l1_load(ti + 1)
                norm_sq(l1_pieces, l1_pbufs, l1_sq, vhat)
            for ob in range(4):
                if ob == 2 and ti + 1 < NT:
                    norm_rest(l1_pieces, l1_pbufs, l1_sq, vhat, PC("no"))
                wv = wload(w_out_o[:, ob * 256:(ob + 1) * 256], 16, 256)
                for dj2 in range(2):
                    dj = ob * 2 + dj2
                    pb = nextPA()
                    for ek in range(16):
                        S.add("pe", lambda e, ek=ek, dj2=dj2, wv=wv, pb=pb: e.matmul(pb[:, :], lhsT=wv[1][:, ek, dj2 * 128:(dj2 + 1) * 128],
                                                                                      rhs=mix[:, ek, :], start=(ek == 0), stop=(ek == 15)),
                              reads=[wv[0], mix], writes=[pb])
                    S.add("dve", lambda e, dj=dj, pb=pb: e.tensor_tensor(out=bigA[:, dj, :], in0=pb[:, :], in1=bigA[:, dj, :], op=ALU.add),
                          reads=[pb, bigA], pwrites=[bigA])
            if stage <= 1.6:
                return
            rmsnorm_fm(bigA, PC("nf"), bigZ, dst_view=cvo)
            S.dma("sp", yT[:, t0:t0 + TT].rearrange("(k p) t -> p k t", p=128), cvo, reads=[bigZ], pwrites=[outbuf], group=bigZ)

        if stage > 1:
            for ti in range(int(_os.environ.get('K_L1T', NT))):
                samp_tab[0] = TAB_L1 if (stage >= 3 and ti == NT - 1) else None
                layer1_tile(ti)
                samp_tab[0] = None
            S.dma("sp", o_ccv_p, glu_last[:, :, :].rearrange("p j k -> p (j k)"), reads=[glu_last], pwrites=[outbuf], group=glu_last)
            S.dma("sp", o_lconv_p, lastraw1[:, :, :].rearrange("p j k -> p (j k)"), reads=[lastraw1], pwrites=[outbuf], group=lastraw1)
            S.dma("sp", o_lru_p, hcarry[:, :], reads=[hcarry], pwrites=[outbuf], group=hcarry)
        if stage >= 3:
            sample_layer1()
        final_reads = [outbuf, bigA]
        S.add("sp", lambda e: e.nop(), reads=final_reads, writes=final_reads)
        S.finalize_and_emit(es)
    return nc


def _consts():
    idf = np.eye(128, dtype=np.float32)
    s = np.arange(128)
    negm = np.where(s[None, :] >= s[:, None], 0.0, -1.0e5).astype(np.float32)
    triu = (s[:, None] <= s[None, :]).astype(np.float32)
    sel16 = np.zeros((16, 16, 128), np.float32)
    for h in range(16):
        sel16[h, h, :] = 1.0
    sellast = np.zeros((128, 128), np.float32)
    sellast[127, :] = 1.0
    return dict(c_idf=idf, c_negm=negm, c_triu=triu, c_sel16=sel16.reshape(16, 2048), c_sellast=sellast)


def _prepare(inp):
    f = lambda a: np.ascontiguousarray(np.asarray(a, np.float32))
    shared = dict(
        w_in_e=f(inp["w_in_even"][0]), w_out_e=f(inp["w_out_even"][0]),
        w_in_o=f(inp["w_in_odd"][0]), w_out_o=f(inp["w_out_odd"][0]),
        pfm=_build_pfm(inp),
        p16=f(np.stack([inp["ssd_dt_bias"][0], inp["ssd_a_log"][0]], 1)),
        drow=f(inp["ssd_d"][0][None, :]),
        wsT=f(np.transpose(inp["gmlp_w_s"][0], (2, 0, 1)).reshape(128, 1024)),
        bsrow=f(inp["gmlp_b_s"][0].reshape(1, 1024)),
    )
    def _bd(w):
        w = np.asarray(w, np.float32)
        o = np.zeros((128, 8, 128), np.float32)
        for j in range(8):
            o[0:64, j, 0:64] = w[2 * j]
            o[64:128, j, 64:128] = w[2 * j + 1]
        return np.ascontiguousarray(o.reshape(128, 1024))
    shared["bda"] = _bd(inp["lru_wa"][0])
    shared["bdx"] = _bd(inp["lru_wx"][0])
    cexp = np.zeros((16, 8, 128), np.float32)
    for j in range(8):
        cexp[2 * j, j, 0:64] = 1.0
        cexp[2 * j + 1, j, 64:128] = 1.0
    shared["c_exp"] = cexp.reshape(16, 1024)
    shared.update(_consts())
    maps = []
    for c in range(NCORES):
        m = dict(shared)
        m["xT"] = f(inp["x_prompt"][c].T)
        sl = slice(c * NB, (c + 1) * NB)
        m["xsT"] = f(inp["x_sample"][sl, 0, :].T)
        m["st_ssm"] = f(inp["state_ssm"][0, sl].reshape(NB, 1024, 128))
        m["st_sconv"] = f(inp["state_ssd_conv"][0, sl])
        m["st_ccv"] = f(inp["state_ccv"][0, sl])
        m["st_lconv"] = f(inp["state_lru_conv"][0, sl])
        m["st_lru"] = f(inp["state_lru"][0, sl])
        maps.append(m)
    return maps


_NC_CACHE = {}


def _run(inp, stage=99):
    maps = _prepare(inp)
    if stage not in _NC_CACHE:
        _NC_CACHE[stage] = build_program(stage)
    nc = _NC_CACHE[stage]
    res = run_bass_kernel_spmd(nc, maps, core_ids=list(range(NCORES)))
    return res.results


def _fm2tm(a, nch):
    return np.ascontiguousarray(a.reshape(128, nch, NB).transpose(2, 1, 0).reshape(NB, nch * 128))


def _fmlast(a, nch, k, keep):
    return np.ascontiguousarray(a.reshape(128, nch, k)[:, :, k - keep:].transpose(2, 1, 0).reshape(keep, nch * 128))


def kernel(**inputs):
    res = _run(inputs, 99)
    B = NCORES
    y_p = np.stack([np.ascontiguousarray(r["yT"].T) for r in res])
    y_s = np.concatenate([_fm2tm(r["o_y_s"], 8) for r in res])[:, None, :]
    ssm_p = np.stack([r["o_ssm_p"].reshape(16, 64, 128) for r in res])[None]
    ssm_s = np.concatenate([r["o_ssm_s"].reshape(NB, 16, 64, 128) for r in res])[None]
    sconv_p = np.stack([_fmlast(r["o_sconv_p"], 12, 4, 3) for r in res])[None]
    sconv_s = np.concatenate([np.concatenate([r["o_sconv_s_hist"], _fm2tm(r["o_sconv_s_new"], 12)[:, None, :]], 1) for r in res])[None]
    gv_s = np.concatenate([_fm2tm(r["o_gv_s"], 8) for r in res])[None, :, None, :]
    ccv_p = np.stack([_fmlast(r["o_ccv_p"], 8, 30, 30) for r in res])[None]
    ccv_s = np.concatenate([np.concatenate([r["o_ccv_s_hist"], _fm2tm(r["o_ccv_s_new"], 8)[:, None, :]], 1) for r in res])[None]
    lconv_p = np.stack([_fmlast(r["o_lconv_p"], 8, 4, 3) for r in res])[None]
    lconv_s = np.concatenate([np.concatenate([r["o_lconv_s_hist"], _fm2tm(r["o_lconv_s_new"], 8)[:, None, :]], 1) for r in res])[None]
    lru_p = np.stack([np.ascontiguousarray(r["o_lru_p"].T).reshape(1024) for r in res])[None]
    lru_s = np.concatenate([_fm2tm(r["o_lru_s"], 8) for r in res])[None]
    outs = (y_p, y_s, ssm_p, ssm_s, sconv_p, sconv_s, gv_s, ccv_p, ccv_s, lconv_p, lconv_s, lru_p, lru_s)
    return tuple(np.ascontiguousarray(o.astype(np.float32)) for o in outs)
```
